# Optimizing a Trainium2 kernel written in Bass

```python
import jax
import jax.numpy as jnp
from jax import lax
import numpy as np

D_MODEL = 2048
BATCH = 2
SEQ = 8192
DEPTH = 1
DEC_BATCH = 16
DEC_SEQ = 2048
PAST_LEN = 128

HEAD_DIM = 128
DN_HEADS = 8
DN_WIDTH = DN_HEADS * HEAD_DIM
DN_CONV = 5
DN_CHUNK = 64
ATTN_GROUPS = ((128, 1), (512, 4), (2048, 16))
N_GROUPS = 3
ATTN_HEADS_PER_GROUP = 4
ATTN_HEADS = N_GROUPS * ATTN_HEADS_PER_GROUP
ATTN_WIDTH = ATTN_HEADS * HEAD_DIM
ATTN_OUT_WIDTH = ATTN_HEADS_PER_GROUP * HEAD_DIM
ROPE_THETA = 10000.0
D_FF = 4 * D_MODEL
N_BRANCH = 2
N_MOD = 6
NORM_EPS = 1e-6
MASK_VALUE = -1e30
IN_SPLIT_SIZES = (3 * DN_WIDTH, DN_WIDTH, 2 * DN_HEADS, 2 * DN_HEADS, 3 * ATTN_WIDTH, N_BRANCH * D_MODEL)
IN_COLS = 3 * DN_WIDTH + DN_WIDTH + 2 * DN_HEADS + 2 * DN_HEADS + 3 * ATTN_WIDTH + N_BRANCH * D_MODEL

kernel_name = 'hybrid_deltanet_dilated_attn_encoder'

F32 = jnp.float32


def rmsnorm(x, w):
    xf = x.astype(F32)
    y = xf * lax.rsqrt(jnp.mean(xf * xf, axis=-1, keepdims=True) + NORM_EPS) * w.astype(F32)
    return y.astype(x.dtype)


def l2norm(t):
    return t * lax.rsqrt(jnp.sum(t * t, axis=-1, keepdims=True) + NORM_EPS)


def rotary(x, pos):
    half = HEAD_DIM // 2
    inv_freq = ROPE_THETA ** (-jnp.arange(half, dtype=F32) / half)
    ang = pos.astype(F32)[:, None] * inv_freq[None, :]
    cos = jnp.cos(ang)[None, :, None, :]
    sin = jnp.sin(ang)[None, :, None, :]
    xf = x.astype(F32)
    x1, x2 = xf[..., :half], xf[..., half:]
    return jnp.concatenate([x1 * cos - x2 * sin, x1 * sin + x2 * cos], axis=-1)


def centred_depthwise_conv(x, w):
    C = x.shape[-1]
    pad = DN_CONV // 2
    return lax.conv_general_dilated(
        x, w.astype(x.dtype)[:, None, :], window_strides=(1,), padding=((pad, pad),),
        dimension_numbers=('NWC', 'WIO', 'NWC'), feature_group_count=C)


def gated_delta_chunked(q, k, v, g, beta):
    B, H, L, Dk = k.shape
    Dv = v.shape[-1]
    C = DN_CHUNK
    N = L // C
    q = q.reshape(B, H, N, C, Dk)
    k = k.reshape(B, H, N, C, Dk)
    v = v.reshape(B, H, N, C, Dv)
    g = g.reshape(B, H, N, C)
    beta = beta.reshape(B, H, N, C)
    G = jnp.cumsum(g, axis=-1)
    tri = jnp.tril(jnp.ones((C, C), dtype=bool))
    strict = jnp.tril(jnp.ones((C, C), dtype=bool), -1)
    diff = G[..., :, None] - G[..., None, :]
    decay = jnp.where(tri, jnp.exp(jnp.where(tri, diff, 0.0)), 0.0)
    kb = k * beta[..., None]
    lmat = jnp.where(strict, jnp.einsum('bhnid,bhnjd->bhnij', kb, k) * decay, 0.0)
    eye = jnp.eye(C, dtype=F32)
    tmat = lax.linalg.triangular_solve(lmat, jnp.broadcast_to(eye, lmat.shape),
                                       left_side=True, lower=True, unit_diagonal=True)
    u = jnp.einsum('bhnij,bhnjv->bhniv', tmat, v * beta[..., None])
    w = jnp.einsum('bhnij,bhnjk->bhnik', tmat, kb * jnp.exp(G)[..., None])
    qk = jnp.where(tri, jnp.einsum('bhnid,bhnjd->bhnij', q, k) * decay, 0.0)
    qg = q * jnp.exp(G)[..., None]
    kdec = k * jnp.exp(G[..., -1:] - G)[..., None]
    glast = jnp.exp(G[..., -1])

    def step(S, xs):
        u_c, w_c, qk_c, qg_c, kd_c, gl_c = xs
        v_new = u_c - jnp.einsum('bhck,bhkv->bhcv', w_c, S)
        o = jnp.einsum('bhck,bhkv->bhcv', qg_c, S) + jnp.einsum('bhij,bhjv->bhiv', qk_c, v_new)
        S = S * gl_c[..., None, None] + jnp.einsum('bhck,bhcv->bhkv', kd_c, v_new)
        return S, o

    xs = tuple(jnp.moveaxis(t, 2, 0) for t in (u, w, qk, qg, kdec, glast))
    S0 = jnp.zeros((B, H, Dk, Dv), F32)
    _, o = lax.scan(step, S0, xs)
    return jnp.moveaxis(o, 0, 2).reshape(B, H, L, Dv)


def deltanet_branch(qkv, z, a, b, conv_w, A_log, dt_bias, norm_w):
    B, L, _ = qkv.shape
    qkv = jax.nn.silu(centred_depthwise_conv(qkv, conv_w)).astype(F32)
    q, k, v = [t.reshape(B, L, DN_HEADS, HEAD_DIM).transpose(0, 2, 1, 3) for t in jnp.split(qkv, 3, axis=-1)]
    q = l2norm(q) * (HEAD_DIM ** -0.5)
    k = l2norm(k)
    log_decay = -jnp.exp(A_log.astype(F32)) * jax.nn.softplus(a.astype(F32) + dt_bias.astype(F32))
    beta = jax.nn.sigmoid(b.astype(F32))
    log_decay = log_decay.transpose(2, 0, 3, 1)
    beta = beta.transpose(2, 0, 3, 1)
    o_fwd = gated_delta_chunked(q, k, v, log_decay[0], beta[0])
    flip = lambda t: jnp.flip(t, axis=2)
    o_bwd = flip(gated_delta_chunked(flip(q), flip(k), flip(v), flip(log_decay[1]), flip(beta[1])))
    o = (o_fwd + o_bwd).transpose(0, 2, 1, 3)
    o = rmsnorm(o, norm_w) * jax.nn.silu(z.astype(F32).reshape(B, L, DN_HEADS, HEAD_DIM))
    return o.reshape(B, L, DN_WIDTH).astype(z.dtype)


def dilated_window_attention(q, k, v, window, dilation):
    B, L, H, Dh = q.shape
    n_side = window // (2 * dilation)
    blk = n_side
    M = L // dilation
    nb = -(-M // blk)
    Mp = nb * blk

    def to_residues(t):
        t = t.reshape(B, M, dilation, H, Dh).transpose(0, 2, 3, 1, 4)
        return jnp.pad(t, ((0, 0), (0, 0), (0, 0), (0, Mp - M), (0, 0)))

    def windows(t):
        tp = jnp.pad(t, ((0, 0), (0, 0), (0, 0), (blk, blk), (0, 0))).reshape(B, dilation, H, nb + 2, blk, Dh)
        return jnp.concatenate([tp[:, :, :, :-2], tp[:, :, :, 1:-1], tp[:, :, :, 2:]], axis=4)

    qb = to_residues(q).reshape(B, dilation, H, nb, blk, Dh)
    kw = windows(to_residues(k))
    vw = windows(to_residues(v))
    s = jnp.einsum('brhnqd,brhnkd->brhnqk', qb, kw) * (Dh ** -0.5)
    qpos = jnp.arange(nb)[:, None] * blk + jnp.arange(blk)[None, :]
    kpos = jnp.arange(nb)[:, None] * blk - blk + jnp.arange(3 * blk)[None, :]
    valid = ((jnp.abs(qpos[:, :, None] - kpos[:, None, :]) <= n_side)
             & (kpos[:, None, :] >= 0) & (kpos[:, None, :] < M))
    s = jnp.where(valid, s, MASK_VALUE)
    m = jnp.max(s, axis=-1, keepdims=True)
    p = jnp.exp(s - m)
    den = jnp.sum(p, axis=-1, keepdims=True)
    o = jnp.einsum('brhnqk,brhnkd->brhnqd', p, vw) / den
    lse = (m + jnp.log(den))[..., 0]
    o = o.reshape(B, dilation, H, Mp, Dh)[:, :, :, :M].transpose(0, 3, 1, 2, 4).reshape(B, L, H, Dh)
    lse = lse.reshape(B, dilation, H, Mp)[:, :, :, :M].transpose(0, 3, 1, 2).reshape(B, L, H)
    return o, lse


def attention_branch(qkv):
    B, L, _ = qkv.shape
    pos = jnp.arange(L)
    q, k, v = [t.reshape(B, L, ATTN_HEADS, HEAD_DIM) for t in jnp.split(qkv, 3, axis=-1)]
    q, k, v = rotary(q, pos), rotary(k, pos), v.astype(F32)
    outs, lses = [], []
    for gi, (window, dilation) in enumerate(ATTN_GROUPS):
        hs = slice(gi * ATTN_HEADS_PER_GROUP, (gi + 1) * ATTN_HEADS_PER_GROUP)
        o, lse = dilated_window_attention(q[:, :, hs], k[:, :, hs], v[:, :, hs], window, dilation)
        outs.append(o)
        lses.append(lse)
    wts = jax.nn.softmax(jnp.stack(lses), axis=0)
    o = jnp.sum(wts[..., None] * jnp.stack(outs), axis=0)
    return o.reshape(B, L, ATTN_OUT_WIDTH).astype(qkv.dtype)


def encoder_layer(x, c, w_ada, b_ada, norm_pre_mix, norm_post_mix, norm_pre_ffn, norm_post_ffn,
                  w_in, conv_w, A_log, dt_bias, dn_norm_w, w_dn_out, w_at_out, w_out, w_ff1, w_ff2):
    B, L, _ = x.shape
    mod = jax.nn.silu(c) @ w_ada + b_ada
    shift1, scale1, gate1, shift2, scale2, gate2 = jnp.split(mod[:, None, :], N_MOD, axis=-1)
    h = rmsnorm(x, norm_pre_mix) * (1.0 + scale1) + shift1
    proj = h @ w_in
    split_at = np.cumsum(IN_SPLIT_SIZES)[:-1].tolist()
    dn_qkv, dn_z, dn_a, dn_b, at_qkv, merge = jnp.split(proj, split_at, axis=-1)
    y_dn = deltanet_branch(dn_qkv, dn_z, dn_a.reshape(B, L, 2, DN_HEADS), dn_b.reshape(B, L, 2, DN_HEADS),
                           conv_w, A_log, dt_bias, dn_norm_w) @ w_dn_out
    y_at = attention_branch(at_qkv) @ w_at_out
    g_dn, g_at = jnp.split(jax.nn.sigmoid(merge), N_BRANCH, axis=-1)
    mixed = (g_dn * y_dn + g_at * y_at) @ w_out
    x = x + gate1 * rmsnorm(mixed, norm_post_mix)
    h2 = rmsnorm(x, norm_pre_ffn) * (1.0 + scale2) + shift2
    f = jnp.square(jax.nn.relu(h2 @ w_ff1)) @ w_ff2
    return x + gate2 * rmsnorm(f, norm_post_ffn)


def trunk(x, c, w_ada, b_ada, norm_pre_mix, norm_post_mix, norm_pre_ffn, norm_post_ffn,
          w_in, conv_w, A_log, dt_bias, dn_norm_w, w_dn_out, w_at_out, w_out, w_ff1, w_ff2):
    for l in range(DEPTH):
        x = encoder_layer(x, c, w_ada[l], b_ada[l], norm_pre_mix[l], norm_post_mix[l], norm_pre_ffn[l],
                          norm_post_ffn[l], w_in[l], conv_w[l], A_log[l], dt_bias[l], dn_norm_w[l],
                          w_dn_out[l], w_at_out[l], w_out[l], w_ff1[l], w_ff2[l])
    return x


def setup_inputs(seed: int = 0) -> dict:
    key = jax.random.key(seed)
    ks = jax.random.split(key, 24)

    def dense(k, shape, fan_in, scale=1.0):
        return jax.random.normal(k, shape, F32) * (scale * fan_in ** -0.5)

    def gain(k, shape):
        return 1.0 + 0.02 * jax.random.normal(k, shape, F32)

    dt = jnp.exp(jax.random.uniform(ks[12], (DEPTH, 2, DN_HEADS), F32, np.log(1e-3), np.log(1e-1)))
    return {
        'x_prompt': jax.random.normal(ks[0], (BATCH, SEQ, D_MODEL), F32),
        'x_sample': jax.random.normal(ks[1], (DEC_BATCH, DEC_SEQ, D_MODEL), F32),
        'c_prompt': jax.random.normal(ks[2], (BATCH, D_MODEL), F32),
        'c_sample': jax.random.normal(ks[3], (DEC_BATCH, D_MODEL), F32),
        'w_ada': dense(ks[4], (DEPTH, D_MODEL, N_MOD * D_MODEL), D_MODEL, 0.5),
        'b_ada': 0.01 * jax.random.normal(ks[5], (DEPTH, N_MOD * D_MODEL), F32),
        'norm_pre_mix': gain(ks[6], (DEPTH, D_MODEL)),
        'norm_post_mix': gain(ks[7], (DEPTH, D_MODEL)),
        'norm_pre_ffn': gain(ks[8], (DEPTH, D_MODEL)),
        'norm_post_ffn': gain(ks[9], (DEPTH, D_MODEL)),
        'w_in': dense(ks[10], (DEPTH, D_MODEL, IN_COLS), D_MODEL),
        'conv_w': dense(ks[11], (DEPTH, DN_CONV, 3 * DN_WIDTH), DN_CONV),
        'A_log': jnp.log(jax.random.uniform(ks[13], (DEPTH, 2, DN_HEADS), F32, 1.0, 16.0)),
        'dt_bias': dt + jnp.log(-jnp.expm1(-dt)),
        'dn_norm_w': gain(ks[14], (DEPTH, HEAD_DIM)),
        'w_dn_out': dense(ks[15], (DEPTH, DN_WIDTH, D_MODEL), DN_WIDTH),
        'w_at_out': dense(ks[16], (DEPTH, ATTN_OUT_WIDTH, D_MODEL), ATTN_OUT_WIDTH),
        'w_out': dense(ks[17], (DEPTH, D_MODEL, D_MODEL), D_MODEL),
        'w_ff1': dense(ks[18], (DEPTH, D_MODEL, D_FF), D_MODEL),
        'w_ff2': dense(ks[19], (DEPTH, D_FF, D_MODEL), D_FF),
    }


def reference(x_prompt, x_sample, c_prompt, c_sample, w_ada, b_ada, norm_pre_mix, norm_post_mix,
              norm_pre_ffn, norm_post_ffn, w_in, conv_w, A_log, dt_bias, dn_norm_w, w_dn_out,
              w_at_out, w_out, w_ff1, w_ff2):
    y_prompt = trunk(x_prompt, c_prompt, w_ada, b_ada, norm_pre_mix, norm_post_mix, norm_pre_ffn,
                     norm_post_ffn, w_in, conv_w, A_log, dt_bias, dn_norm_w, w_dn_out, w_at_out,
                     w_out, w_ff1, w_ff2)
    y_sample = trunk(x_sample, c_sample, w_ada, b_ada, norm_pre_mix, norm_post_mix, norm_pre_ffn,
                     norm_post_ffn, w_in, conv_w, A_log, dt_bias, dn_norm_w, w_dn_out, w_at_out,
                     w_out, w_ff1, w_ff2)
    return (y_prompt, y_sample)
```

```python
from contextlib import ExitStack
import numpy as np
import concourse.bass as bass
import concourse.mybir as mybir
from concourse.bass_utils import run_bass_kernel_spmd

F32 = mybir.dt.float32
BF16 = mybir.dt.bfloat16
I32 = mybir.dt.int32
AF = mybir.ActivationFunctionType
ALU = mybir.AluOpType
AX = mybir.AxisListType

D = 2048
NJ = 16
T = 8192
NSEG = 4
SEG = 2048
DFF = 8192
EPS = 1e-6
IN_COLS = 12832
NEG = -1.0e30

import os
KDEBUG = int(os.environ.get("KDEBUG", "0"))
KTILES = int(os.environ.get("KTILES", "16"))
KDUMP = os.environ.get("KDUMP", "").split(",")
COMPUTE = ("pe", "act", "dve", "pool")
N_DMA_SEMS = 12


class Ticket:
    __slots__ = ("kind", "eng", "val", "sem")

    def __init__(self, kind, eng, val, sem=None):
        self.kind, self.eng, self.val, self.sem = kind, eng, val, sem


class _Rec:
    def __init__(self):
        self.calls = []

    def __getattr__(self, name):
        def f(*a, **k):
            self.calls.append((name, a, k))
            return self
        return f


def _replay(h, calls):
    ins = None
    for name, a, k in calls:
        ins = getattr(h, name)(*a, **k)
    return ins


class FW:
    def __init__(self, nc, es):
        self.nc = nc
        self.streams = {k: [] for k in ("pe", "act", "dve", "pool", "sp")}
        self.sem = {}
        self.count = {}
        for k in COMPUTE:
            self.sem[k] = es.enter_context(nc.semaphore("s_" + k))
            self.count[k] = 0
        self.dsem, self.dcount, self.dnext = {}, {}, {}
        for q in ("sp", "act", "pool"):
            self.dsem[q] = [es.enter_context(nc.semaphore(f"d_{q}{i}")) for i in range(N_DMA_SEMS)]
            self.dcount[q] = [0] * N_DMA_SEMS
            self.dnext[q] = 0
        self.known = {k: {} for k in self.streams}
        self.lastw = {}
        self.readers = {}
        self.n_ops = 0

    def _need(self, stream, t, waits):
        if t is None:
            return
        if t.kind == "c":
            if t.eng == stream and stream == "pe":
                return
            key = ("c", t.eng)
            sem = self.sem[t.eng]
        else:
            key = ("d", id(t.sem))
            sem = t.sem
        if self.known[stream].get(key, 0) >= t.val:
            return
        cur = waits.get(key)
        if cur is None or cur[1] < t.val:
            waits[key] = (sem, t.val)

    def _deps(self, stream, reads, writes):
        waits = {}
        for r in reads:
            self._need(stream, self.lastw.get(r), waits)
        for w in writes:
            self._need(stream, self.lastw.get(w), waits)
            for t in self.readers.get(w, {}).values():
                self._need(stream, t, waits)
        out = []
        for key, (sem, val) in waits.items():
            self.known[stream][key] = val
            out.append((sem, val))
        return out

    def _commit(self, t, reads, writes):
        for w in writes:
            self.lastw[w] = t
            self.readers[w] = {}
        k = ("c", t.eng) if t.kind == "c" else ("d", id(t.sem))
        for r in reads:
            self.readers.setdefault(r, {})[k] = t

    def op(self, eng, fn, reads=(), writes=()):
        waits = self._deps(eng, reads, writes)
        self.count[eng] += 1
        val = self.count[eng]
        sem = self.sem[eng]

        rec = _Rec()
        fn(rec)
        calls = rec.calls

        def emit(h, waits=waits, calls=calls, sem=sem):
            for s, v in waits:
                h.wait_ge(s, v)
            _replay(h, calls).then_inc(sem, 1)

        self.streams[eng].append(emit)
        self._commit(Ticket("c", eng, val), reads, writes)
        self.n_ops += 1

    def dma(self, q, fn, reads=(), writes=()):
        i = self.dnext[q]
        self.dnext[q] = (i + 1) % N_DMA_SEMS
        sem = self.dsem[q][i]
        prev = self.dcount[q][i]
        waits = self._deps(q, reads, writes)
        key = ("d", id(sem))
        if prev > 0 and self.known[q].get(key, 0) < prev:
            waits.append((sem, prev))
            self.known[q][key] = prev
        val = prev + 16
        self.dcount[q][i] = val

        rec = _Rec()
        fn(rec)
        calls = rec.calls

        def emit(h, waits=waits, calls=calls, sem=sem):
            for s, v in waits:
                h.wait_ge(s, v)
            _replay(h, calls).then_inc(sem, 16)

        self.streams[q].append(emit)
        self._commit(Ticket("d", q, val, sem), reads, writes)
        self.n_ops += 1

    def barrier(self):
        targets = [(self.sem[k], self.count[k], ("c", k)) for k in COMPUTE if self.count[k] > 0]
        for q in self.dsem:
            for i, s in enumerate(self.dsem[q]):
                if self.dcount[q][i] > 0:
                    targets.append((s, self.dcount[q][i], ("d", id(s))))
        for stream in self.streams:
            ws = []
            for s, v, key in targets:
                if self.known[stream].get(key, 0) < v:
                    ws.append((s, v))
                    self.known[stream][key] = v

            def emit(h, ws=ws):
                for s, v in ws:
                    h.wait_ge(s, v)

            self.streams[stream].append(emit)
        self.lastw = {}
        self.readers = {}

    def emit_all(self):
        with self.nc.Block() as block:
            @block.tensor
            def _(h):
                for f in self.streams["pe"]:
                    f(h)

            @block.scalar
            def _(h):
                for f in self.streams["act"]:
                    f(h)

            @block.vector
            def _(h):
                for f in self.streams["dve"]:
                    f(h)

            @block.gpsimd
            def _(h):
                for f in self.streams["pool"]:
                    f(h)

            @block.sync
            def _(h):
                for f in self.streams["sp"]:
                    f(h)


C_ID, C_MEAN, C_ONE, C_MLOW, C_MUP, C_GT, C_LT, C_LE, C_GE = 0, 128, 256, 384, 512, 640, 768, 896, 1024
C_RPERM, C_IOTA, C_IFREQ, C_BAND, C_NEGL, C_NEGR, C_TOTAL = 1152, 1280, 1792, 1793, 2049, 2305, 2561


def make_lvlmask():
    p = np.arange(128)[:, None]
    f = np.arange(128)[None, :]
    m = np.zeros((128, 14 * 128), np.float32)
    for k in range(7):
        same_hi = (p >> (k + 1)) == (f >> (k + 1))
        diff_lo = (p >> k) != (f >> k)
        m[:, (2 * k) * 128:(2 * k + 1) * 128] = same_hi & diff_lo & (p > f)
        m[:, (2 * k + 1) * 128:(2 * k + 2) * 128] = same_hi & diff_lo & (p < f)
    return m


def make_consts():
    c = np.zeros((128, C_TOTAL), np.float32)
    p = np.arange(128)[:, None]
    f = np.arange(128)[None, :]
    c[:, C_ID:C_ID + 128] = (p == f)
    c[:, C_MEAN:C_MEAN + 128] = 1.0 / D
    c[:, C_ONE:C_ONE + 128] = 1.0
    c[:, C_MLOW:C_MLOW + 128] = (p <= f)
    c[:, C_MUP:C_MUP + 128] = (p >= f)
    c[:, C_GT:C_GT + 128] = (p > f)
    c[:, C_LT:C_LT + 128] = (p < f)
    c[:, C_LE:C_LE + 128] = (p <= f)
    c[:, C_GE:C_GE + 128] = (p >= f)
    r = np.zeros((128, 128), np.float32)
    for m in range(64):
        r[m + 64, m] = -1.0
        r[m, m + 64] = 1.0
    c[:, C_RPERM:C_RPERM + 128] = r
    c[:, C_IOTA:C_IOTA + 512] = np.arange(512)[None, :]
    c[:, C_IFREQ] = (10000.0 ** (-(np.arange(128) % 64) / 64.0)) / (2 * np.pi)
    a = np.arange(128)[:, None]
    b = np.arange(256)[None, :]
    c[:, C_BAND:C_BAND + 256] = np.where((b - a >= 0) & (b - a <= 128), 0.0, NEG)
    c[:, C_NEGL:C_NEGL + 256] = np.where(b < 64, NEG, 0.0) * np.ones((128, 1))
    c[:, C_NEGR:C_NEGR + 256] = np.where(b >= 192, NEG, 0.0) * np.ones((128, 1))
    return c


def build_program():
    nc = bass.Bass("TRN2", target_bir_lowering=False)

    def EI(name, shape):
        return nc.dram_tensor(name, list(shape), F32, kind="ExternalInput").ap()

    x = EI("x", [T, D])
    cvec = EI("c", [NSEG * NJ, 128])
    segf = EI("segf", [128, 16])
    consts = EI("consts", [128, C_TOTAL])
    lvlmask = EI("lvlmask", [128, 14 * 128])
    w_ada = EI("w_ada", [D, 6 * D])
    b_ada = EI("b_ada", [96, 128])
    norms = EI("norms", [64, 128])
    w_in = EI("w_in", [D, IN_COLS])
    conv_w = EI("conv_w", [120, 128])
    alog = EI("A_log", [1, 16])
    dtb = EI("dt_bias", [1, 16])
    dnw = EI("dn_norm_w", [1, 128])
    w_dn_out = EI("w_dn_out", [1024, D])
    w_at_out = EI("w_at_out", [512, D])
    w_out = EI("w_out", [D, D])
    w_ff1 = EI("w_ff1", [D, DFF])
    w_ff2 = EI("w_ff2", [DFF, D])
    y = nc.dram_tensor("y", [T, D], F32, kind="ExternalOutput").ap()
    dbg_names = []

    def dump(fw, name, ap, keys, dt=F32):
        if not KDEBUG:
            return
        dtn = nc.dram_tensor("dbg_" + name, list(ap.shape), dt, kind="ExternalOutput").ap()
        dbg_names.append("dbg_" + name)
        fw.dma("sp", lambda h: h.dma_start(out=dtn, in_=ap), reads=list(keys), writes=["dbg_" + name])

    def DR(name, shape, dt=F32):
        if KDEBUG and name in KDUMP:
            dbg_names.append(name)
            return nc.dram_tensor(name, list(shape), dt, kind="ExternalOutput").ap()
        return nc.dram_tensor(name, list(shape), dt, kind="Internal").ap()

    wb_mg = DR("wb_mg", [32, 128, 16, 128], BF16)
    wb_dn = DR("wb_dn", [16, 128, 8, 128], BF16)
    wb_at = DR("wb_at", [16, 128, 4, 128], BF16)
    wb_out = DR("wb_out", [16, 128, 16, 128], BF16)
    wb_f1 = DR("wb_f1", [64, 128, 16, 128], BF16)
    wb_f2 = DR("wb_f2", [32, 128, 32, 128], BF16)
    ydnT = DR("ydnT", [1024, T], BF16)
    yatT = DR("yatT", [512, T], BF16)
    MERGE0 = 3072 + 1024 + 32 + 4608

    es = ExitStack()
    with es:
        fw = FW(nc, es)

        uid = [0]

        def SB(st, name, shape, dt=F32):
            uid[0] += 1
            return st.enter_context(nc.sbuf_tensor(f"{name}_{uid[0]}", list(shape), dt))

        def PSM(st, name, shape, dt=F32):
            uid[0] += 1
            return st.enter_context(nc.psum_tensor(f"{name}_{uid[0]}", list(shape), dt))

        cst = SB(es, "cst", [128, C_TOTAL])
        sgf = SB(es, "sgf", [128, 16])
        nrm = SB(es, "nrm", [128, 64])
        cT = SB(es, "cT", [128, 64])
        badaT = SB(es, "badaT", [128, 96])
        modT = SB(es, "modT", [128, 96, 4])
        A1 = SB(es, "A1", [128, 16, 4]); G1 = SB(es, "G1", [128, 16, 4])
        A2 = SB(es, "A2", [128, 16, 4]); G2 = SB(es, "G2", [128, 16, 4])
        ident = cst[:, C_ID:C_ID + 128]
        meanm = cst[:, C_MEAN:C_MEAN + 128]
        fw.dma("sp", lambda h: h.dma_start(out=cst[:], in_=consts[:, :]), writes=["cst"])
        fw.dma("sp", lambda h: h.dma_start(out=sgf[:], in_=segf[:, :]), writes=["sgf"])

        def cast(dst, src, key):
            fw.dma("pool", lambda h: h.dma_start(out=dst, in_=src), writes=[key])

        for b in range(32):
            cast(wb_mg[b], w_in[:, MERGE0 + b * 128:MERGE0 + (b + 1) * 128].rearrange("(j p) c -> p j c", p=128), ("wb_mg", b))
        for b in range(16):
            cast(wb_dn[b], w_dn_out[:, b * 128:(b + 1) * 128].rearrange("(j p) c -> p j c", p=128), ("wb_dn", b))
            cast(wb_at[b], w_at_out[:, b * 128:(b + 1) * 128].rearrange("(j p) c -> p j c", p=128), ("wb_at", b))
            cast(wb_out[b], w_out[:, b * 128:(b + 1) * 128].rearrange("(j p) c -> p j c", p=128), ("wb_out", b))
        for b in range(64):
            cast(wb_f1[b], w_ff1[:, b * 128:(b + 1) * 128].rearrange("(j p) c -> p j c", p=128), ("wb_f1", b))
        for hf in range(2):
            for b in range(16):
                cast(wb_f2[hf * 16 + b],
                     w_ff2[hf * 4096:(hf + 1) * 4096, b * 128:(b + 1) * 128].rearrange("(j p) c -> p j c", p=128),
                     ("wb_f2", hf * 16 + b))

        with ExitStack() as st:
            stg = SB(st, "stg0", [128, 128])
            stg2 = SB(st, "stg1", [128, 128])
            wa = [SB(st, f"wa{i}", [128, 16, 128]) for i in range(2)]
            pt = PSM(st, "p0t", [128, 512])
            pm = [PSM(st, f"p0m{i}", [128, 4]) for i in range(2)]
            fw.dma("sp", lambda h: h.dma_start(out=stg[0:64, :], in_=norms[:, :]), writes=["stg0"])
            fw.dma("sp", lambda h: h.dma_start(out=stg[64:128, :], in_=cvec[:, :]), writes=["stg0"])
            fw.op("pe", lambda h: h.transpose(out=pt[:, 0:128], in_=stg[:], identity=ident), reads=["stg0", "cst"], writes=["p0t"])
            fw.op("dve", lambda h: h.tensor_copy(out=nrm[:], in_=pt[:, 0:64]), reads=["p0t"], writes=["nrm"])
            fw.op("act", lambda h: h.activation(out=cT[:], in_=pt[:, 64:128], func=AF.Silu), reads=["p0t"], writes=["cT"])
            fw.dma("sp", lambda h: h.dma_start(out=stg2[0:96, :], in_=b_ada[:, :]), writes=["stg1"])
            fw.op("pe", lambda h: h.transpose(out=pt[:, 128:224], in_=stg2[0:96, :], identity=ident[0:96, 0:96]), reads=["stg1", "cst"], writes=["p0t"])
            fw.op("dve", lambda h: h.tensor_copy(out=badaT[:], in_=pt[:, 128:224]), reads=["p0t"], writes=["badaT"])
            cTv = cT[:].rearrange("p (s j) -> p j s", j=16)
            for cb in range(96):
                wt = wa[cb % 2]
                fw.dma("sp", lambda h, wt=wt, cb=cb: h.dma_start(
                    out=wt[:], in_=w_ada[:, cb * 128:(cb + 1) * 128].rearrange("(j p) c -> p j c", p=128)),
                    writes=[("wa", cb % 2)])

                def mm(h, wt=wt, cb=cb):
                    for j in range(16):
                        ins = h.matmul(pm[cb % 2][:, :], lhsT=wt[:, j, :], rhs=cTv[:, j, :], start=(j == 0), stop=(j == 15))
                    return ins
                fw.op("pe", mm, reads=[("wa", cb % 2), "cT"], writes=[("p0m", cb % 2)])
                fw.op("dve", lambda h, cb=cb: h.tensor_scalar(out=modT[:, cb, :], in0=pm[cb % 2][:, :], scalar1=badaT[:, cb:cb + 1],
                                                              scalar2=None, op0=ALU.add),
                      reads=[("p0m", cb % 2), "badaT"], writes=["modT"])

            def nv(v):
                return nrm[:, v * 16:(v + 1) * 16].unsqueeze(2).to_broadcast([128, 16, 4])
            fw.op("dve", lambda h: h.scalar_tensor_tensor(out=A1[:], in0=modT[:, 16:32, :], scalar=1.0, in1=nv(0), op0=ALU.add, op1=ALU.mult),
                  reads=["modT", "nrm"], writes=["A1"])
            fw.op("dve", lambda h: h.tensor_tensor(out=G1[:], in0=modT[:, 32:48, :], in1=nv(1), op=ALU.mult), reads=["modT", "nrm"], writes=["G1"])
            fw.op("dve", lambda h: h.scalar_tensor_tensor(out=A2[:], in0=modT[:, 64:80, :], scalar=1.0, in1=nv(2), op0=ALU.add, op1=ALU.mult),
                  reads=["modT", "nrm"], writes=["A2"])
            fw.op("dve", lambda h: h.tensor_tensor(out=G2[:], in0=modT[:, 80:96, :], in1=nv(3), op=ALU.mult), reads=["modT", "nrm"], writes=["G2"])
            dump(fw, "modT", modT[:], ["modT"])
            dump(fw, "A1", A1[:], ["A1"])
            dump(fw, "nrm", nrm[:], ["nrm"])
            fw.barrier()

        def front_end(st_bufs, t0, seg):
            xin, xT, sq, rstd, pxt, pss = st_bufs
            for q in range(4):
                fw.dma("sp", lambda h, q=q: h.dma_start(out=xin[q][:], in_=x[t0 + q * 128:t0 + (q + 1) * 128, :]), writes=[("xin", q)])
            for j in range(16):
                pb = pxt[j % 2]

                def tr(h, j=j, pb=pb):
                    for q in range(4):
                        ins = h.transpose(out=pb[:, q * 128:(q + 1) * 128], in_=xin[q][:, j * 128:(j + 1) * 128], identity=ident)
                    return ins
                fw.op("pe", tr, reads=[("xin", q) for q in range(4)] + ["cst"], writes=[("pxt", j % 2)])
                fw.op("act", lambda h, j=j, pb=pb: h.activation(out=xT[:, j, :], in_=pb[:, :], func=AF.Copy), reads=[("pxt", j % 2)], writes=[("xT", j)])
                fw.op("dve", lambda h, j=j, pb=pb: h.tensor_tensor(out=sq[j % 2][:], in0=pb[:, :], in1=xT[:, j, :], op=ALU.mult),
                      reads=[("pxt", j % 2), ("xT", j)], writes=[("sq", j % 2)])
                fw.op("pe", lambda h, j=j: h.matmul(pss[:, :], lhsT=meanm, rhs=sq[j % 2][:], start=(j == 0), stop=(j == 15)),
                      reads=[("sq", j % 2), "cst"], writes=["pss"])
            fw.op("act", lambda h: h.activation(out=rstd[:], in_=pss[:, :], func=AF.Sqrt, bias=EPS, scale=1.0), reads=["pss"], writes=["rstd"])
            fw.op("dve", lambda h: h.reciprocal(out=rstd[:], in_=rstd[:]), reads=["rstd"], writes=["rstd"])

        def sumsq_rstd(src_fn, nblk, sq, pss, rstd, src_keys, scale_mean=True):
            for j in range(nblk):
                fw.op("pool", lambda h, j=j: h.tensor_tensor(out=sq[j % 2][:], in0=src_fn(j), in1=src_fn(j), op=ALU.mult),
                      reads=[src_keys(j)], writes=[("sq", j % 2)])
                fw.op("pe", lambda h, j=j: h.matmul(pss[:, :], lhsT=meanm, rhs=sq[j % 2][:], start=(j == 0), stop=(j == nblk - 1)),
                      reads=[("sq", j % 2), "cst"], writes=["pss"])
            fw.op("act", lambda h: h.activation(out=rstd[:], in_=pss[:, :], func=AF.Sqrt, bias=EPS, scale=1.0), reads=["pss"], writes=["rstd"])
            fw.op("dve", lambda h: h.reciprocal(out=rstd[:], in_=rstd[:]), reads=["rstd"], writes=["rstd"])

        DNQ0, Z0, AB0, ATQ0, ATK0, ATV0 = 0, 3072, 4096, 4128, 4128 + 1536, 4128 + 3072
        DIL = (1, 4, 16)
        MC = [T // d_ for d_ in DIL]
        MS = [SEG // d_ for d_ in DIL]
        CL = [m_ + 128 for m_ in MC]
        qkvpre = DR("qkvpre", [3072, T + 4])
        szT = DR("szT", [1024, T])
        ab_tm = DR("ab_tm", [T, 32])
        aqT = [DR(f"aqT{g}", [512, DIL[g] * CL[g]], BF16) for g in range(3)]
        akT = [DR(f"akT{g}", [512, DIL[g] * CL[g]], BF16) for g in range(3)]
        avs = [DR(f"av{g}", [DIL[g] * CL[g], 512], BF16) for g in range(3)]
        qn = DR("qn", [1024, T]); kn = DR("kn", [1024, T])
        k_tm = DR("k_tm", [T, 1024]); v_tm = DR("v_tm", [T, 1024])
        GB = DR("GB", [2, 2, 64, 8, 128])
        o_dir = DR("o_dir", [2, T, 1024])
        Oat = DR("Oat", [3, T, 4, 130])
        wb_fm = DR("wb_fm", [56, 128, 16, 128], BF16)
        wb_v = DR("wb_v", [3, 128, 16, 512], BF16)
        wb_ab = DR("wb_ab", [128, 16, 32], BF16)
        for b in range(56):
            c0 = (DNQ0 + 128 * b) if b < 24 else (Z0 + 128 * (b - 24)) if b < 32 else (ATQ0 + 128 * (b - 32)) if b < 44 else (ATK0 + 128 * (b - 44))
            cast(wb_fm[b], w_in[:, c0:c0 + 128].rearrange("(j p) c -> p j c", p=128), ("wb_fm", b))
        for g in range(3):
            cast(wb_v[g], w_in[:, ATV0 + 512 * g:ATV0 + 512 * (g + 1)].rearrange("(j p) c -> p j c", p=128), ("wb_v", g))
        cast(wb_ab[:, :, :], w_in[:, AB0:AB0 + 32].rearrange("(j p) c -> p j c", p=128), "wb_ab")

        convT = SB(es, "convT", [128, 120])
        dnwT = SB(es, "dnwT", [128, 1])
        dtb16 = SB(es, "dtb16", [128, 16])
        nA16 = SB(es, "nA16", [128, 16])
        Gtm = SB(es, "Gtm", [128, 64, 16])
        Btm = SB(es, "Btm", [128, 64, 16])
        ones = cst[:, C_ONE:C_ONE + 128]
        with ExitStack() as st:
            stg = SB(st, "stgc", [128, 128])
            pt = PSM(st, "p1t", [128, 512])
            fw.dma("sp", lambda h: h.dma_start(out=stg[0:120, :], in_=conv_w[:, :]), writes=["stgc"])
            fw.op("pe", lambda h: h.transpose(out=pt[:, 0:120], in_=stg[0:120, :], identity=ident[0:120, 0:120]), reads=["stgc", "cst"], writes=["p1t"])
            fw.op("dve", lambda h: h.tensor_copy(out=convT[:], in_=pt[:, 0:120]), reads=["p1t"], writes=["convT"])
            fw.dma("sp", lambda h: h.dma_start(out=stg[0:1, :], in_=dnw[:, :]), writes=["stgc"])
            fw.op("pe", lambda h: h.transpose(out=pt[:, 128:129], in_=stg[0:1, :], identity=ident[0:1, 0:1]), reads=["stgc", "cst"], writes=["p1t"])
            fw.op("dve", lambda h: h.tensor_copy(out=dnwT[:], in_=pt[:, 128:129]), reads=["p1t"], writes=["dnwT"])
            fw.dma("sp", lambda h: h.dma_start(out=dtb16[:], in_=dtb.partition_broadcast(128)), writes=["dtb16"])
            fw.dma("sp", lambda h: h.dma_start(out=nA16[:], in_=alog.partition_broadcast(128)), writes=["nA16"])
            fw.op("act", lambda h: h.activation(out=nA16[:], in_=nA16[:], func=AF.Exp), reads=["nA16"], writes=["nA16"])
            fw.op("dve", lambda h: h.tensor_scalar(out=nA16[:], in0=nA16[:], scalar1=-1.0, scalar2=None, op0=ALU.mult), reads=["nA16"], writes=["nA16"])
            zb = SB(st, "zb", [128, 16, 512], BF16)
            fw.op("pool", lambda h: h.memset(zb[:], 0.0), writes=["zb"])
            for g in range(3):
                dil = DIL[g]
                for hh in range(4):
                    kv = akT[g][hh * 128:(hh + 1) * 128, :].rearrange("p (r c) -> p r c", c=CL[g])
                    fw.dma("sp", lambda h, kv=kv, dil=dil: h.dma_start(out=kv[:, :, 0:64], in_=zb[:, 0:dil, 0:64]), reads=["zb"], writes=[("akT", g)])
                    fw.dma("sp", lambda h, kv=kv, dil=dil, g=g: h.dma_start(out=kv[:, :, 64 + MC[g]:128 + MC[g]], in_=zb[:, 0:dil, 0:64]), reads=["zb"], writes=[("akT", g)])
                vv = avs[g].rearrange("(r c) d -> c r d", c=CL[g])
                fw.dma("sp", lambda h, vv=vv, dil=dil: h.dma_start(out=vv[0:64, :, :], in_=zb[0:64, 0:dil, :]), reads=["zb"], writes=[("av", g)])
                fw.dma("sp", lambda h, vv=vv, dil=dil, g=g: h.dma_start(out=vv[64 + MC[g]:128 + MC[g], :, :], in_=zb[0:64, 0:dil, :]), reads=["zb"], writes=[("av", g)])
            fw.barrier()

        with ExitStack() as st:
            xin = [SB(st, f"xin{q}", [128, D]) for q in range(4)]
            hTs = SB(st, "hTs", [128, 16, SEG], BF16)
            junk = SB(st, "junk", [128, D], BF16)
            ssq = SB(st, "ssq", [128, 4])
            stgf = [SB(st, f"stgf{i}", [128, SEG]) for i in range(2)]
            stgb = [SB(st, f"stgb{i}", [128, SEG], BF16) for i in range(2)]
            wk = [SB(st, f"wk{i}", [128, 16, 128], BF16) for i in range(3)]
            wv = SB(st, "wv", [128, 16, 512], BF16)
            wab = SB(st, "wab", [128, 16, 32], BF16)
            cosT = SB(st, "cosT", [128, 4, 512]); sinT = SB(st, "sinT", [128, 4, 512])
            tA = SB(st, "tA", [128, 512]); tB = SB(st, "tB", [128, 512]); tC = SB(st, "tC", [128, 512]); tD = SB(st, "tD", [128, 512])
            tI = SB(st, "tI", [128, 512], I32)
            hpi = SB(st, "hpi", [128, 1])
            vst = [SB(st, f"vst{i}", [128, 512], BF16) for i in range(2)]
            abst = SB(st, "abst", [128, 16, 32])
            pxt = [PSM(st, f"pxt{i}", [128, 512]) for i in range(2)]
            pmm = [PSM(st, f"pmm{i}", [128, 512]) for i in range(4)]
            prr = PSM(st, "prr", [128, 512])
            print("P1 sbuf remaining", nc.sbuf_bytes_remaining)
            fw.op("pool", lambda h: h.memset(hpi[:], float(np.pi / 2)), writes=["hpi"])
            fw.dma("sp", lambda h: h.dma_start(out=wab[:], in_=wb_ab[:, :, :]), reads=["wb_ab"], writes=["wab"])
            rperm = cst[:, C_RPERM:C_RPERM + 128]
            ifr = cst[:, C_IFREQ:C_IFREQ + 1]
            iota = cst[:, C_IOTA:C_IOTA + 512]
            wsl = [0]

            def trig(dst, yap):
                fw.op("dve", lambda h: h.tensor_copy(out=tI[:], in_=yap), reads=["tA"], writes=["tI"])
                fw.op("dve", lambda h: h.tensor_copy(out=tB[:], in_=tI[:]), reads=["tI"], writes=["tB"])
                fw.op("dve", lambda h: h.tensor_tensor(out=tB[:], in0=yap, in1=tB[:], op=ALU.subtract), reads=["tA", "tB"], writes=["tB"])
                fw.op("act", lambda h: h.activation(out=tC[:], in_=tB[:], func=AF.Abs), reads=["tB"], writes=["tC"])
                fw.op("act", lambda h: h.activation(out=tD[:], in_=tB[:], func=AF.Sin, scale=float(np.pi)), reads=["tB"], writes=["tD"])
                fw.op("act", lambda h: h.activation(out=tC[:], in_=tC[:], func=AF.Sin, bias=hpi[:], scale=-float(np.pi)), reads=["tC", "hpi"], writes=["tC"])
                fw.op("dve", lambda h: h.scalar_tensor_tensor(out=dst, in0=tD[:], scalar=2.0, in1=tC[:], op0=ALU.mult, op1=ALU.mult),
                      reads=["tC", "tD"], writes=["trig"])

            for s in range(NSEG):
                for tt in range(4):
                    t0 = s * SEG + tt * 512
                    for q in range(4):
                        fw.dma("sp", lambda h, q=q, t0=t0: h.dma_start(out=xin[q][:], in_=x[t0 + q * 128:t0 + (q + 1) * 128, :]), writes=[("xin", q)])
                        fw.op("act", lambda h, q=q: h.activation(out=junk[:], in_=xin[q][:], func=AF.Square, accum_out=ssq[:, q:q + 1]),
                              reads=[("xin", q)], writes=["junk", ("ssq", q)])
                        fw.op("act", lambda h, q=q: h.activation(out=ssq[:, q:q + 1], in_=ssq[:, q:q + 1], func=AF.Sqrt, bias=EPS, scale=1.0 / D),
                              reads=[("ssq", q)], writes=[("ssq", q)])
                        fw.op("dve", lambda h, q=q: h.reciprocal(out=ssq[:, q:q + 1], in_=ssq[:, q:q + 1]), reads=[("ssq", q)], writes=[("ssq", q)])
                        fw.op("dve", lambda h, q=q: h.tensor_scalar(out=xin[q][:], in0=xin[q][:], scalar1=ssq[:, q:q + 1], scalar2=None, op0=ALU.mult),
                              reads=[("xin", q), ("ssq", q)], writes=[("xin", q)])
                    for j in range(16):
                        pb = pxt[j % 2]

                        def tr(h, j=j, pb=pb):
                            for q in range(4):
                                ins = h.transpose(out=pb[:, q * 128:(q + 1) * 128], in_=xin[q][:, j * 128:(j + 1) * 128], identity=ident)
                            return ins
                        fw.op("pe", tr, reads=[("xin", q) for q in range(4)] + ["cst"], writes=[("pxt", j % 2)])
                        fw.op("dve" if j % 2 else "act",
                              (lambda h, j=j, pb=pb, tt=tt, s=s: h.tensor_scalar(out=hTs[:, j, tt * 512:(tt + 1) * 512], in0=pb[:, :], scalar1=A1[:, j, s:s + 1],
                                                                              scalar2=modT[:, j, s:s + 1], op0=ALU.mult, op1=ALU.add)) if j % 2 else
                              (lambda h, j=j, pb=pb, tt=tt, s=s: h.activation(out=hTs[:, j, tt * 512:(tt + 1) * 512], in_=pb[:, :], func=AF.Identity,
                                                                           bias=modT[:, j, s:s + 1], scale=A1[:, j, s:s + 1])),
                              reads=[("pxt", j % 2), "A1", "modT"], writes=[("hTs", j, tt)])
                hkeys = [("hTs", j, tt) for j in range(16) for tt in range(4)]

                def wload(b):
                    i = wsl[0] % 3
                    wsl[0] += 1
                    fw.dma("sp", lambda h, i=i, b=b: h.dma_start(out=wk[i][:], in_=wb_fm[b]), reads=[("wb_fm", b)], writes=[("wk", i)])
                    return i

                def fm_mm(wi, rhs_fn, pb, out_ap=None):
                    def mm(h):
                        for j in range(16):
                            ins = h.matmul(out_ap if out_ap is not None else pmm[pb][:, :], lhsT=wk[wi][:, j, :], rhs=rhs_fn(j), start=(j == 0), stop=(j == 15))
                        return ins
                    fw.op("pe", mm, reads=[("wk", wi)] + hkeys, writes=[("pmm", pb)])

                for b in range(32):
                    wi = wload(b)
                    sf = stgf[b % 2]
                    for tt in range(4):
                        pb = (b * 4 + tt) % 4
                        fm_mm(wi, lambda j, tt=tt: hTs[:, j, tt * 512:(tt + 1) * 512], pb)
                        fw.op("act", lambda h, sf=sf, tt=tt, pb=pb, b=b: h.activation(out=sf[:, tt * 512:(tt + 1) * 512], in_=pmm[pb][:, :],
                                                                                     func=(AF.Copy if b < 24 else AF.Silu)),
                              reads=[("pmm", pb)], writes=[("stgf", b % 2)])
                    if b < 24:
                        fw.dma("sp", lambda h, sf=sf, b=b, s=s: h.dma_start(out=qkvpre[b * 128:(b + 1) * 128, 2 + s * SEG:2 + (s + 1) * SEG], in_=sf[:]),
                               reads=[("stgf", b % 2)], writes=["qkvpre"])
                    else:
                        fw.dma("sp", lambda h, sf=sf, b=b, s=s: h.dma_start(out=szT[(b - 24) * 128:(b - 23) * 128, s * SEG:(s + 1) * SEG], in_=sf[:]),
                               reads=[("stgf", b % 2)], writes=["szT"])
                def abmm(h):
                    for n in range(16):
                        for j in range(16):
                            ins = h.matmul(pmm[0][:, n * 32:(n + 1) * 32], lhsT=hTs[:, j, n * 128:(n + 1) * 128], rhs=wab[:, j, :], start=(j == 0), stop=(j == 15))
                    return ins
                fw.op("pe", abmm, reads=["wab"] + hkeys, writes=[("pmm", 0)])
                fw.op("dve", lambda h: h.tensor_copy(out=abst[:].rearrange("p n c -> p (n c)"), in_=pmm[0][:, :]), reads=[("pmm", 0)], writes=["abst"])
                fw.dma("sp", lambda h, s=s: h.dma_start(out=ab_tm[s * SEG:(s + 1) * SEG, :].rearrange("(n p) c -> p n c", p=128), in_=abst[:]),
                       reads=["abst"], writes=["ab_tm"])
                for g in range(3):
                    dil = DIL[g]
                    for tt in range(4):
                        if dil == 16:
                            for rl in range(4):
                                fw.op("dve", lambda h, rl=rl, tt=tt, s=s: h.tensor_scalar(out=tA[:, rl * 128:(rl + 1) * 128], in0=iota[:, 0:128], scalar1=16.0,
                                                                                         scalar2=sgf[:, 8 + s:9 + s], op0=ALU.mult, op1=ALU.add),
                                      reads=["cst", "sgf"], writes=["tA"])
                                fw.op("dve", lambda h, rl=rl, tt=tt: h.tensor_scalar(out=tA[:, rl * 128:(rl + 1) * 128], in0=tA[:, rl * 128:(rl + 1) * 128],
                                                                                    scalar1=float(4 * tt + rl), scalar2=ifr, op0=ALU.add, op1=ALU.mult),
                                      reads=["tA", "cst"], writes=["tA"])
                        else:
                            cadd = float(512 * tt) if dil == 1 else float(tt)
                            fw.op("dve", lambda h, s=s, dil=dil: h.tensor_scalar(out=tA[:], in0=iota, scalar1=float(dil), scalar2=sgf[:, 8 + s:9 + s],
                                                                                op0=ALU.mult, op1=ALU.add), reads=["cst", "sgf"], writes=["tA"])
                            fw.op("dve", lambda h, cadd=cadd: h.tensor_scalar(out=tA[:], in0=tA[:], scalar1=cadd, scalar2=ifr, op0=ALU.add, op1=ALU.mult),
                                  reads=["tA", "cst"], writes=["tA"])
                        trig(sinT[:, tt, :], tA[:])
                        fw.op("dve", lambda h: h.tensor_scalar(out=tA[:], in0=tA[:], scalar1=0.25, scalar2=None, op0=ALU.add), reads=["tA", "trig"], writes=["tA"])
                        trig(cosT[:, tt, :], tA[:])
                    for qk in range(2):
                        for hh in range(4):
                            b = 32 + qk * 12 + g * 4 + hh
                            wi = wload(b)
                            sb_ = stgb[(qk * 4 + hh) % 2]
                            hview = None
                            for tt in range(4):
                                pb = tt % 4
                                if dil == 1:
                                    rf = lambda j, tt=tt: hTs[:, j, tt * 512:(tt + 1) * 512]
                                    oap = None
                                elif dil == 4:
                                    rf = lambda j, tt=tt: hTs[:, j, :].rearrange("p (m r) -> p r m", r=4)[:, tt, :]
                                    oap = None
                                else:
                                    rf = lambda j, tt=tt: hTs[:, j, :].rearrange("p (m r) -> p r m", r=16)[:, 4 * tt:4 * tt + 4, :]
                                    oap = pmm[pb][:, :].rearrange("p (a b) -> p a b", a=4)
                                fm_mm(wi, rf, pb, oap)
                                fw.op("act", lambda h, pb=pb, qk=qk: h.activation(out=tB[:], in_=pmm[pb][:, :], func=AF.Copy,
                                                                                 scale=(128.0 ** -0.5 if qk == 0 else 1.0)),
                                      reads=[("pmm", pb)], writes=["tB"])
                                fw.op("pe", lambda h: h.matmul(prr[:, :], lhsT=rperm, rhs=tB[:], start=True, stop=True), reads=["tB", "cst"], writes=["prr"])
                                fw.op("pool", lambda h, tt=tt: h.tensor_tensor(out=tC[:], in0=tB[:], in1=cosT[:, tt, :], op=ALU.mult), reads=["tB", "trig"], writes=["tC"])
                                fw.op("dve", lambda h, tt=tt: h.tensor_tensor(out=tD[:], in0=prr[:, :], in1=sinT[:, tt, :], op=ALU.mult), reads=["prr", "trig"], writes=["tD"])
                                fw.op("pool", lambda h, tt=tt, sb_=sb_: h.tensor_tensor(out=sb_[:, tt * 512:(tt + 1) * 512], in0=tC[:], in1=tD[:], op=ALU.add),
                                      reads=["tC", "tD"], writes=[("stgb", (qk * 4 + hh) % 2)])
                            dst = (aqT if qk == 0 else akT)[g][hh * 128:(hh + 1) * 128, :].rearrange("p (r c) -> p r c", c=CL[g])
                            fw.dma("sp", lambda h, dst=dst, sb_=sb_, g=g, s=s, dil=dil: h.dma_start(
                                out=dst[:, :, 64 + s * MS[g]:64 + (s + 1) * MS[g]], in_=sb_[:].rearrange("p (r m) -> p r m", r=dil)),
                                reads=[("stgb", (qk * 4 + hh) % 2)], writes=[("aqk", g)])
                    fw.dma("sp", lambda h, g=g: h.dma_start(out=wv[:], in_=wb_v[g]), reads=[("wb_v", g)], writes=["wv"])
                    for ct in range(16):
                        if dil == 1:
                            lf = lambda j, ct=ct: hTs[:, j, ct * 128:(ct + 1) * 128]
                            r_, m0 = 0, ct * 128
                        elif dil == 4:
                            lf = lambda j, ct=ct: hTs[:, j, :].rearrange("p (m r) -> p r m", r=4)[:, ct // 4, (ct % 4) * 128:(ct % 4 + 1) * 128]
                            r_, m0 = ct // 4, (ct % 4) * 128
                        else:
                            lf = lambda j, ct=ct: hTs[:, j, :].rearrange("p (m r) -> p r m", r=16)[:, ct, :]
                            r_, m0 = ct, 0
                        pb = ct % 4

                        def vmm(h, lf=lf, pb=pb):
                            for j in range(16):
                                ins = h.matmul(pmm[pb][:, :], lhsT=lf(j), rhs=wv[:, j, :], start=(j == 0), stop=(j == 15))
                            return ins
                        fw.op("pe", vmm, reads=["wv"] + hkeys, writes=[("pmm", pb)])
                        vs = vst[ct % 2]
                        if ct % 2:
                            fw.op("dve", lambda h, vs=vs, pb=pb: h.tensor_copy(out=vs[:], in_=pmm[pb][:, :]), reads=[("pmm", pb)], writes=[("vst", ct % 2)])
                        else:
                            fw.op("act", lambda h, vs=vs, pb=pb: h.activation(out=vs[:], in_=pmm[pb][:, :], func=AF.Copy), reads=[("pmm", pb)], writes=[("vst", ct % 2)])
                        row0 = r_ * CL[g] + 64 + s * MS[g] + m0
                        fw.dma("sp", lambda h, vs=vs, row0=row0, g=g: h.dma_start(out=avs[g][row0:row0 + 128, :], in_=vs[:]),
                               reads=[("vst", ct % 2)], writes=[("av", g)])
            fw.barrier()

        with ExitStack() as st:
            xc = [SB(st, f"xc{i}", [128, 516]) for i in range(2)]
            ca = SB(st, "ca", [128, 512]); cs_ = SB(st, "cs_", [128, 512]); cq = SB(st, "cq", [128, 512]); cr = SB(st, "cr", [128, 512])
            cn = [SB(st, f"cn{i}", [128, 512]) for i in range(2)]
            ctm = [SB(st, f"ctm{i}", [128, 4, 128]) for i in range(2)]
            pss2 = PSM(st, "pss2", [128, 512])
            ptr = [PSM(st, f"ptr{i}", [128, 512]) for i in range(2)]
            it = 0
            for cb in range(24):
                for tile in range(16):
                    t0 = tile * 512
                    s = tile // 4
                    xw = xc[it % 2]
                    kx = ("xc", it % 2)
                    fw.dma("sp", lambda h, xw=xw, cb=cb, t0=t0: h.dma_start(out=xw[:], in_=qkvpre[cb * 128:(cb + 1) * 128, t0:t0 + 516]), reads=["qkvpre"], writes=[kx])
                    if tile == 0:
                        fw.op("pool", lambda h, xw=xw: h.memset(xw[:, 0:2], 0.0), writes=[kx])
                    elif tile % 4 == 0:
                        fw.op("pool", lambda h, xw=xw, s=s: h.tensor_scalar(out=xw[:, 0:2], in0=xw[:, 0:2], scalar1=sgf[:, s:s + 1], scalar2=None, op0=ALU.mult),
                              reads=[kx, "sgf"], writes=[kx])
                    if tile == 15:
                        fw.op("pool", lambda h, xw=xw: h.memset(xw[:, 514:516], 0.0), writes=[kx])
                    elif tile % 4 == 3:
                        fw.op("pool", lambda h, xw=xw, s=s: h.tensor_scalar(out=xw[:, 514:516], in0=xw[:, 514:516], scalar1=sgf[:, 4 + s:5 + s], scalar2=None, op0=ALU.mult),
                              reads=[kx, "sgf"], writes=[kx])
                    eng = "dve"
                    fw.op(eng, lambda h, xw=xw, cb=cb: h.tensor_scalar(out=ca[:], in0=xw[:, 0:512], scalar1=convT[:, cb:cb + 1], scalar2=None, op0=ALU.mult),
                          reads=[kx, "convT"], writes=["ca"])
                    for k in range(1, 5):
                        fw.op(eng, lambda h, xw=xw, cb=cb, k=k: h.scalar_tensor_tensor(out=ca[:], in0=xw[:, k:k + 512], scalar=convT[:, k * 24 + cb:k * 24 + cb + 1],
                                                                                      in1=ca[:], op0=ALU.mult, op1=ALU.add),
                              reads=[kx, "convT", "ca"], writes=["ca"])
                    fw.op("act", lambda h: h.activation(out=cs_[:], in_=ca[:], func=AF.Silu), reads=["ca"], writes=["cs_"])
                    res_t = cs_
                    res_k = "cs_"
                    if cb < 16:
                        fw.op("pool", lambda h: h.tensor_tensor(out=cq[:], in0=cs_[:], in1=cs_[:], op=ALU.mult), reads=["cs_"], writes=["cq"])
                        fw.op("pe", lambda h: h.matmul(pss2[:, :], lhsT=ones, rhs=cq[:], start=True, stop=True), reads=["cq", "cst"], writes=["pss2"])
                        if cb < 8:
                            fw.op("act", lambda h: h.activation(out=cr[:], in_=pss2[:, :], func=AF.Sqrt, bias=128.0 * EPS, scale=128.0), reads=["pss2"], writes=["cr"])
                        else:
                            fw.op("act", lambda h: h.activation(out=cr[:], in_=pss2[:, :], func=AF.Sqrt, bias=EPS, scale=1.0), reads=["pss2"], writes=["cr"])
                        fw.op("dve", lambda h: h.reciprocal(out=cr[:], in_=cr[:]), reads=["cr"], writes=["cr"])
                        cnb = cn[it % 2]
                        fw.op("dve", lambda h, cnb=cnb: h.tensor_tensor(out=cnb[:], in0=cs_[:], in1=cr[:], op=ALU.mult), reads=["cs_", "cr"], writes=[("cn", it % 2)])
                        res_t, res_k = cnb, ("cn", it % 2)
                        dstn = qn if cb < 8 else kn
                        fw.dma("sp", lambda h, cnb=cnb, dstn=dstn, cb=cb, t0=t0: h.dma_start(out=dstn[(cb % 8) * 128:(cb % 8 + 1) * 128, t0:t0 + 512], in_=cnb[:]),
                               reads=[res_k], writes=["qkn"])
                    if cb >= 8:
                        pb = ptr[it % 2]

                        def tr(h, res_t=res_t, pb=pb):
                            for q in range(4):
                                ins = h.transpose(out=pb[:, q * 128:(q + 1) * 128], in_=res_t[:, q * 128:(q + 1) * 128], identity=ident)
                            return ins
                        fw.op("pe", tr, reads=[res_k, "cst"], writes=[("ptr", it % 2)])
                        cm = ctm[it % 2]
                        fw.op("act", lambda h, cm=cm, pb=pb: h.activation(out=cm[:].rearrange("p q c -> p (q c)"), in_=pb[:, :], func=AF.Copy),
                              reads=[("ptr", it % 2)], writes=[("ctm", it % 2)])
                        dstt = k_tm if cb < 16 else v_tm
                        fw.dma("sp", lambda h, cm=cm, dstt=dstt, cb=cb, t0=t0: h.dma_start(
                            out=dstt[t0:t0 + 512, (cb % 8) * 128:(cb % 8 + 1) * 128].rearrange("(q p) c -> p q c", p=128), in_=cm[:]),
                            reads=[("ctm", it % 2)], writes=["kv_tm"])
                    it += 1
            fw.barrier()

        with ExitStack() as st:
            ab = SB(st, "ab", [128, 64, 32])
            g_ = SB(st, "g_", [128, 64, 16]); t1_ = SB(st, "t1_", [128, 64, 16]); t2_ = SB(st, "t2_", [128, 64, 16])
            rows = [SB(st, f"rows{i}", [8, 8, 4, 128]) for i in range(2)]
            pg = PSM(st, "pg", [128, 1024])
            pr = [PSM(st, f"pr{i}", [8, 512]) for i in range(2)]
            fw.dma("sp", lambda h: h.dma_start(out=ab[:], in_=ab_tm.rearrange("(n p) c -> p n c", p=128)), reads=["ab_tm"], writes=["ab"])
            bc = lambda t: t[:].unsqueeze(1).to_broadcast([128, 64, 16])
            fw.op("dve", lambda h: h.tensor_tensor(out=t1_[:], in0=ab[:, :, 0:16], in1=bc(dtb16), op=ALU.add), reads=["ab", "dtb16"], writes=["t1_"])
            fw.op("act", lambda h: h.activation(out=t2_[:], in_=t1_[:], func=AF.Abs), reads=["t1_"], writes=["t2_"])
            fw.op("act", lambda h: h.activation(out=t2_[:], in_=t2_[:], func=AF.Exp, scale=-1.0), reads=["t2_"], writes=["t2_"])
            fw.op("act", lambda h: h.activation(out=t2_[:], in_=t2_[:], func=AF.Ln, bias=1.0, scale=1.0), reads=["t2_"], writes=["t2_"])
            fw.op("dve", lambda h: h.scalar_tensor_tensor(out=t1_[:], in0=t1_[:], scalar=0.0, in1=t2_[:], op0=ALU.max, op1=ALU.add), reads=["t1_", "t2_"], writes=["t1_"])
            fw.op("dve", lambda h: h.tensor_tensor(out=g_[:], in0=t1_[:], in1=bc(nA16), op=ALU.mult), reads=["t1_", "nA16"], writes=["g_"])
            fw.op("act", lambda h: h.activation(out=Btm[:], in_=ab[:, :, 16:32], func=AF.Sigmoid), reads=["ab"], writes=["Btm"])
            mlow = cst[:, C_MLOW:C_MLOW + 128]; mup = cst[:, C_MUP:C_MUP + 128]

            def gmm(h):
                for c in range(64):
                    h.matmul(pg[:, c * 16:c * 16 + 8], lhsT=mlow, rhs=g_[:, c, 0:8], start=True, stop=True)
                    ins = h.matmul(pg[:, c * 16 + 8:c * 16 + 16], lhsT=mup, rhs=g_[:, c, 8:16], start=True, stop=True)
                return ins
            fw.op("pe", gmm, reads=["g_", "cst"], writes=["pg"])
            fw.op("dve", lambda h: h.tensor_copy(out=Gtm[:].rearrange("p c k -> p (c k)"), in_=pg[:, :]), reads=["pg"], writes=["Gtm"])
            for c in range(64):
                pp = pr[c % 2]
                rw = rows[(c // 8) % 2]

                def rmm(h, c=c, pp=pp):
                    h.matmul(pp[:, 0:128], lhsT=g_[:, c, 0:8], rhs=mlow, start=True, stop=True)
                    h.matmul(pp[:, 128:256], lhsT=g_[:, c, 8:16], rhs=mup, start=True, stop=True)
                    h.matmul(pp[:, 256:384], lhsT=Btm[:, c, 0:8], rhs=ident, start=True, stop=True)
                    return h.matmul(pp[:, 384:512], lhsT=Btm[:, c, 8:16], rhs=ident, start=True, stop=True)
                fw.op("pe", rmm, reads=["g_", "Btm", "cst"], writes=[("pr", c % 2)])
                fw.op("dve", lambda h, c=c, pp=pp, rw=rw: h.tensor_copy(out=rw[:, c % 8, :, :].rearrange("p k t -> p (k t)"), in_=pp[:, :]),
                      reads=[("pr", c % 2)], writes=[("rows", (c // 8) % 2)])
                if c % 8 == 7:
                    c0 = c - 7
                    for k4 in range(4):
                        dr_, kd_ = k4 % 2, k4 // 2
                        fw.dma("sp", lambda h, rw=rw, c0=c0, k4=k4, dr_=dr_, kd_=kd_: h.dma_start(
                            out=GB[dr_, kd_, c0:c0 + 8, :, :].rearrange("c h t -> h c t"), in_=rw[:, :, k4, :]),
                            reads=[("rows", (c // 8) % 2)], writes=["GB"])
            fw.barrier()

        with ExitStack() as st:
            ld_q = [SB(st, f"ldq{i}", [128, 8, 128]) for i in range(2)]
            ld_k = [SB(st, f"ldk{i}", [128, 8, 128]) for i in range(2)]
            ld_kt = [SB(st, f"ldkt{i}", [128, 8, 128]) for i in range(2)]
            ld_vt = [SB(st, f"ldvt{i}", [128, 8, 128]) for i in range(2)]
            ld_g = [SB(st, f"ldg{i}", [128, 8, 128]) for i in range(2)]
            ld_b = [SB(st, f"ldb{i}", [128, 8, 128]) for i in range(2)]
            f1 = SB(st, "f1", [128, 8, 128]); f2 = SB(st, "f2", [128, 8, 128]); f3 = SB(st, "f3", [128, 8, 128]); f4 = SB(st, "f4", [128, 8, 128])
            f5 = SB(st, "f5", [128, 8, 128]); f6 = SB(st, "f6", [128, 8, 128])
            kbf = SB(st, "kbf", [128, 8, 128], BF16); qbf = SB(st, "qbf", [128, 8, 128], BF16)
            P1b = SB(st, "P1b", [128, 8, 128], BF16); Q1b = SB(st, "Q1b", [128, 8, 128], BF16)
            Dm = SB(st, "Dm", [128, 8, 128], BF16); Em = SB(st, "Em", [128, 8, 128], BF16)
            Xb = SB(st, "Xb", [128, 8, 128], BF16); X2b = SB(st, "X2b", [128, 8, 128], BF16)
            lvl = SB(st, "lvl", [128, 14 * 128])
            fw.dma("sp", lambda h: h.dma_start(out=lvl[:], in_=lvlmask[:, :]), writes=["lvl"])
            qkm = SB(st, "qkm", [128, 8, 128], BF16); qgT = SB(st, "qgT", [128, 8, 128], BF16)
            kbg = SB(st, "kbg", [128, 8, 128], BF16); vb = SB(st, "vb", [128, 8, 128], BF16); kd = SB(st, "kd", [128, 8, 128], BF16)
            wT = SB(st, "wT", [128, 8, 128], BF16); vnew = SB(st, "vnew", [128, 8, 128], BF16)
            u_ = SB(st, "u_", [128, 8, 128]); S_ = SB(st, "S_", [128, 8, 128]); Sbf = SB(st, "Sbf", [128, 8, 128], BF16)
            ost = [SB(st, f"ost{i}", [128, 8, 128]) for i in range(2)]
            sm = SB(st, "sm", [128, 8, 4])
            PA = PSM(st, "PA", [128, 1024]); PB = PSM(st, "PB", [128, 1024]); PC = PSM(st, "PC", [128, 1024]); PD = PSM(st, "PD", [128, 1024])
            print("P4 sbuf remaining", nc.sbuf_bytes_remaining)
            v3 = lambda t: t[:, :].rearrange("p (h c) -> p h c", h=8)
            maskc = lambda off: cst[:, off:off + 128].unsqueeze(1).to_broadcast([128, 8, 128])
            identb = cst[:, C_ID:C_ID + 128].unsqueeze(1).to_broadcast([128, 8, 128])
            it = 0
            for dr_ in range(2):
                fw.op("pool", lambda h: h.memset(S_[:], 0.0), writes=["S_"])
                fw.op("pool", lambda h: h.memset(Sbf[:], 0.0), writes=["Sbf"])
                order = list(range(64)) if dr_ == 0 else list(range(63, -1, -1))
                m_p1 = C_GT if dr_ == 0 else C_LT
                m_q1 = C_LT if dr_ == 0 else C_GT
                m_qk = C_LE if dr_ == 0 else C_GE
                last = 127 if dr_ == 0 else 0
                for c in order:
                    b2 = it % 2
                    it += 1
                    t0 = c * 128
                    lq, lk, lkt, lvt, lg, lb = ld_q[b2], ld_k[b2], ld_kt[b2], ld_vt[b2], ld_g[b2], ld_b[b2]
                    K = lambda n: (n, b2)
                    fw.dma("sp", lambda h, lq=lq, t0=t0: h.dma_start(out=lq[:], in_=qn[:, t0:t0 + 128].rearrange("(h p) t -> p h t", p=128)), reads=["qkn"], writes=[K("ldq")])
                    fw.dma("sp", lambda h, lk=lk, t0=t0: h.dma_start(out=lk[:], in_=kn[:, t0:t0 + 128].rearrange("(h p) t -> p h t", p=128)), reads=["qkn"], writes=[K("ldk")])
                    fw.dma("sp", lambda h, lkt=lkt, t0=t0: h.dma_start(out=lkt[:], in_=k_tm[t0:t0 + 128, :].rearrange("p (h c) -> p h c", h=8)), reads=["kv_tm"], writes=[K("ldkt")])
                    fw.dma("sp", lambda h, lvt=lvt, t0=t0: h.dma_start(out=lvt[:], in_=v_tm[t0:t0 + 128, :].rearrange("p (h c) -> p h c", h=8)), reads=["kv_tm"], writes=[K("ldvt")])
                    fw.dma("sp", lambda h, lg=lg, c=c, dr_=dr_: h.dma_start(out=lg[:].rearrange("p h t -> p (h t)"),
                                                                          in_=GB[dr_, 0, c:c + 1, :, :].rearrange("c h t -> c (h t)").partition_broadcast(128)),
                           reads=["GB"], writes=[K("ldg")])
                    fw.dma("sp", lambda h, lb=lb, c=c, dr_=dr_: h.dma_start(out=lb[:].rearrange("p h t -> p (h t)"),
                                                                          in_=GB[dr_, 1, c:c + 1, :, :].rearrange("c h t -> c (h t)").partition_broadcast(128)),
                           reads=["GB"], writes=[K("ldb")])
                    Gp = Gtm[:, c, dr_ * 8:dr_ * 8 + 8]
                    Bp = Btm[:, c, dr_ * 8:dr_ * 8 + 8]
                    Gpb = Gp.unsqueeze(2).to_broadcast([128, 8, 128])
                    Bpb = Bp.unsqueeze(2).to_broadcast([128, 8, 128])
                    fw.op("act", lambda h, lk=lk: h.activation(out=kbf[:], in_=lk[:], func=AF.Copy), reads=[K("ldk")], writes=["kbf"])
                    fw.op("act", lambda h, lq=lq: h.activation(out=qbf[:], in_=lq[:], func=AF.Copy), reads=[K("ldq")], writes=["qbf"])

                    def kk(h):
                        for hd in range(8):
                            h.matmul(PA[:, hd * 128:(hd + 1) * 128], lhsT=kbf[:, hd, :], rhs=kbf[:, hd, :], start=True, stop=True)
                            ins = h.matmul(PB[:, hd * 128:(hd + 1) * 128], lhsT=kbf[:, hd, :], rhs=qbf[:, hd, :], start=True, stop=True)
                        return ins
                    fw.op("pe", kk, reads=["kbf", "qbf"], writes=["PA", "PB"])
                    fw.op("dve", lambda h, lg=lg, Gpb=Gpb: h.tensor_tensor(out=f1[:], in0=lg[:], in1=Gpb, op=ALU.subtract), reads=[K("ldg"), "Gtm"], writes=["f1"])
                    fw.op("pool", lambda h: h.tensor_scalar(out=f2[:], in0=f1[:], scalar1=0.0, scalar2=None, op0=ALU.max), reads=["f1"], writes=["f2"])
                    fw.op("act", lambda h: h.activation(out=f2[:], in_=f2[:], func=AF.Exp, scale=-1.0), reads=["f2"], writes=["f2"])
                    fw.op("pool", lambda h: h.tensor_scalar(out=f3[:], in0=f1[:], scalar1=0.0, scalar2=None, op0=ALU.min), reads=["f1"], writes=["f3"])
                    fw.op("act", lambda h: h.activation(out=f3[:], in_=f3[:], func=AF.Exp), reads=["f3"], writes=["f3"])
                    fw.op("pool", lambda h, m_p1=m_p1: h.tensor_tensor(out=f2[:], in0=f2[:], in1=maskc(m_p1), op=ALU.mult), reads=["f2", "cst"], writes=["f2"])
                    fw.op("dve", lambda h, Bpb=Bpb: h.scalar_tensor_tensor(out=f2[:], in0=f2[:], scalar=-1.0, in1=Bpb, op0=ALU.mult, op1=ALU.mult),
                          reads=["f2", "Btm"], writes=["f2"])
                    fw.op("dve", lambda h: h.tensor_tensor(out=P1b[:], in0=v3(PA), in1=f2[:], op=ALU.mult), reads=["PA", "f2"], writes=["P1b"])
                    fw.op("dve", lambda h, lb=lb: h.scalar_tensor_tensor(out=f4[:], in0=f3[:], scalar=-1.0, in1=lb[:], op0=ALU.mult, op1=ALU.mult),
                          reads=["f3", K("ldb")], writes=["f4"])
                    fw.op("pool", lambda h, m_q1=m_q1: h.tensor_tensor(out=f4[:], in0=f4[:], in1=maskc(m_q1), op=ALU.mult), reads=["f4", "cst"], writes=["f4"])
                    fw.op("dve", lambda h: h.tensor_tensor(out=Q1b[:], in0=v3(PA), in1=f4[:], op=ALU.mult), reads=["PA", "f4"], writes=["Q1b"])
                    fw.op("pool", lambda h, m_qk=m_qk: h.tensor_tensor(out=f3[:], in0=f3[:], in1=maskc(m_qk), op=ALU.mult), reads=["f3", "cst", "f4"], writes=["f3"])
                    fw.op("dve", lambda h: h.tensor_tensor(out=qkm[:], in0=v3(PB), in1=f3[:], op=ALU.mult), reads=["PB", "f3"], writes=["qkm"])
                    fw.op("act", lambda h, lg=lg: h.activation(out=f5[:], in_=lg[:], func=AF.Exp), reads=[K("ldg")], writes=["f5"])
                    fw.op("pool", lambda h, lq=lq: h.tensor_tensor(out=qgT[:], in0=lq[:], in1=f5[:], op=ALU.mult), reads=[K("ldq"), "f5"], writes=["qgT"])
                    fw.op("act", lambda h, Gp=Gp: h.activation(out=sm[:, :, 0], in_=Gp, func=AF.Exp), reads=["Gtm"], writes=["sm0"])
                    fw.op("dve", lambda h, Bp=Bp: h.tensor_tensor(out=sm[:, :, 0], in0=sm[:, :, 0], in1=Bp, op=ALU.mult), reads=["sm0", "Btm"], writes=["sm0"])
                    fw.op("dve", lambda h, lg=lg, Gp=Gp, last=last: h.tensor_tensor(out=sm[:, :, 1], in0=lg[:, :, last], in1=Gp, op=ALU.subtract),
                          reads=[K("ldg"), "Gtm"], writes=["sm1"])
                    fw.op("act", lambda h: h.activation(out=sm[:, :, 1], in_=sm[:, :, 1], func=AF.Exp), reads=["sm1"], writes=["sm1"])
                    fw.op("act", lambda h, lg=lg, last=last: h.activation(out=sm[:, :, 2], in_=lg[:, :, last], func=AF.Exp), reads=[K("ldg")], writes=["sm2"])
                    smb = lambda i: sm[:, :, i:i + 1].to_broadcast([128, 8, 128])
                    fw.op("pool", lambda h, lkt=lkt: h.tensor_tensor(out=kbg[:], in0=lkt[:], in1=smb(0), op=ALU.mult), reads=[K("ldkt"), "sm0"], writes=["kbg"])
                    fw.op("pool", lambda h, lvt=lvt, Bpb=Bpb: h.tensor_tensor(out=vb[:], in0=lvt[:], in1=Bpb, op=ALU.mult), reads=[K("ldvt"), "Btm"], writes=["vb"])
                    fw.op("pool", lambda h, lkt=lkt: h.tensor_tensor(out=kd[:], in0=lkt[:], in1=smb(1), op=ALU.mult), reads=[K("ldkt"), "sm1"], writes=["kd"])
                    lm = lambda k, up: lvl[:, (2 * k + up) * 128:(2 * k + up + 1) * 128].unsqueeze(1).to_broadcast([128, 8, 128])
                    dsel = 0 if dr_ == 0 else 1
                    fw.op("pool", lambda h: h.tensor_tensor(out=Dm[:], in0=P1b[:], in1=lm(0, dsel), op=ALU.mult), reads=["P1b", "lvl"], writes=["Dm"])
                    fw.op("pool", lambda h: h.tensor_tensor(out=Dm[:], in0=Dm[:], in1=identb, op=ALU.add), reads=["Dm", "cst"], writes=["Dm"])
                    fw.op("pool", lambda h: h.tensor_tensor(out=Em[:], in0=Q1b[:], in1=lm(0, 1 - dsel), op=ALU.mult), reads=["Q1b", "lvl"], writes=["Em"])
                    fw.op("pool", lambda h: h.tensor_tensor(out=Em[:], in0=Em[:], in1=identb, op=ALU.add), reads=["Em", "cst"], writes=["Em"])
                    for k in range(1, 7):
                        def xmm(h):
                            for hd in range(8):
                                h.matmul(PC[:, hd * 128:(hd + 1) * 128], lhsT=Q1b[:, hd, :], rhs=Dm[:, hd, :], start=True, stop=True)
                                ins = h.matmul(PD[:, hd * 128:(hd + 1) * 128], lhsT=P1b[:, hd, :], rhs=Em[:, hd, :], start=True, stop=True)
                            return ins
                        fw.op("pe", xmm, reads=["Q1b", "P1b", "Dm", "Em"], writes=["PC", "PD"])
                        fw.op("act", lambda h: h.activation(out=Xb[:], in_=v3(PC), func=AF.Copy), reads=["PC"], writes=["Xb"])
                        fw.op("act", lambda h: h.activation(out=X2b[:], in_=v3(PD), func=AF.Copy), reads=["PD"], writes=["X2b"])

                        def zmm(h):
                            for hd in range(8):
                                h.matmul(PA[:, hd * 128:(hd + 1) * 128], lhsT=Em[:, hd, :], rhs=Xb[:, hd, :], start=True, stop=True)
                                ins = h.matmul(PB[:, hd * 128:(hd + 1) * 128], lhsT=Dm[:, hd, :], rhs=X2b[:, hd, :], start=True, stop=True)
                            return ins
                        fw.op("pe", zmm, reads=["Em", "Dm", "Xb", "X2b"], writes=["PA", "PB"])
                        fw.op("dve", lambda h, k=k: h.tensor_tensor(out=f1[:], in0=v3(PA), in1=lm(k, dsel), op=ALU.mult), reads=["PA", "lvl"], writes=["f1"])
                        fw.op("pool", lambda h: h.tensor_tensor(out=Dm[:], in0=Dm[:], in1=f1[:], op=ALU.add), reads=["Dm", "f1"], writes=["Dm"])
                        fw.op("dve", lambda h, k=k: h.tensor_tensor(out=f6[:], in0=v3(PB), in1=lm(k, 1 - dsel), op=ALU.mult), reads=["PB", "lvl"], writes=["f6"])
                        fw.op("pool", lambda h: h.tensor_tensor(out=Em[:], in0=Em[:], in1=f6[:], op=ALU.add), reads=["Em", "f6"], writes=["Em"])

                    def umm(h):
                        for hd in range(8):
                            h.matmul(PB[:, hd * 128:(hd + 1) * 128], lhsT=Em[:, hd, :], rhs=vb[:, hd, :], start=True, stop=True)
                            ins = h.matmul(PA[:, hd * 128:(hd + 1) * 128], lhsT=kbg[:, hd, :], rhs=Em[:, hd, :], start=True, stop=True)
                        return ins
                    fw.op("pe", umm, reads=["Em", "vb", "kbg"], writes=["PA", "PB"])
                    fw.op("act", lambda h: h.activation(out=u_[:], in_=v3(PB), func=AF.Copy), reads=["PB"], writes=["u_"])
                    fw.op("dve", lambda h: h.tensor_copy(out=wT[:], in_=v3(PA)), reads=["PA"], writes=["wT"])
                    seg = c // 16
                    if dr_ == 0 and c % 16 == 0 and c > 0:
                        fw.op("dve", lambda h, seg=seg: h.tensor_scalar(out=S_[:], in0=S_[:], scalar1=sgf[:, seg:seg + 1], scalar2=None, op0=ALU.mult), reads=["S_", "sgf"], writes=["S_"])
                        fw.op("act", lambda h: h.activation(out=Sbf[:], in_=S_[:], func=AF.Copy), reads=["S_"], writes=["Sbf"])
                    if dr_ == 1 and c % 16 == 15 and c < 63:
                        fw.op("dve", lambda h, seg=seg: h.tensor_scalar(out=S_[:], in0=S_[:], scalar1=sgf[:, 4 + seg:5 + seg], scalar2=None, op0=ALU.mult), reads=["S_", "sgf"], writes=["S_"])
                        fw.op("act", lambda h: h.activation(out=Sbf[:], in_=S_[:], func=AF.Copy), reads=["S_"], writes=["Sbf"])

                    def wsmm(h):
                        for hd in range(8):
                            ins = h.matmul(PC[:, hd * 128:(hd + 1) * 128], lhsT=wT[:, hd, :], rhs=Sbf[:, hd, :], start=True, stop=True)
                        return ins
                    fw.op("pe", wsmm, reads=["wT", "Sbf"], writes=["PC"])
                    fw.op("dve", lambda h: h.tensor_tensor(out=vnew[:], in0=u_[:], in1=v3(PC), op=ALU.subtract), reads=["u_", "PC"], writes=["vnew"])

                    def omm(h):
                        for hd in range(8):
                            h.matmul(PD[:, hd * 128:(hd + 1) * 128], lhsT=qgT[:, hd, :], rhs=Sbf[:, hd, :], start=True, stop=False)
                            h.matmul(PD[:, hd * 128:(hd + 1) * 128], lhsT=qkm[:, hd, :], rhs=vnew[:, hd, :], start=False, stop=True)
                            ins = h.matmul(PB[:, hd * 128:(hd + 1) * 128], lhsT=kd[:, hd, :], rhs=vnew[:, hd, :], start=True, stop=True)
                        return ins
                    fw.op("pe", omm, reads=["qgT", "Sbf", "qkm", "vnew", "kd"], writes=["PD", "PB"])
                    os_ = ost[b2]
                    fw.op("act", lambda h, os_=os_: h.activation(out=os_[:], in_=v3(PD), func=AF.Copy), reads=["PD"], writes=[K("ost")])
                    fw.dma("sp", lambda h, os_=os_, dr_=dr_, t0=t0: h.dma_start(out=o_dir[dr_, t0:t0 + 128, :].rearrange("p (h c) -> p h c", h=8), in_=os_[:]),
                           reads=[K("ost")], writes=["o_dir"])
                    fw.op("pool", lambda h: h.tensor_tensor(out=S_[:], in0=S_[:], in1=smb(2), op=ALU.mult), reads=["S_", "sm2"], writes=["S_"])
                    fw.op("dve", lambda h: h.tensor_tensor(out=S_[:], in0=S_[:], in1=v3(PB), op=ALU.add), reads=["S_", "PB"], writes=["S_"])
                    fw.op("act", lambda h: h.activation(out=Sbf[:], in_=S_[:], func=AF.Copy), reads=["S_"], writes=["Sbf"])
            fw.barrier()

        with ExitStack() as st:
            kw = [SB(st, f"kw{i}", [128, 256], BF16) for i in range(2)]
            qw = [SB(st, f"qw{i}", [128, 128], BF16) for i in range(2)]
            vw = [SB(st, f"vw{i}", [128, 2, 128], BF16) for i in range(2)]
            sc_ = [SB(st, f"sc{i}", [128, 256]) for i in range(2)]
            pe_ = [SB(st, f"pe{i}", [128, 256], BF16) for i in range(2)]
            pT = [SB(st, f"pT{i}", [128, 2, 128], BF16) for i in range(2)]
            oo = [SB(st, f"oo{i}", [128, 130]) for i in range(2)]
            nmx = [SB(st, f"nmx{i}", [128, 1]) for i in range(2)]
            idb = SB(st, "idb", [128, 128], BF16)
            msk = SB(st, "msk", [128, 4, 3, 256])
            mk1 = SB(st, "mk1", [128, 1])
            psc = [PSM(st, f"psc{i}", [128, 256]) for i in range(2)]
            ppt = [PSM(st, f"ppt{i}", [128, 2, 128], BF16) for i in range(2)]
            pov = [PSM(st, f"pov{i}", [128, 128]) for i in range(2)]
            fw.op("dve", lambda h: h.tensor_copy(out=idb[:], in_=ident), reads=["cst"], writes=["idb"])
            band = cst[:, C_BAND:C_BAND + 256]; negl = cst[:, C_NEGL:C_NEGL + 256]; negr = cst[:, C_NEGR:C_NEGR + 256]
            for s in range(NSEG):
                fw.op("dve", lambda h, s=s: h.tensor_scalar(out=mk1[:], in0=sgf[:, s:s + 1], scalar1=-1.0, scalar2=1.0, op0=ALU.mult, op1=ALU.add), reads=["sgf"], writes=["mk1"])
                fw.op("dve", lambda h, s=s: h.scalar_tensor_tensor(out=msk[:, s, 0, :], in0=negl, scalar=mk1[:, 0:1], in1=band, op0=ALU.mult, op1=ALU.add),
                      reads=["mk1", "cst"], writes=["msk"])
                fw.op("dve", lambda h, s=s: h.tensor_scalar(out=mk1[:], in0=sgf[:, 4 + s:5 + s], scalar1=-1.0, scalar2=1.0, op0=ALU.mult, op1=ALU.add), reads=["sgf", "msk"], writes=["mk1"])
                fw.op("dve", lambda h, s=s: h.scalar_tensor_tensor(out=msk[:, s, 1, :], in0=negr, scalar=mk1[:, 0:1], in1=band, op0=ALU.mult, op1=ALU.add),
                      reads=["mk1", "cst"], writes=["msk"])
                fw.op("dve", lambda h, s=s: h.scalar_tensor_tensor(out=msk[:, s, 2, :], in0=negr, scalar=mk1[:, 0:1], in1=msk[:, s, 0, :], op0=ALU.mult, op1=ALU.add),
                      reads=["mk1", "cst", "msk"], writes=["msk"])
            it = 0
            for g in range(3):
                dil = DIL[g]
                ntile = MC[g] // 128
                tps = MS[g] // 128
                for hh in range(4):
                    kfull = akT[g][hh * 128:(hh + 1) * 128, :].rearrange("p (r c) -> p r c", c=CL[g])
                    qfull = aqT[g][hh * 128:(hh + 1) * 128, :].rearrange("p (r c) -> p r c", c=CL[g])
                    vfull = avs[g][:, hh * 128:(hh + 1) * 128].rearrange("(r c) d -> r c d", c=CL[g])
                    for r in range(dil):
                        for b in range(ntile):
                            i2 = it % 2
                            it += 1
                            K = lambda n: (n, i2)
                            s = b // tps
                            first = (b % tps == 0)
                            lastt = (b % tps == tps - 1)
                            fw.dma("sp", lambda h, i2=i2, kfull=kfull, r=r, b=b: h.dma_start(out=kw[i2][:], in_=kfull[:, r, 128 * b:128 * b + 256]), reads=[("aqk", g), ("akT", g)], writes=[K("kw")])
                            fw.dma("sp", lambda h, i2=i2, qfull=qfull, r=r, b=b: h.dma_start(out=qw[i2][:], in_=qfull[:, r, 64 + 128 * b:64 + 128 * b + 128]), reads=[("aqk", g)], writes=[K("qw")])
                            fw.dma("sp", lambda h, i2=i2, vfull=vfull, r=r, b=b: h.dma_start(out=vw[i2][:], in_=vfull[r, 128 * b:128 * b + 256, :].rearrange("(k p) d -> p k d", p=128)),
                                   reads=[("av", g)], writes=[K("vw")])
                            fw.op("pe", lambda h, i2=i2: h.matmul(psc[i2][:, :], lhsT=qw[i2][:], rhs=kw[i2][:], start=True, stop=True), reads=[K("qw"), K("kw")], writes=[K("psc")])
                            if first and lastt:
                                mask = msk[:, s, 2, :]
                            elif first:
                                mask = msk[:, s, 0, :]
                            elif lastt:
                                mask = msk[:, s, 1, :]
                            else:
                                mask = band
                            fw.op("dve", lambda h, i2=i2, mask=mask: h.tensor_tensor(out=sc_[i2][:], in0=psc[i2][:, :], in1=mask, op=ALU.add), reads=[K("psc"), "msk", "cst"], writes=[K("sc")])
                            fw.op("dve", lambda h, i2=i2: h.reduce_max(out=oo[i2][:, 128:129], in_=sc_[i2][:], axis=AX.X), reads=[K("sc")], writes=[K("oo")])
                            fw.op("dve", lambda h, i2=i2: h.tensor_scalar(out=nmx[i2][:], in0=oo[i2][:, 128:129], scalar1=-1.0, scalar2=None, op0=ALU.mult), reads=[K("oo")], writes=[K("nmx")])
                            fw.op("act", lambda h, i2=i2: h.activation(out=pe_[i2][:], in_=sc_[i2][:], func=AF.Exp, bias=nmx[i2][:], scale=1.0, accum_out=oo[i2][:, 129:130]),
                                  reads=[K("sc"), K("nmx")], writes=[K("pe"), K("oo")])

                            def ptr_(h, i2=i2):
                                h.transpose(out=ppt[i2][:, 0, :], in_=pe_[i2][:, 0:128], identity=idb[:])
                                return h.transpose(out=ppt[i2][:, 1, :], in_=pe_[i2][:, 128:256], identity=idb[:])
                            fw.op("pe", ptr_, reads=[K("pe"), "idb"], writes=[K("ppt")])
                            fw.op("act", lambda h, i2=i2: h.activation(out=pT[i2][:], in_=ppt[i2][:, :, :], func=AF.Copy), reads=[K("ppt")], writes=[K("pT")])

                            def pv(h, i2=i2):
                                h.matmul(pov[i2][:, :], lhsT=pT[i2][:, 0, :], rhs=vw[i2][:, 0, :], start=True, stop=False)
                                return h.matmul(pov[i2][:, :], lhsT=pT[i2][:, 1, :], rhs=vw[i2][:, 1, :], start=False, stop=True)
                            fw.op("pe", pv, reads=[K("pT"), K("vw")], writes=[K("pov")])
                            fw.op("dve", lambda h, i2=i2: h.tensor_copy(out=oo[i2][:, 0:128], in_=pov[i2][:, :]), reads=[K("pov")], writes=[K("oo")])
                            dst = Oat[g, :, hh, :].rearrange("(m r) c -> r m c", r=dil)[r, 128 * b:128 * b + 128, :]
                            fw.dma("sp", lambda h, i2=i2, dst=dst: h.dma_start(out=dst, in_=oo[i2][:]), reads=[K("oo")], writes=["Oat"])
            fw.barrier()

        with ExitStack() as st:
            of_ = [SB(st, f"of{i}", [128, 8, 128]) for i in range(2)]
            ob_ = [SB(st, f"ob{i}", [128, 8, 128]) for i in range(2)]
            szt = [SB(st, f"szt{i}", [128, 8, 128]) for i in range(2)]
            osq = SB(st, "osq", [128, 8, 128])
            ss8 = SB(st, "ss8", [128, 8])
            yd = [SB(st, f"yd{i}", [128, 8, 128], BF16) for i in range(2)]
            og = [[SB(st, f"og{g}_{i}", [128, 4, 130]) for i in range(2)] for g in range(3)]
            m4 = SB(st, "m4", [128, 4]); w4 = SB(st, "w4", [128, 3, 4]); d4 = SB(st, "d4", [128, 4]); t4 = SB(st, "t4", [128, 4])
            am = SB(st, "am", [128, 4, 128]); am2 = SB(st, "am2", [128, 4, 128])
            ya = [SB(st, f"ya{i}", [128, 4, 128], BF16) for i in range(2)]
            pdn = PSM(st, "pdn", [128, 1024])
            pat = PSM(st, "pat", [128, 512])
            for tl in range(64):
                t0 = tl * 128
                i2 = tl % 2
                K = lambda n: (n, i2)
                fw.dma("sp", lambda h, i2=i2, t0=t0: h.dma_start(out=of_[i2][:], in_=o_dir[0, t0:t0 + 128, :].rearrange("p (h c) -> p h c", h=8)), reads=["o_dir"], writes=[K("of")])
                fw.dma("sp", lambda h, i2=i2, t0=t0: h.dma_start(out=ob_[i2][:], in_=o_dir[1, t0:t0 + 128, :].rearrange("p (h c) -> p h c", h=8)), reads=["o_dir"], writes=[K("ob")])
                fw.dma("sp", lambda h, i2=i2, t0=t0: h.dma_start(out=szt[i2][:], in_=szT[:, t0:t0 + 128].rearrange("(h p) t -> p h t", p=128)), reads=["szT"], writes=[K("szt")])
                fw.op("pool", lambda h, i2=i2: h.tensor_tensor(out=of_[i2][:], in0=of_[i2][:], in1=ob_[i2][:], op=ALU.add), reads=[K("of"), K("ob")], writes=[K("of")])
                fw.op("pool", lambda h, i2=i2: h.tensor_tensor(out=osq[:], in0=of_[i2][:], in1=of_[i2][:], op=ALU.mult), reads=[K("of")], writes=["osq"])
                fw.op("dve", lambda h: h.tensor_reduce(out=ss8[:], in_=osq[:], axis=AX.X, op=ALU.add), reads=["osq"], writes=["ss8"])
                fw.op("act", lambda h: h.activation(out=ss8[:], in_=ss8[:], func=AF.Sqrt, bias=EPS, scale=1.0 / 128.0), reads=["ss8"], writes=["ss8"])
                fw.op("dve", lambda h: h.reciprocal(out=ss8[:], in_=ss8[:]), reads=["ss8"], writes=["ss8"])
                fw.op("dve", lambda h, i2=i2: h.tensor_tensor(out=of_[i2][:], in0=of_[i2][:], in1=ss8[:].unsqueeze(2).to_broadcast([128, 8, 128]), op=ALU.mult),
                      reads=[K("of"), "ss8"], writes=[K("of")])

                def trd(h, i2=i2):
                    for hd in range(8):
                        ins = h.transpose(out=pdn[:, hd * 128:(hd + 1) * 128], in_=of_[i2][:, hd, :], identity=ident)
                    return ins
                fw.op("pe", trd, reads=[K("of"), "cst"], writes=["pdn"])
                fw.op("dve", lambda h, i2=i2: h.scalar_tensor_tensor(out=yd[i2][:], in0=pdn[:, :].rearrange("p (h c) -> p h c", h=8), scalar=dnwT[:, 0:1], in1=szt[i2][:],
                                                                   op0=ALU.mult, op1=ALU.mult), reads=["pdn", "dnwT", K("szt")], writes=[K("yd")])
                fw.dma("sp", lambda h, i2=i2, t0=t0: h.dma_start(out=ydnT[:, t0:t0 + 128].rearrange("(h p) t -> p h t", p=128), in_=yd[i2][:]), reads=[K("yd")], writes=["ydnT"])
                for g in range(3):
                    fw.dma("sp", lambda h, g=g, i2=i2, t0=t0: h.dma_start(out=og[g][i2][:], in_=Oat[g, t0:t0 + 128, :, :]), reads=["Oat"], writes=[K(f"og{g}")])
                mxs = [og[g][i2][:, :, 128] for g in range(3)]
                dns = [og[g][i2][:, :, 129] for g in range(3)]
                fw.op("dve", lambda h: h.tensor_tensor(out=m4[:], in0=mxs[0], in1=mxs[1], op=ALU.max), reads=[K("og0"), K("og1")], writes=["m4"])
                fw.op("dve", lambda h: h.tensor_tensor(out=m4[:], in0=m4[:], in1=mxs[2], op=ALU.max), reads=["m4", K("og2")], writes=["m4"])
                for g in range(3):
                    fw.op("dve", lambda h, g=g: h.tensor_tensor(out=w4[:, g, :], in0=mxs[g], in1=m4[:], op=ALU.subtract), reads=[K(f"og{g}"), "m4"], writes=["w4"])
                fw.op("act", lambda h: h.activation(out=w4[:], in_=w4[:], func=AF.Exp), reads=["w4"], writes=["w4"])
                fw.op("dve", lambda h: h.tensor_tensor(out=d4[:], in0=w4[:, 0, :], in1=dns[0], op=ALU.mult), reads=["w4", K("og0")], writes=["d4"])
                for g in (1, 2):
                    fw.op("dve", lambda h, g=g: h.tensor_tensor(out=t4[:], in0=w4[:, g, :], in1=dns[g], op=ALU.mult), reads=["w4", K(f"og{g}")], writes=["t4"])
                    fw.op("dve", lambda h: h.tensor_tensor(out=d4[:], in0=d4[:], in1=t4[:], op=ALU.add), reads=["d4", "t4"], writes=["d4"])
                fw.op("dve", lambda h: h.reciprocal(out=d4[:], in_=d4[:]), reads=["d4"], writes=["d4"])
                for g in range(3):
                    fw.op("dve", lambda h, g=g: h.tensor_tensor(out=w4[:, g, :], in0=w4[:, g, :], in1=d4[:], op=ALU.mult), reads=["w4", "d4"], writes=["w4"])
                fw.op("pool", lambda h, i2=i2: h.tensor_tensor(out=am[:], in0=og[0][i2][:, :, 0:128], in1=w4[:, 0, :].unsqueeze(2).to_broadcast([128, 4, 128]), op=ALU.mult),
                      reads=[K("og0"), "w4"], writes=["am"])
                for g in (1, 2):
                    fw.op("pool", lambda h, g=g, i2=i2: h.tensor_tensor(out=am2[:], in0=og[g][i2][:, :, 0:128], in1=w4[:, g, :].unsqueeze(2).to_broadcast([128, 4, 128]), op=ALU.mult),
                          reads=[K(f"og{g}"), "w4"], writes=["am2"])
                    fw.op("pool", lambda h: h.tensor_tensor(out=am[:], in0=am[:], in1=am2[:], op=ALU.add), reads=["am", "am2"], writes=["am"])

                def tra(h):
                    for hd in range(4):
                        ins = h.transpose(out=pat[:, hd * 128:(hd + 1) * 128], in_=am[:, hd, :], identity=ident)
                    return ins
                fw.op("pe", tra, reads=["am", "cst"], writes=["pat"])
                fw.op("act", lambda h, i2=i2: h.activation(out=ya[i2][:], in_=pat[:, :].rearrange("p (h c) -> p h c", h=4), func=AF.Copy), reads=["pat"], writes=[K("ya")])
                fw.dma("sp", lambda h, i2=i2, t0=t0: h.dma_start(out=yatT[:, t0:t0 + 128].rearrange("(h p) t -> p h t", p=128), in_=ya[i2][:]), reads=[K("ya")], writes=["yatT"])
            fw.barrier()

        with ExitStack() as st:
            xin = [SB(st, f"xin{q}", [128, D]) for q in range(4)]
            xT = SB(st, "xT", [128, 16, 512])
            acc = SB(st, "acc", [128, 16, 512])
            sq = [SB(st, f"sq{i}", [128, 512]) for i in range(2)]
            rstd = SB(st, "rstd", [128, 512])
            big = SB(st, "big", [128, 32, 512], BF16)
            h2T = SB(st, "h2T", [128, 16, 512], BF16)
            wk = [SB(st, f"wk{i}", [128, 32, 128], BF16) for i in range(3)]
            print("sbuf remaining", nc.sbuf_bytes_remaining)
            tmp = [SB(st, f"tmp{i}", [128, 512]) for i in range(4)]
            pxt = [PSM(st, f"pxt{i}", [128, 512]) for i in range(2)]
            pss = PSM(st, "pss", [128, 512])
            pmm = [PSM(st, f"pmm{i}", [128, 512]) for i in range(4)]
            hT = big[:, 0:16, :]
            ydn_s = big[:, 16:24, :]
            yat_s = big[:, 24:28, :]
            mixedT = h2T

            wslot = [0]

            def load_w(src_ap, nj, key):
                i = wslot[0] % 3
                wslot[0] += 1
                fw.dma("sp", lambda h, i=i: h.dma_start(out=wk[i][:, 0:nj, :], in_=src_ap), reads=[key], writes=[("wk", i)])
                return i

            def proj(widx, nj, rhs_fn, rhs_keys, pbank, first=True, last=True, j0=0):
                def mm(h):
                    for j in range(nj):
                        ins = h.matmul(pmm[pbank][:, :], lhsT=wk[widx][:, j, :], rhs=rhs_fn(j),
                                       start=(first and j == 0), stop=(last and j == nj - 1))
                    return ins
                fw.op("pe", mm, reads=[("wk", widx)] + rhs_keys, writes=[("pmm", pbank)])

            for tile in range(min(T // 512, KTILES)):
                t0 = tile * 512
                s = tile // 4
                front_end((xin, xT, sq, rstd, pxt, pss), t0, s)
                for j in range(16):
                    fw.op("pool", lambda h, j=j: h.tensor_tensor(out=tmp[j % 2][:], in0=xT[:, j, :], in1=rstd[:], op=ALU.mult),
                          reads=[("xT", j), "rstd"], writes=[("tmp", j % 2)])
                    fw.op("dve", lambda h, j=j: h.tensor_scalar(out=hT[:, j, :], in0=tmp[j % 2][:], scalar1=A1[:, j, s:s + 1],
                                                                scalar2=modT[:, j, s:s + 1], op0=ALU.mult, op1=ALU.add),
                          reads=[("tmp", j % 2), "A1", "modT"], writes=[("big", j)])
                fw.dma("sp", lambda h: h.dma_start(out=ydn_s, in_=ydnT[:, t0:t0 + 512].rearrange("(j p) t -> p j t", p=128)),
                       reads=["ydnT"], writes=[("big", 16 + j) for j in range(8)])
                fw.dma("sp", lambda h: h.dma_start(out=yat_s, in_=yatT[:, t0:t0 + 512].rearrange("(j p) t -> p j t", p=128)),
                       reads=["yatT"], writes=[("big", 24 + j) for j in range(4)])
                for cb in range(16):
                    w1 = load_w(wb_mg[cb], 16, ("wb_mg", cb))
                    proj(w1, 16, lambda j: hT[:, j, :], [("big", j) for j in range(16)], 0)
                    w2 = load_w(wb_mg[16 + cb], 16, ("wb_mg", 16 + cb))
                    proj(w2, 16, lambda j: hT[:, j, :], [("big", j) for j in range(16)], 1)
                    w3 = load_w(wb_dn[cb], 8, ("wb_dn", cb))
                    proj(w3, 8, lambda j: ydn_s[:, j, :], [("big", 16 + j) for j in range(8)], 2)
                    w4 = load_w(wb_at[cb], 4, ("wb_at", cb))
                    proj(w4, 4, lambda j: yat_s[:, j, :], [("big", 24 + j) for j in range(4)], 3)
                    fw.op("act", lambda h: h.activation(out=tmp[0][:], in_=pmm[0][:, :], func=AF.Sigmoid), reads=[("pmm", 0)], writes=[("tmp", 0)])
                    fw.op("act", lambda h: h.activation(out=tmp[1][:], in_=pmm[1][:, :], func=AF.Sigmoid), reads=[("pmm", 1)], writes=[("tmp", 1)])
                    fw.op("dve", lambda h: h.tensor_tensor(out=tmp[2][:], in0=pmm[2][:, :], in1=tmp[0][:], op=ALU.mult),
                          reads=[("pmm", 2), ("tmp", 0)], writes=[("tmp", 2)])
                    fw.op("dve", lambda h: h.tensor_tensor(out=tmp[3][:], in0=pmm[3][:, :], in1=tmp[1][:], op=ALU.mult),
                          reads=[("pmm", 3), ("tmp", 1)], writes=[("tmp", 3)])
                    fw.op("pool", lambda h, cb=cb: h.tensor_tensor(out=mixedT[:, cb, :], in0=tmp[2][:], in1=tmp[3][:], op=ALU.add),
                          reads=[("tmp", 2), ("tmp", 3)], writes=[("h2T", cb)])
                for cb in range(16):
                    w1 = load_w(wb_out[cb], 16, ("wb_out", cb))
                    proj(w1, 16, lambda j: mixedT[:, j, :], [("h2T", j) for j in range(16)], cb % 4)
                    fw.op("act", lambda h, cb=cb: h.activation(out=acc[:, cb, :], in_=pmm[cb % 4][:, :], func=AF.Copy),
                          reads=[("pmm", cb % 4)], writes=[("acc", cb)])
                sumsq_rstd(lambda j: acc[:, j, :], 16, sq, pss, rstd, lambda j: ("acc", j))
                for j in range(16):
                    fw.op("pool", lambda h, j=j: h.tensor_tensor(out=tmp[j % 2][:], in0=acc[:, j, :], in1=rstd[:], op=ALU.mult),
                          reads=[("acc", j), "rstd"], writes=[("tmp", j % 2)])
                    fw.op("dve", lambda h, j=j: h.scalar_tensor_tensor(out=xT[:, j, :], in0=tmp[j % 2][:], scalar=G1[:, j, s:s + 1], in1=xT[:, j, :],
                                                                       op0=ALU.mult, op1=ALU.add),
                          reads=[("tmp", j % 2), "G1", ("xT", j)], writes=[("xT", j)])
                sumsq_rstd(lambda j: xT[:, j, :], 16, sq, pss, rstd, lambda j: ("xT", j))
                for j in range(16):
                    fw.op("pool", lambda h, j=j: h.tensor_tensor(out=tmp[j % 2][:], in0=xT[:, j, :], in1=rstd[:], op=ALU.mult),
                          reads=[("xT", j), "rstd"], writes=[("tmp", j % 2)])
                    fw.op("dve", lambda h, j=j: h.tensor_scalar(out=h2T[:, j, :], in0=tmp[j % 2][:], scalar1=A2[:, j, s:s + 1],
                                                                scalar2=modT[:, 48 + j, s:s + 1], op0=ALU.mult, op1=ALU.add),
                          reads=[("tmp", j % 2), "A2", "modT"], writes=[("h2T", j)])
                for hf in range(2):
                    for fb in range(32):
                        w1 = load_w(wb_f1[hf * 32 + fb], 16, ("wb_f1", hf * 32 + fb))
                        pb = fb % 4
                        proj(w1, 16, lambda j: h2T[:, j, :], [("h2T", j) for j in range(16)], pb)
                        if fb % 2 == 0:
                            fw.op("act", lambda h, pb=pb: h.activation(out=tmp[pb][:], in_=pmm[pb][:, :], func=AF.Relu), reads=[("pmm", pb)], writes=[("tmp", pb)])
                            fw.op("pool", lambda h, pb=pb, fb=fb: h.tensor_tensor(out=big[:, fb, :], in0=tmp[pb][:], in1=tmp[pb][:], op=ALU.mult),
                                  reads=[("tmp", pb)], writes=[("big", fb)])
                        else:
                            fw.op("dve", lambda h, pb=pb: h.tensor_scalar(out=tmp[pb][:], in0=pmm[pb][:, :], scalar1=0.0, scalar2=None, op0=ALU.max),
                                  reads=[("pmm", pb)], writes=[("tmp", pb)])
                            fw.op("pool", lambda h, pb=pb, fb=fb: h.tensor_tensor(out=big[:, fb, :], in0=tmp[pb][:], in1=tmp[pb][:], op=ALU.mult),
                                  reads=[("tmp", pb)], writes=[("big", fb)])
                    for cb in range(16):
                        w1 = load_w(wb_f2[hf * 16 + cb], 32, ("wb_f2", hf * 16 + cb))
                        pb = cb % 4
                        proj(w1, 32, lambda j: big[:, j, :], [("big", j) for j in range(32)], pb)
                        if hf == 0:
                            fw.op("act", lambda h, cb=cb, pb=pb: h.activation(out=acc[:, cb, :], in_=pmm[pb][:, :], func=AF.Copy),
                                  reads=[("pmm", pb)], writes=[("acc", cb)])
                        else:
                            fw.op("dve", lambda h, cb=cb, pb=pb: h.tensor_tensor(out=acc[:, cb, :], in0=pmm[pb][:, :], in1=acc[:, cb, :], op=ALU.add),
                                  reads=[("pmm", pb), ("acc", cb)], writes=[("acc", cb)])
                sumsq_rstd(lambda j: acc[:, j, :], 16, sq, pss, rstd, lambda j: ("acc", j))
                for j in range(16):
                    fw.op("pool", lambda h, j=j: h.tensor_tensor(out=tmp[j % 2][:], in0=acc[:, j, :], in1=rstd[:], op=ALU.mult),
                          reads=[("acc", j), "rstd"], writes=[("tmp", j % 2)])
                    fw.op("dve", lambda h, j=j: h.scalar_tensor_tensor(out=acc[:, j, :], in0=tmp[j % 2][:], scalar=G2[:, j, s:s + 1], in1=xT[:, j, :],
                                                                       op0=ALU.mult, op1=ALU.add),
                          reads=[("tmp", j % 2), "G2", ("xT", j)], writes=[("acc", j)])
                for q in range(4):
                    for jg in range(4):
                        pb = pmm[(q * 4 + jg) % 4]

                        def tr(h, q=q, jg=jg, pb=pb):
                            for jj in range(4):
                                j = jg * 4 + jj
                                ins = h.transpose(out=pb[:, jj * 128:(jj + 1) * 128], in_=acc[:, j, q * 128:(q + 1) * 128], identity=ident)
                            return ins
                        fw.op("pe", tr, reads=[("acc", jg * 4 + jj) for jj in range(4)] + ["cst"], writes=[("pmm", (q * 4 + jg) % 4)])
                        eng = "act" if jg % 2 == 0 else "dve"
                        if eng == "act":
                            fw.op("act", lambda h, q=q, jg=jg, pb=pb: h.activation(out=xin[q][:, jg * 512:(jg + 1) * 512], in_=pb[:, :], func=AF.Copy),
                                  reads=[("pmm", (q * 4 + jg) % 4)], writes=[("xin", q)])
                        else:
                            fw.op("dve", lambda h, q=q, jg=jg, pb=pb: h.tensor_copy(out=xin[q][:, jg * 512:(jg + 1) * 512], in_=pb[:, :]),
                                  reads=[("pmm", (q * 4 + jg) % 4)], writes=[("xin", q)])
                    fw.dma("sp", lambda h, q=q: h.dma_start(out=y[t0 + q * 128:t0 + (q + 1) * 128, :], in_=xin[q][:]),
                           reads=[("xin", q)], writes=["y"])
            fw.barrier()
        fw.emit_all()
    return nc


_NC_CACHE = {}

SAMPLE_MAP = {2: [0, 1, 2], 3: [3, 4, 5], 4: [6, 7, 8], 5: [9, 10, 11], 6: [12, 13], 7: [14, 15]}


def kernel(x_prompt, x_sample, c_prompt, c_sample, w_ada, b_ada, norm_pre_mix, norm_post_mix,
           norm_pre_ffn, norm_post_ffn, w_in, conv_w, A_log, dt_bias, dn_norm_w, w_dn_out,
           w_at_out, w_out, w_ff1, w_ff2):
    f = lambda a: np.ascontiguousarray(np.asarray(a, dtype=np.float32))
    x_prompt, x_sample, c_prompt, c_sample = f(x_prompt), f(x_sample), f(c_prompt), f(c_sample)
    if "nc" not in _NC_CACHE:
        _NC_CACHE["nc"] = build_program()
    nc = _NC_CACHE["nc"]
    shared = {
        "consts": make_consts(), "lvlmask": make_lvlmask(),
        "w_ada": f(w_ada)[0], "b_ada": f(b_ada)[0].reshape(96, 128),
        "norms": np.concatenate([f(norm_pre_mix)[0], f(norm_post_mix)[0], f(norm_pre_ffn)[0], f(norm_post_ffn)[0]]).reshape(64, 128),
        "w_in": f(w_in)[0], "conv_w": f(conv_w)[0].reshape(120, 128), "A_log": f(A_log)[0].reshape(1, 16),
        "dt_bias": f(dt_bias)[0].reshape(1, 16), "dn_norm_w": f(dn_norm_w)[0].reshape(1, 128),
        "w_dn_out": f(w_dn_out)[0], "w_at_out": f(w_at_out)[0], "w_out": f(w_out)[0],
        "w_ff1": f(w_ff1)[0], "w_ff2": f(w_ff2)[0],
    }
    in_maps = []
    for core in range(8):
        xs = np.zeros((T, D), np.float32)
        cs = np.zeros((NSEG, D), np.float32)
        sg = np.zeros((128, 16), np.float32)
        if core < 2:
            xs[:] = x_prompt[core]
            cs[:] = c_prompt[core][None, :]
            for s in range(NSEG):
                sg[:, s] = 1.0 if s > 0 else 0.0
                sg[:, 4 + s] = 1.0 if s < NSEG - 1 else 0.0
                sg[:, 8 + s] = s * SEG
        else:
            seqs = SAMPLE_MAP[core]
            for s in range(NSEG):
                b = seqs[s] if s < len(seqs) else seqs[0]
                xs[s * SEG:(s + 1) * SEG] = x_sample[b]
                cs[s] = c_sample[b]
        m = dict(shared)
        m["x"] = xs
        m["c"] = cs.reshape(NSEG * NJ, 128)
        m["segf"] = sg
        in_maps.append(m)
    res = run_bass_kernel_spmd(nc, in_maps, core_ids=list(range(8)))
    if KDEBUG:
        _NC_CACHE["res"] = res
    y_prompt = np.stack([np.asarray(res.results[c]["y"], dtype=np.float32) for c in range(2)])
    y_sample = np.zeros_like(x_sample)
    for core, seqs in SAMPLE_MAP.items():
        yc = np.asarray(res.results[core]["y"], dtype=np.float32)
        for s, b in enumerate(seqs):
            y_sample[b] = yc[s * SEG:(s + 1) * SEG]
    return (y_prompt, y_sample)
```

```python
from contextlib import ExitStack
import numpy as np
import concourse.bass as bass
import concourse.mybir as mybir
from concourse.bass_utils import run_bass_kernel_spmd

F32 = mybir.dt.float32
BF16 = mybir.dt.bfloat16
I32 = mybir.dt.int32
AF = mybir.ActivationFunctionType
ALU = mybir.AluOpType
AX = mybir.AxisListType

D = 2048
NJ = 16
T = 8192
NSEG = 4
SEG = 2048
DFF = 8192
EPS = 1e-6
IN_COLS = 12832
NEG = -1.0e30

import os
KDEBUG = int(os.environ.get("KDEBUG", "0"))
KTILES = int(os.environ.get("KTILES", "16"))
KDUMP = os.environ.get("KDUMP", "").split(",")
KSCOPES = int(os.environ.get("KSCOPES", "0"))
COMPUTE = ("pe", "act", "dve", "pool")
N_DMA_SEMS = 12


class Ticket:
    __slots__ = ("kind", "eng", "val", "sem")

    def __init__(self, kind, eng, val, sem=None):
        self.kind, self.eng, self.val, self.sem = kind, eng, val, sem


class _Rec:
    def __init__(self):
        self.calls = []

    def __getattr__(self, name):
        def f(*a, **k):
            self.calls.append((name, a, k))
            return self
        return f


def _replay(h, calls):
    ins = None
    for name, a, k in calls:
        ins = getattr(h, name)(*a, **k)
    return ins


class FW:
    def __init__(self, nc, es):
        self.nc = nc
        self.streams = {k: [] for k in ("pe", "act", "dve", "pool", "sp")}
        self.sem = {}
        self.count = {}
        for k in COMPUTE:
            self.sem[k] = es.enter_context(nc.semaphore("s_" + k))
            self.count[k] = 0
        self.dsem, self.dcount, self.dnext = {}, {}, {}
        for q in ("sp", "act", "pool"):
            self.dsem[q] = [es.enter_context(nc.semaphore(f"d_{q}{i}")) for i in range(N_DMA_SEMS)]
            self.dcount[q] = [0] * N_DMA_SEMS
            self.dnext[q] = 0
        self.known = {k: {} for k in self.streams}
        self.lastw = {}
        self.readers = {}
        self.n_ops = 0
        self.phase = "P0"

    def _need(self, stream, t, waits):
        if t is None:
            return
        if t.kind == "c":
            if t.eng == stream and stream == "pe":
                return
            key = ("c", t.eng)
            sem = self.sem[t.eng]
        else:
            key = ("d", id(t.sem))
            sem = t.sem
        if self.known[stream].get(key, 0) >= t.val:
            return
        cur = waits.get(key)
        if cur is None or cur[1] < t.val:
            waits[key] = (sem, t.val)

    def _deps(self, stream, reads, writes):
        waits = {}
        for r in reads:
            self._need(stream, self.lastw.get(r), waits)
        for w in writes:
            self._need(stream, self.lastw.get(w), waits)
            for t in self.readers.get(w, {}).values():
                self._need(stream, t, waits)
        out = []
        for key, (sem, val) in waits.items():
            self.known[stream][key] = val
            out.append((sem, val))
        return out

    def _commit(self, t, reads, writes):
        for w in writes:
            self.lastw[w] = t
            self.readers[w] = {}
        k = ("c", t.eng) if t.kind == "c" else ("d", id(t.sem))
        for r in reads:
            self.readers.setdefault(r, {})[k] = t

    def op(self, eng, fn, reads=(), writes=()):
        waits = self._deps(eng, reads, writes)
        self.count[eng] += 1
        val = self.count[eng]
        sem = self.sem[eng]

        rec = _Rec()
        fn(rec)
        calls = rec.calls

        def emit(h, waits=waits, calls=calls, sem=sem):
            for s, v in waits:
                h.wait_ge(s, v)
            _replay(h, calls).then_inc(sem, 1)

        emit.phase = self.phase
        self.streams[eng].append(emit)
        self._commit(Ticket("c", eng, val), reads, writes)
        self.n_ops += 1

    def dma(self, q, fn, reads=(), writes=()):
        i = self.dnext[q]
        self.dnext[q] = (i + 1) % N_DMA_SEMS
        sem = self.dsem[q][i]
        prev = self.dcount[q][i]
        waits = self._deps(q, reads, writes)
        key = ("d", id(sem))
        if prev > 0 and self.known[q].get(key, 0) < prev:
            waits.append((sem, prev))
            self.known[q][key] = prev
        val = prev + 16
        self.dcount[q][i] = val

        rec = _Rec()
        fn(rec)
        calls = rec.calls

        def emit(h, waits=waits, calls=calls, sem=sem):
            for s, v in waits:
                h.wait_ge(s, v)
            _replay(h, calls).then_inc(sem, 16)

        emit.phase = self.phase
        self.streams[q].append(emit)
        self._commit(Ticket("d", q, val, sem), reads, writes)
        self.n_ops += 1

    def barrier(self):
        targets = [(self.sem[k], self.count[k], ("c", k)) for k in COMPUTE if self.count[k] > 0]
        for q in self.dsem:
            for i, s in enumerate(self.dsem[q]):
                if self.dcount[q][i] > 0:
                    targets.append((s, self.dcount[q][i], ("d", id(s))))
        for stream in self.streams:
            ws = []
            for s, v, key in targets:
                if self.known[stream].get(key, 0) < v:
                    ws.append((s, v))
                    self.known[stream][key] = v

            def emit(h, ws=ws):
                for s, v in ws:
                    h.wait_ge(s, v)

            self.streams[stream].append(emit)
        self.lastw = {}
        self.readers = {}

    def emit_all(self):
        nc = self.nc

        def run(h, fs):
            if not KSCOPES:
                for f in fs:
                    f(h)
                return
            cur = None
            sid = None
            for f in fs:
                ph = getattr(f, "phase", cur)
                if ph != cur:
                    if cur is not None:
                        nc.leave_named_scope(cur, sid, False)
                    sid, _ = nc.enter_named_scope(ph, False)
                    cur = ph
                f(h)
            if cur is not None:
                nc.leave_named_scope(cur, sid, False)

        with nc.Block() as block:
            @block.tensor
            def _(h):
                run(h, self.streams["pe"])

            @block.scalar
            def _(h):
                run(h, self.streams["act"])

            @block.vector
            def _(h):
                run(h, self.streams["dve"])

            @block.gpsimd
            def _(h):
                run(h, self.streams["pool"])

            @block.sync
            def _(h):
                run(h, self.streams["sp"])


C_ID, C_MEAN, C_ONE, C_MLOW, C_MUP, C_GT, C_LT, C_LE, C_GE = 0, 128, 256, 384, 512, 640, 768, 896, 1024
C_RPERM, C_IOTA, C_IFREQ, C_BAND, C_NEGL, C_NEGR, C_TOTAL = 1152, 1280, 1792, 1793, 2049, 2305, 2561


def make_lvlmask():
    p = np.arange(128)[:, None]
    f = np.arange(128)[None, :]
    m = np.zeros((128, 14 * 128), np.float32)
    for k in range(7):
        same_hi = (p >> (k + 1)) == (f >> (k + 1))
        diff_lo = (p >> k) != (f >> k)
        m[:, (2 * k) * 128:(2 * k + 1) * 128] = same_hi & diff_lo & (p > f)
        m[:, (2 * k + 1) * 128:(2 * k + 2) * 128] = same_hi & diff_lo & (p < f)
    return m


def make_consts():
    c = np.zeros((128, C_TOTAL), np.float32)
    p = np.arange(128)[:, None]
    f = np.arange(128)[None, :]
    c[:, C_ID:C_ID + 128] = (p == f)
    c[:, C_MEAN:C_MEAN + 128] = 1.0 / D
    c[:, C_ONE:C_ONE + 128] = 1.0
    c[:, C_MLOW:C_MLOW + 128] = (p <= f)
    c[:, C_MUP:C_MUP + 128] = (p >= f)
    c[:, C_GT:C_GT + 128] = (p > f)
    c[:, C_LT:C_LT + 128] = (p < f)
    c[:, C_LE:C_LE + 128] = (p <= f)
    c[:, C_GE:C_GE + 128] = (p >= f)
    r = np.zeros((128, 128), np.float32)
    for m in range(64):
        r[m + 64, m] = -1.0
        r[m, m + 64] = 1.0
    c[:, C_RPERM:C_RPERM + 128] = r
    c[:, C_IOTA:C_IOTA + 512] = np.arange(512)[None, :]
    c[:, C_IFREQ] = (10000.0 ** (-(np.arange(128) % 64) / 64.0)) / (2 * np.pi)
    a = np.arange(128)[:, None]
    b = np.arange(256)[None, :]
    c[:, C_BAND:C_BAND + 256] = np.where((b - a >= 0) & (b - a <= 128), 0.0, NEG)
    c[:, C_NEGL:C_NEGL + 256] = np.where(b < 64, NEG, 0.0) * np.ones((128, 1))
    c[:, C_NEGR:C_NEGR + 256] = np.where(b >= 192, NEG, 0.0) * np.ones((128, 1))
    return c


def build_program():
    nc = bass.Bass("TRN2", target_bir_lowering=False)

    def EI(name, shape):
        return nc.dram_tensor(name, list(shape), F32, kind="ExternalInput").ap()

    x = EI("x", [T, D])
    cvec = EI("c", [NSEG * NJ, 128])
    segf = EI("segf", [128, 16])
    consts = EI("consts", [128, C_TOTAL])
    lvlmask = EI("lvlmask", [128, 14 * 128])
    w_ada = EI("w_ada", [D, 6 * D])
    b_ada = EI("b_ada", [96, 128])
    norms = EI("norms", [64, 128])
    w_in = EI("w_in", [D, IN_COLS])
    conv_w = EI("conv_w", [120, 128])
    alog = EI("A_log", [1, 16])
    dtb = EI("dt_bias", [1, 16])
    dnw = EI("dn_norm_w", [1, 128])
    w_dn_out = EI("w_dn_out", [1024, D])
    w_at_out = EI("w_at_out", [512, D])
    w_out = EI("w_out", [D, D])
    w_ff1 = EI("w_ff1", [D, DFF])
    w_ff2 = EI("w_ff2", [DFF, D])
    y = nc.dram_tensor("y", [T, D], F32, kind="ExternalOutput").ap()
    dbg_names = []

    def dump(fw, name, ap, keys, dt=F32):
        if not KDEBUG:
            return
        dtn = nc.dram_tensor("dbg_" + name, list(ap.shape), dt, kind="ExternalOutput").ap()
        dbg_names.append("dbg_" + name)
        fw.dma("sp", lambda h: h.dma_start(out=dtn, in_=ap), reads=list(keys), writes=["dbg_" + name])

    def DR(name, shape, dt=F32):
        if KDEBUG and name in KDUMP:
            dbg_names.append(name)
            return nc.dram_tensor(name, list(shape), dt, kind="ExternalOutput").ap()
        return nc.dram_tensor(name, list(shape), dt, kind="Internal").ap()

    wb_mg = DR("wb_mg", [32, 128, 16, 128], BF16)
    wb_dn = DR("wb_dn", [16, 128, 8, 128], BF16)
    wb_at = DR("wb_at", [16, 128, 4, 128], BF16)
    wb_out = DR("wb_out", [16, 128, 16, 128], BF16)
    wb_f1 = DR("wb_f1", [64, 128, 16, 128], BF16)
    wb_f2 = DR("wb_f2", [32, 128, 32, 128], BF16)
    ydnT = DR("ydnT", [1024, T], BF16)
    yatT = DR("yatT", [512, T], BF16)
    MERGE0 = 3072 + 1024 + 32 + 4608

    es = ExitStack()
    with es:
        fw = FW(nc, es)

        uid = [0]

        def SB(st, name, shape, dt=F32):
            uid[0] += 1
            return st.enter_context(nc.sbuf_tensor(f"{name}_{uid[0]}", list(shape), dt))

        def PSM(st, name, shape, dt=F32):
            uid[0] += 1
            return st.enter_context(nc.psum_tensor(f"{name}_{uid[0]}", list(shape), dt))

        cst = SB(es, "cst", [128, C_TOTAL])
        sgf = SB(es, "sgf", [128, 16])
        nrm = SB(es, "nrm", [128, 64])
        cT = SB(es, "cT", [128, 64])
        badaT = SB(es, "badaT", [128, 96])
        modT = SB(es, "modT", [128, 96, 4])
        A1 = SB(es, "A1", [128, 16, 4]); G1 = SB(es, "G1", [128, 16, 4])
        A2 = SB(es, "A2", [128, 16, 4]); G2 = SB(es, "G2", [128, 16, 4])
        ident = cst[:, C_ID:C_ID + 128]
        meanm = cst[:, C_MEAN:C_MEAN + 128]
        fw.dma("sp", lambda h: h.dma_start(out=cst[:], in_=consts[:, :]), writes=["cst"])
        fw.dma("sp", lambda h: h.dma_start(out=sgf[:], in_=segf[:, :]), writes=["sgf"])

        def cast(dst, src, key):
            fw.dma("pool", lambda h: h.dma_start(out=dst, in_=src), writes=[key])

        for b in range(32):
            cast(wb_mg[b], w_in[:, MERGE0 + b * 128:MERGE0 + (b + 1) * 128].rearrange("(j p) c -> p j c", p=128), ("wb_mg", b))
        for b in range(16):
            cast(wb_dn[b], w_dn_out[:, b * 128:(b + 1) * 128].rearrange("(j p) c -> p j c", p=128), ("wb_dn", b))
            cast(wb_at[b], w_at_out[:, b * 128:(b + 1) * 128].rearrange("(j p) c -> p j c", p=128), ("wb_at", b))
            cast(wb_out[b], w_out[:, b * 128:(b + 1) * 128].rearrange("(j p) c -> p j c", p=128), ("wb_out", b))
        for b in range(64):
            cast(wb_f1[b], w_ff1[:, b * 128:(b + 1) * 128].rearrange("(j p) c -> p j c", p=128), ("wb_f1", b))
        for hf in range(2):
            for b in range(16):
                cast(wb_f2[hf * 16 + b],
                     w_ff2[hf * 4096:(hf + 1) * 4096, b * 128:(b + 1) * 128].rearrange("(j p) c -> p j c", p=128),
                     ("wb_f2", hf * 16 + b))

        fw.phase = "P0mod"
        with ExitStack() as st:
            stg = SB(st, "stg0", [128, 128])
            stg2 = SB(st, "stg1", [128, 128])
            wa = [SB(st, f"wa{i}", [128, 16, 128]) for i in range(2)]
            pt = PSM(st, "p0t", [128, 512])
            pm = [PSM(st, f"p0m{i}", [128, 4]) for i in range(2)]
            fw.dma("sp", lambda h: h.dma_start(out=stg[0:64, :], in_=norms[:, :]), writes=["stg0"])
            fw.dma("sp", lambda h: h.dma_start(out=stg[64:128, :], in_=cvec[:, :]), writes=["stg0"])
            fw.op("pe", lambda h: h.transpose(out=pt[:, 0:128], in_=stg[:], identity=ident), reads=["stg0", "cst"], writes=["p0t"])
            fw.op("dve", lambda h: h.tensor_copy(out=nrm[:], in_=pt[:, 0:64]), reads=["p0t"], writes=["nrm"])
            fw.op("act", lambda h: h.activation(out=cT[:], in_=pt[:, 64:128], func=AF.Silu), reads=["p0t"], writes=["cT"])
            fw.dma("sp", lambda h: h.dma_start(out=stg2[0:96, :], in_=b_ada[:, :]), writes=["stg1"])
            fw.op("pe", lambda h: h.transpose(out=pt[:, 128:224], in_=stg2[0:96, :], identity=ident[0:96, 0:96]), reads=["stg1", "cst"], writes=["p0t"])
            fw.op("dve", lambda h: h.tensor_copy(out=badaT[:], in_=pt[:, 128:224]), reads=["p0t"], writes=["badaT"])
            cTv = cT[:].rearrange("p (s j) -> p j s", j=16)
            for cb in range(96):
                wt = wa[cb % 2]
                fw.dma("sp", lambda h, wt=wt, cb=cb: h.dma_start(
                    out=wt[:], in_=w_ada[:, cb * 128:(cb + 1) * 128].rearrange("(j p) c -> p j c", p=128)),
                    writes=[("wa", cb % 2)])

                def mm(h, wt=wt, cb=cb):
                    for j in range(16):
                        ins = h.matmul(pm[cb % 2][:, :], lhsT=wt[:, j, :], rhs=cTv[:, j, :], start=(j == 0), stop=(j == 15))
                    return ins
                fw.op("pe", mm, reads=[("wa", cb % 2), "cT"], writes=[("p0m", cb % 2)])
                fw.op("dve", lambda h, cb=cb: h.tensor_scalar(out=modT[:, cb, :], in0=pm[cb % 2][:, :], scalar1=badaT[:, cb:cb + 1],
                                                              scalar2=None, op0=ALU.add),
                      reads=[("p0m", cb % 2), "badaT"], writes=["modT"])

            def nv(v):
                return nrm[:, v * 16:(v + 1) * 16].unsqueeze(2).to_broadcast([128, 16, 4])
            fw.op("dve", lambda h: h.scalar_tensor_tensor(out=A1[:], in0=modT[:, 16:32, :], scalar=1.0, in1=nv(0), op0=ALU.add, op1=ALU.mult),
                  reads=["modT", "nrm"], writes=["A1"])
            fw.op("dve", lambda h: h.tensor_tensor(out=G1[:], in0=modT[:, 32:48, :], in1=nv(1), op=ALU.mult), reads=["modT", "nrm"], writes=["G1"])
            fw.op("dve", lambda h: h.scalar_tensor_tensor(out=A2[:], in0=modT[:, 64:80, :], scalar=1.0, in1=nv(2), op0=ALU.add, op1=ALU.mult),
                  reads=["modT", "nrm"], writes=["A2"])
            fw.op("dve", lambda h: h.tensor_tensor(out=G2[:], in0=modT[:, 80:96, :], in1=nv(3), op=ALU.mult), reads=["modT", "nrm"], writes=["G2"])
            dump(fw, "modT", modT[:], ["modT"])
            dump(fw, "A1", A1[:], ["A1"])
            dump(fw, "nrm", nrm[:], ["nrm"])
            fw.barrier()

        def front_end(st_bufs, t0, seg):
            xin, xT, sq, rstd, pxt, pss = st_bufs
            for q in range(4):
                fw.dma("sp", lambda h, q=q: h.dma_start(out=xin[q][:], in_=x[t0 + q * 128:t0 + (q + 1) * 128, :]), writes=[("xin", q)])
            for j in range(16):
                pb = pxt[j % 2]

                def tr(h, j=j, pb=pb):
                    for q in range(4):
                        ins = h.transpose(out=pb[:, q * 128:(q + 1) * 128], in_=xin[q][:, j * 128:(j + 1) * 128], identity=ident)
                    return ins
                fw.op("pe", tr, reads=[("xin", q) for q in range(4)] + ["cst"], writes=[("pxt", j % 2)])
                fw.op("act", lambda h, j=j, pb=pb: h.activation(out=xT[:, j, :], in_=pb[:, :], func=AF.Copy), reads=[("pxt", j % 2)], writes=[("xT", j)])
                fw.op("dve", lambda h, j=j, pb=pb: h.tensor_tensor(out=sq[j % 2][:], in0=pb[:, :], in1=xT[:, j, :], op=ALU.mult),
                      reads=[("pxt", j % 2), ("xT", j)], writes=[("sq", j % 2)])
                fw.op("pe", lambda h, j=j: h.matmul(pss[:, :], lhsT=meanm, rhs=sq[j % 2][:], start=(j == 0), stop=(j == 15)),
                      reads=[("sq", j % 2), "cst"], writes=["pss"])
            fw.op("act", lambda h: h.activation(out=rstd[:], in_=pss[:, :], func=AF.Sqrt, bias=EPS, scale=1.0), reads=["pss"], writes=["rstd"])
            fw.op("dve", lambda h: h.reciprocal(out=rstd[:], in_=rstd[:]), reads=["rstd"], writes=["rstd"])

        def sumsq_rstd(src_fn, nblk, sq, pss, rstd, src_keys, scale_mean=True):
            for j in range(nblk):
                fw.op("pool", lambda h, j=j: h.tensor_tensor(out=sq[j % 2][:], in0=src_fn(j), in1=src_fn(j), op=ALU.mult),
                      reads=[src_keys(j)], writes=[("sq", j % 2)])
                fw.op("pe", lambda h, j=j: h.matmul(pss[:, :], lhsT=meanm, rhs=sq[j % 2][:], start=(j == 0), stop=(j == nblk - 1)),
                      reads=[("sq", j % 2), "cst"], writes=["pss"])
            fw.op("act", lambda h: h.activation(out=rstd[:], in_=pss[:, :], func=AF.Sqrt, bias=EPS, scale=1.0), reads=["pss"], writes=["rstd"])
            fw.op("dve", lambda h: h.reciprocal(out=rstd[:], in_=rstd[:]), reads=["rstd"], writes=["rstd"])

        DNQ0, Z0, AB0, ATQ0, ATK0, ATV0 = 0, 3072, 4096, 4128, 4128 + 1536, 4128 + 3072
        DIL = (1, 4, 16)
        MC = [T // d_ for d_ in DIL]
        MS = [SEG // d_ for d_ in DIL]
        CL = [m_ + 128 for m_ in MC]
        qkvpre = DR("qkvpre", [3072, T + 4])
        szT = DR("szT", [1024, T])
        ab_tm = DR("ab_tm", [T, 32])
        aqT = [DR(f"aqT{g}", [512, DIL[g] * CL[g]], BF16) for g in range(3)]
        akT = [DR(f"akT{g}", [512, DIL[g] * CL[g]], BF16) for g in range(3)]
        avs = [DR(f"av{g}", [DIL[g] * CL[g], 512], BF16) for g in range(3)]
        qn = DR("qn", [1024, T]); kn = DR("kn", [1024, T])
        k_tm = DR("k_tm", [T, 1024]); v_tm = DR("v_tm", [T, 1024])
        GB = DR("GB", [2, 2, 64, 8, 128])
        o_dir = DR("o_dir", [2, T, 1024])
        Oat = DR("Oat", [3, T, 4, 130])
        wb_fm = DR("wb_fm", [56, 128, 16, 128], BF16)
        wb_v = DR("wb_v", [3, 128, 16, 512], BF16)
        wb_ab = DR("wb_ab", [128, 16, 32], BF16)
        for b in range(56):
            c0 = (DNQ0 + 128 * b) if b < 24 else (Z0 + 128 * (b - 24)) if b < 32 else (ATQ0 + 128 * (b - 32)) if b < 44 else (ATK0 + 128 * (b - 44))
            cast(wb_fm[b], w_in[:, c0:c0 + 128].rearrange("(j p) c -> p j c", p=128), ("wb_fm", b))
        for g in range(3):
            cast(wb_v[g], w_in[:, ATV0 + 512 * g:ATV0 + 512 * (g + 1)].rearrange("(j p) c -> p j c", p=128), ("wb_v", g))
        cast(wb_ab[:, :, :], w_in[:, AB0:AB0 + 32].rearrange("(j p) c -> p j c", p=128), "wb_ab")

        convT = SB(es, "convT", [128, 120])
        dnwT = SB(es, "dnwT", [128, 1])
        dtb16 = SB(es, "dtb16", [128, 16])
        nA16 = SB(es, "nA16", [128, 16])
        Gtm = SB(es, "Gtm", [128, 64, 16])
        Btm = SB(es, "Btm", [128, 64, 16])
        ones = cst[:, C_ONE:C_ONE + 128]
        with ExitStack() as st:
            stg = SB(st, "stgc", [128, 128])
            pt = PSM(st, "p1t", [128, 512])
            fw.dma("sp", lambda h: h.dma_start(out=stg[0:120, :], in_=conv_w[:, :]), writes=["stgc"])
            fw.op("pe", lambda h: h.transpose(out=pt[:, 0:120], in_=stg[0:120, :], identity=ident[0:120, 0:120]), reads=["stgc", "cst"], writes=["p1t"])
            fw.op("dve", lambda h: h.tensor_copy(out=convT[:], in_=pt[:, 0:120]), reads=["p1t"], writes=["convT"])
            fw.dma("sp", lambda h: h.dma_start(out=stg[0:1, :], in_=dnw[:, :]), writes=["stgc"])
            fw.op("pe", lambda h: h.transpose(out=pt[:, 128:129], in_=stg[0:1, :], identity=ident[0:1, 0:1]), reads=["stgc", "cst"], writes=["p1t"])
            fw.op("dve", lambda h: h.tensor_copy(out=dnwT[:], in_=pt[:, 128:129]), reads=["p1t"], writes=["dnwT"])
            fw.dma("sp", lambda h: h.dma_start(out=dtb16[:], in_=dtb.partition_broadcast(128)), writes=["dtb16"])
            fw.dma("sp", lambda h: h.dma_start(out=nA16[:], in_=alog.partition_broadcast(128)), writes=["nA16"])
            fw.op("act", lambda h: h.activation(out=nA16[:], in_=nA16[:], func=AF.Exp), reads=["nA16"], writes=["nA16"])
            fw.op("dve", lambda h: h.tensor_scalar(out=nA16[:], in0=nA16[:], scalar1=-1.0, scalar2=None, op0=ALU.mult), reads=["nA16"], writes=["nA16"])
            zb = SB(st, "zb", [128, 16, 512], BF16)
            fw.op("pool", lambda h: h.memset(zb[:], 0.0), writes=["zb"])
            for g in range(3):
                dil = DIL[g]
                for hh in range(4):
                    kv = akT[g][hh * 128:(hh + 1) * 128, :].rearrange("p (r c) -> p r c", c=CL[g])
                    fw.dma("sp", lambda h, kv=kv, dil=dil: h.dma_start(out=kv[:, :, 0:64], in_=zb[:, 0:dil, 0:64]), reads=["zb"], writes=[("akT", g)])
                    fw.dma("sp", lambda h, kv=kv, dil=dil, g=g: h.dma_start(out=kv[:, :, 64 + MC[g]:128 + MC[g]], in_=zb[:, 0:dil, 0:64]), reads=["zb"], writes=[("akT", g)])
                vv = avs[g].rearrange("(r c) d -> c r d", c=CL[g])
                fw.dma("sp", lambda h, vv=vv, dil=dil: h.dma_start(out=vv[0:64, :, :], in_=zb[0:64, 0:dil, :]), reads=["zb"], writes=[("av", g)])
                fw.dma("sp", lambda h, vv=vv, dil=dil, g=g: h.dma_start(out=vv[64 + MC[g]:128 + MC[g], :, :], in_=zb[0:64, 0:dil, :]), reads=["zb"], writes=[("av", g)])
            fw.barrier()

        fw.phase = "P1"
        with ExitStack() as st:
            xin = [SB(st, f"xin{q}", [128, D]) for q in range(4)]
            hTs = SB(st, "hTs", [128, 16, SEG], BF16)
            junk = SB(st, "junk", [128, D], BF16)
            ssq = SB(st, "ssq", [128, 4])
            stgf = [SB(st, f"stgf{i}", [128, SEG]) for i in range(2)]
            stgb = [SB(st, f"stgb{i}", [128, SEG], BF16) for i in range(2)]
            wk = [SB(st, f"wk{i}", [128, 16, 128], BF16) for i in range(3)]
            wv = SB(st, "wv", [128, 16, 512], BF16)
            wab = SB(st, "wab", [128, 16, 32], BF16)
            cosT = SB(st, "cosT", [128, 4, 512]); sinT = SB(st, "sinT", [128, 4, 512])
            tA = SB(st, "tA", [128, 512]); tB = SB(st, "tB", [128, 512]); tC = SB(st, "tC", [128, 512]); tD = SB(st, "tD", [128, 512])
            tI = SB(st, "tI", [128, 512], I32)
            hpi = SB(st, "hpi", [128, 1])
            vst = [SB(st, f"vst{i}", [128, 512], BF16) for i in range(2)]
            abst = SB(st, "abst", [128, 16, 32])
            pxt = [PSM(st, f"pxt{i}", [128, 512]) for i in range(2)]
            pmm = [PSM(st, f"pmm{i}", [128, 512]) for i in range(4)]
            prr = PSM(st, "prr", [128, 512])
            print("P1 sbuf remaining", nc.sbuf_bytes_remaining)
            fw.op("pool", lambda h: h.memset(hpi[:], float(np.pi / 2)), writes=["hpi"])
            fw.dma("sp", lambda h: h.dma_start(out=wab[:], in_=wb_ab[:, :, :]), reads=["wb_ab"], writes=["wab"])
            rperm = cst[:, C_RPERM:C_RPERM + 128]
            ifr = cst[:, C_IFREQ:C_IFREQ + 1]
            iota = cst[:, C_IOTA:C_IOTA + 512]
            wsl = [0]

            def trig(dst, yap):
                fw.op("dve", lambda h: h.tensor_copy(out=tI[:], in_=yap), reads=["tA"], writes=["tI"])
                fw.op("dve", lambda h: h.tensor_copy(out=tB[:], in_=tI[:]), reads=["tI"], writes=["tB"])
                fw.op("dve", lambda h: h.tensor_tensor(out=tB[:], in0=yap, in1=tB[:], op=ALU.subtract), reads=["tA", "tB"], writes=["tB"])
                fw.op("act", lambda h: h.activation(out=tC[:], in_=tB[:], func=AF.Abs), reads=["tB"], writes=["tC"])
                fw.op("act", lambda h: h.activation(out=tD[:], in_=tB[:], func=AF.Sin, scale=float(np.pi)), reads=["tB"], writes=["tD"])
                fw.op("act", lambda h: h.activation(out=tC[:], in_=tC[:], func=AF.Sin, bias=hpi[:], scale=-float(np.pi)), reads=["tC", "hpi"], writes=["tC"])
                fw.op("dve", lambda h: h.scalar_tensor_tensor(out=dst, in0=tD[:], scalar=2.0, in1=tC[:], op0=ALU.mult, op1=ALU.mult),
                      reads=["tC", "tD"], writes=["trig"])

            for s in range(NSEG):
                for tt in range(4):
                    t0 = s * SEG + tt * 512
                    for q in range(4):
                        fw.dma("sp", lambda h, q=q, t0=t0: h.dma_start(out=xin[q][:], in_=x[t0 + q * 128:t0 + (q + 1) * 128, :]), writes=[("xin", q)])
                        fw.op("act", lambda h, q=q: h.activation(out=junk[:], in_=xin[q][:], func=AF.Square, accum_out=ssq[:, q:q + 1]),
                              reads=[("xin", q)], writes=["junk", ("ssq", q)])
                        fw.op("act", lambda h, q=q: h.activation(out=ssq[:, q:q + 1], in_=ssq[:, q:q + 1], func=AF.Sqrt, bias=EPS, scale=1.0 / D),
                              reads=[("ssq", q)], writes=[("ssq", q)])
                        fw.op("dve", lambda h, q=q: h.reciprocal(out=ssq[:, q:q + 1], in_=ssq[:, q:q + 1]), reads=[("ssq", q)], writes=[("ssq", q)])
                        fw.op("dve", lambda h, q=q: h.tensor_scalar(out=xin[q][:], in0=xin[q][:], scalar1=ssq[:, q:q + 1], scalar2=None, op0=ALU.mult),
                              reads=[("xin", q), ("ssq", q)], writes=[("xin", q)])
                    for j in range(16):
                        pb = pxt[j % 2]

                        def tr(h, j=j, pb=pb):
                            for q in range(4):
                                ins = h.transpose(out=pb[:, q * 128:(q + 1) * 128], in_=xin[q][:, j * 128:(j + 1) * 128], identity=ident)
                            return ins
                        fw.op("pe", tr, reads=[("xin", q) for q in range(4)] + ["cst"], writes=[("pxt", j % 2)])
                        fw.op("dve" if j % 2 else "act",
                              (lambda h, j=j, pb=pb, tt=tt, s=s: h.tensor_scalar(out=hTs[:, j, tt * 512:(tt + 1) * 512], in0=pb[:, :], scalar1=A1[:, j, s:s + 1],
                                                                              scalar2=modT[:, j, s:s + 1], op0=ALU.mult, op1=ALU.add)) if j % 2 else
                              (lambda h, j=j, pb=pb, tt=tt, s=s: h.activation(out=hTs[:, j, tt * 512:(tt + 1) * 512], in_=pb[:, :], func=AF.Identity,
                                                                           bias=modT[:, j, s:s + 1], scale=A1[:, j, s:s + 1])),
                              reads=[("pxt", j % 2), "A1", "modT"], writes=[("hTs", j, tt)])
                hkeys = [("hTs", j, tt) for j in range(16) for tt in range(4)]

                def wload(b):
                    i = wsl[0] % 3
                    wsl[0] += 1
                    fw.dma("sp", lambda h, i=i, b=b: h.dma_start(out=wk[i][:], in_=wb_fm[b]), reads=[("wb_fm", b)], writes=[("wk", i)])
                    return i

                def fm_mm(wi, rhs_fn, pb, out_ap=None):
                    def mm(h):
                        for j in range(16):
                            ins = h.matmul(out_ap if out_ap is not None else pmm[pb][:, :], lhsT=wk[wi][:, j, :], rhs=rhs_fn(j), start=(j == 0), stop=(j == 15))
                        return ins
                    fw.op("pe", mm, reads=[("wk", wi)] + hkeys, writes=[("pmm", pb)])

                for b in range(32):
                    wi = wload(b)
                    sf = stgf[b % 2]
                    for tt in range(4):
                        pb = (b * 4 + tt) % 4
                        fm_mm(wi, lambda j, tt=tt: hTs[:, j, tt * 512:(tt + 1) * 512], pb)
                        fw.op("act", lambda h, sf=sf, tt=tt, pb=pb, b=b: h.activation(out=sf[:, tt * 512:(tt + 1) * 512], in_=pmm[pb][:, :],
                                                                                     func=(AF.Copy if b < 24 else AF.Silu)),
                              reads=[("pmm", pb)], writes=[("stgf", b % 2)])
                    if b < 24:
                        fw.dma("sp", lambda h, sf=sf, b=b, s=s: h.dma_start(out=qkvpre[b * 128:(b + 1) * 128, 2 + s * SEG:2 + (s + 1) * SEG], in_=sf[:]),
                               reads=[("stgf", b % 2)], writes=["qkvpre"])
                    else:
                        fw.dma("sp", lambda h, sf=sf, b=b, s=s: h.dma_start(out=szT[(b - 24) * 128:(b - 23) * 128, s * SEG:(s + 1) * SEG], in_=sf[:]),
                               reads=[("stgf", b % 2)], writes=["szT"])
                def abmm(h):
                    for n in range(16):
                        for j in range(16):
                            ins = h.matmul(pmm[0][:, n * 32:(n + 1) * 32], lhsT=hTs[:, j, n * 128:(n + 1) * 128], rhs=wab[:, j, :], start=(j == 0), stop=(j == 15))
                    return ins
                fw.op("pe", abmm, reads=["wab"] + hkeys, writes=[("pmm", 0)])
                fw.op("dve", lambda h: h.tensor_copy(out=abst[:].rearrange("p n c -> p (n c)"), in_=pmm[0][:, :]), reads=[("pmm", 0)], writes=["abst"])
                fw.dma("sp", lambda h, s=s: h.dma_start(out=ab_tm[s * SEG:(s + 1) * SEG, :].rearrange("(n p) c -> p n c", p=128), in_=abst[:]),
                       reads=["abst"], writes=["ab_tm"])
                for g in range(3):
                    dil = DIL[g]
                    for tt in range(4):
                        if dil == 16:
                            for rl in range(4):
                                fw.op("dve", lambda h, rl=rl, tt=tt, s=s: h.tensor_scalar(out=tA[:, rl * 128:(rl + 1) * 128], in0=iota[:, 0:128], scalar1=16.0,
                                                                                         scalar2=sgf[:, 8 + s:9 + s], op0=ALU.mult, op1=ALU.add),
                                      reads=["cst", "sgf"], writes=["tA"])
                                fw.op("dve", lambda h, rl=rl, tt=tt: h.tensor_scalar(out=tA[:, rl * 128:(rl + 1) * 128], in0=tA[:, rl * 128:(rl + 1) * 128],
                                                                                    scalar1=float(4 * tt + rl), scalar2=ifr, op0=ALU.add, op1=ALU.mult),
                                      reads=["tA", "cst"], writes=["tA"])
                        else:
                            cadd = float(512 * tt) if dil == 1 else float(tt)
                            fw.op("dve", lambda h, s=s, dil=dil: h.tensor_scalar(out=tA[:], in0=iota, scalar1=float(dil), scalar2=sgf[:, 8 + s:9 + s],
                                                                                op0=ALU.mult, op1=ALU.add), reads=["cst", "sgf"], writes=["tA"])
                            fw.op("dve", lambda h, cadd=cadd: h.tensor_scalar(out=tA[:], in0=tA[:], scalar1=cadd, scalar2=ifr, op0=ALU.add, op1=ALU.mult),
                                  reads=["tA", "cst"], writes=["tA"])
                        trig(sinT[:, tt, :], tA[:])
                        fw.op("dve", lambda h: h.tensor_scalar(out=tA[:], in0=tA[:], scalar1=0.25, scalar2=None, op0=ALU.add), reads=["tA", "trig"], writes=["tA"])
                        trig(cosT[:, tt, :], tA[:])
                    for qk in range(2):
                        for hh in range(4):
                            b = 32 + qk * 12 + g * 4 + hh
                            wi = wload(b)
                            sb_ = stgb[(qk * 4 + hh) % 2]
                            hview = None
                            for tt in range(4):
                                pb = tt % 4
                                if dil == 1:
                                    rf = lambda j, tt=tt: hTs[:, j, tt * 512:(tt + 1) * 512]
                                    oap = None
                                elif dil == 4:
                                    rf = lambda j, tt=tt: hTs[:, j, :].rearrange("p (m r) -> p r m", r=4)[:, tt, :]
                                    oap = None
                                else:
                                    rf = lambda j, tt=tt: hTs[:, j, :].rearrange("p (m r) -> p r m", r=16)[:, 4 * tt:4 * tt + 4, :]
                                    oap = pmm[pb][:, :].rearrange("p (a b) -> p a b", a=4)
                                fm_mm(wi, rf, pb, oap)
                                fw.op("act", lambda h, pb=pb, qk=qk: h.activation(out=tB[:], in_=pmm[pb][:, :], func=AF.Copy,
                                                                                 scale=(128.0 ** -0.5 if qk == 0 else 1.0)),
                                      reads=[("pmm", pb)], writes=["tB"])
                                fw.op("pe", lambda h: h.matmul(prr[:, :], lhsT=rperm, rhs=tB[:], start=True, stop=True), reads=["tB", "cst"], writes=["prr"])
                                fw.op("pool", lambda h, tt=tt: h.tensor_tensor(out=tC[:], in0=tB[:], in1=cosT[:, tt, :], op=ALU.mult), reads=["tB", "trig"], writes=["tC"])
                                fw.op("dve", lambda h, tt=tt: h.tensor_tensor(out=tD[:], in0=prr[:, :], in1=sinT[:, tt, :], op=ALU.mult), reads=["prr", "trig"], writes=["tD"])
                                fw.op("pool", lambda h, tt=tt, sb_=sb_: h.tensor_tensor(out=sb_[:, tt * 512:(tt + 1) * 512], in0=tC[:], in1=tD[:], op=ALU.add),
                                      reads=["tC", "tD"], writes=[("stgb", (qk * 4 + hh) % 2)])
                            dst = (aqT if qk == 0 else akT)[g][hh * 128:(hh + 1) * 128, :].rearrange("p (r c) -> p r c", c=CL[g])
                            fw.dma("sp", lambda h, dst=dst, sb_=sb_, g=g, s=s, dil=dil: h.dma_start(
                                out=dst[:, :, 64 + s * MS[g]:64 + (s + 1) * MS[g]], in_=sb_[:].rearrange("p (r m) -> p r m", r=dil)),
                                reads=[("stgb", (qk * 4 + hh) % 2)], writes=[("aqk", g)])
                    fw.dma("sp", lambda h, g=g: h.dma_start(out=wv[:], in_=wb_v[g]), reads=[("wb_v", g)], writes=["wv"])
                    for ct in range(16):
                        if dil == 1:
                            lf = lambda j, ct=ct: hTs[:, j, ct * 128:(ct + 1) * 128]
                            r_, m0 = 0, ct * 128
                        elif dil == 4:
                            lf = lambda j, ct=ct: hTs[:, j, :].rearrange("p (m r) -> p r m", r=4)[:, ct // 4, (ct % 4) * 128:(ct % 4 + 1) * 128]
                            r_, m0 = ct // 4, (ct % 4) * 128
                        else:
                            lf = lambda j, ct=ct: hTs[:, j, :].rearrange("p (m r) -> p r m", r=16)[:, ct, :]
                            r_, m0 = ct, 0
                        pb = ct % 4

                        def vmm(h, lf=lf, pb=pb):
                            for j in range(16):
                                ins = h.matmul(pmm[pb][:, :], lhsT=lf(j), rhs=wv[:, j, :], start=(j == 0), stop=(j == 15))
                            return ins
                        fw.op("pe", vmm, reads=["wv"] + hkeys, writes=[("pmm", pb)])
                        vs = vst[ct % 2]
                        if ct % 2:
                            fw.op("dve", lambda h, vs=vs, pb=pb: h.tensor_copy(out=vs[:], in_=pmm[pb][:, :]), reads=[("pmm", pb)], writes=[("vst", ct % 2)])
                        else:
                            fw.op("act", lambda h, vs=vs, pb=pb: h.activation(out=vs[:], in_=pmm[pb][:, :], func=AF.Copy), reads=[("pmm", pb)], writes=[("vst", ct % 2)])
                        row0 = r_ * CL[g] + 64 + s * MS[g] + m0
                        fw.dma("sp", lambda h, vs=vs, row0=row0, g=g: h.dma_start(out=avs[g][row0:row0 + 128, :], in_=vs[:]),
                               reads=[("vst", ct % 2)], writes=[("av", g)])
            fw.barrier()

        fw.phase = "P2"
        with ExitStack() as st:
            xc = [SB(st, f"xc{i}", [128, 516]) for i in range(2)]
            ca_l = [SB(st, f"ca{i}", [128, 512]) for i in range(2)]; cs_l = [SB(st, f"cs{i}", [128, 512]) for i in range(2)]
            cq_l = [SB(st, f"cq{i}", [128, 512]) for i in range(2)]; cr_l = [SB(st, f"cr{i}", [128, 512]) for i in range(2)]
            cn = [SB(st, f"cn{i}", [128, 512]) for i in range(2)]
            ctm = [SB(st, f"ctm{i}", [128, 4, 128]) for i in range(2)]
            pss2_l = [PSM(st, f"pss2{i}", [128, 512]) for i in range(2)]
            ptr = [PSM(st, f"ptr{i}", [128, 512]) for i in range(2)]
            it = 0
            for cb in range(24):
                for tile in range(16):
                    t0 = tile * 512
                    s = tile // 4
                    xw = xc[it % 2]
                    kx = ("xc", it % 2)
                    pss2 = pss2_l[it % 2]
                    ca, cs_, cq, cr = ca_l[it % 2], cs_l[it % 2], cq_l[it % 2], cr_l[it % 2]
                    kca, kcs, kcq, kcr = ("ca", it % 2), ("cs_", it % 2), ("cq", it % 2), ("cr", it % 2)
                    fw.dma("sp", lambda h, xw=xw, cb=cb, t0=t0: h.dma_start(out=xw[:], in_=qkvpre[cb * 128:(cb + 1) * 128, t0:t0 + 516]), reads=["qkvpre"], writes=[kx])
                    if tile == 0:
                        fw.op("pool", lambda h, xw=xw: h.memset(xw[:, 0:2], 0.0), writes=[kx])
                    elif tile % 4 == 0:
                        fw.op("pool", lambda h, xw=xw, s=s: h.tensor_scalar(out=xw[:, 0:2], in0=xw[:, 0:2], scalar1=sgf[:, s:s + 1], scalar2=None, op0=ALU.mult),
                              reads=[kx, "sgf"], writes=[kx])
                    if tile == 15:
                        fw.op("pool", lambda h, xw=xw: h.memset(xw[:, 514:516], 0.0), writes=[kx])
                    elif tile % 4 == 3:
                        fw.op("pool", lambda h, xw=xw, s=s: h.tensor_scalar(out=xw[:, 514:516], in0=xw[:, 514:516], scalar1=sgf[:, 4 + s:5 + s], scalar2=None, op0=ALU.mult),
                              reads=[kx, "sgf"], writes=[kx])
                    eng = "dve"
                    fw.op(eng, lambda h, xw=xw, cb=cb: h.tensor_scalar(out=ca[:], in0=xw[:, 0:512], scalar1=convT[:, cb:cb + 1], scalar2=None, op0=ALU.mult),
                          reads=[kx, "convT"], writes=[kca])
                    for k in range(1, 5):
                        fw.op(eng, lambda h, xw=xw, cb=cb, k=k: h.scalar_tensor_tensor(out=ca[:], in0=xw[:, k:k + 512], scalar=convT[:, k * 24 + cb:k * 24 + cb + 1],
                                                                                      in1=ca[:], op0=ALU.mult, op1=ALU.add),
                              reads=[kx, "convT", kca], writes=[kca])
                    fw.op("act", lambda h: h.activation(out=cs_[:], in_=ca[:], func=AF.Silu), reads=[kca], writes=[kcs])
                    res_t = cs_
                    res_k = kcs
                    if cb < 16:
                        fw.op("pool", lambda h: h.tensor_tensor(out=cq[:], in0=cs_[:], in1=cs_[:], op=ALU.mult), reads=[kcs], writes=[kcq])
                        fw.op("pe", lambda h: h.matmul(pss2[:, :], lhsT=ones, rhs=cq[:], start=True, stop=True), reads=[kcq, "cst"], writes=[("pss2", it % 2)])
                        if cb < 8:
                            fw.op("act", lambda h: h.activation(out=cr[:], in_=pss2[:, :], func=AF.Sqrt, bias=128.0 * EPS, scale=128.0), reads=[("pss2", it % 2)], writes=[kcr])
                        else:
                            fw.op("act", lambda h: h.activation(out=cr[:], in_=pss2[:, :], func=AF.Sqrt, bias=EPS, scale=1.0), reads=[("pss2", it % 2)], writes=[kcr])
                        fw.op("dve", lambda h: h.reciprocal(out=cr[:], in_=cr[:]), reads=[kcr], writes=[kcr])
                        cnb = cn[it % 2]
                        fw.op("pool", lambda h, cnb=cnb: h.tensor_tensor(out=cnb[:], in0=cs_[:], in1=cr[:], op=ALU.mult), reads=[kcs, kcr], writes=[("cn", it % 2)])
                        res_t, res_k = cnb, ("cn", it % 2)
                        dstn = qn if cb < 8 else kn
                        fw.dma("sp", lambda h, cnb=cnb, dstn=dstn, cb=cb, t0=t0: h.dma_start(out=dstn[(cb % 8) * 128:(cb % 8 + 1) * 128, t0:t0 + 512], in_=cnb[:]),
                               reads=[res_k], writes=["qkn"])
                    if cb >= 8:
                        pb = ptr[it % 2]

                        def tr(h, res_t=res_t, pb=pb):
                            for q in range(4):
                                ins = h.transpose(out=pb[:, q * 128:(q + 1) * 128], in_=res_t[:, q * 128:(q + 1) * 128], identity=ident)
                            return ins
                        fw.op("pe", tr, reads=[res_k, "cst"], writes=[("ptr", it % 2)])
                        cm = ctm[it % 2]
                        fw.op("act", lambda h, cm=cm, pb=pb: h.activation(out=cm[:].rearrange("p q c -> p (q c)"), in_=pb[:, :], func=AF.Copy),
                              reads=[("ptr", it % 2)], writes=[("ctm", it % 2)])
                        dstt = k_tm if cb < 16 else v_tm
                        fw.dma("sp", lambda h, cm=cm, dstt=dstt, cb=cb, t0=t0: h.dma_start(
                            out=dstt[t0:t0 + 512, (cb % 8) * 128:(cb % 8 + 1) * 128].rearrange("(q p) c -> p q c", p=128), in_=cm[:]),
                            reads=[("ctm", it % 2)], writes=["kv_tm"])
                    it += 1
            fw.barrier()

        fw.phase = "P3"
        with ExitStack() as st:
            ab = SB(st, "ab", [128, 64, 32])
            g_ = SB(st, "g_", [128, 64, 16]); t1_ = SB(st, "t1_", [128, 64, 16]); t2_ = SB(st, "t2_", [128, 64, 16])
            rows = [SB(st, f"rows{i}", [8, 8, 4, 128]) for i in range(2)]
            pg = PSM(st, "pg", [128, 1024])
            pr = [PSM(st, f"pr{i}", [8, 512]) for i in range(2)]
            fw.dma("sp", lambda h: h.dma_start(out=ab[:], in_=ab_tm.rearrange("(n p) c -> p n c", p=128)), reads=["ab_tm"], writes=["ab"])
            bc = lambda t: t[:].unsqueeze(1).to_broadcast([128, 64, 16])
            fw.op("dve", lambda h: h.tensor_tensor(out=t1_[:], in0=ab[:, :, 0:16], in1=bc(dtb16), op=ALU.add), reads=["ab", "dtb16"], writes=["t1_"])
            fw.op("act", lambda h: h.activation(out=t2_[:], in_=t1_[:], func=AF.Abs), reads=["t1_"], writes=["t2_"])
            fw.op("act", lambda h: h.activation(out=t2_[:], in_=t2_[:], func=AF.Exp, scale=-1.0), reads=["t2_"], writes=["t2_"])
            fw.op("act", lambda h: h.activation(out=t2_[:], in_=t2_[:], func=AF.Ln, bias=1.0, scale=1.0), reads=["t2_"], writes=["t2_"])
            fw.op("dve", lambda h: h.scalar_tensor_tensor(out=t1_[:], in0=t1_[:], scalar=0.0, in1=t2_[:], op0=ALU.max, op1=ALU.add), reads=["t1_", "t2_"], writes=["t1_"])
            fw.op("dve", lambda h: h.tensor_tensor(out=g_[:], in0=t1_[:], in1=bc(nA16), op=ALU.mult), reads=["t1_", "nA16"], writes=["g_"])
            fw.op("act", lambda h: h.activation(out=Btm[:], in_=ab[:, :, 16:32], func=AF.Sigmoid), reads=["ab"], writes=["Btm"])
            mlow = cst[:, C_MLOW:C_MLOW + 128]; mup = cst[:, C_MUP:C_MUP + 128]

            def gmm(h):
                for c in range(64):
                    h.matmul(pg[:, c * 16:c * 16 + 8], lhsT=mlow, rhs=g_[:, c, 0:8], start=True, stop=True)
                    ins = h.matmul(pg[:, c * 16 + 8:c * 16 + 16], lhsT=mup, rhs=g_[:, c, 8:16], start=True, stop=True)
                return ins
            fw.op("pe", gmm, reads=["g_", "cst"], writes=["pg"])
            fw.op("dve", lambda h: h.tensor_copy(out=Gtm[:].rearrange("p c k -> p (c k)"), in_=pg[:, :]), reads=["pg"], writes=["Gtm"])
            for c in range(64):
                pp = pr[c % 2]
                rw = rows[(c // 8) % 2]

                def rmm(h, c=c, pp=pp):
                    h.matmul(pp[:, 0:128], lhsT=g_[:, c, 0:8], rhs=mlow, start=True, stop=True)
                    h.matmul(pp[:, 128:256], lhsT=g_[:, c, 8:16], rhs=mup, start=True, stop=True)
                    h.matmul(pp[:, 256:384], lhsT=Btm[:, c, 0:8], rhs=ident, start=True, stop=True)
                    return h.matmul(pp[:, 384:512], lhsT=Btm[:, c, 8:16], rhs=ident, start=True, stop=True)
                fw.op("pe", rmm, reads=["g_", "Btm", "cst"], writes=[("pr", c % 2)])
                fw.op("dve", lambda h, c=c, pp=pp, rw=rw: h.tensor_copy(out=rw[:, c % 8, :, :].rearrange("p k t -> p (k t)"), in_=pp[:, :]),
                      reads=[("pr", c % 2)], writes=[("rows", (c // 8) % 2)])
                if c % 8 == 7:
                    c0 = c - 7
                    for k4 in range(4):
                        dr_, kd_ = k4 % 2, k4 // 2
                        fw.dma("sp", lambda h, rw=rw, c0=c0, k4=k4, dr_=dr_, kd_=kd_: h.dma_start(
                            out=GB[dr_, kd_, c0:c0 + 8, :, :].rearrange("c h t -> h c t"), in_=rw[:, :, k4, :]),
                            reads=[("rows", (c // 8) % 2)], writes=["GB"])
            fw.barrier()

        fw.phase = "P4"
        with ExitStack() as st:
            ld_q = [SB(st, f"ldq{i}", [128, 8, 128]) for i in range(2)]
            ld_k = [SB(st, f"ldk{i}", [128, 8, 128]) for i in range(2)]
            ld_kt = [SB(st, f"ldkt{i}", [128, 8, 128]) for i in range(2)]
            ld_vt = [SB(st, f"ldvt{i}", [128, 8, 128]) for i in range(2)]
            ld_g = [SB(st, f"ldg{i}", [128, 8, 128]) for i in range(2)]
            ld_b = [SB(st, f"ldb{i}", [128, 8, 128]) for i in range(2)]
            f1 = SB(st, "f1", [128, 8, 128]); f2 = SB(st, "f2", [128, 8, 128]); f3 = SB(st, "f3", [128, 8, 128]); f4 = SB(st, "f4", [128, 8, 128])
            f5 = SB(st, "f5", [128, 8, 128]); f6 = SB(st, "f6", [128, 8, 128])
            kbf = SB(st, "kbf", [128, 8, 128], BF16); qbf = SB(st, "qbf", [128, 8, 128], BF16)
            P1b = SB(st, "P1b", [128, 8, 128], BF16); Q1b = SB(st, "Q1b", [128, 8, 128], BF16)
            Dm = SB(st, "Dm", [128, 8, 128], BF16); Em = SB(st, "Em", [128, 8, 128], BF16)
            Xb = SB(st, "Xb", [128, 8, 128], BF16); X2b = SB(st, "X2b", [128, 8, 128], BF16)
            lvl = SB(st, "lvl", [128, 14 * 128])
            fw.dma("sp", lambda h: h.dma_start(out=lvl[:], in_=lvlmask[:, :]), writes=["lvl"])
            qkm = SB(st, "qkm", [128, 8, 128], BF16); qgT = SB(st, "qgT", [128, 8, 128], BF16)
            kbg = SB(st, "kbg", [128, 8, 128], BF16); vb = SB(st, "vb", [128, 8, 128], BF16); kd = SB(st, "kd", [128, 8, 128], BF16)
            wT = SB(st, "wT", [128, 8, 128], BF16); vnew = SB(st, "vnew", [128, 8, 128], BF16)
            u_ = SB(st, "u_", [128, 8, 128]); S_ = SB(st, "S_", [128, 8, 128]); Sbf = SB(st, "Sbf", [128, 8, 128], BF16)
            ost = [SB(st, f"ost{i}", [128, 8, 128]) for i in range(2)]
            sm = SB(st, "sm", [128, 8, 4])
            PA = PSM(st, "PA", [128, 1024]); PB = PSM(st, "PB", [128, 1024]); PC = PSM(st, "PC", [128, 1024]); PD = PSM(st, "PD", [128, 1024])
            print("P4 sbuf remaining", nc.sbuf_bytes_remaining)
            v3 = lambda t: t[:, :].rearrange("p (h c) -> p h c", h=8)
            maskc = lambda off: cst[:, off:off + 128].unsqueeze(1).to_broadcast([128, 8, 128])
            identb = cst[:, C_ID:C_ID + 128].unsqueeze(1).to_broadcast([128, 8, 128])
            it = 0
            for dr_ in range(2):
                fw.op("pool", lambda h: h.memset(S_[:], 0.0), writes=[("S_", 0), ("S_", 1)])
                fw.op("pool", lambda h: h.memset(Sbf[:], 0.0), writes=[("Sbf", 0), ("Sbf", 1)])
                order = list(range(64)) if dr_ == 0 else list(range(63, -1, -1))
                m_p1 = C_GT if dr_ == 0 else C_LT
                m_q1 = C_LT if dr_ == 0 else C_GT
                m_qk = C_LE if dr_ == 0 else C_GE
                last = 127 if dr_ == 0 else 0
                for c in order:
                    b2 = it % 2
                    it += 1
                    t0 = c * 128
                    lq, lk, lkt, lvt, lg, lb = ld_q[b2], ld_k[b2], ld_kt[b2], ld_vt[b2], ld_g[b2], ld_b[b2]
                    K = lambda n: (n, b2)
                    fw.dma("sp", lambda h, lq=lq, t0=t0: h.dma_start(out=lq[:], in_=qn[:, t0:t0 + 128].rearrange("(h p) t -> p h t", p=128)), reads=["qkn"], writes=[K("ldq")])
                    fw.dma("sp", lambda h, lk=lk, t0=t0: h.dma_start(out=lk[:], in_=kn[:, t0:t0 + 128].rearrange("(h p) t -> p h t", p=128)), reads=["qkn"], writes=[K("ldk")])
                    fw.dma("sp", lambda h, lkt=lkt, t0=t0: h.dma_start(out=lkt[:], in_=k_tm[t0:t0 + 128, :].rearrange("p (h c) -> p h c", h=8)), reads=["kv_tm"], writes=[K("ldkt")])
                    fw.dma("sp", lambda h, lvt=lvt, t0=t0: h.dma_start(out=lvt[:], in_=v_tm[t0:t0 + 128, :].rearrange("p (h c) -> p h c", h=8)), reads=["kv_tm"], writes=[K("ldvt")])
                    fw.dma("sp", lambda h, lg=lg, c=c, dr_=dr_: h.dma_start(out=lg[:].rearrange("p h t -> p (h t)"),
                                                                          in_=GB[dr_, 0, c:c + 1, :, :].rearrange("c h t -> c (h t)").partition_broadcast(128)),
                           reads=["GB"], writes=[K("ldg")])
                    fw.dma("sp", lambda h, lb=lb, c=c, dr_=dr_: h.dma_start(out=lb[:].rearrange("p h t -> p (h t)"),
                                                                          in_=GB[dr_, 1, c:c + 1, :, :].rearrange("c h t -> c (h t)").partition_broadcast(128)),
                           reads=["GB"], writes=[K("ldb")])
                    Gp = Gtm[:, c, dr_ * 8:dr_ * 8 + 8]
                    Bp = Btm[:, c, dr_ * 8:dr_ * 8 + 8]
                    Gpb = Gp.unsqueeze(2).to_broadcast([128, 8, 128])
                    Bpb = Bp.unsqueeze(2).to_broadcast([128, 8, 128])
                    HV = (slice(0, 4), slice(4, 8))
                    lm = lambda k, up: lvl[:, (2 * k + up) * 128:(2 * k + up + 1) * 128].unsqueeze(1).to_broadcast([128, 4, 128])
                    mk4 = lambda off: cst[:, off:off + 128].unsqueeze(1).to_broadcast([128, 4, 128])
                    id4 = cst[:, C_ID:C_ID + 128].unsqueeze(1).to_broadcast([128, 4, 128])
                    dsel = 0 if dr_ == 0 else 1
                    seg = c // 16

                    def H(n, hf):
                        return (n, hf)

                    def pv(PX, hf):
                        return v3(PX)[:, HV[hf], :]

                    def both(fn):
                        for hf in range(2):
                            fn(hf, HV[hf])

                    def s_cast(hf, hs):
                        fw.op("act", lambda h: h.activation(out=kbf[:, hs, :], in_=lk[:, hs, :], func=AF.Copy), reads=[K("ldk")], writes=[H("kbf", hf)])
                        fw.op("act", lambda h: h.activation(out=qbf[:, hs, :], in_=lq[:, hs, :], func=AF.Copy), reads=[K("ldq")], writes=[H("qbf", hf)])

                        def kk(h):
                            for hd in range(hs.start, hs.stop):
                                h.matmul(PA[:, hd * 128:(hd + 1) * 128], lhsT=kbf[:, hd, :], rhs=kbf[:, hd, :], start=True, stop=True)
                                ins = h.matmul(PB[:, hd * 128:(hd + 1) * 128], lhsT=kbf[:, hd, :], rhs=qbf[:, hd, :], start=True, stop=True)
                            return ins
                        fw.op("pe", kk, reads=[H("kbf", hf), H("qbf", hf)], writes=[H("PA", hf), H("PB", hf)])
                    both(s_cast)

                    def s_dec1(hf, hs):
                        Gpb4 = Gp[:, hs].unsqueeze(2).to_broadcast([128, 4, 128])
                        fw.op("dve", lambda h: h.tensor_tensor(out=f1[:, hs, :], in0=lg[:, hs, :], in1=Gpb4, op=ALU.subtract), reads=[K("ldg"), "Gtm"], writes=[H("f1", hf)])
                        fw.op("pool", lambda h: h.tensor_scalar(out=f2[:, hs, :], in0=f1[:, hs, :], scalar1=0.0, scalar2=None, op0=ALU.max), reads=[H("f1", hf)], writes=[H("f2", hf)])
                        fw.op("act", lambda h: h.activation(out=f2[:, hs, :], in_=f2[:, hs, :], func=AF.Exp, scale=-1.0), reads=[H("f2", hf)], writes=[H("f2", hf)])
                        fw.op("pool", lambda h: h.tensor_scalar(out=f3[:, hs, :], in0=f1[:, hs, :], scalar1=0.0, scalar2=None, op0=ALU.min), reads=[H("f1", hf)], writes=[H("f3", hf)])
                        fw.op("act", lambda h: h.activation(out=f3[:, hs, :], in_=f3[:, hs, :], func=AF.Exp), reads=[H("f3", hf)], writes=[H("f3", hf)])
                    both(s_dec1)

                    def s_dec2(hf, hs):
                        Bpb4 = Bp[:, hs].unsqueeze(2).to_broadcast([128, 4, 128])
                        fw.op("pool", lambda h: h.tensor_tensor(out=f2[:, hs, :], in0=f2[:, hs, :], in1=mk4(m_p1), op=ALU.mult), reads=[H("f2", hf), "cst"], writes=[H("f2", hf)])
                        fw.op("dve", lambda h: h.scalar_tensor_tensor(out=f2[:, hs, :], in0=f2[:, hs, :], scalar=-1.0, in1=Bpb4, op0=ALU.mult, op1=ALU.mult),
                              reads=[H("f2", hf), "Btm"], writes=[H("f2", hf)])
                        fw.op("dve", lambda h: h.tensor_tensor(out=P1b[:, hs, :], in0=pv(PA, hf), in1=f2[:, hs, :], op=ALU.mult), reads=[H("PA", hf), H("f2", hf)], writes=[H("P1b", hf)])
                        fw.op("dve", lambda h: h.scalar_tensor_tensor(out=f4[:, hs, :], in0=f3[:, hs, :], scalar=-1.0, in1=lb[:, hs, :], op0=ALU.mult, op1=ALU.mult),
                              reads=[H("f3", hf), K("ldb")], writes=[H("f4", hf)])
                        fw.op("pool", lambda h: h.tensor_tensor(out=f4[:, hs, :], in0=f4[:, hs, :], in1=mk4(m_q1), op=ALU.mult), reads=[H("f4", hf), "cst"], writes=[H("f4", hf)])
                        fw.op("dve", lambda h: h.tensor_tensor(out=Q1b[:, hs, :], in0=pv(PA, hf), in1=f4[:, hs, :], op=ALU.mult), reads=[H("PA", hf), H("f4", hf)], writes=[H("Q1b", hf)])
                        fw.op("pool", lambda h: h.tensor_tensor(out=f3[:, hs, :], in0=f3[:, hs, :], in1=mk4(m_qk), op=ALU.mult), reads=[H("f3", hf), "cst", H("f4", hf)], writes=[H("f3", hf)])
                        fw.op("dve", lambda h: h.tensor_tensor(out=qkm[:, hs, :], in0=pv(PB, hf), in1=f3[:, hs, :], op=ALU.mult), reads=[H("PB", hf), H("f3", hf)], writes=[H("qkm", hf)])
                    both(s_dec2)

                    def s_lvl0(hf, hs):
                        fw.op("pool", lambda h: h.tensor_tensor(out=Dm[:, hs, :], in0=P1b[:, hs, :], in1=lm(0, dsel), op=ALU.mult), reads=[H("P1b", hf), "lvl"], writes=[H("Dm", hf)])
                        fw.op("pool", lambda h: h.tensor_tensor(out=Dm[:, hs, :], in0=Dm[:, hs, :], in1=id4, op=ALU.add), reads=[H("Dm", hf), "cst"], writes=[H("Dm", hf)])
                        fw.op("pool", lambda h: h.tensor_tensor(out=Em[:, hs, :], in0=Q1b[:, hs, :], in1=lm(0, 1 - dsel), op=ALU.mult), reads=[H("Q1b", hf), "lvl"], writes=[H("Em", hf)])
                        fw.op("pool", lambda h: h.tensor_tensor(out=Em[:, hs, :], in0=Em[:, hs, :], in1=id4, op=ALU.add), reads=[H("Em", hf), "cst"], writes=[H("Em", hf)])
                    both(s_lvl0)

                    fw.op("act", lambda h: h.activation(out=f5[:], in_=lg[:], func=AF.Exp), reads=[K("ldg")], writes=["f5"])
                    fw.op("pool", lambda h: h.tensor_tensor(out=qgT[:], in0=lq[:], in1=f5[:], op=ALU.mult), reads=[K("ldq"), "f5"], writes=[H("qgT", 0), H("qgT", 1)])
                    fw.op("act", lambda h: h.activation(out=sm[:, :, 0], in_=Gp, func=AF.Exp), reads=["Gtm"], writes=["sm0"])
                    fw.op("dve", lambda h: h.tensor_tensor(out=sm[:, :, 0], in0=sm[:, :, 0], in1=Bp, op=ALU.mult), reads=["sm0", "Btm"], writes=["sm0"])
                    fw.op("dve", lambda h: h.tensor_tensor(out=sm[:, :, 1], in0=lg[:, :, last], in1=Gp, op=ALU.subtract), reads=[K("ldg"), "Gtm"], writes=["sm1"])
                    fw.op("act", lambda h: h.activation(out=sm[:, :, 1], in_=sm[:, :, 1], func=AF.Exp), reads=["sm1"], writes=["sm1"])
                    fw.op("act", lambda h: h.activation(out=sm[:, :, 2], in_=lg[:, :, last], func=AF.Exp), reads=[K("ldg")], writes=["sm2"])
                    smb = lambda i: sm[:, :, i:i + 1].to_broadcast([128, 8, 128])
                    fw.op("pool", lambda h: h.tensor_tensor(out=kbg[:], in0=lkt[:], in1=smb(0), op=ALU.mult), reads=[K("ldkt"), "sm0"], writes=[H("kbg", 0), H("kbg", 1)])
                    fw.op("pool", lambda h: h.tensor_tensor(out=vb[:], in0=lvt[:], in1=Bpb, op=ALU.mult), reads=[K("ldvt"), "Btm"], writes=[H("vb", 0), H("vb", 1)])
                    fw.op("pool", lambda h: h.tensor_tensor(out=kd[:], in0=lkt[:], in1=smb(1), op=ALU.mult), reads=[K("ldkt"), "sm1"], writes=[H("kd", 0), H("kd", 1)])

                    for k in range(1, 7):
                        def s_x(hf, hs):
                            def xmm(h):
                                for hd in range(hs.start, hs.stop):
                                    h.matmul(PC[:, hd * 128:(hd + 1) * 128], lhsT=Q1b[:, hd, :], rhs=Dm[:, hd, :], start=True, stop=True)
                                    ins = h.matmul(PD[:, hd * 128:(hd + 1) * 128], lhsT=P1b[:, hd, :], rhs=Em[:, hd, :], start=True, stop=True)
                                return ins
                            fw.op("pe", xmm, reads=[H("Q1b", hf), H("P1b", hf), H("Dm", hf), H("Em", hf)], writes=[H("PC", hf), H("PD", hf)])
                            fw.op("act", lambda h: h.activation(out=Xb[:, hs, :], in_=pv(PC, hf), func=AF.Copy), reads=[H("PC", hf)], writes=[H("Xb", hf)])
                            fw.op("act", lambda h: h.activation(out=X2b[:, hs, :], in_=pv(PD, hf), func=AF.Copy), reads=[H("PD", hf)], writes=[H("X2b", hf)])
                        both(s_x)

                        def s_z(hf, hs):
                            def zmm(h):
                                for hd in range(hs.start, hs.stop):
                                    h.matmul(PA[:, hd * 128:(hd + 1) * 128], lhsT=Em[:, hd, :], rhs=Xb[:, hd, :], start=True, stop=True)
                                    ins = h.matmul(PB[:, hd * 128:(hd + 1) * 128], lhsT=Dm[:, hd, :], rhs=X2b[:, hd, :], start=True, stop=True)
                                return ins
                            fw.op("pe", zmm, reads=[H("Em", hf), H("Dm", hf), H("Xb", hf), H("X2b", hf)], writes=[H("PA", hf), H("PB", hf)])
                            fw.op("dve", lambda h: h.tensor_tensor(out=f1[:, hs, :], in0=pv(PA, hf), in1=lm(k, dsel), op=ALU.mult), reads=[H("PA", hf), "lvl"], writes=[H("f1", hf)])
                            fw.op("pool", lambda h: h.tensor_tensor(out=Dm[:, hs, :], in0=Dm[:, hs, :], in1=f1[:, hs, :], op=ALU.add), reads=[H("Dm", hf), H("f1", hf)], writes=[H("Dm", hf)])
                            fw.op("dve", lambda h: h.tensor_tensor(out=f6[:, hs, :], in0=pv(PB, hf), in1=lm(k, 1 - dsel), op=ALU.mult), reads=[H("PB", hf), "lvl"], writes=[H("f6", hf)])
                            fw.op("pool", lambda h: h.tensor_tensor(out=Em[:, hs, :], in0=Em[:, hs, :], in1=f6[:, hs, :], op=ALU.add), reads=[H("Em", hf), H("f6", hf)], writes=[H("Em", hf)])
                        both(s_z)

                    def s_u(hf, hs):
                        def umm(h):
                            for hd in range(hs.start, hs.stop):
                                h.matmul(PB[:, hd * 128:(hd + 1) * 128], lhsT=Em[:, hd, :], rhs=vb[:, hd, :], start=True, stop=True)
                                ins = h.matmul(PA[:, hd * 128:(hd + 1) * 128], lhsT=kbg[:, hd, :], rhs=Em[:, hd, :], start=True, stop=True)
                            return ins
                        fw.op("pe", umm, reads=[H("Em", hf), H("vb", hf), H("kbg", hf)], writes=[H("PA", hf), H("PB", hf)])
                        fw.op("act", lambda h: h.activation(out=u_[:, hs, :], in_=pv(PB, hf), func=AF.Copy), reads=[H("PB", hf)], writes=[H("u_", hf)])
                        fw.op("dve", lambda h: h.tensor_copy(out=wT[:, hs, :], in_=pv(PA, hf)), reads=[H("PA", hf)], writes=[H("wT", hf)])
                    both(s_u)

                    def s_seq(hf, hs):
                        if dr_ == 0 and c % 16 == 0 and c > 0:
                            fw.op("dve", lambda h: h.tensor_scalar(out=S_[:, hs, :], in0=S_[:, hs, :], scalar1=sgf[:, seg:seg + 1], scalar2=None, op0=ALU.mult), reads=[H("S_", hf), "sgf"], writes=[H("S_", hf)])
                            fw.op("act", lambda h: h.activation(out=Sbf[:, hs, :], in_=S_[:, hs, :], func=AF.Copy), reads=[H("S_", hf)], writes=[H("Sbf", hf)])
                        if dr_ == 1 and c % 16 == 15 and c < 63:
                            fw.op("dve", lambda h: h.tensor_scalar(out=S_[:, hs, :], in0=S_[:, hs, :], scalar1=sgf[:, 4 + seg:5 + seg], scalar2=None, op0=ALU.mult), reads=[H("S_", hf), "sgf"], writes=[H("S_", hf)])
                            fw.op("act", lambda h: h.activation(out=Sbf[:, hs, :], in_=S_[:, hs, :], func=AF.Copy), reads=[H("S_", hf)], writes=[H("Sbf", hf)])

                        def wsmm(h):
                            for hd in range(hs.start, hs.stop):
                                ins = h.matmul(PC[:, hd * 128:(hd + 1) * 128], lhsT=wT[:, hd, :], rhs=Sbf[:, hd, :], start=True, stop=True)
                            return ins
                        fw.op("pe", wsmm, reads=[H("wT", hf), H("Sbf", hf)], writes=[H("PC", hf)])
                        fw.op("dve", lambda h: h.tensor_tensor(out=vnew[:, hs, :], in0=u_[:, hs, :], in1=pv(PC, hf), op=ALU.subtract), reads=[H("u_", hf), H("PC", hf)], writes=[H("vnew", hf)])

                        def omm(h):
                            for hd in range(hs.start, hs.stop):
                                h.matmul(PD[:, hd * 128:(hd + 1) * 128], lhsT=qgT[:, hd, :], rhs=Sbf[:, hd, :], start=True, stop=False)
                                h.matmul(PD[:, hd * 128:(hd + 1) * 128], lhsT=qkm[:, hd, :], rhs=vnew[:, hd, :], start=False, stop=True)
                                ins = h.matmul(PB[:, hd * 128:(hd + 1) * 128], lhsT=kd[:, hd, :], rhs=vnew[:, hd, :], start=True, stop=True)
                            return ins
                        fw.op("pe", omm, reads=[H("qgT", hf), H("Sbf", hf), H("qkm", hf), H("vnew", hf), H("kd", hf)], writes=[H("PD", hf), H("PB", hf)])
                        fw.op("act", lambda h: h.activation(out=ost[b2][:, hs, :], in_=pv(PD, hf), func=AF.Copy), reads=[H("PD", hf)], writes=[K("ost")])
                        fw.op("pool", lambda h: h.tensor_tensor(out=S_[:, hs, :], in0=S_[:, hs, :], in1=sm[:, hs, 2:3].to_broadcast([128, 4, 128]), op=ALU.mult), reads=[H("S_", hf), "sm2"], writes=[H("S_", hf)])
                        fw.op("dve", lambda h: h.tensor_tensor(out=S_[:, hs, :], in0=S_[:, hs, :], in1=pv(PB, hf), op=ALU.add), reads=[H("S_", hf), H("PB", hf)], writes=[H("S_", hf)])
                        fw.op("act", lambda h: h.activation(out=Sbf[:, hs, :], in_=S_[:, hs, :], func=AF.Copy), reads=[H("S_", hf)], writes=[H("Sbf", hf)])
                    both(s_seq)
                    fw.dma("sp", lambda h: h.dma_start(out=o_dir[dr_, t0:t0 + 128, :].rearrange("p (h c) -> p h c", h=8), in_=ost[b2][:]),
                           reads=[K("ost")], writes=["o_dir"])
            fw.barrier()

        fw.phase = "P5"
        with ExitStack() as st:
            kw = [SB(st, f"kw{i}", [128, 256], BF16) for i in range(4)]
            qw = [SB(st, f"qw{i}", [128, 128], BF16) for i in range(4)]
            vw = [SB(st, f"vw{i}", [128, 2, 128], BF16) for i in range(4)]
            sc_ = [SB(st, f"sc{i}", [128, 256]) for i in range(4)]
            pe_ = [SB(st, f"pe{i}", [128, 256], BF16) for i in range(4)]
            pT = [SB(st, f"pT{i}", [128, 2, 128], BF16) for i in range(4)]
            oo = [SB(st, f"oo{i}", [128, 130]) for i in range(4)]
            nmx = [SB(st, f"nmx{i}", [128, 1]) for i in range(4)]
            idb = SB(st, "idb", [128, 128], BF16)
            msk = SB(st, "msk", [128, 4, 3, 256])
            mk1 = SB(st, "mk1", [128, 1])
            psc_t = [PSM(st, f"psc{i}", [128, 512]) for i in range(2)]
            psc = [psc_t[i // 2][:, (i % 2) * 256:(i % 2 + 1) * 256] for i in range(4)]
            ppt_t = PSM(st, "ppt", [128, 4, 2, 128], BF16)
            ppt = [ppt_t[:, i, :, :] for i in range(4)]
            pov_t = PSM(st, "pov", [128, 4, 128])
            pov = [pov_t[:, i, :] for i in range(4)]
            fw.op("dve", lambda h: h.tensor_copy(out=idb[:], in_=ident), reads=["cst"], writes=["idb"])
            band = cst[:, C_BAND:C_BAND + 256]; negl = cst[:, C_NEGL:C_NEGL + 256]; negr = cst[:, C_NEGR:C_NEGR + 256]
            for s in range(NSEG):
                fw.op("dve", lambda h, s=s: h.tensor_scalar(out=mk1[:], in0=sgf[:, s:s + 1], scalar1=-1.0, scalar2=1.0, op0=ALU.mult, op1=ALU.add), reads=["sgf"], writes=["mk1"])
                fw.op("dve", lambda h, s=s: h.scalar_tensor_tensor(out=msk[:, s, 0, :], in0=negl, scalar=mk1[:, 0:1], in1=band, op0=ALU.mult, op1=ALU.add),
                      reads=["mk1", "cst"], writes=["msk"])
                fw.op("dve", lambda h, s=s: h.tensor_scalar(out=mk1[:], in0=sgf[:, 4 + s:5 + s], scalar1=-1.0, scalar2=1.0, op0=ALU.mult, op1=ALU.add), reads=["sgf", "msk"], writes=["mk1"])
                fw.op("dve", lambda h, s=s: h.scalar_tensor_tensor(out=msk[:, s, 1, :], in0=negr, scalar=mk1[:, 0:1], in1=band, op0=ALU.mult, op1=ALU.add),
                      reads=["mk1", "cst"], writes=["msk"])
                fw.op("dve", lambda h, s=s: h.scalar_tensor_tensor(out=msk[:, s, 2, :], in0=negr, scalar=mk1[:, 0:1], in1=msk[:, s, 0, :], op0=ALU.mult, op1=ALU.add),
                      reads=["mk1", "cst", "msk"], writes=["msk"])
            it = 0
            for g in range(3):
                dil = DIL[g]
                ntile = MC[g] // 128
                tps = MS[g] // 128
                for hh in range(4):
                    kfull = akT[g][hh * 128:(hh + 1) * 128, :].rearrange("p (r c) -> p r c", c=CL[g])
                    qfull = aqT[g][hh * 128:(hh + 1) * 128, :].rearrange("p (r c) -> p r c", c=CL[g])
                    vfull = avs[g][:, hh * 128:(hh + 1) * 128].rearrange("(r c) d -> r c d", c=CL[g])
                    for r in range(dil):
                        for b in range(ntile):
                            i2 = it % 4
                            it += 1
                            K = lambda n: (n, i2)
                            s = b // tps
                            first = (b % tps == 0)
                            lastt = (b % tps == tps - 1)
                            fw.dma("sp", lambda h, i2=i2, kfull=kfull, r=r, b=b: h.dma_start(out=kw[i2][:], in_=kfull[:, r, 128 * b:128 * b + 256]), reads=[("aqk", g), ("akT", g)], writes=[K("kw")])
                            fw.dma("sp", lambda h, i2=i2, qfull=qfull, r=r, b=b: h.dma_start(out=qw[i2][:], in_=qfull[:, r, 64 + 128 * b:64 + 128 * b + 128]), reads=[("aqk", g)], writes=[K("qw")])
                            fw.dma("sp", lambda h, i2=i2, vfull=vfull, r=r, b=b: h.dma_start(out=vw[i2][:], in_=vfull[r, 128 * b:128 * b + 256, :].rearrange("(k p) d -> p k d", p=128)),
                                   reads=[("av", g)], writes=[K("vw")])
                            fw.op("pe", lambda h, i2=i2: h.matmul(psc[i2], lhsT=qw[i2][:], rhs=kw[i2][:], start=True, stop=True), reads=[K("qw"), K("kw")], writes=[K("psc")])
                            if first and lastt:
                                mask = msk[:, s, 2, :]
                            elif first:
                                mask = msk[:, s, 0, :]
                            elif lastt:
                                mask = msk[:, s, 1, :]
                            else:
                                mask = band
                            fw.op("dve", lambda h, i2=i2, mask=mask: h.tensor_tensor(out=sc_[i2][:], in0=psc[i2], in1=mask, op=ALU.add), reads=[K("psc"), "msk", "cst"], writes=[K("sc")])
                            fw.op("dve", lambda h, i2=i2: h.reduce_max(out=oo[i2][:, 128:129], in_=sc_[i2][:], axis=AX.X), reads=[K("sc")], writes=[K("oo")])
                            fw.op("dve", lambda h, i2=i2: h.tensor_scalar(out=nmx[i2][:], in0=oo[i2][:, 128:129], scalar1=-1.0, scalar2=None, op0=ALU.mult), reads=[K("oo")], writes=[K("nmx")])
                            fw.op("act", lambda h, i2=i2: h.activation(out=pe_[i2][:], in_=sc_[i2][:], func=AF.Exp, bias=nmx[i2][:], scale=1.0, accum_out=oo[i2][:, 129:130]),
                                  reads=[K("sc"), K("nmx")], writes=[K("pe"), K("oo")])

                            def ptr_(h, i2=i2):
                                h.transpose(out=ppt[i2][:, 0, :], in_=pe_[i2][:, 0:128], identity=idb[:])
                                return h.transpose(out=ppt[i2][:, 1, :], in_=pe_[i2][:, 128:256], identity=idb[:])
                            fw.op("pe", ptr_, reads=[K("pe"), "idb"], writes=[K("ppt")])
                            fw.op("act", lambda h, i2=i2: h.activation(out=pT[i2][:], in_=ppt[i2], func=AF.Copy), reads=[K("ppt")], writes=[K("pT")])

                            def pv(h, i2=i2):
                                h.matmul(pov[i2], lhsT=pT[i2][:, 0, :], rhs=vw[i2][:, 0, :], start=True, stop=False)
                                return h.matmul(pov[i2], lhsT=pT[i2][:, 1, :], rhs=vw[i2][:, 1, :], start=False, stop=True)
                            fw.op("pe", pv, reads=[K("pT"), K("vw")], writes=[K("pov")])
                            fw.op("dve", lambda h, i2=i2: h.tensor_copy(out=oo[i2][:, 0:128], in_=pov[i2]), reads=[K("pov")], writes=[K("oo")])
                            dst = Oat[g, :, hh, :].rearrange("(m r) c -> r m c", r=dil)[r, 128 * b:128 * b + 128, :]
                            fw.dma("sp", lambda h, i2=i2, dst=dst: h.dma_start(out=dst, in_=oo[i2][:]), reads=[K("oo")], writes=["Oat"])
            fw.barrier()

        fw.phase = "P5b"
        with ExitStack() as st:
            of_ = [SB(st, f"of{i}", [128, 8, 128]) for i in range(2)]
            ob_ = [SB(st, f"ob{i}", [128, 8, 128]) for i in range(2)]
            szt = [SB(st, f"szt{i}", [128, 8, 128]) for i in range(2)]
            osq = SB(st, "osq", [128, 8, 128])
            ss8 = SB(st, "ss8", [128, 8])
            yd = [SB(st, f"yd{i}", [128, 8, 128], BF16) for i in range(2)]
            og = [[SB(st, f"og{g}_{i}", [128, 4, 130]) for i in range(2)] for g in range(3)]
            m4 = SB(st, "m4", [128, 4]); w4 = SB(st, "w4", [128, 3, 4]); d4 = SB(st, "d4", [128, 4]); t4 = SB(st, "t4", [128, 4])
            am = SB(st, "am", [128, 4, 128]); am2 = SB(st, "am2", [128, 4, 128])
            ya = [SB(st, f"ya{i}", [128, 4, 128], BF16) for i in range(2)]
            pdn = PSM(st, "pdn", [128, 1024])
            pat = PSM(st, "pat", [128, 512])
            for tl in range(64):
                t0 = tl * 128
                i2 = tl % 2
                K = lambda n: (n, i2)
                fw.dma("sp", lambda h, i2=i2, t0=t0: h.dma_start(out=of_[i2][:], in_=o_dir[0, t0:t0 + 128, :].rearrange("p (h c) -> p h c", h=8)), reads=["o_dir"], writes=[K("of")])
                fw.dma("sp", lambda h, i2=i2, t0=t0: h.dma_start(out=ob_[i2][:], in_=o_dir[1, t0:t0 + 128, :].rearrange("p (h c) -> p h c", h=8)), reads=["o_dir"], writes=[K("ob")])
                fw.dma("sp", lambda h, i2=i2, t0=t0: h.dma_start(out=szt[i2][:], in_=szT[:, t0:t0 + 128].rearrange("(h p) t -> p h t", p=128)), reads=["szT"], writes=[K("szt")])
                fw.op("pool", lambda h, i2=i2: h.tensor_tensor(out=of_[i2][:], in0=of_[i2][:], in1=ob_[i2][:], op=ALU.add), reads=[K("of"), K("ob")], writes=[K("of")])
                fw.op("pool", lambda h, i2=i2: h.tensor_tensor(out=osq[:], in0=of_[i2][:], in1=of_[i2][:], op=ALU.mult), reads=[K("of")], writes=["osq"])
                fw.op("dve", lambda h: h.tensor_reduce(out=ss8[:], in_=osq[:], axis=AX.X, op=ALU.add), reads=["osq"], writes=["ss8"])
                fw.op("act", lambda h: h.activation(out=ss8[:], in_=ss8[:], func=AF.Sqrt, bias=EPS, scale=1.0 / 128.0), reads=["ss8"], writes=["ss8"])
                fw.op("dve", lambda h: h.reciprocal(out=ss8[:], in_=ss8[:]), reads=["ss8"], writes=["ss8"])
                fw.op("dve", lambda h, i2=i2: h.tensor_tensor(out=of_[i2][:], in0=of_[i2][:], in1=ss8[:].unsqueeze(2).to_broadcast([128, 8, 128]), op=ALU.mult),
                      reads=[K("of"), "ss8"], writes=[K("of")])

                def trd(h, i2=i2):
                    for hd in range(8):
                        ins = h.transpose(out=pdn[:, hd * 128:(hd + 1) * 128], in_=of_[i2][:, hd, :], identity=ident)
                    return ins
                fw.op("pe", trd, reads=[K("of"), "cst"], writes=["pdn"])
                fw.op("dve", lambda h, i2=i2: h.scalar_tensor_tensor(out=yd[i2][:], in0=pdn[:, :].rearrange("p (h c) -> p h c", h=8), scalar=dnwT[:, 0:1], in1=szt[i2][:],
                                                                   op0=ALU.mult, op1=ALU.mult), reads=["pdn", "dnwT", K("szt")], writes=[K("yd")])
                fw.dma("sp", lambda h, i2=i2, t0=t0: h.dma_start(out=ydnT[:, t0:t0 + 128].rearrange("(h p) t -> p h t", p=128), in_=yd[i2][:]), reads=[K("yd")], writes=["ydnT"])
                for g in range(3):
                    fw.dma("sp", lambda h, g=g, i2=i2, t0=t0: h.dma_start(out=og[g][i2][:], in_=Oat[g, t0:t0 + 128, :, :]), reads=["Oat"], writes=[K(f"og{g}")])
                mxs = [og[g][i2][:, :, 128] for g in range(3)]
                dns = [og[g][i2][:, :, 129] for g in range(3)]
                fw.op("dve", lambda h: h.tensor_tensor(out=m4[:], in0=mxs[0], in1=mxs[1], op=ALU.max), reads=[K("og0"), K("og1")], writes=["m4"])
                fw.op("dve", lambda h: h.tensor_tensor(out=m4[:], in0=m4[:], in1=mxs[2], op=ALU.max), reads=["m4", K("og2")], writes=["m4"])
                for g in range(3):
                    fw.op("dve", lambda h, g=g: h.tensor_tensor(out=w4[:, g, :], in0=mxs[g], in1=m4[:], op=ALU.subtract), reads=[K(f"og{g}"), "m4"], writes=["w4"])
                fw.op("act", lambda h: h.activation(out=w4[:], in_=w4[:], func=AF.Exp), reads=["w4"], writes=["w4"])
                fw.op("dve", lambda h: h.tensor_tensor(out=d4[:], in0=w4[:, 0, :], in1=dns[0], op=ALU.mult), reads=["w4", K("og0")], writes=["d4"])
                for g in (1, 2):
                    fw.op("dve", lambda h, g=g: h.tensor_tensor(out=t4[:], in0=w4[:, g, :], in1=dns[g], op=ALU.mult), reads=["w4", K(f"og{g}")], writes=["t4"])
                    fw.op("dve", lambda h: h.tensor_tensor(out=d4[:], in0=d4[:], in1=t4[:], op=ALU.add), reads=["d4", "t4"], writes=["d4"])
                fw.op("dve", lambda h: h.reciprocal(out=d4[:], in_=d4[:]), reads=["d4"], writes=["d4"])
                for g in range(3):
                    fw.op("dve", lambda h, g=g: h.tensor_tensor(out=w4[:, g, :], in0=w4[:, g, :], in1=d4[:], op=ALU.mult), reads=["w4", "d4"], writes=["w4"])
                fw.op("pool", lambda h, i2=i2: h.tensor_tensor(out=am[:], in0=og[0][i2][:, :, 0:128], in1=w4[:, 0, :].unsqueeze(2).to_broadcast([128, 4, 128]), op=ALU.mult),
                      reads=[K("og0"), "w4"], writes=["am"])
                for g in (1, 2):
                    fw.op("pool", lambda h, g=g, i2=i2: h.tensor_tensor(out=am2[:], in0=og[g][i2][:, :, 0:128], in1=w4[:, g, :].unsqueeze(2).to_broadcast([128, 4, 128]), op=ALU.mult),
                          reads=[K(f"og{g}"), "w4"], writes=["am2"])
                    fw.op("pool", lambda h: h.tensor_tensor(out=am[:], in0=am[:], in1=am2[:], op=ALU.add), reads=["am", "am2"], writes=["am"])

                def tra(h):
                    for hd in range(4):
                        ins = h.transpose(out=pat[:, hd * 128:(hd + 1) * 128], in_=am[:, hd, :], identity=ident)
                    return ins
                fw.op("pe", tra, reads=["am", "cst"], writes=["pat"])
                fw.op("act", lambda h, i2=i2: h.activation(out=ya[i2][:], in_=pat[:, :].rearrange("p (h c) -> p h c", h=4), func=AF.Copy), reads=["pat"], writes=[K("ya")])
                fw.dma("sp", lambda h, i2=i2, t0=t0: h.dma_start(out=yatT[:, t0:t0 + 128].rearrange("(h p) t -> p h t", p=128), in_=ya[i2][:]), reads=[K("ya")], writes=["yatT"])
            fw.barrier()

        fw.phase = "P6"
        with ExitStack() as st:
            xin = [SB(st, f"xin{q}", [128, D]) for q in range(4)]
            xT = SB(st, "xT", [128, 16, 512])
            acc = SB(st, "acc", [128, 16, 512])
            sq = [SB(st, f"sq{i}", [128, 512]) for i in range(2)]
            rstd = SB(st, "rstd", [128, 512])
            big = SB(st, "big", [128, 32, 512], BF16)
            h2T = SB(st, "h2T", [128, 16, 512], BF16)
            wk = [SB(st, f"wk{i}", [128, 32, 128], BF16) for i in range(3)]
            print("sbuf remaining", nc.sbuf_bytes_remaining)
            tmp = [SB(st, f"tmp{i}", [128, 512]) for i in range(4)]
            pxt = [PSM(st, f"pxt{i}", [128, 512]) for i in range(2)]
            pss = PSM(st, "pss", [128, 512])
            pmm = [PSM(st, f"pmm{i}", [128, 512]) for i in range(4)]
            hT = big[:, 0:16, :]
            ydn_s = big[:, 16:24, :]
            yat_s = big[:, 24:28, :]
            mixedT = h2T

            wslot = [0]

            def load_w(src_ap, nj, key):
                i = wslot[0] % 3
                wslot[0] += 1
                fw.dma("sp", lambda h, i=i: h.dma_start(out=wk[i][:, 0:nj, :], in_=src_ap), reads=[key], writes=[("wk", i)])
                return i

            def proj(widx, nj, rhs_fn, rhs_keys, pbank, first=True, last=True, j0=0):
                def mm(h):
                    for j in range(nj):
                        ins = h.matmul(pmm[pbank][:, :], lhsT=wk[widx][:, j, :], rhs=rhs_fn(j),
                                       start=(first and j == 0), stop=(last and j == nj - 1))
                    return ins
                fw.op("pe", mm, reads=[("wk", widx)] + rhs_keys, writes=[("pmm", pbank)])

            for tile in range(min(T // 512, KTILES)):
                t0 = tile * 512
                s = tile // 4
                front_end((xin, xT, sq, rstd, pxt, pss), t0, s)
                for j in range(16):
                    fw.op("pool", lambda h, j=j: h.tensor_tensor(out=tmp[j % 2][:], in0=xT[:, j, :], in1=rstd[:], op=ALU.mult),
                          reads=[("xT", j), "rstd"], writes=[("tmp", j % 2)])
                    fw.op("dve", lambda h, j=j: h.tensor_scalar(out=hT[:, j, :], in0=tmp[j % 2][:], scalar1=A1[:, j, s:s + 1],
                                                                scalar2=modT[:, j, s:s + 1], op0=ALU.mult, op1=ALU.add),
                          reads=[("tmp", j % 2), "A1", "modT"], writes=[("big", j)])
                fw.dma("sp", lambda h: h.dma_start(out=ydn_s, in_=ydnT[:, t0:t0 + 512].rearrange("(j p) t -> p j t", p=128)),
                       reads=["ydnT"], writes=[("big", 16 + j) for j in range(8)])
                fw.dma("sp", lambda h: h.dma_start(out=yat_s, in_=yatT[:, t0:t0 + 512].rearrange("(j p) t -> p j t", p=128)),
                       reads=["yatT"], writes=[("big", 24 + j) for j in range(4)])
                for cb in range(16):
                    w1 = load_w(wb_mg[cb], 16, ("wb_mg", cb))
                    proj(w1, 16, lambda j: hT[:, j, :], [("big", j) for j in range(16)], 0)
                    w2 = load_w(wb_mg[16 + cb], 16, ("wb_mg", 16 + cb))
                    proj(w2, 16, lambda j: hT[:, j, :], [("big", j) for j in range(16)], 1)
                    w3 = load_w(wb_dn[cb], 8, ("wb_dn", cb))
                    proj(w3, 8, lambda j: ydn_s[:, j, :], [("big", 16 + j) for j in range(8)], 2)
                    w4 = load_w(wb_at[cb], 4, ("wb_at", cb))
                    proj(w4, 4, lambda j: yat_s[:, j, :], [("big", 24 + j) for j in range(4)], 3)
                    fw.op("act", lambda h: h.activation(out=tmp[0][:], in_=pmm[0][:, :], func=AF.Sigmoid), reads=[("pmm", 0)], writes=[("tmp", 0)])
                    fw.op("act", lambda h: h.activation(out=tmp[1][:], in_=pmm[1][:, :], func=AF.Sigmoid), reads=[("pmm", 1)], writes=[("tmp", 1)])
                    fw.op("dve", lambda h: h.tensor_tensor(out=tmp[2][:], in0=pmm[2][:, :], in1=tmp[0][:], op=ALU.mult),
                          reads=[("pmm", 2), ("tmp", 0)], writes=[("tmp", 2)])
                    fw.op("dve", lambda h: h.tensor_tensor(out=tmp[3][:], in0=pmm[3][:, :], in1=tmp[1][:], op=ALU.mult),
                          reads=[("pmm", 3), ("tmp", 1)], writes=[("tmp", 3)])
                    fw.op("pool", lambda h, cb=cb: h.tensor_tensor(out=mixedT[:, cb, :], in0=tmp[2][:], in1=tmp[3][:], op=ALU.add),
                          reads=[("tmp", 2), ("tmp", 3)], writes=[("h2T", cb)])
                for cb in range(16):
                    w1 = load_w(wb_out[cb], 16, ("wb_out", cb))
                    proj(w1, 16, lambda j: mixedT[:, j, :], [("h2T", j) for j in range(16)], cb % 4)
                    fw.op("act", lambda h, cb=cb: h.activation(out=acc[:, cb, :], in_=pmm[cb % 4][:, :], func=AF.Copy),
                          reads=[("pmm", cb % 4)], writes=[("acc", cb)])
                sumsq_rstd(lambda j: acc[:, j, :], 16, sq, pss, rstd, lambda j: ("acc", j))
                for j in range(16):
                    fw.op("pool", lambda h, j=j: h.tensor_tensor(out=tmp[j % 2][:], in0=acc[:, j, :], in1=rstd[:], op=ALU.mult),
                          reads=[("acc", j), "rstd"], writes=[("tmp", j % 2)])
                    fw.op("dve", lambda h, j=j: h.scalar_tensor_tensor(out=xT[:, j, :], in0=tmp[j % 2][:], scalar=G1[:, j, s:s + 1], in1=xT[:, j, :],
                                                                       op0=ALU.mult, op1=ALU.add),
                          reads=[("tmp", j % 2), "G1", ("xT", j)], writes=[("xT", j)])
                sumsq_rstd(lambda j: xT[:, j, :], 16, sq, pss, rstd, lambda j: ("xT", j))
                for j in range(16):
                    fw.op("pool", lambda h, j=j: h.tensor_tensor(out=tmp[j % 2][:], in0=xT[:, j, :], in1=rstd[:], op=ALU.mult),
                          reads=[("xT", j), "rstd"], writes=[("tmp", j % 2)])
                    fw.op("dve", lambda h, j=j: h.tensor_scalar(out=h2T[:, j, :], in0=tmp[j % 2][:], scalar1=A2[:, j, s:s + 1],
                                                                scalar2=modT[:, 48 + j, s:s + 1], op0=ALU.mult, op1=ALU.add),
                          reads=[("tmp", j % 2), "A2", "modT"], writes=[("h2T", j)])
                for hf in range(2):
                    for fb in range(32):
                        w1 = load_w(wb_f1[hf * 32 + fb], 16, ("wb_f1", hf * 32 + fb))
                        pb = fb % 4
                        proj(w1, 16, lambda j: h2T[:, j, :], [("h2T", j) for j in range(16)], pb)
                        if fb % 2 == 0:
                            fw.op("act", lambda h, pb=pb: h.activation(out=tmp[pb][:], in_=pmm[pb][:, :], func=AF.Relu), reads=[("pmm", pb)], writes=[("tmp", pb)])
                            fw.op("pool", lambda h, pb=pb, fb=fb: h.tensor_tensor(out=big[:, fb, :], in0=tmp[pb][:], in1=tmp[pb][:], op=ALU.mult),
                                  reads=[("tmp", pb)], writes=[("big", fb)])
                        else:
                            fw.op("dve", lambda h, pb=pb: h.tensor_scalar(out=tmp[pb][:], in0=pmm[pb][:, :], scalar1=0.0, scalar2=None, op0=ALU.max),
                                  reads=[("pmm", pb)], writes=[("tmp", pb)])
                            fw.op("pool", lambda h, pb=pb, fb=fb: h.tensor_tensor(out=big[:, fb, :], in0=tmp[pb][:], in1=tmp[pb][:], op=ALU.mult),
                                  reads=[("tmp", pb)], writes=[("big", fb)])
                    for cb in range(16):
                        w1 = load_w(wb_f2[hf * 16 + cb], 32, ("wb_f2", hf * 16 + cb))
                        pb = cb % 4
                        proj(w1, 32, lambda j: big[:, j, :], [("big", j) for j in range(32)], pb)
                        if hf == 0:
                            fw.op("act", lambda h, cb=cb, pb=pb: h.activation(out=acc[:, cb, :], in_=pmm[pb][:, :], func=AF.Copy),
                                  reads=[("pmm", pb)], writes=[("acc", cb)])
                        else:
                            fw.op("dve", lambda h, cb=cb, pb=pb: h.tensor_tensor(out=acc[:, cb, :], in0=pmm[pb][:, :], in1=acc[:, cb, :], op=ALU.add),
                                  reads=[("pmm", pb), ("acc", cb)], writes=[("acc", cb)])
                sumsq_rstd(lambda j: acc[:, j, :], 16, sq, pss, rstd, lambda j: ("acc", j))
                for j in range(16):
                    fw.op("pool", lambda h, j=j: h.tensor_tensor(out=tmp[j % 2][:], in0=acc[:, j, :], in1=rstd[:], op=ALU.mult),
                          reads=[("acc", j), "rstd"], writes=[("tmp", j % 2)])
                    fw.op("dve", lambda h, j=j: h.scalar_tensor_tensor(out=acc[:, j, :], in0=tmp[j % 2][:], scalar=G2[:, j, s:s + 1], in1=xT[:, j, :],
                                                                       op0=ALU.mult, op1=ALU.add),
                          reads=[("tmp", j % 2), "G2", ("xT", j)], writes=[("acc", j)])
                for q in range(4):
                    for jg in range(4):
                        pb = pmm[(q * 4 + jg) % 4]

                        def tr(h, q=q, jg=jg, pb=pb):
                            for jj in range(4):
                                j = jg * 4 + jj
                                ins = h.transpose(out=pb[:, jj * 128:(jj + 1) * 128], in_=acc[:, j, q * 128:(q + 1) * 128], identity=ident)
                            return ins
                        fw.op("pe", tr, reads=[("acc", jg * 4 + jj) for jj in range(4)] + ["cst"], writes=[("pmm", (q * 4 + jg) % 4)])
                        eng = "act" if jg % 2 == 0 else "dve"
                        if eng == "act":
                            fw.op("act", lambda h, q=q, jg=jg, pb=pb: h.activation(out=xin[q][:, jg * 512:(jg + 1) * 512], in_=pb[:, :], func=AF.Copy),
                                  reads=[("pmm", (q * 4 + jg) % 4)], writes=[("xin", q)])
                        else:
                            fw.op("dve", lambda h, q=q, jg=jg, pb=pb: h.tensor_copy(out=xin[q][:, jg * 512:(jg + 1) * 512], in_=pb[:, :]),
                                  reads=[("pmm", (q * 4 + jg) % 4)], writes=[("xin", q)])
                    fw.dma("sp", lambda h, q=q: h.dma_start(out=y[t0 + q * 128:t0 + (q + 1) * 128, :], in_=xin[q][:]),
                           reads=[("xin", q)], writes=["y"])
            fw.barrier()
        fw.emit_all()
    return nc


_NC_CACHE = {}

SAMPLE_MAP = {2: [0, 1, 2], 3: [3, 4, 5], 4: [6, 7, 8], 5: [9, 10, 11], 6: [12, 13], 7: [14, 15]}


def kernel(x_prompt, x_sample, c_prompt, c_sample, w_ada, b_ada, norm_pre_mix, norm_post_mix,
           norm_pre_ffn, norm_post_ffn, w_in, conv_w, A_log, dt_bias, dn_norm_w, w_dn_out,
           w_at_out, w_out, w_ff1, w_ff2):
    f = lambda a: np.ascontiguousarray(np.asarray(a, dtype=np.float32))
    x_prompt, x_sample, c_prompt, c_sample = f(x_prompt), f(x_sample), f(c_prompt), f(c_sample)
    if "nc" not in _NC_CACHE:
        _NC_CACHE["nc"] = build_program()
    nc = _NC_CACHE["nc"]
    shared = {
        "consts": make_consts(), "lvlmask": make_lvlmask(),
        "w_ada": f(w_ada)[0], "b_ada": f(b_ada)[0].reshape(96, 128),
        "norms": np.concatenate([f(norm_pre_mix)[0], f(norm_post_mix)[0], f(norm_pre_ffn)[0], f(norm_post_ffn)[0]]).reshape(64, 128),
        "w_in": f(w_in)[0], "conv_w": f(conv_w)[0].reshape(120, 128), "A_log": f(A_log)[0].reshape(1, 16),
        "dt_bias": f(dt_bias)[0].reshape(1, 16), "dn_norm_w": f(dn_norm_w)[0].reshape(1, 128),
        "w_dn_out": f(w_dn_out)[0], "w_at_out": f(w_at_out)[0], "w_out": f(w_out)[0],
        "w_ff1": f(w_ff1)[0], "w_ff2": f(w_ff2)[0],
    }
    in_maps = []
    for core in range(8):
        xs = np.zeros((T, D), np.float32)
        cs = np.zeros((NSEG, D), np.float32)
        sg = np.zeros((128, 16), np.float32)
        if core < 2:
            xs[:] = x_prompt[core]
            cs[:] = c_prompt[core][None, :]
            for s in range(NSEG):
                sg[:, s] = 1.0 if s > 0 else 0.0
                sg[:, 4 + s] = 1.0 if s < NSEG - 1 else 0.0
                sg[:, 8 + s] = s * SEG
        else:
            seqs = SAMPLE_MAP[core]
            for s in range(NSEG):
                b = seqs[s] if s < len(seqs) else seqs[0]
                xs[s * SEG:(s + 1) * SEG] = x_sample[b]
                cs[s] = c_sample[b]
        m = dict(shared)
        m["x"] = xs
        m["c"] = cs.reshape(NSEG * NJ, 128)
        m["segf"] = sg
        in_maps.append(m)
    if KSCOPES:
        _NC_CACHE["in_maps"] = in_maps
        res = run_bass_kernel_spmd(nc, in_maps, core_ids=list(range(8)), trace=True)
        _NC_CACHE["res"] = res
    else:
        res = run_bass_kernel_spmd(nc, in_maps, core_ids=list(range(8)))
    if KDEBUG:
        _NC_CACHE["res"] = res
    y_prompt = np.stack([np.asarray(res.results[c]["y"], dtype=np.float32) for c in range(2)])
    y_sample = np.zeros_like(x_sample)
    for core, seqs in SAMPLE_MAP.items():
        yc = np.asarray(res.results[core]["y"], dtype=np.float32)
        for s, b in enumerate(seqs):
            y_sample[b] = yc[s * SEG:(s + 1) * SEG]
    return (y_prompt, y_sample)
```

```python
from contextlib import ExitStack
import numpy as np
import concourse.bass as bass
import concourse.mybir as mybir
from concourse.bass_utils import run_bass_kernel_spmd

F32 = mybir.dt.float32
BF16 = mybir.dt.bfloat16
I32 = mybir.dt.int32
AF = mybir.ActivationFunctionType
ALU = mybir.AluOpType
AX = mybir.AxisListType

D = 2048
NJ = 16
T = 8192
NSEG = 4
SEG = 2048
DFF = 8192
EPS = 1e-6
IN_COLS = 12832
NEG = -1.0e30

import os
KDEBUG = int(os.environ.get("KDEBUG", "0"))
KTILES = int(os.environ.get("KTILES", "16"))
KDUMP = os.environ.get("KDUMP", "").split(",")
KSCOPES = int(os.environ.get("KSCOPES", "0"))
COMPUTE = ("pe", "act", "dve", "pool")
N_DMA_SEMS = 12


class Ticket:
    __slots__ = ("kind", "eng", "val", "sem")

    def __init__(self, kind, eng, val, sem=None):
        self.kind, self.eng, self.val, self.sem = kind, eng, val, sem


class _Rec:
    def __init__(self):
        self.calls = []

    def __getattr__(self, name):
        def f(*a, **k):
            self.calls.append((name, a, k))
            return self
        return f


def _replay(h, calls):
    ins = None
    for name, a, k in calls:
        ins = getattr(h, name)(*a, **k)
    return ins


class FW:
    def __init__(self, nc, es):
        self.nc = nc
        self.streams = {k: [] for k in ("pe", "act", "dve", "pool", "sp")}
        self.sem = {}
        self.count = {}
        for k in COMPUTE:
            self.sem[k] = es.enter_context(nc.semaphore("s_" + k))
            self.count[k] = 0
        self.dsem, self.dcount, self.dnext = {}, {}, {}
        for q in ("sp", "act", "pool"):
            self.dsem[q] = [es.enter_context(nc.semaphore(f"d_{q}{i}")) for i in range(N_DMA_SEMS)]
            self.dcount[q] = [0] * N_DMA_SEMS
            self.dnext[q] = 0
        self.known = {k: {} for k in self.streams}
        self.lastw = {}
        self.readers = {}
        self.n_ops = 0
        self.phase = "P0"

    def _need(self, stream, t, waits):
        if t is None:
            return
        if t.kind == "c":
            if t.eng == stream and stream == "pe":
                return
            key = ("c", t.eng)
            sem = self.sem[t.eng]
        else:
            key = ("d", id(t.sem))
            sem = t.sem
        if self.known[stream].get(key, 0) >= t.val:
            return
        cur = waits.get(key)
        if cur is None or cur[1] < t.val:
            waits[key] = (sem, t.val)

    def _deps(self, stream, reads, writes):
        waits = {}
        for r in reads:
            self._need(stream, self.lastw.get(r), waits)
        for w in writes:
            self._need(stream, self.lastw.get(w), waits)
            for t in self.readers.get(w, {}).values():
                self._need(stream, t, waits)
        out = []
        for key, (sem, val) in waits.items():
            self.known[stream][key] = val
            out.append((sem, val))
        return out

    def _commit(self, t, reads, writes):
        for w in writes:
            self.lastw[w] = t
            self.readers[w] = {}
        k = ("c", t.eng) if t.kind == "c" else ("d", id(t.sem))
        for r in reads:
            self.readers.setdefault(r, {})[k] = t

    def op(self, eng, fn, reads=(), writes=()):
        waits = self._deps(eng, reads, writes)
        self.count[eng] += 1
        val = self.count[eng]
        sem = self.sem[eng]

        rec = _Rec()
        fn(rec)
        calls = rec.calls

        def emit(h, waits=waits, calls=calls, sem=sem):
            for s, v in waits:
                h.wait_ge(s, v)
            _replay(h, calls).then_inc(sem, 1)

        emit.phase = self.phase
        self.streams[eng].append(emit)
        self._commit(Ticket("c", eng, val), reads, writes)
        self.n_ops += 1

    def dma(self, q, fn, reads=(), writes=()):
        i = self.dnext[q]
        self.dnext[q] = (i + 1) % N_DMA_SEMS
        sem = self.dsem[q][i]
        prev = self.dcount[q][i]
        waits = self._deps(q, reads, writes)
        key = ("d", id(sem))
        if prev > 0 and self.known[q].get(key, 0) < prev:
            waits.append((sem, prev))
            self.known[q][key] = prev
        val = prev + 16
        self.dcount[q][i] = val

        rec = _Rec()
        fn(rec)
        calls = rec.calls

        def emit(h, waits=waits, calls=calls, sem=sem):
            for s, v in waits:
                h.wait_ge(s, v)
            _replay(h, calls).then_inc(sem, 16)

        emit.phase = self.phase
        self.streams[q].append(emit)
        self._commit(Ticket("d", q, val, sem), reads, writes)
        self.n_ops += 1

    def barrier(self):
        targets = [(self.sem[k], self.count[k], ("c", k)) for k in COMPUTE if self.count[k] > 0]
        for q in self.dsem:
            for i, s in enumerate(self.dsem[q]):
                if self.dcount[q][i] > 0:
                    targets.append((s, self.dcount[q][i], ("d", id(s))))
        for stream in self.streams:
            ws = []
            for s, v, key in targets:
                if self.known[stream].get(key, 0) < v:
                    ws.append((s, v))
                    self.known[stream][key] = v

            def emit(h, ws=ws):
                for s, v in ws:
                    h.wait_ge(s, v)

            self.streams[stream].append(emit)
        self.lastw = {}
        self.readers = {}

    def emit_all(self):
        nc = self.nc

        def run(h, fs):
            if not KSCOPES:
                for f in fs:
                    f(h)
                return
            cur = None
            sid = None
            for f in fs:
                ph = getattr(f, "phase", cur)
                if ph != cur:
                    if cur is not None:
                        nc.leave_named_scope(cur, sid, False)
                    sid, _ = nc.enter_named_scope(ph, False)
                    cur = ph
                f(h)
            if cur is not None:
                nc.leave_named_scope(cur, sid, False)

        with nc.Block() as block:
            @block.tensor
            def _(h):
                run(h, self.streams["pe"])

            @block.scalar
            def _(h):
                run(h, self.streams["act"])

            @block.vector
            def _(h):
                run(h, self.streams["dve"])

            @block.gpsimd
            def _(h):
                run(h, self.streams["pool"])

            @block.sync
            def _(h):
                run(h, self.streams["sp"])


C_ID, C_MEAN, C_ONE, C_MLOW, C_MUP, C_GT, C_LT, C_LE, C_GE = 0, 128, 256, 384, 512, 640, 768, 896, 1024
C_RPERM, C_IOTA, C_IFREQ, C_BAND, C_NEGL, C_NEGR, C_TOTAL = 1152, 1280, 1792, 1793, 2049, 2305, 2561


def make_lvlmask():
    p = np.arange(128)[:, None]
    f = np.arange(128)[None, :]
    m = np.zeros((128, 14 * 128), np.float32)
    for k in range(7):
        same_hi = (p >> (k + 1)) == (f >> (k + 1))
        diff_lo = (p >> k) != (f >> k)
        m[:, (2 * k) * 128:(2 * k + 1) * 128] = same_hi & diff_lo & (p > f)
        m[:, (2 * k + 1) * 128:(2 * k + 2) * 128] = same_hi & diff_lo & (p < f)
    return m


def make_consts():
    c = np.zeros((128, C_TOTAL), np.float32)
    p = np.arange(128)[:, None]
    f = np.arange(128)[None, :]
    c[:, C_ID:C_ID + 128] = (p == f)
    c[:, C_MEAN:C_MEAN + 128] = 1.0 / D
    c[:, C_ONE:C_ONE + 128] = 1.0
    c[:, C_MLOW:C_MLOW + 128] = (p <= f)
    c[:, C_MUP:C_MUP + 128] = (p >= f)
    c[:, C_GT:C_GT + 128] = (p > f)
    c[:, C_LT:C_LT + 128] = (p < f)
    c[:, C_LE:C_LE + 128] = (p <= f)
    c[:, C_GE:C_GE + 128] = (p >= f)
    r = np.zeros((128, 128), np.float32)
    for m in range(64):
        r[m + 64, m] = -1.0
        r[m, m + 64] = 1.0
    c[:, C_RPERM:C_RPERM + 128] = r
    c[:, C_IOTA:C_IOTA + 512] = np.arange(512)[None, :]
    c[:, C_IFREQ] = (10000.0 ** (-(np.arange(128) % 64) / 64.0)) / (2 * np.pi)
    a = np.arange(128)[:, None]
    b = np.arange(256)[None, :]
    c[:, C_BAND:C_BAND + 256] = np.where((b - a >= 0) & (b - a <= 128), 0.0, NEG)
    c[:, C_NEGL:C_NEGL + 256] = np.where(b < 64, NEG, 0.0) * np.ones((128, 1))
    c[:, C_NEGR:C_NEGR + 256] = np.where(b >= 192, NEG, 0.0) * np.ones((128, 1))
    return c


def build_program():
    nc = bass.Bass("TRN2", target_bir_lowering=False)

    def EI(name, shape):
        return nc.dram_tensor(name, list(shape), F32, kind="ExternalInput").ap()

    x = EI("x", [T, D])
    cvec = EI("c", [NSEG * NJ, 128])
    segf = EI("segf", [128, 16])
    consts = EI("consts", [128, C_TOTAL])
    lvlmask = EI("lvlmask", [128, 14 * 128])
    w_ada = EI("w_ada", [D, 6 * D])
    b_ada = EI("b_ada", [96, 128])
    norms = EI("norms", [64, 128])
    w_in = EI("w_in", [D, IN_COLS])
    conv_w = EI("conv_w", [120, 128])
    alog = EI("A_log", [1, 16])
    dtb = EI("dt_bias", [1, 16])
    dnw = EI("dn_norm_w", [1, 128])
    w_dn_out = EI("w_dn_out", [1024, D])
    w_at_out = EI("w_at_out", [512, D])
    w_out = EI("w_out", [D, D])
    w_ff1 = EI("w_ff1", [D, DFF])
    w_ff2 = EI("w_ff2", [DFF, D])
    y = nc.dram_tensor("y", [T, D], F32, kind="ExternalOutput").ap()
    dbg_names = []

    def dump(fw, name, ap, keys, dt=F32):
        if not KDEBUG:
            return
        dtn = nc.dram_tensor("dbg_" + name, list(ap.shape), dt, kind="ExternalOutput").ap()
        dbg_names.append("dbg_" + name)
        fw.dma("sp", lambda h: h.dma_start(out=dtn, in_=ap), reads=list(keys), writes=["dbg_" + name])

    def DR(name, shape, dt=F32):
        if KDEBUG and name in KDUMP:
            dbg_names.append(name)
            return nc.dram_tensor(name, list(shape), dt, kind="ExternalOutput").ap()
        return nc.dram_tensor(name, list(shape), dt, kind="Internal").ap()

    wb_mg = DR("wb_mg", [32, 128, 16, 128], BF16)
    wb_dn = DR("wb_dn", [16, 128, 8, 128], BF16)
    wb_at = DR("wb_at", [16, 128, 4, 128], BF16)
    wb_out = DR("wb_out", [16, 128, 16, 128], BF16)
    wb_f1 = DR("wb_f1", [64, 128, 16, 128], BF16)
    wb_f2 = DR("wb_f2", [32, 128, 32, 128], BF16)
    ydnT = DR("ydnT", [1024, T], BF16)
    yatT = DR("yatT", [512, T], BF16)
    MERGE0 = 3072 + 1024 + 32 + 4608

    es = ExitStack()
    with es:
        fw = FW(nc, es)

        uid = [0]

        def SB(st, name, shape, dt=F32):
            uid[0] += 1
            return st.enter_context(nc.sbuf_tensor(f"{name}_{uid[0]}", list(shape), dt))

        def PSM(st, name, shape, dt=F32):
            uid[0] += 1
            return st.enter_context(nc.psum_tensor(f"{name}_{uid[0]}", list(shape), dt))

        cst = SB(es, "cst", [128, C_TOTAL])
        sgf = SB(es, "sgf", [128, 16])
        nrm = SB(es, "nrm", [128, 64])
        cT = SB(es, "cT", [128, 64])
        badaT = SB(es, "badaT", [128, 96])
        modT = SB(es, "modT", [128, 96, 4])
        A1 = SB(es, "A1", [128, 16, 4]); G1 = SB(es, "G1", [128, 16, 4])
        A2 = SB(es, "A2", [128, 16, 4]); G2 = SB(es, "G2", [128, 16, 4])
        ident = cst[:, C_ID:C_ID + 128]
        meanm = cst[:, C_MEAN:C_MEAN + 128]
        fw.dma("sp", lambda h: h.dma_start(out=cst[:], in_=consts[:, :]), writes=["cst"])
        fw.dma("sp", lambda h: h.dma_start(out=sgf[:], in_=segf[:, :]), writes=["sgf"])

        def cast(dst, src, key):
            fw.dma("pool", lambda h: h.dma_start(out=dst, in_=src), writes=[key])

        for b in range(32):
            cast(wb_mg[b], w_in[:, MERGE0 + b * 128:MERGE0 + (b + 1) * 128].rearrange("(j p) c -> p j c", p=128), ("wb_mg", b))
        for b in range(16):
            cast(wb_dn[b], w_dn_out[:, b * 128:(b + 1) * 128].rearrange("(j p) c -> p j c", p=128), ("wb_dn", b))
            cast(wb_at[b], w_at_out[:, b * 128:(b + 1) * 128].rearrange("(j p) c -> p j c", p=128), ("wb_at", b))
            cast(wb_out[b], w_out[:, b * 128:(b + 1) * 128].rearrange("(j p) c -> p j c", p=128), ("wb_out", b))
        for b in range(64):
            cast(wb_f1[b], w_ff1[:, b * 128:(b + 1) * 128].rearrange("(j p) c -> p j c", p=128), ("wb_f1", b))
        for hf in range(2):
            for b in range(16):
                cast(wb_f2[hf * 16 + b],
                     w_ff2[hf * 4096:(hf + 1) * 4096, b * 128:(b + 1) * 128].rearrange("(j p) c -> p j c", p=128),
                     ("wb_f2", hf * 16 + b))

        fw.phase = "P0mod"
        with ExitStack() as st:
            stg = SB(st, "stg0", [128, 128])
            stg2 = SB(st, "stg1", [128, 128])
            wa = [SB(st, f"wa{i}", [128, 16, 128]) for i in range(2)]
            pt = PSM(st, "p0t", [128, 512])
            pm = [PSM(st, f"p0m{i}", [128, 4]) for i in range(2)]
            fw.dma("sp", lambda h: h.dma_start(out=stg[0:64, :], in_=norms[:, :]), writes=["stg0"])
            fw.dma("sp", lambda h: h.dma_start(out=stg[64:128, :], in_=cvec[:, :]), writes=["stg0"])
            fw.op("pe", lambda h: h.transpose(out=pt[:, 0:128], in_=stg[:], identity=ident), reads=["stg0", "cst"], writes=["p0t"])
            fw.op("dve", lambda h: h.tensor_copy(out=nrm[:], in_=pt[:, 0:64]), reads=["p0t"], writes=["nrm"])
            fw.op("act", lambda h: h.activation(out=cT[:], in_=pt[:, 64:128], func=AF.Silu), reads=["p0t"], writes=["cT"])
            fw.dma("sp", lambda h: h.dma_start(out=stg2[0:96, :], in_=b_ada[:, :]), writes=["stg1"])
            fw.op("pe", lambda h: h.transpose(out=pt[:, 128:224], in_=stg2[0:96, :], identity=ident[0:96, 0:96]), reads=["stg1", "cst"], writes=["p0t"])
            fw.op("dve", lambda h: h.tensor_copy(out=badaT[:], in_=pt[:, 128:224]), reads=["p0t"], writes=["badaT"])
            cTv = cT[:].rearrange("p (s j) -> p j s", j=16)
            for cb in range(96):
                wt = wa[cb % 2]
                fw.dma("sp", lambda h, wt=wt, cb=cb: h.dma_start(
                    out=wt[:], in_=w_ada[:, cb * 128:(cb + 1) * 128].rearrange("(j p) c -> p j c", p=128)),
                    writes=[("wa", cb % 2)])

                def mm(h, wt=wt, cb=cb):
                    for j in range(16):
                        ins = h.matmul(pm[cb % 2][:, :], lhsT=wt[:, j, :], rhs=cTv[:, j, :], start=(j == 0), stop=(j == 15))
                    return ins
                fw.op("pe", mm, reads=[("wa", cb % 2), "cT"], writes=[("p0m", cb % 2)])
                fw.op("dve", lambda h, cb=cb: h.tensor_scalar(out=modT[:, cb, :], in0=pm[cb % 2][:, :], scalar1=badaT[:, cb:cb + 1],
                                                              scalar2=None, op0=ALU.add),
                      reads=[("p0m", cb % 2), "badaT"], writes=["modT"])

            def nv(v):
                return nrm[:, v * 16:(v + 1) * 16].unsqueeze(2).to_broadcast([128, 16, 4])
            fw.op("dve", lambda h: h.scalar_tensor_tensor(out=A1[:], in0=modT[:, 16:32, :], scalar=1.0, in1=nv(0), op0=ALU.add, op1=ALU.mult),
                  reads=["modT", "nrm"], writes=["A1"])
            fw.op("dve", lambda h: h.tensor_tensor(out=G1[:], in0=modT[:, 32:48, :], in1=nv(1), op=ALU.mult), reads=["modT", "nrm"], writes=["G1"])
            fw.op("dve", lambda h: h.scalar_tensor_tensor(out=A2[:], in0=modT[:, 64:80, :], scalar=1.0, in1=nv(2), op0=ALU.add, op1=ALU.mult),
                  reads=["modT", "nrm"], writes=["A2"])
            fw.op("dve", lambda h: h.tensor_tensor(out=G2[:], in0=modT[:, 80:96, :], in1=nv(3), op=ALU.mult), reads=["modT", "nrm"], writes=["G2"])
            dump(fw, "modT", modT[:], ["modT"])
            dump(fw, "A1", A1[:], ["A1"])
            dump(fw, "nrm", nrm[:], ["nrm"])
            fw.barrier()

        def front_end(st_bufs, t0, seg):
            xin, xT, sq, rstd, pxt, pss = st_bufs
            for q in range(4):
                fw.dma("sp", lambda h, q=q: h.dma_start(out=xin[q][:], in_=x[t0 + q * 128:t0 + (q + 1) * 128, :]), writes=[("xin", q)])
            for j in range(16):
                pb = pxt[j % 2]

                def tr(h, j=j, pb=pb):
                    for q in range(4):
                        ins = h.transpose(out=pb[:, q * 128:(q + 1) * 128], in_=xin[q][:, j * 128:(j + 1) * 128], identity=ident)
                    return ins
                fw.op("pe", tr, reads=[("xin", q) for q in range(4)] + ["cst"], writes=[("pxt", j % 2)])
                fw.op("act", lambda h, j=j, pb=pb: h.activation(out=xT[:, j, :], in_=pb[:, :], func=AF.Copy), reads=[("pxt", j % 2)], writes=[("xT", j)])
                fw.op("dve", lambda h, j=j, pb=pb: h.tensor_tensor(out=sq[j % 2][:], in0=pb[:, :], in1=xT[:, j, :], op=ALU.mult),
                      reads=[("pxt", j % 2), ("xT", j)], writes=[("sq", j % 2)])
                fw.op("pe", lambda h, j=j: h.matmul(pss[:, :], lhsT=meanm, rhs=sq[j % 2][:], start=(j == 0), stop=(j == 15)),
                      reads=[("sq", j % 2), "cst"], writes=["pss"])
            fw.op("act", lambda h: h.activation(out=rstd[:], in_=pss[:, :], func=AF.Sqrt, bias=EPS, scale=1.0), reads=["pss"], writes=["rstd"])
            fw.op("dve", lambda h: h.reciprocal(out=rstd[:], in_=rstd[:]), reads=["rstd"], writes=["rstd"])

        def sumsq_rstd(src_fn, nblk, sq, pss, rstd, src_keys, scale_mean=True):
            for j in range(nblk):
                fw.op("pool", lambda h, j=j: h.tensor_tensor(out=sq[j % 2][:], in0=src_fn(j), in1=src_fn(j), op=ALU.mult),
                      reads=[src_keys(j)], writes=[("sq", j % 2)])
                fw.op("pe", lambda h, j=j: h.matmul(pss[:, :], lhsT=meanm, rhs=sq[j % 2][:], start=(j == 0), stop=(j == nblk - 1)),
                      reads=[("sq", j % 2), "cst"], writes=["pss"])
            fw.op("act", lambda h: h.activation(out=rstd[:], in_=pss[:, :], func=AF.Sqrt, bias=EPS, scale=1.0), reads=["pss"], writes=["rstd"])
            fw.op("dve", lambda h: h.reciprocal(out=rstd[:], in_=rstd[:]), reads=["rstd"], writes=["rstd"])

        DNQ0, Z0, AB0, ATQ0, ATK0, ATV0 = 0, 3072, 4096, 4128, 4128 + 1536, 4128 + 3072
        DIL = (1, 4, 16)
        MC = [T // d_ for d_ in DIL]
        MS = [SEG // d_ for d_ in DIL]
        CL = [m_ + 128 for m_ in MC]
        qkvpre = DR("qkvpre", [3072, T + 4])
        szT = DR("szT", [1024, T])
        ab_tm = DR("ab_tm", [T, 32])
        aqT = [DR(f"aqT{g}", [512, DIL[g] * CL[g]], BF16) for g in range(3)]
        akT = [DR(f"akT{g}", [512, DIL[g] * CL[g]], BF16) for g in range(3)]
        avs = [DR(f"av{g}", [DIL[g] * CL[g], 512], BF16) for g in range(3)]
        qn = DR("qn", [1024, T]); kn = DR("kn", [1024, T])
        k_tm = DR("k_tm", [T, 1024]); v_tm = DR("v_tm", [T, 1024])
        GB = DR("GB", [2, 2, 64, 8, 128])
        o_dir = DR("o_dir", [2, T, 1024])
        Oat = DR("Oat", [3, T, 4, 130])
        wb_fm = DR("wb_fm", [56, 128, 16, 128], BF16)
        wb_v = DR("wb_v", [3, 128, 16, 512], BF16)
        wb_ab = DR("wb_ab", [128, 16, 32], BF16)
        for b in range(56):
            c0 = (DNQ0 + 128 * b) if b < 24 else (Z0 + 128 * (b - 24)) if b < 32 else (ATQ0 + 128 * (b - 32)) if b < 44 else (ATK0 + 128 * (b - 44))
            cast(wb_fm[b], w_in[:, c0:c0 + 128].rearrange("(j p) c -> p j c", p=128), ("wb_fm", b))
        for g in range(3):
            cast(wb_v[g], w_in[:, ATV0 + 512 * g:ATV0 + 512 * (g + 1)].rearrange("(j p) c -> p j c", p=128), ("wb_v", g))
        cast(wb_ab[:, :, :], w_in[:, AB0:AB0 + 32].rearrange("(j p) c -> p j c", p=128), "wb_ab")

        convT = SB(es, "convT", [128, 120])
        dnwT = SB(es, "dnwT", [128, 1])
        dtb16 = SB(es, "dtb16", [128, 16])
        nA16 = SB(es, "nA16", [128, 16])
        Gtm = SB(es, "Gtm", [128, 64, 16])
        Btm = SB(es, "Btm", [128, 64, 16])
        ones = cst[:, C_ONE:C_ONE + 128]
        with ExitStack() as st:
            stg = SB(st, "stgc", [128, 128])
            pt = PSM(st, "p1t", [128, 512])
            fw.dma("sp", lambda h: h.dma_start(out=stg[0:120, :], in_=conv_w[:, :]), writes=["stgc"])
            fw.op("pe", lambda h: h.transpose(out=pt[:, 0:120], in_=stg[0:120, :], identity=ident[0:120, 0:120]), reads=["stgc", "cst"], writes=["p1t"])
            fw.op("dve", lambda h: h.tensor_copy(out=convT[:], in_=pt[:, 0:120]), reads=["p1t"], writes=["convT"])
            fw.dma("sp", lambda h: h.dma_start(out=stg[0:1, :], in_=dnw[:, :]), writes=["stgc"])
            fw.op("pe", lambda h: h.transpose(out=pt[:, 128:129], in_=stg[0:1, :], identity=ident[0:1, 0:1]), reads=["stgc", "cst"], writes=["p1t"])
            fw.op("dve", lambda h: h.tensor_copy(out=dnwT[:], in_=pt[:, 128:129]), reads=["p1t"], writes=["dnwT"])
            fw.dma("sp", lambda h: h.dma_start(out=dtb16[:], in_=dtb.partition_broadcast(128)), writes=["dtb16"])
            fw.dma("sp", lambda h: h.dma_start(out=nA16[:], in_=alog.partition_broadcast(128)), writes=["nA16"])
            fw.op("act", lambda h: h.activation(out=nA16[:], in_=nA16[:], func=AF.Exp), reads=["nA16"], writes=["nA16"])
            fw.op("dve", lambda h: h.tensor_scalar(out=nA16[:], in0=nA16[:], scalar1=-1.0, scalar2=None, op0=ALU.mult), reads=["nA16"], writes=["nA16"])
            zb = SB(st, "zb", [128, 16, 512], BF16)
            fw.op("pool", lambda h: h.memset(zb[:], 0.0), writes=["zb"])
            for g in range(3):
                dil = DIL[g]
                for hh in range(4):
                    kv = akT[g][hh * 128:(hh + 1) * 128, :].rearrange("p (r c) -> p r c", c=CL[g])
                    fw.dma("sp", lambda h, kv=kv, dil=dil: h.dma_start(out=kv[:, :, 0:64], in_=zb[:, 0:dil, 0:64]), reads=["zb"], writes=[("akT", g)])
                    fw.dma("sp", lambda h, kv=kv, dil=dil, g=g: h.dma_start(out=kv[:, :, 64 + MC[g]:128 + MC[g]], in_=zb[:, 0:dil, 0:64]), reads=["zb"], writes=[("akT", g)])
                vv = avs[g].rearrange("(r c) d -> c r d", c=CL[g])
                fw.dma("sp", lambda h, vv=vv, dil=dil: h.dma_start(out=vv[0:64, :, :], in_=zb[0:64, 0:dil, :]), reads=["zb"], writes=[("av", g)])
                fw.dma("sp", lambda h, vv=vv, dil=dil, g=g: h.dma_start(out=vv[64 + MC[g]:128 + MC[g], :, :], in_=zb[0:64, 0:dil, :]), reads=["zb"], writes=[("av", g)])
            fw.barrier()

        fw.phase = "P1"
        with ExitStack() as st:
            xin = [SB(st, f"xin{q}", [128, D]) for q in range(4)]
            hTs = SB(st, "hTs", [128, 16, SEG], BF16)
            ssq = SB(st, "ssq", [128, 4])
            stgf = [SB(st, f"stgf{i}", [128, SEG]) for i in range(2)]
            stgb = [SB(st, f"stgb{i}", [128, SEG], BF16) for i in range(2)]
            wk = [SB(st, f"wk{i}", [128, 16, 128], BF16) for i in range(3)]
            wv = SB(st, "wv", [128, 16, 512], BF16)
            wab = SB(st, "wab", [128, 16, 32], BF16)
            cosT = SB(st, "cosT", [128, 4, 512]); sinT = SB(st, "sinT", [128, 4, 512])
            tA = SB(st, "tA", [128, 512]); tB = SB(st, "tB", [128, 512]); tB2 = SB(st, "tB2", [128, 512]); tC = SB(st, "tC", [128, 512]); tD = SB(st, "tD", [128, 512])
            tI = SB(st, "tI", [128, 512], I32)
            hpi = SB(st, "hpi", [128, 1])
            vst = [SB(st, f"vst{i}", [128, 512], BF16) for i in range(2)]
            abst = SB(st, "abst", [128, 16, 32])
            pxt = [PSM(st, f"pxt{i}", [128, 512]) for i in range(2)]
            pmm = [PSM(st, f"pmm{i}", [128, 512]) for i in range(4)]
            prr = PSM(st, "prr", [128, 512])
            print("P1 sbuf remaining", nc.sbuf_bytes_remaining)
            fw.op("pool", lambda h: h.memset(hpi[:], float(np.pi / 2)), writes=["hpi"])
            fw.dma("sp", lambda h: h.dma_start(out=wab[:], in_=wb_ab[:, :, :]), reads=["wb_ab"], writes=["wab"])
            rperm = cst[:, C_RPERM:C_RPERM + 128]
            ifr = cst[:, C_IFREQ:C_IFREQ + 1]
            iota = cst[:, C_IOTA:C_IOTA + 512]
            wsl = [0]

            def trig(dst, yap):
                fw.op("dve", lambda h: h.tensor_copy(out=tI[:], in_=yap), reads=["tA"], writes=["tI"])
                fw.op("dve", lambda h: h.tensor_copy(out=tB[:], in_=tI[:]), reads=["tI"], writes=["tB"])
                fw.op("dve", lambda h: h.tensor_tensor(out=tB[:], in0=yap, in1=tB[:], op=ALU.subtract), reads=["tA", "tB"], writes=["tB"])
                fw.op("act", lambda h: h.activation(out=tC[:], in_=tB[:], func=AF.Abs), reads=["tB"], writes=["tC"])
                fw.op("act", lambda h: h.activation(out=tD[:], in_=tB[:], func=AF.Sin, scale=float(np.pi)), reads=["tB"], writes=["tD"])
                fw.op("act", lambda h: h.activation(out=tC[:], in_=tC[:], func=AF.Sin, bias=hpi[:], scale=-float(np.pi)), reads=["tC", "hpi"], writes=["tC"])
                fw.op("dve", lambda h: h.scalar_tensor_tensor(out=dst, in0=tD[:], scalar=2.0, in1=tC[:], op0=ALU.mult, op1=ALU.mult),
                      reads=["tC", "tD"], writes=["trig"])

            for s in range(NSEG):
                for tt in range(4):
                    t0 = s * SEG + tt * 512
                    for q in range(4):
                        fw.dma("sp", lambda h, q=q, t0=t0: h.dma_start(out=xin[q][:], in_=x[t0 + q * 128:t0 + (q + 1) * 128, :]), writes=[("xin", q)])
                        fw.op("act", lambda h, q=q: h.activation(out=stgb[0][:], in_=xin[q][:], func=AF.Square, accum_out=ssq[:, q:q + 1]),
                              reads=[("xin", q)], writes=[("stgb", 0), ("ssq", q)])
                        fw.op("act", lambda h, q=q: h.activation(out=ssq[:, q:q + 1], in_=ssq[:, q:q + 1], func=AF.Sqrt, bias=EPS, scale=1.0 / D),
                              reads=[("ssq", q)], writes=[("ssq", q)])
                        fw.op("dve", lambda h, q=q: h.reciprocal(out=ssq[:, q:q + 1], in_=ssq[:, q:q + 1]), reads=[("ssq", q)], writes=[("ssq", q)])
                        fw.op("dve", lambda h, q=q: h.tensor_scalar(out=xin[q][:], in0=xin[q][:], scalar1=ssq[:, q:q + 1], scalar2=None, op0=ALU.mult),
                              reads=[("xin", q), ("ssq", q)], writes=[("xin", q)])
                    for j in range(16):
                        pb = pxt[j % 2]

                        def tr(h, j=j, pb=pb):
                            for q in range(4):
                                ins = h.transpose(out=pb[:, q * 128:(q + 1) * 128], in_=xin[q][:, j * 128:(j + 1) * 128], identity=ident)
                            return ins
                        fw.op("pe", tr, reads=[("xin", q) for q in range(4)] + ["cst"], writes=[("pxt", j % 2)])
                        fw.op("dve" if j % 2 else "act",
                              (lambda h, j=j, pb=pb, tt=tt, s=s: h.tensor_scalar(out=hTs[:, j, tt * 512:(tt + 1) * 512], in0=pb[:, :], scalar1=A1[:, j, s:s + 1],
                                                                              scalar2=modT[:, j, s:s + 1], op0=ALU.mult, op1=ALU.add)) if j % 2 else
                              (lambda h, j=j, pb=pb, tt=tt, s=s: h.activation(out=hTs[:, j, tt * 512:(tt + 1) * 512], in_=pb[:, :], func=AF.Identity,
                                                                           bias=modT[:, j, s:s + 1], scale=A1[:, j, s:s + 1])),
                              reads=[("pxt", j % 2), "A1", "modT"], writes=[("hTs", j, tt)])
                hkeys = [("hTs", j, tt) for j in range(16) for tt in range(4)]

                def wload_(b):
                    i = wsl[0] % 3
                    wsl[0] += 1
                    fw.dma("sp", lambda h, i=i, b=b: h.dma_start(out=wk[i][:], in_=wb_fm[b]), reads=[("wb_fm", b)], writes=[("wk", i)])
                    return i
                plan = list(range(32)) + [32 + qk_ * 12 + g_ * 4 + hh_ for g_ in range(3) for qk_ in range(2) for hh_ in range(4)]
                wst = {"issued": 0, "used": 0, "slots": {}}

                def wload(b):
                    while wst["issued"] < len(plan) and wst["issued"] <= wst["used"] + 2:
                        k_ = wst["issued"]
                        wst["slots"][k_] = wload_(plan[k_])
                        wst["issued"] += 1
                    k_ = wst["used"]
                    assert plan[k_] == b, (plan[k_], b)
                    wst["used"] += 1
                    return wst["slots"][k_]

                def fm_mm(wi, rhs_fn, pb, out_ap=None):
                    def mm(h):
                        for j in range(16):
                            ins = h.matmul(out_ap if out_ap is not None else pmm[pb][:, :], lhsT=wk[wi][:, j, :], rhs=rhs_fn(j), start=(j == 0), stop=(j == 15))
                        return ins
                    fw.op("pe", mm, reads=[("wk", wi)] + hkeys, writes=[("pmm", pb)])

                for b in range(32):
                    wi = wload(b)
                    sf = stgf[b % 2]
                    for tt in range(4):
                        pb = (b * 4 + tt) % 4
                        fm_mm(wi, lambda j, tt=tt: hTs[:, j, tt * 512:(tt + 1) * 512], pb)
                        fw.op("act", lambda h, sf=sf, tt=tt, pb=pb, b=b: h.activation(out=sf[:, tt * 512:(tt + 1) * 512], in_=pmm[pb][:, :],
                                                                                     func=(AF.Copy if b < 24 else AF.Silu)),
                              reads=[("pmm", pb)], writes=[("stgf", b % 2)])
                    if b < 24:
                        fw.dma("sp", lambda h, sf=sf, b=b, s=s: h.dma_start(out=qkvpre[b * 128:(b + 1) * 128, 2 + s * SEG:2 + (s + 1) * SEG], in_=sf[:]),
                               reads=[("stgf", b % 2)], writes=["qkvpre"])
                    else:
                        fw.dma("sp", lambda h, sf=sf, b=b, s=s: h.dma_start(out=szT[(b - 24) * 128:(b - 23) * 128, s * SEG:(s + 1) * SEG], in_=sf[:]),
                               reads=[("stgf", b % 2)], writes=["szT"])
                def abmm(h):
                    for n in range(16):
                        for j in range(16):
                            ins = h.matmul(pmm[0][:, n * 32:(n + 1) * 32], lhsT=hTs[:, j, n * 128:(n + 1) * 128], rhs=wab[:, j, :], start=(j == 0), stop=(j == 15))
                    return ins
                fw.op("pe", abmm, reads=["wab"] + hkeys, writes=[("pmm", 0)])
                fw.op("dve", lambda h: h.tensor_copy(out=abst[:].rearrange("p n c -> p (n c)"), in_=pmm[0][:, :]), reads=[("pmm", 0)], writes=["abst"])
                fw.dma("sp", lambda h, s=s: h.dma_start(out=ab_tm[s * SEG:(s + 1) * SEG, :].rearrange("(n p) c -> p n c", p=128), in_=abst[:]),
                       reads=["abst"], writes=["ab_tm"])
                for g in range(3):
                    dil = DIL[g]
                    for tt in range(4):
                        if dil == 16:
                            for rl in range(4):
                                fw.op("dve", lambda h, rl=rl, tt=tt, s=s: h.tensor_scalar(out=tA[:, rl * 128:(rl + 1) * 128], in0=iota[:, 0:128], scalar1=16.0,
                                                                                         scalar2=sgf[:, 8 + s:9 + s], op0=ALU.mult, op1=ALU.add),
                                      reads=["cst", "sgf"], writes=["tA"])
                                fw.op("dve", lambda h, rl=rl, tt=tt: h.tensor_scalar(out=tA[:, rl * 128:(rl + 1) * 128], in0=tA[:, rl * 128:(rl + 1) * 128],
                                                                                    scalar1=float(4 * tt + rl), scalar2=ifr, op0=ALU.add, op1=ALU.mult),
                                      reads=["tA", "cst"], writes=["tA"])
                        else:
                            cadd = float(512 * tt) if dil == 1 else float(tt)
                            fw.op("dve", lambda h, s=s, dil=dil: h.tensor_scalar(out=tA[:], in0=iota, scalar1=float(dil), scalar2=sgf[:, 8 + s:9 + s],
                                                                                op0=ALU.mult, op1=ALU.add), reads=["cst", "sgf"], writes=["tA"])
                            fw.op("dve", lambda h, cadd=cadd: h.tensor_scalar(out=tA[:], in0=tA[:], scalar1=cadd, scalar2=ifr, op0=ALU.add, op1=ALU.mult),
                                  reads=["tA", "cst"], writes=["tA"])
                        trig(sinT[:, tt, :], tA[:])
                        fw.op("dve", lambda h: h.tensor_scalar(out=tA[:], in0=tA[:], scalar1=0.25, scalar2=None, op0=ALU.add), reads=["tA", "trig"], writes=["tA"])
                        trig(cosT[:, tt, :], tA[:])
                    for qk in range(2):
                        for hh in range(4):
                            b = 32 + qk * 12 + g * 4 + hh
                            wi = wload(b)
                            sb_ = stgb[(qk * 4 + hh) % 2]
                            hview = None
                            def rot_stage(tt, sb_=None):
                                tBx = tB if tt % 2 == 0 else tB2
                                kB = "tB" if tt % 2 == 0 else "tB2"
                                fw.op("pe", lambda h: h.matmul(prr[:, :], lhsT=rperm, rhs=tBx[:], start=True, stop=True), reads=[kB, "cst"], writes=["prr"])
                                fw.op("pool", lambda h: h.tensor_tensor(out=tC[:], in0=tBx[:], in1=cosT[:, tt, :], op=ALU.mult), reads=[kB, "trig"], writes=["tC"])
                                fw.op("dve", lambda h: h.tensor_tensor(out=tD[:], in0=prr[:, :], in1=sinT[:, tt, :], op=ALU.mult), reads=["prr", "trig"], writes=["tD"])
                                fw.op("pool", lambda h: h.tensor_tensor(out=sb_[:, tt * 512:(tt + 1) * 512], in0=tC[:], in1=tD[:], op=ALU.add),
                                      reads=["tC", "tD"], writes=[("stgb", (qk * 4 + hh) % 2)])

                            for tt in range(4):
                                pb = tt % 4
                                if dil == 1:
                                    rf = lambda j, tt=tt: hTs[:, j, tt * 512:(tt + 1) * 512]
                                    oap = None
                                elif dil == 4:
                                    rf = lambda j, tt=tt: hTs[:, j, :].rearrange("p (m r) -> p r m", r=4)[:, tt, :]
                                    oap = None
                                else:
                                    rf = lambda j, tt=tt: hTs[:, j, :].rearrange("p (m r) -> p r m", r=16)[:, 4 * tt:4 * tt + 4, :]
                                    oap = pmm[pb][:, :].rearrange("p (a b) -> p a b", a=4)
                                fm_mm(wi, rf, pb, oap)
                                tBx = tB if tt % 2 == 0 else tB2
                                kB = "tB" if tt % 2 == 0 else "tB2"
                                fw.op("act", lambda h: h.activation(out=tBx[:], in_=pmm[pb][:, :], func=AF.Copy, scale=(128.0 ** -0.5 if qk == 0 else 1.0)),
                                      reads=[("pmm", pb)], writes=[kB])
                                if tt >= 1:
                                    rot_stage(tt - 1, sb_)
                            rot_stage(3, sb_)
                            dst = (aqT if qk == 0 else akT)[g][hh * 128:(hh + 1) * 128, :].rearrange("p (r c) -> p r c", c=CL[g])
                            fw.dma("sp", lambda h, dst=dst, sb_=sb_, g=g, s=s, dil=dil: h.dma_start(
                                out=dst[:, :, 64 + s * MS[g]:64 + (s + 1) * MS[g]], in_=sb_[:].rearrange("p (r m) -> p r m", r=dil)),
                                reads=[("stgb", (qk * 4 + hh) % 2)], writes=[("aqk", g)])
                    fw.dma("sp", lambda h, g=g: h.dma_start(out=wv[:], in_=wb_v[g]), reads=[("wb_v", g)], writes=["wv"])
                    for ct in range(16):
                        if dil == 1:
                            lf = lambda j, ct=ct: hTs[:, j, ct * 128:(ct + 1) * 128]
                            r_, m0 = 0, ct * 128
                        elif dil == 4:
                            lf = lambda j, ct=ct: hTs[:, j, :].rearrange("p (m r) -> p r m", r=4)[:, ct // 4, (ct % 4) * 128:(ct % 4 + 1) * 128]
                            r_, m0 = ct // 4, (ct % 4) * 128
                        else:
                            lf = lambda j, ct=ct: hTs[:, j, :].rearrange("p (m r) -> p r m", r=16)[:, ct, :]
                            r_, m0 = ct, 0
                        pb = ct % 4

                        def vmm(h, lf=lf, pb=pb):
                            for j in range(16):
                                ins = h.matmul(pmm[pb][:, :], lhsT=lf(j), rhs=wv[:, j, :], start=(j == 0), stop=(j == 15))
                            return ins
                        fw.op("pe", vmm, reads=["wv"] + hkeys, writes=[("pmm", pb)])
                        vs = vst[ct % 2]
                        if ct % 2:
                            fw.op("dve", lambda h, vs=vs, pb=pb: h.tensor_copy(out=vs[:], in_=pmm[pb][:, :]), reads=[("pmm", pb)], writes=[("vst", ct % 2)])
                        else:
                            fw.op("act", lambda h, vs=vs, pb=pb: h.activation(out=vs[:], in_=pmm[pb][:, :], func=AF.Copy), reads=[("pmm", pb)], writes=[("vst", ct % 2)])
                        row0 = r_ * CL[g] + 64 + s * MS[g] + m0
                        fw.dma("sp", lambda h, vs=vs, row0=row0, g=g: h.dma_start(out=avs[g][row0:row0 + 128, :], in_=vs[:]),
                               reads=[("vst", ct % 2)], writes=[("av", g)])
            fw.barrier()

        fw.phase = "P2"
        with ExitStack() as st:
            xc = [SB(st, f"xc{i}", [128, 516]) for i in range(2)]
            ca_l = [SB(st, f"ca{i}", [128, 512]) for i in range(2)]; cs_l = [SB(st, f"cs{i}", [128, 512]) for i in range(2)]
            cq_l = [SB(st, f"cq{i}", [128, 512]) for i in range(2)]; cr_l = [SB(st, f"cr{i}", [128, 512]) for i in range(2)]
            cn = [SB(st, f"cn{i}", [128, 512]) for i in range(2)]
            ctm = [SB(st, f"ctm{i}", [128, 4, 128]) for i in range(2)]
            pss2_l = [PSM(st, f"pss2{i}", [128, 512]) for i in range(2)]
            ptr = [PSM(st, f"ptr{i}", [128, 512]) for i in range(2)]
            def p2_load(it):
                cb, tile = it // 16, it % 16
                t0 = tile * 512
                xw = xc[it % 2]
                fw.dma("sp", lambda h: h.dma_start(out=xw[:], in_=qkvpre[cb * 128:(cb + 1) * 128, t0:t0 + 516]), reads=["qkvpre"], writes=[("xc", it % 2)])

            p2_load(0)
            for it in range(24 * 16):
                if True:
                    cb, tile = it // 16, it % 16
                    if it + 1 < 24 * 16:
                        p2_load(it + 1)
                    t0 = tile * 512
                    s = tile // 4
                    xw = xc[it % 2]
                    kx = ("xc", it % 2)
                    pss2 = pss2_l[it % 2]
                    ca, cs_, cq, cr = ca_l[it % 2], cs_l[it % 2], cq_l[it % 2], cr_l[it % 2]
                    kca, kcs, kcq, kcr = ("ca", it % 2), ("cs_", it % 2), ("cq", it % 2), ("cr", it % 2)
                    if tile == 0:
                        fw.op("pool", lambda h, xw=xw: h.memset(xw[:, 0:2], 0.0), writes=[kx])
                    elif tile % 4 == 0:
                        fw.op("pool", lambda h, xw=xw, s=s: h.tensor_scalar(out=xw[:, 0:2], in0=xw[:, 0:2], scalar1=sgf[:, s:s + 1], scalar2=None, op0=ALU.mult),
                              reads=[kx, "sgf"], writes=[kx])
                    if tile == 15:
                        fw.op("pool", lambda h, xw=xw: h.memset(xw[:, 514:516], 0.0), writes=[kx])
                    elif tile % 4 == 3:
                        fw.op("pool", lambda h, xw=xw, s=s: h.tensor_scalar(out=xw[:, 514:516], in0=xw[:, 514:516], scalar1=sgf[:, 4 + s:5 + s], scalar2=None, op0=ALU.mult),
                              reads=[kx, "sgf"], writes=[kx])
                    eng = "dve"
                    fw.op(eng, lambda h, xw=xw, cb=cb: h.tensor_scalar(out=ca[:], in0=xw[:, 0:512], scalar1=convT[:, cb:cb + 1], scalar2=None, op0=ALU.mult),
                          reads=[kx, "convT"], writes=[kca])
                    for k in range(1, 5):
                        fw.op(eng, lambda h, xw=xw, cb=cb, k=k: h.scalar_tensor_tensor(out=ca[:], in0=xw[:, k:k + 512], scalar=convT[:, k * 24 + cb:k * 24 + cb + 1],
                                                                                      in1=ca[:], op0=ALU.mult, op1=ALU.add),
                              reads=[kx, "convT", kca], writes=[kca])
                    fw.op("act", lambda h: h.activation(out=cs_[:], in_=ca[:], func=AF.Silu), reads=[kca], writes=[kcs])
                    res_t = cs_
                    res_k = kcs
                    if cb < 16:
                        fw.op("pool", lambda h: h.tensor_tensor(out=cq[:], in0=cs_[:], in1=cs_[:], op=ALU.mult), reads=[kcs], writes=[kcq])
                        fw.op("pe", lambda h: h.matmul(pss2[:, :], lhsT=ones, rhs=cq[:], start=True, stop=True), reads=[kcq, "cst"], writes=[("pss2", it % 2)])
                        if cb < 8:
                            fw.op("act", lambda h: h.activation(out=cr[:], in_=pss2[:, :], func=AF.Sqrt, bias=128.0 * EPS, scale=128.0), reads=[("pss2", it % 2)], writes=[kcr])
                        else:
                            fw.op("act", lambda h: h.activation(out=cr[:], in_=pss2[:, :], func=AF.Sqrt, bias=EPS, scale=1.0), reads=[("pss2", it % 2)], writes=[kcr])
                        fw.op("dve", lambda h: h.reciprocal(out=cr[:], in_=cr[:]), reads=[kcr], writes=[kcr])
                        cnb = cn[it % 2]
                        fw.op("pool", lambda h, cnb=cnb: h.tensor_tensor(out=cnb[:], in0=cs_[:], in1=cr[:], op=ALU.mult), reads=[kcs, kcr], writes=[("cn", it % 2)])
                        res_t, res_k = cnb, ("cn", it % 2)
                        dstn = qn if cb < 8 else kn
                        fw.dma("sp", lambda h, cnb=cnb, dstn=dstn, cb=cb, t0=t0: h.dma_start(out=dstn[(cb % 8) * 128:(cb % 8 + 1) * 128, t0:t0 + 512], in_=cnb[:]),
                               reads=[res_k], writes=["qkn"])
                    if cb >= 8:
                        pb = ptr[it % 2]

                        def tr(h, res_t=res_t, pb=pb):
                            for q in range(4):
                                ins = h.transpose(out=pb[:, q * 128:(q + 1) * 128], in_=res_t[:, q * 128:(q + 1) * 128], identity=ident)
                            return ins
                        fw.op("pe", tr, reads=[res_k, "cst"], writes=[("ptr", it % 2)])
                        cm = ctm[it % 2]
                        fw.op("act", lambda h, cm=cm, pb=pb: h.activation(out=cm[:].rearrange("p q c -> p (q c)"), in_=pb[:, :], func=AF.Copy),
                              reads=[("ptr", it % 2)], writes=[("ctm", it % 2)])
                        dstt = k_tm if cb < 16 else v_tm
                        fw.dma("sp", lambda h, cm=cm, dstt=dstt, cb=cb, t0=t0: h.dma_start(
                            out=dstt[t0:t0 + 512, (cb % 8) * 128:(cb % 8 + 1) * 128].rearrange("(q p) c -> p q c", p=128), in_=cm[:]),
                            reads=[("ctm", it % 2)], writes=["kv_tm"])
            fw.barrier()

        fw.phase = "P3"
        with ExitStack() as st:
            ab = SB(st, "ab", [128, 64, 32])
            g_ = SB(st, "g_", [128, 64, 16]); t1_ = SB(st, "t1_", [128, 64, 16]); t2_ = SB(st, "t2_", [128, 64, 16])
            rows = [SB(st, f"rows{i}", [8, 8, 4, 128]) for i in range(2)]
            pg = PSM(st, "pg", [128, 1024])
            pr = [PSM(st, f"pr{i}", [8, 512]) for i in range(2)]
            fw.dma("sp", lambda h: h.dma_start(out=ab[:], in_=ab_tm.rearrange("(n p) c -> p n c", p=128)), reads=["ab_tm"], writes=["ab"])
            bc = lambda t: t[:].unsqueeze(1).to_broadcast([128, 64, 16])
            fw.op("dve", lambda h: h.tensor_tensor(out=t1_[:], in0=ab[:, :, 0:16], in1=bc(dtb16), op=ALU.add), reads=["ab", "dtb16"], writes=["t1_"])
            fw.op("act", lambda h: h.activation(out=t2_[:], in_=t1_[:], func=AF.Abs), reads=["t1_"], writes=["t2_"])
            fw.op("act", lambda h: h.activation(out=t2_[:], in_=t2_[:], func=AF.Exp, scale=-1.0), reads=["t2_"], writes=["t2_"])
            fw.op("act", lambda h: h.activation(out=t2_[:], in_=t2_[:], func=AF.Ln, bias=1.0, scale=1.0), reads=["t2_"], writes=["t2_"])
            fw.op("dve", lambda h: h.scalar_tensor_tensor(out=t1_[:], in0=t1_[:], scalar=0.0, in1=t2_[:], op0=ALU.max, op1=ALU.add), reads=["t1_", "t2_"], writes=["t1_"])
            fw.op("dve", lambda h: h.tensor_tensor(out=g_[:], in0=t1_[:], in1=bc(nA16), op=ALU.mult), reads=["t1_", "nA16"], writes=["g_"])
            fw.op("act", lambda h: h.activation(out=Btm[:], in_=ab[:, :, 16:32], func=AF.Sigmoid), reads=["ab"], writes=["Btm"])
            mlow = cst[:, C_MLOW:C_MLOW + 128]; mup = cst[:, C_MUP:C_MUP + 128]

            def gmm(h):
                for c in range(64):
                    h.matmul(pg[:, c * 16:c * 16 + 8], lhsT=mlow, rhs=g_[:, c, 0:8], start=True, stop=True)
                    ins = h.matmul(pg[:, c * 16 + 8:c * 16 + 16], lhsT=mup, rhs=g_[:, c, 8:16], start=True, stop=True)
                return ins
            fw.op("pe", gmm, reads=["g_", "cst"], writes=["pg"])
            fw.op("dve", lambda h: h.tensor_copy(out=Gtm[:].rearrange("p c k -> p (c k)"), in_=pg[:, :]), reads=["pg"], writes=["Gtm"])
            for c in range(64):
                pp = pr[c % 2]
                rw = rows[(c // 8) % 2]

                def rmm(h, c=c, pp=pp):
                    h.matmul(pp[:, 0:128], lhsT=g_[:, c, 0:8], rhs=mlow, start=True, stop=True)
                    h.matmul(pp[:, 128:256], lhsT=g_[:, c, 8:16], rhs=mup, start=True, stop=True)
                    h.matmul(pp[:, 256:384], lhsT=Btm[:, c, 0:8], rhs=ident, start=True, stop=True)
                    return h.matmul(pp[:, 384:512], lhsT=Btm[:, c, 8:16], rhs=ident, start=True, stop=True)
                fw.op("pe", rmm, reads=["g_", "Btm", "cst"], writes=[("pr", c % 2)])
                fw.op("dve", lambda h, c=c, pp=pp, rw=rw: h.tensor_copy(out=rw[:, c % 8, :, :].rearrange("p k t -> p (k t)"), in_=pp[:, :]),
                      reads=[("pr", c % 2)], writes=[("rows", (c // 8) % 2)])
                if c % 8 == 7:
                    c0 = c - 7
                    for k4 in range(4):
                        dr_, kd_ = k4 % 2, k4 // 2
                        fw.dma("sp", lambda h, rw=rw, c0=c0, k4=k4, dr_=dr_, kd_=kd_: h.dma_start(
                            out=GB[dr_, kd_, c0:c0 + 8, :, :].rearrange("c h t -> h c t"), in_=rw[:, :, k4, :]),
                            reads=[("rows", (c // 8) % 2)], writes=["GB"])
            fw.barrier()

        fw.phase = "P4"
        with ExitStack() as st:
            ld_q = [SB(st, f"ldq{i}", [128, 8, 128]) for i in range(2)]
            ld_k = [SB(st, f"ldk{i}", [128, 8, 128]) for i in range(2)]
            ld_kt = [SB(st, f"ldkt{i}", [128, 8, 128]) for i in range(2)]
            ld_vt = [SB(st, f"ldvt{i}", [128, 8, 128]) for i in range(2)]
            ld_g = [SB(st, f"ldg{i}", [128, 8, 128]) for i in range(2)]
            ld_b = [SB(st, f"ldb{i}", [128, 8, 128]) for i in range(2)]
            f1 = SB(st, "f1", [128, 8, 128]); f2 = SB(st, "f2", [128, 8, 128]); f3 = SB(st, "f3", [128, 8, 128]); f4 = SB(st, "f4", [128, 8, 128])
            f5 = SB(st, "f5", [128, 8, 128]); f6 = SB(st, "f6", [128, 8, 128])
            kbf = SB(st, "kbf", [128, 8, 128], BF16); qbf = SB(st, "qbf", [128, 8, 128], BF16)
            P1b = SB(st, "P1b", [128, 8, 128], BF16); Q1b = SB(st, "Q1b", [128, 8, 128], BF16)
            Dm = SB(st, "Dm", [128, 8, 128], BF16); Em = SB(st, "Em", [128, 8, 128], BF16)
            Xb = SB(st, "Xb", [128, 8, 128], BF16); X2b = SB(st, "X2b", [128, 8, 128], BF16)
            lvl = SB(st, "lvl", [128, 14 * 128])
            fw.dma("sp", lambda h: h.dma_start(out=lvl[:], in_=lvlmask[:, :]), writes=["lvl"])
            qkm = SB(st, "qkm", [128, 8, 128], BF16); qgT = SB(st, "qgT", [128, 8, 128], BF16)
            kbg = SB(st, "kbg", [128, 8, 128], BF16); vb = SB(st, "vb", [128, 8, 128], BF16); kd = SB(st, "kd", [128, 8, 128], BF16)
            wT = SB(st, "wT", [128, 8, 128], BF16); vnew = SB(st, "vnew", [128, 8, 128], BF16)
            u_ = SB(st, "u_", [128, 8, 128]); S_ = SB(st, "S_", [128, 8, 128]); Sbf = SB(st, "Sbf", [128, 8, 128], BF16)
            ost = [SB(st, f"ost{i}", [128, 8, 128]) for i in range(2)]
            sm = SB(st, "sm", [128, 8, 4])
            PA = PSM(st, "PA", [128, 1024]); PB = PSM(st, "PB", [128, 1024]); PC = PSM(st, "PC", [128, 1024]); PD = PSM(st, "PD", [128, 1024])
            print("P4 sbuf remaining", nc.sbuf_bytes_remaining)
            v3 = lambda t: t[:, :].rearrange("p (h c) -> p h c", h=8)
            maskc = lambda off: cst[:, off:off + 128].unsqueeze(1).to_broadcast([128, 8, 128])
            identb = cst[:, C_ID:C_ID + 128].unsqueeze(1).to_broadcast([128, 8, 128])
            def p4_load(dr_, c, b2):
                t0 = c * 128
                lq, lk, lkt, lvt, lg, lb = ld_q[b2], ld_k[b2], ld_kt[b2], ld_vt[b2], ld_g[b2], ld_b[b2]
                K = lambda n: (n, b2)
                fw.dma("sp", lambda h, lq=lq, t0=t0: h.dma_start(out=lq[:], in_=qn[:, t0:t0 + 128].rearrange("(h p) t -> p h t", p=128)), reads=["qkn"], writes=[K("ldq")])
                fw.dma("sp", lambda h, lk=lk, t0=t0: h.dma_start(out=lk[:], in_=kn[:, t0:t0 + 128].rearrange("(h p) t -> p h t", p=128)), reads=["qkn"], writes=[K("ldk")])
                fw.dma("sp", lambda h, lkt=lkt, t0=t0: h.dma_start(out=lkt[:], in_=k_tm[t0:t0 + 128, :].rearrange("p (h c) -> p h c", h=8)), reads=["kv_tm"], writes=[K("ldkt")])
                fw.dma("sp", lambda h, lvt=lvt, t0=t0: h.dma_start(out=lvt[:], in_=v_tm[t0:t0 + 128, :].rearrange("p (h c) -> p h c", h=8)), reads=["kv_tm"], writes=[K("ldvt")])
                fw.dma("sp", lambda h, lg=lg, c=c, dr_=dr_: h.dma_start(out=lg[:].rearrange("p h t -> p (h t)"),
                                                                      in_=GB[dr_, 0, c:c + 1, :, :].rearrange("c h t -> c (h t)").partition_broadcast(128)),
                       reads=["GB"], writes=[K("ldg")])
                fw.dma("sp", lambda h, lb=lb, c=c, dr_=dr_: h.dma_start(out=lb[:].rearrange("p h t -> p (h t)"),
                                                                      in_=GB[dr_, 1, c:c + 1, :, :].rearrange("c h t -> c (h t)").partition_broadcast(128)),
                       reads=["GB"], writes=[K("ldb")])

            seq_all = [(0, c) for c in range(64)] + [(1, c) for c in range(63, -1, -1)]
            p4_load(0, 0, 0)
            it = 0
            for dr_ in range(2):
                fw.op("pool", lambda h: h.memset(S_[:], 0.0), writes=[("S_", 0), ("S_", 1)])
                fw.op("pool", lambda h: h.memset(Sbf[:], 0.0), writes=[("Sbf", 0), ("Sbf", 1)])
                order = list(range(64)) if dr_ == 0 else list(range(63, -1, -1))
                m_p1 = C_GT if dr_ == 0 else C_LT
                m_q1 = C_LT if dr_ == 0 else C_GT
                m_qk = C_LE if dr_ == 0 else C_GE
                last = 127 if dr_ == 0 else 0
                for c in order:
                    b2 = it % 2
                    it += 1
                    if it < 128:
                        p4_load(seq_all[it][0], seq_all[it][1], it % 2)
                    t0 = c * 128
                    lq, lk, lkt, lvt, lg, lb = ld_q[b2], ld_k[b2], ld_kt[b2], ld_vt[b2], ld_g[b2], ld_b[b2]
                    K = lambda n: (n, b2)
                    Gp = Gtm[:, c, dr_ * 8:dr_ * 8 + 8]
                    Bp = Btm[:, c, dr_ * 8:dr_ * 8 + 8]
                    Gpb = Gp.unsqueeze(2).to_broadcast([128, 8, 128])
                    Bpb = Bp.unsqueeze(2).to_broadcast([128, 8, 128])
                    HV = (slice(0, 4), slice(4, 8))
                    lm = lambda k, up: lvl[:, (2 * k + up) * 128:(2 * k + up + 1) * 128].unsqueeze(1).to_broadcast([128, 4, 128])
                    mk4 = lambda off: cst[:, off:off + 128].unsqueeze(1).to_broadcast([128, 4, 128])
                    id4 = cst[:, C_ID:C_ID + 128].unsqueeze(1).to_broadcast([128, 4, 128])
                    dsel = 0 if dr_ == 0 else 1
                    seg = c // 16

                    def H(n, hf):
                        return (n, hf)

                    def pv(PX, hf):
                        return v3(PX)[:, HV[hf], :]

                    def both(fn):
                        for hf in range(2):
                            fn(hf, HV[hf])

                    def s_cast(hf, hs):
                        fw.op("act", lambda h: h.activation(out=kbf[:, hs, :], in_=lk[:, hs, :], func=AF.Copy), reads=[K("ldk")], writes=[H("kbf", hf)])
                        fw.op("act", lambda h: h.activation(out=qbf[:, hs, :], in_=lq[:, hs, :], func=AF.Copy), reads=[K("ldq")], writes=[H("qbf", hf)])

                        def kk(h):
                            for hd in range(hs.start, hs.stop):
                                h.matmul(PA[:, hd * 128:(hd + 1) * 128], lhsT=kbf[:, hd, :], rhs=kbf[:, hd, :], start=True, stop=True)
                                ins = h.matmul(PB[:, hd * 128:(hd + 1) * 128], lhsT=kbf[:, hd, :], rhs=qbf[:, hd, :], start=True, stop=True)
                            return ins
                        fw.op("pe", kk, reads=[H("kbf", hf), H("qbf", hf)], writes=[H("PA", hf), H("PB", hf)])
                    both(s_cast)

                    def s_dec1(hf, hs):
                        Gpb4 = Gp[:, hs].unsqueeze(2).to_broadcast([128, 4, 128])
                        fw.op("dve", lambda h: h.tensor_tensor(out=f1[:, hs, :], in0=lg[:, hs, :], in1=Gpb4, op=ALU.subtract), reads=[K("ldg"), "Gtm"], writes=[H("f1", hf)])
                        fw.op("pool", lambda h: h.tensor_scalar(out=f2[:, hs, :], in0=f1[:, hs, :], scalar1=0.0, scalar2=None, op0=ALU.max), reads=[H("f1", hf)], writes=[H("f2", hf)])
                        fw.op("act", lambda h: h.activation(out=f2[:, hs, :], in_=f2[:, hs, :], func=AF.Exp, scale=-1.0), reads=[H("f2", hf)], writes=[H("f2", hf)])
                        fw.op("pool", lambda h: h.tensor_scalar(out=f3[:, hs, :], in0=f1[:, hs, :], scalar1=0.0, scalar2=None, op0=ALU.min), reads=[H("f1", hf)], writes=[H("f3", hf)])
                        fw.op("act", lambda h: h.activation(out=f3[:, hs, :], in_=f3[:, hs, :], func=AF.Exp), reads=[H("f3", hf)], writes=[H("f3", hf)])
                    both(s_dec1)

                    def s_dec2(hf, hs):
                        Bpb4 = Bp[:, hs].unsqueeze(2).to_broadcast([128, 4, 128])
                        fw.op("pool", lambda h: h.tensor_tensor(out=f2[:, hs, :], in0=f2[:, hs, :], in1=mk4(m_p1), op=ALU.mult), reads=[H("f2", hf), "cst"], writes=[H("f2", hf)])
                        fw.op("dve", lambda h: h.scalar_tensor_tensor(out=f2[:, hs, :], in0=f2[:, hs, :], scalar=-1.0, in1=Bpb4, op0=ALU.mult, op1=ALU.mult),
                              reads=[H("f2", hf), "Btm"], writes=[H("f2", hf)])
                        fw.op("dve", lambda h: h.tensor_tensor(out=P1b[:, hs, :], in0=pv(PA, hf), in1=f2[:, hs, :], op=ALU.mult), reads=[H("PA", hf), H("f2", hf)], writes=[H("P1b", hf)])
                        fw.op("dve", lambda h: h.scalar_tensor_tensor(out=f4[:, hs, :], in0=f3[:, hs, :], scalar=-1.0, in1=lb[:, hs, :], op0=ALU.mult, op1=ALU.mult),
                              reads=[H("f3", hf), K("ldb")], writes=[H("f4", hf)])
                        fw.op("pool", lambda h: h.tensor_tensor(out=f4[:, hs, :], in0=f4[:, hs, :], in1=mk4(m_q1), op=ALU.mult), reads=[H("f4", hf), "cst"], writes=[H("f4", hf)])
                        fw.op("dve", lambda h: h.tensor_tensor(out=Q1b[:, hs, :], in0=pv(PA, hf), in1=f4[:, hs, :], op=ALU.mult), reads=[H("PA", hf), H("f4", hf)], writes=[H("Q1b", hf)])
                        fw.op("pool", lambda h: h.tensor_tensor(out=f3[:, hs, :], in0=f3[:, hs, :], in1=mk4(m_qk), op=ALU.mult), reads=[H("f3", hf), "cst", H("f4", hf)], writes=[H("f3", hf)])
                        fw.op("dve", lambda h: h.tensor_tensor(out=qkm[:, hs, :], in0=pv(PB, hf), in1=f3[:, hs, :], op=ALU.mult), reads=[H("PB", hf), H("f3", hf)], writes=[H("qkm", hf)])
                    both(s_dec2)

                    def s_lvl0(hf, hs):
                        fw.op("pool", lambda h: h.tensor_tensor(out=Dm[:, hs, :], in0=P1b[:, hs, :], in1=lm(0, dsel), op=ALU.mult), reads=[H("P1b", hf), "lvl"], writes=[H("Dm", hf)])
                        fw.op("pool", lambda h: h.tensor_tensor(out=Dm[:, hs, :], in0=Dm[:, hs, :], in1=id4, op=ALU.add), reads=[H("Dm", hf), "cst"], writes=[H("Dm", hf)])
                        fw.op("pool", lambda h: h.tensor_tensor(out=Em[:, hs, :], in0=Q1b[:, hs, :], in1=lm(0, 1 - dsel), op=ALU.mult), reads=[H("Q1b", hf), "lvl"], writes=[H("Em", hf)])
                        fw.op("pool", lambda h: h.tensor_tensor(out=Em[:, hs, :], in0=Em[:, hs, :], in1=id4, op=ALU.add), reads=[H("Em", hf), "cst"], writes=[H("Em", hf)])
                    both(s_lvl0)

                    fw.op("act", lambda h: h.activation(out=f5[:], in_=lg[:], func=AF.Exp), reads=[K("ldg")], writes=["f5"])
                    fw.op("pool", lambda h: h.tensor_tensor(out=qgT[:], in0=lq[:], in1=f5[:], op=ALU.mult), reads=[K("ldq"), "f5"], writes=[H("qgT", 0), H("qgT", 1)])
                    fw.op("act", lambda h: h.activation(out=sm[:, :, 0], in_=Gp, func=AF.Exp), reads=["Gtm"], writes=["sm0"])
                    fw.op("dve", lambda h: h.tensor_tensor(out=sm[:, :, 0], in0=sm[:, :, 0], in1=Bp, op=ALU.mult), reads=["sm0", "Btm"], writes=["sm0"])
                    fw.op("dve", lambda h: h.tensor_tensor(out=sm[:, :, 1], in0=lg[:, :, last], in1=Gp, op=ALU.subtract), reads=[K("ldg"), "Gtm"], writes=["sm1"])
                    fw.op("act", lambda h: h.activation(out=sm[:, :, 1], in_=sm[:, :, 1], func=AF.Exp), reads=["sm1"], writes=["sm1"])
                    fw.op("act", lambda h: h.activation(out=sm[:, :, 2], in_=lg[:, :, last], func=AF.Exp), reads=[K("ldg")], writes=["sm2"])
                    smb = lambda i: sm[:, :, i:i + 1].to_broadcast([128, 8, 128])
                    fw.op("pool", lambda h: h.tensor_tensor(out=kbg[:], in0=lkt[:], in1=smb(0), op=ALU.mult), reads=[K("ldkt"), "sm0"], writes=[H("kbg", 0), H("kbg", 1)])
                    fw.op("pool", lambda h: h.tensor_tensor(out=vb[:], in0=lvt[:], in1=Bpb, op=ALU.mult), reads=[K("ldvt"), "Btm"], writes=[H("vb", 0), H("vb", 1)])
                    fw.op("pool", lambda h: h.tensor_tensor(out=kd[:], in0=lkt[:], in1=smb(1), op=ALU.mult), reads=[K("ldkt"), "sm1"], writes=[H("kd", 0), H("kd", 1)])

                    for k in range(1, 7):
                        def s_x(hf, hs):
                            def xmm(h):
                                for hd in range(hs.start, hs.stop):
                                    h.matmul(PC[:, hd * 128:(hd + 1) * 128], lhsT=Q1b[:, hd, :], rhs=Dm[:, hd, :], start=True, stop=True)
                                    ins = h.matmul(PD[:, hd * 128:(hd + 1) * 128], lhsT=P1b[:, hd, :], rhs=Em[:, hd, :], start=True, stop=True)
                                return ins
                            fw.op("pe", xmm, reads=[H("Q1b", hf), H("P1b", hf), H("Dm", hf), H("Em", hf)], writes=[H("PC", hf), H("PD", hf)])
                            fw.op("act", lambda h: h.activation(out=Xb[:, hs, :], in_=pv(PC, hf), func=AF.Copy), reads=[H("PC", hf)], writes=[H("Xb", hf)])
                            fw.op("act", lambda h: h.activation(out=X2b[:, hs, :], in_=pv(PD, hf), func=AF.Copy), reads=[H("PD", hf)], writes=[H("X2b", hf)])
                        both(s_x)

                        def s_z(hf, hs):
                            def zmm(h):
                                for hd in range(hs.start, hs.stop):
                                    h.matmul(PA[:, hd * 128:(hd + 1) * 128], lhsT=Em[:, hd, :], rhs=Xb[:, hd, :], start=True, stop=True)
                                    ins = h.matmul(PB[:, hd * 128:(hd + 1) * 128], lhsT=Dm[:, hd, :], rhs=X2b[:, hd, :], start=True, stop=True)
                                return ins
                            fw.op("pe", zmm, reads=[H("Em", hf), H("Dm", hf), H("Xb", hf), H("X2b", hf)], writes=[H("PA", hf), H("PB", hf)])
                            fw.op("dve", lambda h: h.tensor_tensor(out=f1[:, hs, :], in0=pv(PA, hf), in1=lm(k, dsel), op=ALU.mult), reads=[H("PA", hf), "lvl"], writes=[H("f1", hf)])
                            fw.op("pool", lambda h: h.tensor_tensor(out=Dm[:, hs, :], in0=Dm[:, hs, :], in1=f1[:, hs, :], op=ALU.add), reads=[H("Dm", hf), H("f1", hf)], writes=[H("Dm", hf)])
                            fw.op("dve", lambda h: h.tensor_tensor(out=f6[:, hs, :], in0=pv(PB, hf), in1=lm(k, 1 - dsel), op=ALU.mult), reads=[H("PB", hf), "lvl"], writes=[H("f6", hf)])
                            fw.op("pool", lambda h: h.tensor_tensor(out=Em[:, hs, :], in0=Em[:, hs, :], in1=f6[:, hs, :], op=ALU.add), reads=[H("Em", hf), H("f6", hf)], writes=[H("Em", hf)])
                        both(s_z)

                    def s_u(hf, hs):
                        def umm(h):
                            for hd in range(hs.start, hs.stop):
                                h.matmul(PB[:, hd * 128:(hd + 1) * 128], lhsT=Em[:, hd, :], rhs=vb[:, hd, :], start=True, stop=True)
                                ins = h.matmul(PA[:, hd * 128:(hd + 1) * 128], lhsT=kbg[:, hd, :], rhs=Em[:, hd, :], start=True, stop=True)
                            return ins
                        fw.op("pe", umm, reads=[H("Em", hf), H("vb", hf), H("kbg", hf)], writes=[H("PA", hf), H("PB", hf)])
                        fw.op("act", lambda h: h.activation(out=u_[:, hs, :], in_=pv(PB, hf), func=AF.Copy), reads=[H("PB", hf)], writes=[H("u_", hf)])
                        fw.op("dve", lambda h: h.tensor_copy(out=wT[:, hs, :], in_=pv(PA, hf)), reads=[H("PA", hf)], writes=[H("wT", hf)])
                    both(s_u)

                    def s_seq(hf, hs):
                        if dr_ == 0 and c % 16 == 0 and c > 0:
                            fw.op("dve", lambda h: h.tensor_scalar(out=S_[:, hs, :], in0=S_[:, hs, :], scalar1=sgf[:, seg:seg + 1], scalar2=None, op0=ALU.mult), reads=[H("S_", hf), "sgf"], writes=[H("S_", hf)])
                            fw.op("act", lambda h: h.activation(out=Sbf[:, hs, :], in_=S_[:, hs, :], func=AF.Copy), reads=[H("S_", hf)], writes=[H("Sbf", hf)])
                        if dr_ == 1 and c % 16 == 15 and c < 63:
                            fw.op("dve", lambda h: h.tensor_scalar(out=S_[:, hs, :], in0=S_[:, hs, :], scalar1=sgf[:, 4 + seg:5 + seg], scalar2=None, op0=ALU.mult), reads=[H("S_", hf), "sgf"], writes=[H("S_", hf)])
                            fw.op("act", lambda h: h.activation(out=Sbf[:, hs, :], in_=S_[:, hs, :], func=AF.Copy), reads=[H("S_", hf)], writes=[H("Sbf", hf)])

                        def wsmm(h):
                            for hd in range(hs.start, hs.stop):
                                ins = h.matmul(PC[:, hd * 128:(hd + 1) * 128], lhsT=wT[:, hd, :], rhs=Sbf[:, hd, :], start=True, stop=True)
                            return ins
                        fw.op("pe", wsmm, reads=[H("wT", hf), H("Sbf", hf)], writes=[H("PC", hf)])
                        fw.op("dve", lambda h: h.tensor_tensor(out=vnew[:, hs, :], in0=u_[:, hs, :], in1=pv(PC, hf), op=ALU.subtract), reads=[H("u_", hf), H("PC", hf)], writes=[H("vnew", hf)])

                        def omm(h):
                            for hd in range(hs.start, hs.stop):
                                h.matmul(PD[:, hd * 128:(hd + 1) * 128], lhsT=qgT[:, hd, :], rhs=Sbf[:, hd, :], start=True, stop=False)
                                h.matmul(PD[:, hd * 128:(hd + 1) * 128], lhsT=qkm[:, hd, :], rhs=vnew[:, hd, :], start=False, stop=True)
                                ins = h.matmul(PB[:, hd * 128:(hd + 1) * 128], lhsT=kd[:, hd, :], rhs=vnew[:, hd, :], start=True, stop=True)
                            return ins
                        fw.op("pe", omm, reads=[H("qgT", hf), H("Sbf", hf), H("qkm", hf), H("vnew", hf), H("kd", hf)], writes=[H("PD", hf), H("PB", hf)])
                        fw.op("act", lambda h: h.activation(out=ost[b2][:, hs, :], in_=pv(PD, hf), func=AF.Copy), reads=[H("PD", hf)], writes=[K("ost")])
                        fw.op("pool", lambda h: h.tensor_tensor(out=S_[:, hs, :], in0=S_[:, hs, :], in1=sm[:, hs, 2:3].to_broadcast([128, 4, 128]), op=ALU.mult), reads=[H("S_", hf), "sm2"], writes=[H("S_", hf)])
                        fw.op("dve", lambda h: h.tensor_tensor(out=S_[:, hs, :], in0=S_[:, hs, :], in1=pv(PB, hf), op=ALU.add), reads=[H("S_", hf), H("PB", hf)], writes=[H("S_", hf)])
                        fw.op("act", lambda h: h.activation(out=Sbf[:, hs, :], in_=S_[:, hs, :], func=AF.Copy), reads=[H("S_", hf)], writes=[H("Sbf", hf)])
                    both(s_seq)
                    fw.dma("sp", lambda h: h.dma_start(out=o_dir[dr_, t0:t0 + 128, :].rearrange("p (h c) -> p h c", h=8), in_=ost[b2][:]),
                           reads=[K("ost")], writes=["o_dir"])
            fw.barrier()

        fw.phase = "P5"
        with ExitStack() as st:
            kw = [SB(st, f"kw{i}", [128, 256], BF16) for i in range(4)]
            qw = [SB(st, f"qw{i}", [128, 128], BF16) for i in range(4)]
            vw = [SB(st, f"vw{i}", [128, 2, 128], BF16) for i in range(4)]
            sc_ = [SB(st, f"sc{i}", [128, 256]) for i in range(4)]
            pe_ = [SB(st, f"pe{i}", [128, 256], BF16) for i in range(4)]
            pT = [SB(st, f"pT{i}", [128, 2, 128], BF16) for i in range(4)]
            oo = [SB(st, f"oo{i}", [128, 130]) for i in range(4)]
            nmx = [SB(st, f"nmx{i}", [128, 1]) for i in range(4)]
            idb = SB(st, "idb", [128, 128], BF16)
            msk = SB(st, "msk", [128, 4, 3, 256])
            mk1 = SB(st, "mk1", [128, 1])
            psc_t = [PSM(st, f"psc{i}", [128, 512]) for i in range(2)]
            psc = [psc_t[i // 2][:, (i % 2) * 256:(i % 2 + 1) * 256] for i in range(4)]
            ppt_t = PSM(st, "ppt", [128, 4, 2, 128], BF16)
            ppt = [ppt_t[:, i, :, :] for i in range(4)]
            pov_t = PSM(st, "pov", [128, 4, 128])
            pov = [pov_t[:, i, :] for i in range(4)]
            fw.op("dve", lambda h: h.tensor_copy(out=idb[:], in_=ident), reads=["cst"], writes=["idb"])
            band = cst[:, C_BAND:C_BAND + 256]; negl = cst[:, C_NEGL:C_NEGL + 256]; negr = cst[:, C_NEGR:C_NEGR + 256]
            for s in range(NSEG):
                fw.op("dve", lambda h, s=s: h.tensor_scalar(out=mk1[:], in0=sgf[:, s:s + 1], scalar1=-1.0, scalar2=1.0, op0=ALU.mult, op1=ALU.add), reads=["sgf"], writes=["mk1"])
                fw.op("dve", lambda h, s=s: h.scalar_tensor_tensor(out=msk[:, s, 0, :], in0=negl, scalar=mk1[:, 0:1], in1=band, op0=ALU.mult, op1=ALU.add),
                      reads=["mk1", "cst"], writes=["msk"])
                fw.op("dve", lambda h, s=s: h.tensor_scalar(out=mk1[:], in0=sgf[:, 4 + s:5 + s], scalar1=-1.0, scalar2=1.0, op0=ALU.mult, op1=ALU.add), reads=["sgf", "msk"], writes=["mk1"])
                fw.op("dve", lambda h, s=s: h.scalar_tensor_tensor(out=msk[:, s, 1, :], in0=negr, scalar=mk1[:, 0:1], in1=band, op0=ALU.mult, op1=ALU.add),
                      reads=["mk1", "cst"], writes=["msk"])
                fw.op("dve", lambda h, s=s: h.scalar_tensor_tensor(out=msk[:, s, 2, :], in0=negr, scalar=mk1[:, 0:1], in1=msk[:, s, 0, :], op0=ALU.mult, op1=ALU.add),
                      reads=["mk1", "cst", "msk"], writes=["msk"])
            units = []
            for g in range(3):
                for hh in range(4):
                    for r in range(DIL[g]):
                        for b in range(MC[g] // 128):
                            units.append((g, hh, r, b))

            def u_load(it):
                g, hh, r, b = units[it]
                i2 = it % 4
                K = lambda n: (n, i2)
                kfull = akT[g][hh * 128:(hh + 1) * 128, :].rearrange("p (r c) -> p r c", c=CL[g])
                qfull = aqT[g][hh * 128:(hh + 1) * 128, :].rearrange("p (r c) -> p r c", c=CL[g])
                vfull = avs[g][:, hh * 128:(hh + 1) * 128].rearrange("(r c) d -> r c d", c=CL[g])
                fw.dma("sp", lambda h: h.dma_start(out=kw[i2][:], in_=kfull[:, r, 128 * b:128 * b + 256]), reads=[("aqk", g), ("akT", g)], writes=[K("kw")])
                fw.dma("sp", lambda h: h.dma_start(out=qw[i2][:], in_=qfull[:, r, 64 + 128 * b:64 + 128 * b + 128]), reads=[("aqk", g)], writes=[K("qw")])
                fw.dma("sp", lambda h: h.dma_start(out=vw[i2][:], in_=vfull[r, 128 * b:128 * b + 256, :].rearrange("(k p) d -> p k d", p=128)),
                       reads=[("av", g)], writes=[K("vw")])

            def u_comp(it):
                g, hh, r, b = units[it]
                dil = DIL[g]
                tps = MS[g] // 128
                i2 = it % 4
                K = lambda n: (n, i2)
                s = b // tps
                first = (b % tps == 0)
                lastt = (b % tps == tps - 1)
                fw.op("pe", lambda h: h.matmul(psc[i2], lhsT=qw[i2][:], rhs=kw[i2][:], start=True, stop=True), reads=[K("qw"), K("kw")], writes=[K("psc")])
                if first and lastt:
                    mask = msk[:, s, 2, :]
                elif first:
                    mask = msk[:, s, 0, :]
                elif lastt:
                    mask = msk[:, s, 1, :]
                else:
                    mask = band
                fw.op("dve", lambda h: h.tensor_tensor(out=sc_[i2][:], in0=psc[i2], in1=mask, op=ALU.add), reads=[K("psc"), "msk", "cst"], writes=[K("sc")])
                fw.op("dve", lambda h: h.reduce_max(out=oo[i2][:, 128:129], in_=sc_[i2][:], axis=AX.X), reads=[K("sc")], writes=[K("oo")])
                fw.op("dve", lambda h: h.tensor_scalar(out=nmx[i2][:], in0=oo[i2][:, 128:129], scalar1=-1.0, scalar2=None, op0=ALU.mult), reads=[K("oo")], writes=[K("nmx")])
                fw.op("act", lambda h: h.activation(out=pe_[i2][:], in_=sc_[i2][:], func=AF.Exp, bias=nmx[i2][:], scale=1.0, accum_out=oo[i2][:, 129:130]),
                      reads=[K("sc"), K("nmx")], writes=[K("pe"), K("oo")])

                def ptr_(h):
                    h.transpose(out=ppt[i2][:, 0, :], in_=pe_[i2][:, 0:128], identity=idb[:])
                    return h.transpose(out=ppt[i2][:, 1, :], in_=pe_[i2][:, 128:256], identity=idb[:])
                fw.op("pe", ptr_, reads=[K("pe"), "idb"], writes=[K("ppt")])
                fw.op("act", lambda h: h.activation(out=pT[i2][:], in_=ppt[i2], func=AF.Copy), reads=[K("ppt")], writes=[K("pT")])

                def pv(h):
                    h.matmul(pov[i2], lhsT=pT[i2][:, 0, :], rhs=vw[i2][:, 0, :], start=True, stop=False)
                    return h.matmul(pov[i2], lhsT=pT[i2][:, 1, :], rhs=vw[i2][:, 1, :], start=False, stop=True)
                fw.op("pe", pv, reads=[K("pT"), K("vw")], writes=[K("pov")])
                fw.op("dve", lambda h: h.tensor_copy(out=oo[i2][:, 0:128], in_=pov[i2]), reads=[K("pov")], writes=[K("oo")])
                dst = Oat[g, :, hh, :].rearrange("(m r) c -> r m c", r=dil)[r, 128 * b:128 * b + 128, :]
                fw.dma("pool", lambda h: h.dma_start(out=dst, in_=oo[i2][:]), reads=[K("oo")], writes=["Oat"])

            NU = len(units)
            for it in range(NU + 2):
                if it < NU:
                    u_load(it)
                if it >= 2:
                    u_comp(it - 2)
            fw.barrier()

        fw.phase = "P5b"
        with ExitStack() as st:
            of_ = [SB(st, f"of{i}", [128, 8, 128]) for i in range(2)]
            ob_ = [SB(st, f"ob{i}", [128, 8, 128]) for i in range(2)]
            szt = [SB(st, f"szt{i}", [128, 8, 128]) for i in range(2)]
            osq = SB(st, "osq", [128, 8, 128])
            ss8 = SB(st, "ss8", [128, 8])
            yd = [SB(st, f"yd{i}", [128, 8, 128], BF16) for i in range(2)]
            og = [[SB(st, f"og{g}_{i}", [128, 4, 130]) for i in range(2)] for g in range(3)]
            m4 = SB(st, "m4", [128, 4]); w4 = SB(st, "w4", [128, 3, 4]); d4 = SB(st, "d4", [128, 4]); t4 = SB(st, "t4", [128, 4])
            am = SB(st, "am", [128, 4, 128]); am2 = SB(st, "am2", [128, 4, 128])
            ya = [SB(st, f"ya{i}", [128, 4, 128], BF16) for i in range(2)]
            pdn = PSM(st, "pdn", [128, 1024])
            pat = PSM(st, "pat", [128, 512])
            for tl in range(64):
                t0 = tl * 128
                i2 = tl % 2
                K = lambda n: (n, i2)
                fw.dma("sp", lambda h, i2=i2, t0=t0: h.dma_start(out=of_[i2][:], in_=o_dir[0, t0:t0 + 128, :].rearrange("p (h c) -> p h c", h=8)), reads=["o_dir"], writes=[K("of")])
                fw.dma("sp", lambda h, i2=i2, t0=t0: h.dma_start(out=ob_[i2][:], in_=o_dir[1, t0:t0 + 128, :].rearrange("p (h c) -> p h c", h=8)), reads=["o_dir"], writes=[K("ob")])
                fw.dma("sp", lambda h, i2=i2, t0=t0: h.dma_start(out=szt[i2][:], in_=szT[:, t0:t0 + 128].rearrange("(h p) t -> p h t", p=128)), reads=["szT"], writes=[K("szt")])
                fw.op("pool", lambda h, i2=i2: h.tensor_tensor(out=of_[i2][:], in0=of_[i2][:], in1=ob_[i2][:], op=ALU.add), reads=[K("of"), K("ob")], writes=[K("of")])
                fw.op("pool", lambda h, i2=i2: h.tensor_tensor(out=osq[:], in0=of_[i2][:], in1=of_[i2][:], op=ALU.mult), reads=[K("of")], writes=["osq"])
                fw.op("dve", lambda h: h.tensor_reduce(out=ss8[:], in_=osq[:], axis=AX.X, op=ALU.add), reads=["osq"], writes=["ss8"])
                fw.op("act", lambda h: h.activation(out=ss8[:], in_=ss8[:], func=AF.Sqrt, bias=EPS, scale=1.0 / 128.0), reads=["ss8"], writes=["ss8"])
                fw.op("dve", lambda h: h.reciprocal(out=ss8[:], in_=ss8[:]), reads=["ss8"], writes=["ss8"])
                fw.op("dve", lambda h, i2=i2: h.tensor_tensor(out=of_[i2][:], in0=of_[i2][:], in1=ss8[:].unsqueeze(2).to_broadcast([128, 8, 128]), op=ALU.mult),
                      reads=[K("of"), "ss8"], writes=[K("of")])

                def trd(h, i2=i2):
                    for hd in range(8):
                        ins = h.transpose(out=pdn[:, hd * 128:(hd + 1) * 128], in_=of_[i2][:, hd, :], identity=ident)
                    return ins
                fw.op("pe", trd, reads=[K("of"), "cst"], writes=["pdn"])
                fw.op("dve", lambda h, i2=i2: h.scalar_tensor_tensor(out=yd[i2][:], in0=pdn[:, :].rearrange("p (h c) -> p h c", h=8), scalar=dnwT[:, 0:1], in1=szt[i2][:],
                                                                   op0=ALU.mult, op1=ALU.mult), reads=["pdn", "dnwT", K("szt")], writes=[K("yd")])
                fw.dma("sp", lambda h, i2=i2, t0=t0: h.dma_start(out=ydnT[:, t0:t0 + 128].rearrange("(h p) t -> p h t", p=128), in_=yd[i2][:]), reads=[K("yd")], writes=["ydnT"])
                for g in range(3):
                    fw.dma("sp", lambda h, g=g, i2=i2, t0=t0: h.dma_start(out=og[g][i2][:], in_=Oat[g, t0:t0 + 128, :, :]), reads=["Oat"], writes=[K(f"og{g}")])
                mxs = [og[g][i2][:, :, 128] for g in range(3)]
                dns = [og[g][i2][:, :, 129] for g in range(3)]
                fw.op("dve", lambda h: h.tensor_tensor(out=m4[:], in0=mxs[0], in1=mxs[1], op=ALU.max), reads=[K("og0"), K("og1")], writes=["m4"])
                fw.op("dve", lambda h: h.tensor_tensor(out=m4[:], in0=m4[:], in1=mxs[2], op=ALU.max), reads=["m4", K("og2")], writes=["m4"])
                for g in range(3):
                    fw.op("dve", lambda h, g=g: h.tensor_tensor(out=w4[:, g, :], in0=mxs[g], in1=m4[:], op=ALU.subtract), reads=[K(f"og{g}"), "m4"], writes=["w4"])
                fw.op("act", lambda h: h.activation(out=w4[:], in_=w4[:], func=AF.Exp), reads=["w4"], writes=["w4"])
                fw.op("dve", lambda h: h.tensor_tensor(out=d4[:], in0=w4[:, 0, :], in1=dns[0], op=ALU.mult), reads=["w4", K("og0")], writes=["d4"])
                for g in (1, 2):
                    fw.op("dve", lambda h, g=g: h.tensor_tensor(out=t4[:], in0=w4[:, g, :], in1=dns[g], op=ALU.mult), reads=["w4", K(f"og{g}")], writes=["t4"])
                    fw.op("dve", lambda h: h.tensor_tensor(out=d4[:], in0=d4[:], in1=t4[:], op=ALU.add), reads=["d4", "t4"], writes=["d4"])
                fw.op("dve", lambda h: h.reciprocal(out=d4[:], in_=d4[:]), reads=["d4"], writes=["d4"])
                for g in range(3):
                    fw.op("dve", lambda h, g=g: h.tensor_tensor(out=w4[:, g, :], in0=w4[:, g, :], in1=d4[:], op=ALU.mult), reads=["w4", "d4"], writes=["w4"])
                fw.op("pool", lambda h, i2=i2: h.tensor_tensor(out=am[:], in0=og[0][i2][:, :, 0:128], in1=w4[:, 0, :].unsqueeze(2).to_broadcast([128, 4, 128]), op=ALU.mult),
                      reads=[K("og0"), "w4"], writes=["am"])
                for g in (1, 2):
                    fw.op("pool", lambda h, g=g, i2=i2: h.tensor_tensor(out=am2[:], in0=og[g][i2][:, :, 0:128], in1=w4[:, g, :].unsqueeze(2).to_broadcast([128, 4, 128]), op=ALU.mult),
                          reads=[K(f"og{g}"), "w4"], writes=["am2"])
                    fw.op("pool", lambda h: h.tensor_tensor(out=am[:], in0=am[:], in1=am2[:], op=ALU.add), reads=["am", "am2"], writes=["am"])

                def tra(h):
                    for hd in range(4):
                        ins = h.transpose(out=pat[:, hd * 128:(hd + 1) * 128], in_=am[:, hd, :], identity=ident)
                    return ins
                fw.op("pe", tra, reads=["am", "cst"], writes=["pat"])
                fw.op("act", lambda h, i2=i2: h.activation(out=ya[i2][:], in_=pat[:, :].rearrange("p (h c) -> p h c", h=4), func=AF.Copy), reads=["pat"], writes=[K("ya")])
                fw.dma("sp", lambda h, i2=i2, t0=t0: h.dma_start(out=yatT[:, t0:t0 + 128].rearrange("(h p) t -> p h t", p=128), in_=ya[i2][:]), reads=[K("ya")], writes=["yatT"])
            fw.barrier()

        fw.phase = "P6"
        with ExitStack() as st:
            xin = [SB(st, f"xin{q}", [128, D]) for q in range(4)]
            xT = SB(st, "xT", [128, 16, 512])
            acc = SB(st, "acc", [128, 16, 512])
            sq = [SB(st, f"sq{i}", [128, 512]) for i in range(2)]
            rstd = SB(st, "rstd", [128, 512])
            big = SB(st, "big", [128, 32, 512], BF16)
            h2T = SB(st, "h2T", [128, 16, 512], BF16)
            wk = [SB(st, f"wk{i}", [128, 32, 128], BF16) for i in range(3)]
            print("sbuf remaining", nc.sbuf_bytes_remaining)
            tmp = [SB(st, f"tmp{i}", [128, 512]) for i in range(4)]
            pxt = [PSM(st, f"pxt{i}", [128, 512]) for i in range(2)]
            pss = PSM(st, "pss", [128, 512])
            pmm = [PSM(st, f"pmm{i}", [128, 512]) for i in range(4)]
            hT = big[:, 0:16, :]
            ydn_s = big[:, 16:24, :]
            yat_s = big[:, 24:28, :]
            mixedT = h2T

            wslot = [0]

            def load_w(src_ap, nj, key):
                i = wslot[0] % 3
                wslot[0] += 1
                fw.dma("sp", lambda h, i=i: h.dma_start(out=wk[i][:, 0:nj, :], in_=src_ap), reads=[key], writes=[("wk", i)])
                return i

            def proj(widx, nj, rhs_fn, rhs_keys, pbank, first=True, last=True, j0=0):
                def mm(h):
                    for j in range(nj):
                        ins = h.matmul(pmm[pbank][:, :], lhsT=wk[widx][:, j, :], rhs=rhs_fn(j),
                                       start=(first and j == 0), stop=(last and j == nj - 1))
                    return ins
                fw.op("pe", mm, reads=[("wk", widx)] + rhs_keys, writes=[("pmm", pbank)])

            for tile in range(min(T // 512, KTILES)):
                t0 = tile * 512
                s = tile // 4
                front_end((xin, xT, sq, rstd, pxt, pss), t0, s)
                for j in range(16):
                    fw.op("pool", lambda h, j=j: h.tensor_tensor(out=tmp[j % 2][:], in0=xT[:, j, :], in1=rstd[:], op=ALU.mult),
                          reads=[("xT", j), "rstd"], writes=[("tmp", j % 2)])
                    fw.op("dve", lambda h, j=j: h.tensor_scalar(out=hT[:, j, :], in0=tmp[j % 2][:], scalar1=A1[:, j, s:s + 1],
                                                                scalar2=modT[:, j, s:s + 1], op0=ALU.mult, op1=ALU.add),
                          reads=[("tmp", j % 2), "A1", "modT"], writes=[("big", j)])
                fw.dma("sp", lambda h: h.dma_start(out=ydn_s, in_=ydnT[:, t0:t0 + 512].rearrange("(j p) t -> p j t", p=128)),
                       reads=["ydnT"], writes=[("big", 16 + j) for j in range(8)])
                fw.dma("sp", lambda h: h.dma_start(out=yat_s, in_=yatT[:, t0:t0 + 512].rearrange("(j p) t -> p j t", p=128)),
                       reads=["yatT"], writes=[("big", 24 + j) for j in range(4)])
                for cb in range(16):
                    w1 = load_w(wb_mg[cb], 16, ("wb_mg", cb))
                    proj(w1, 16, lambda j: hT[:, j, :], [("big", j) for j in range(16)], 0)
                    w2 = load_w(wb_mg[16 + cb], 16, ("wb_mg", 16 + cb))
                    proj(w2, 16, lambda j: hT[:, j, :], [("big", j) for j in range(16)], 1)
                    w3 = load_w(wb_dn[cb], 8, ("wb_dn", cb))
                    proj(w3, 8, lambda j: ydn_s[:, j, :], [("big", 16 + j) for j in range(8)], 2)
                    w4 = load_w(wb_at[cb], 4, ("wb_at", cb))
                    proj(w4, 4, lambda j: yat_s[:, j, :], [("big", 24 + j) for j in range(4)], 3)
                    fw.op("act", lambda h: h.activation(out=tmp[0][:], in_=pmm[0][:, :], func=AF.Sigmoid), reads=[("pmm", 0)], writes=[("tmp", 0)])
                    fw.op("act", lambda h: h.activation(out=tmp[1][:], in_=pmm[1][:, :], func=AF.Sigmoid), reads=[("pmm", 1)], writes=[("tmp", 1)])
                    fw.op("dve", lambda h: h.tensor_tensor(out=tmp[2][:], in0=pmm[2][:, :], in1=tmp[0][:], op=ALU.mult),
                          reads=[("pmm", 2), ("tmp", 0)], writes=[("tmp", 2)])
                    fw.op("dve", lambda h: h.tensor_tensor(out=tmp[3][:], in0=pmm[3][:, :], in1=tmp[1][:], op=ALU.mult),
                          reads=[("pmm", 3), ("tmp", 1)], writes=[("tmp", 3)])
                    fw.op("pool", lambda h, cb=cb: h.tensor_tensor(out=mixedT[:, cb, :], in0=tmp[2][:], in1=tmp[3][:], op=ALU.add),
                          reads=[("tmp", 2), ("tmp", 3)], writes=[("h2T", cb)])
                for cb in range(16):
                    w1 = load_w(wb_out[cb], 16, ("wb_out", cb))
                    proj(w1, 16, lambda j: mixedT[:, j, :], [("h2T", j) for j in range(16)], cb % 4)
                    fw.op("act", lambda h, cb=cb: h.activation(out=acc[:, cb, :], in_=pmm[cb % 4][:, :], func=AF.Copy),
                          reads=[("pmm", cb % 4)], writes=[("acc", cb)])
                sumsq_rstd(lambda j: acc[:, j, :], 16, sq, pss, rstd, lambda j: ("acc", j))
                for j in range(16):
                    fw.op("pool", lambda h, j=j: h.tensor_tensor(out=tmp[j % 2][:], in0=acc[:, j, :], in1=rstd[:], op=ALU.mult),
                          reads=[("acc", j), "rstd"], writes=[("tmp", j % 2)])
                    fw.op("dve", lambda h, j=j: h.scalar_tensor_tensor(out=xT[:, j, :], in0=tmp[j % 2][:], scalar=G1[:, j, s:s + 1], in1=xT[:, j, :],
                                                                       op0=ALU.mult, op1=ALU.add),
                          reads=[("tmp", j % 2), "G1", ("xT", j)], writes=[("xT", j)])
                sumsq_rstd(lambda j: xT[:, j, :], 16, sq, pss, rstd, lambda j: ("xT", j))
                for j in range(16):
                    fw.op("pool", lambda h, j=j: h.tensor_tensor(out=tmp[j % 2][:], in0=xT[:, j, :], in1=rstd[:], op=ALU.mult),
                          reads=[("xT", j), "rstd"], writes=[("tmp", j % 2)])
                    fw.op("dve", lambda h, j=j: h.tensor_scalar(out=h2T[:, j, :], in0=tmp[j % 2][:], scalar1=A2[:, j, s:s + 1],
                                                                scalar2=modT[:, 48 + j, s:s + 1], op0=ALU.mult, op1=ALU.add),
                          reads=[("tmp", j % 2), "A2", "modT"], writes=[("h2T", j)])
                for hf in range(2):
                    for fb in range(32):
                        w1 = load_w(wb_f1[hf * 32 + fb], 16, ("wb_f1", hf * 32 + fb))
                        pb = fb % 4
                        proj(w1, 16, lambda j: h2T[:, j, :], [("h2T", j) for j in range(16)], pb)
                        if fb % 2 == 0:
                            fw.op("act", lambda h, pb=pb: h.activation(out=tmp[pb][:], in_=pmm[pb][:, :], func=AF.Relu), reads=[("pmm", pb)], writes=[("tmp", pb)])
                            fw.op("pool", lambda h, pb=pb, fb=fb: h.tensor_tensor(out=big[:, fb, :], in0=tmp[pb][:], in1=tmp[pb][:], op=ALU.mult),
                                  reads=[("tmp", pb)], writes=[("big", fb)])
                        else:
                            fw.op("dve", lambda h, pb=pb: h.tensor_scalar(out=tmp[pb][:], in0=pmm[pb][:, :], scalar1=0.0, scalar2=None, op0=ALU.max),
                                  reads=[("pmm", pb)], writes=[("tmp", pb)])
                            fw.op("pool", lambda h, pb=pb, fb=fb: h.tensor_tensor(out=big[:, fb, :], in0=tmp[pb][:], in1=tmp[pb][:], op=ALU.mult),
                                  reads=[("tmp", pb)], writes=[("big", fb)])
                    for cb in range(16):
                        w1 = load_w(wb_f2[hf * 16 + cb], 32, ("wb_f2", hf * 16 + cb))
                        pb = cb % 4
                        proj(w1, 32, lambda j: big[:, j, :], [("big", j) for j in range(32)], pb)
                        if hf == 0:
                            fw.op("act", lambda h, cb=cb, pb=pb: h.activation(out=acc[:, cb, :], in_=pmm[pb][:, :], func=AF.Copy),
                                  reads=[("pmm", pb)], writes=[("acc", cb)])
                        else:
                            fw.op("dve", lambda h, cb=cb, pb=pb: h.tensor_tensor(out=acc[:, cb, :], in0=pmm[pb][:, :], in1=acc[:, cb, :], op=ALU.add),
                                  reads=[("pmm", pb), ("acc", cb)], writes=[("acc", cb)])
                sumsq_rstd(lambda j: acc[:, j, :], 16, sq, pss, rstd, lambda j: ("acc", j))
                for j in range(16):
                    fw.op("pool", lambda h, j=j: h.tensor_tensor(out=tmp[j % 2][:], in0=acc[:, j, :], in1=rstd[:], op=ALU.mult),
                          reads=[("acc", j), "rstd"], writes=[("tmp", j % 2)])
                    fw.op("dve", lambda h, j=j: h.scalar_tensor_tensor(out=acc[:, j, :], in0=tmp[j % 2][:], scalar=G2[:, j, s:s + 1], in1=xT[:, j, :],
                                                                       op0=ALU.mult, op1=ALU.add),
                          reads=[("tmp", j % 2), "G2", ("xT", j)], writes=[("acc", j)])
                for q in range(4):
                    for jg in range(4):
                        pb = pmm[(q * 4 + jg) % 4]

                        def tr(h, q=q, jg=jg, pb=pb):
                            for jj in range(4):
                                j = jg * 4 + jj
                                ins = h.transpose(out=pb[:, jj * 128:(jj + 1) * 128], in_=acc[:, j, q * 128:(q + 1) * 128], identity=ident)
                            return ins
                        fw.op("pe", tr, reads=[("acc", jg * 4 + jj) for jj in range(4)] + ["cst"], writes=[("pmm", (q * 4 + jg) % 4)])
                        eng = "act" if jg % 2 == 0 else "dve"
                        if eng == "act":
                            fw.op("act", lambda h, q=q, jg=jg, pb=pb: h.activation(out=xin[q][:, jg * 512:(jg + 1) * 512], in_=pb[:, :], func=AF.Copy),
                                  reads=[("pmm", (q * 4 + jg) % 4)], writes=[("xin", q)])
                        else:
                            fw.op("dve", lambda h, q=q, jg=jg, pb=pb: h.tensor_copy(out=xin[q][:, jg * 512:(jg + 1) * 512], in_=pb[:, :]),
                                  reads=[("pmm", (q * 4 + jg) % 4)], writes=[("xin", q)])
                    fw.dma("sp", lambda h, q=q: h.dma_start(out=y[t0 + q * 128:t0 + (q + 1) * 128, :], in_=xin[q][:]),
                           reads=[("xin", q)], writes=["y"])
            fw.barrier()
        fw.emit_all()
    return nc


_NC_CACHE = {}

SAMPLE_MAP = {2: [0, 1, 2], 3: [3, 4, 5], 4: [6, 7, 8], 5: [9, 10, 11], 6: [12, 13], 7: [14, 15]}


def kernel(x_prompt, x_sample, c_prompt, c_sample, w_ada, b_ada, norm_pre_mix, norm_post_mix,
           norm_pre_ffn, norm_post_ffn, w_in, conv_w, A_log, dt_bias, dn_norm_w, w_dn_out,
           w_at_out, w_out, w_ff1, w_ff2):
    f = lambda a: np.ascontiguousarray(np.asarray(a, dtype=np.float32))
    x_prompt, x_sample, c_prompt, c_sample = f(x_prompt), f(x_sample), f(c_prompt), f(c_sample)
    if "nc" not in _NC_CACHE:
        _NC_CACHE["nc"] = build_program()
    nc = _NC_CACHE["nc"]
    shared = {
        "consts": make_consts(), "lvlmask": make_lvlmask(),
        "w_ada": f(w_ada)[0], "b_ada": f(b_ada)[0].reshape(96, 128),
        "norms": np.concatenate([f(norm_pre_mix)[0], f(norm_post_mix)[0], f(norm_pre_ffn)[0], f(norm_post_ffn)[0]]).reshape(64, 128),
        "w_in": f(w_in)[0], "conv_w": f(conv_w)[0].reshape(120, 128), "A_log": f(A_log)[0].reshape(1, 16),
        "dt_bias": f(dt_bias)[0].reshape(1, 16), "dn_norm_w": f(dn_norm_w)[0].reshape(1, 128),
        "w_dn_out": f(w_dn_out)[0], "w_at_out": f(w_at_out)[0], "w_out": f(w_out)[0],
        "w_ff1": f(w_ff1)[0], "w_ff2": f(w_ff2)[0],
    }
    in_maps = []
    for core in range(8):
        xs = np.zeros((T, D), np.float32)
        cs = np.zeros((NSEG, D), np.float32)
        sg = np.zeros((128, 16), np.float32)
        if core < 2:
            xs[:] = x_prompt[core]
            cs[:] = c_prompt[core][None, :]
            for s in range(NSEG):
                sg[:, s] = 1.0 if s > 0 else 0.0
                sg[:, 4 + s] = 1.0 if s < NSEG - 1 else 0.0
                sg[:, 8 + s] = s * SEG
        else:
            seqs = SAMPLE_MAP[core]
            for s in range(NSEG):
                b = seqs[s] if s < len(seqs) else seqs[0]
                xs[s * SEG:(s + 1) * SEG] = x_sample[b]
                cs[s] = c_sample[b]
        m = dict(shared)
        m["x"] = xs
        m["c"] = cs.reshape(NSEG * NJ, 128)
        m["segf"] = sg
        in_maps.append(m)
    if KSCOPES:
        _NC_CACHE["in_maps"] = in_maps
        res = run_bass_kernel_spmd(nc, in_maps, core_ids=list(range(8)), trace=True)
        _NC_CACHE["res"] = res
    else:
        res = run_bass_kernel_spmd(nc, in_maps, core_ids=list(range(8)))
    if KDEBUG:
        _NC_CACHE["res"] = res
    y_prompt = np.stack([np.asarray(res.results[c]["y"], dtype=np.float32) for c in range(2)])
    y_sample = np.zeros_like(x_sample)
    for core, seqs in SAMPLE_MAP.items():
        yc = np.asarray(res.results[core]["y"], dtype=np.float32)
        for s, b in enumerate(seqs):
            y_sample[b] = yc[s * SEG:(s + 1) * SEG]
    return (y_prompt, y_sample)
```

```python
from contextlib import ExitStack
import numpy as np
import concourse.bass as bass
import concourse.mybir as mybir
from concourse.bass_utils import run_bass_kernel_spmd

F32 = mybir.dt.float32
BF16 = mybir.dt.bfloat16
I32 = mybir.dt.int32
AF = mybir.ActivationFunctionType
ALU = mybir.AluOpType
AX = mybir.AxisListType

D = 2048
NJ = 16
T = 8192
NSEG = 4
SEG = 2048
DFF = 8192
EPS = 1e-6
IN_COLS = 12832
NEG = -1.0e30

import os
KDEBUG = int(os.environ.get("KDEBUG", "0"))
KTILES = int(os.environ.get("KTILES", "16"))
KDUMP = os.environ.get("KDUMP", "").split(",")
KSCOPES = int(os.environ.get("KSCOPES", "0"))
COMPUTE = ("pe", "act", "dve", "pool")
N_DMA_SEMS = 12


class Ticket:
    __slots__ = ("kind", "eng", "val", "sem")

    def __init__(self, kind, eng, val, sem=None):
        self.kind, self.eng, self.val, self.sem = kind, eng, val, sem


class _Rec:
    def __init__(self):
        self.calls = []

    def __getattr__(self, name):
        def f(*a, **k):
            self.calls.append((name, a, k))
            return self
        return f


def _replay(h, calls):
    ins = None
    for name, a, k in calls:
        ins = getattr(h, name)(*a, **k)
    return ins


class FW:
    def __init__(self, nc, es):
        self.nc = nc
        self.streams = {k: [] for k in ("pe", "act", "dve", "pool", "sp")}
        self.sem = {}
        self.count = {}
        for k in COMPUTE:
            self.sem[k] = es.enter_context(nc.semaphore("s_" + k))
            self.count[k] = 0
        self.dsem, self.dcount, self.dnext = {}, {}, {}
        for q in ("sp", "act", "pool"):
            self.dsem[q] = [es.enter_context(nc.semaphore(f"d_{q}{i}")) for i in range(N_DMA_SEMS)]
            self.dcount[q] = [0] * N_DMA_SEMS
            self.dnext[q] = 0
        self.known = {k: {} for k in self.streams}
        self.lastw = {}
        self.readers = {}
        self.n_ops = 0
        self.phase = "P0"

    def _need(self, stream, t, waits):
        if t is None:
            return
        if t.kind == "c":
            if t.eng == stream and stream == "pe":
                return
            key = ("c", t.eng)
            sem = self.sem[t.eng]
        else:
            key = ("d", id(t.sem))
            sem = t.sem
        if self.known[stream].get(key, 0) >= t.val:
            return
        cur = waits.get(key)
        if cur is None or cur[1] < t.val:
            waits[key] = (sem, t.val)

    def _deps(self, stream, reads, writes):
        waits = {}
        for r in reads:
            self._need(stream, self.lastw.get(r), waits)
        for w in writes:
            self._need(stream, self.lastw.get(w), waits)
            for t in self.readers.get(w, {}).values():
                self._need(stream, t, waits)
        out = []
        for key, (sem, val) in waits.items():
            self.known[stream][key] = val
            out.append((sem, val))
        return out

    def _commit(self, t, reads, writes):
        for w in writes:
            self.lastw[w] = t
            self.readers[w] = {}
        k = ("c", t.eng) if t.kind == "c" else ("d", id(t.sem))
        for r in reads:
            self.readers.setdefault(r, {})[k] = t

    def op(self, eng, fn, reads=(), writes=()):
        waits = self._deps(eng, reads, writes)
        self.count[eng] += 1
        val = self.count[eng]
        sem = self.sem[eng]

        rec = _Rec()
        fn(rec)
        calls = rec.calls

        def emit(h, waits=waits, calls=calls, sem=sem):
            for s, v in waits:
                h.wait_ge(s, v)
            _replay(h, calls).then_inc(sem, 1)

        emit.phase = self.phase
        self.streams[eng].append(emit)
        self._commit(Ticket("c", eng, val), reads, writes)
        self.n_ops += 1

    def dma(self, q, fn, reads=(), writes=()):
        i = self.dnext[q]
        self.dnext[q] = (i + 1) % N_DMA_SEMS
        sem = self.dsem[q][i]
        prev = self.dcount[q][i]
        waits = self._deps(q, reads, writes)
        key = ("d", id(sem))
        if prev > 0 and self.known[q].get(key, 0) < prev:
            waits.append((sem, prev))
            self.known[q][key] = prev
        val = prev + 16
        self.dcount[q][i] = val

        rec = _Rec()
        fn(rec)
        calls = rec.calls

        def emit(h, waits=waits, calls=calls, sem=sem):
            for s, v in waits:
                h.wait_ge(s, v)
            _replay(h, calls).then_inc(sem, 16)

        emit.phase = self.phase
        self.streams[q].append(emit)
        self._commit(Ticket("d", q, val, sem), reads, writes)
        self.n_ops += 1

    def barrier(self):
        targets = [(self.sem[k], self.count[k], ("c", k)) for k in COMPUTE if self.count[k] > 0]
        for q in self.dsem:
            for i, s in enumerate(self.dsem[q]):
                if self.dcount[q][i] > 0:
                    targets.append((s, self.dcount[q][i], ("d", id(s))))
        for stream in self.streams:
            ws = []
            for s, v, key in targets:
                if self.known[stream].get(key, 0) < v:
                    ws.append((s, v))
                    self.known[stream][key] = v

            def emit(h, ws=ws):
                for s, v in ws:
                    h.wait_ge(s, v)

            self.streams[stream].append(emit)
        self.lastw = {}
        self.readers = {}

    def emit_all(self):
        nc = self.nc

        def run(h, fs):
            if not KSCOPES:
                for f in fs:
                    f(h)
                return
            cur = None
            sid = None
            for f in fs:
                ph = getattr(f, "phase", cur)
                if ph != cur:
                    if cur is not None:
                        nc.leave_named_scope(cur, sid, False)
                    sid, _ = nc.enter_named_scope(ph, False)
                    cur = ph
                f(h)
            if cur is not None:
                nc.leave_named_scope(cur, sid, False)

        with nc.Block() as block:
            @block.tensor
            def _(h):
                run(h, self.streams["pe"])

            @block.scalar
            def _(h):
                run(h, self.streams["act"])

            @block.vector
            def _(h):
                run(h, self.streams["dve"])

            @block.gpsimd
            def _(h):
                run(h, self.streams["pool"])

            @block.sync
            def _(h):
                run(h, self.streams["sp"])


C_ID, C_MEAN, C_ONE, C_MLOW, C_MUP, C_GT, C_LT, C_LE, C_GE = 0, 128, 256, 384, 512, 640, 768, 896, 1024
C_RPERM, C_IOTA, C_IFREQ, C_BAND, C_NEGL, C_NEGR, C_TOTAL = 1152, 1280, 1792, 1793, 2049, 2305, 2561


def make_lvlmask():
    p = np.arange(128)[:, None]
    f = np.arange(128)[None, :]
    m = np.zeros((128, 14 * 128), np.float32)
    for k in range(7):
        same_hi = (p >> (k + 1)) == (f >> (k + 1))
        diff_lo = (p >> k) != (f >> k)
        m[:, (2 * k) * 128:(2 * k + 1) * 128] = same_hi & diff_lo & (p > f)
        m[:, (2 * k + 1) * 128:(2 * k + 2) * 128] = same_hi & diff_lo & (p < f)
    return m


def make_consts():
    c = np.zeros((128, C_TOTAL), np.float32)
    p = np.arange(128)[:, None]
    f = np.arange(128)[None, :]
    c[:, C_ID:C_ID + 128] = (p == f)
    c[:, C_MEAN:C_MEAN + 128] = 1.0 / D
    c[:, C_ONE:C_ONE + 128] = 1.0
    c[:, C_MLOW:C_MLOW + 128] = (p <= f)
    c[:, C_MUP:C_MUP + 128] = (p >= f)
    c[:, C_GT:C_GT + 128] = (p > f)
    c[:, C_LT:C_LT + 128] = (p < f)
    c[:, C_LE:C_LE + 128] = (p <= f)
    c[:, C_GE:C_GE + 128] = (p >= f)
    r = np.zeros((128, 128), np.float32)
    for m in range(64):
        r[m + 64, m] = -1.0
        r[m, m + 64] = 1.0
    c[:, C_RPERM:C_RPERM + 128] = r
    c[:, C_IOTA:C_IOTA + 512] = np.arange(512)[None, :]
    c[:, C_IFREQ] = (10000.0 ** (-(np.arange(128) % 64) / 64.0)) / (2 * np.pi)
    a = np.arange(128)[:, None]
    b = np.arange(256)[None, :]
    c[:, C_BAND:C_BAND + 256] = np.where((b - a >= 0) & (b - a <= 128), 0.0, NEG)
    c[:, C_NEGL:C_NEGL + 256] = np.where(b < 64, NEG, 0.0) * np.ones((128, 1))
    c[:, C_NEGR:C_NEGR + 256] = np.where(b >= 192, NEG, 0.0) * np.ones((128, 1))
    return c


def build_program():
    nc = bass.Bass("TRN2", target_bir_lowering=False)

    def EI(name, shape):
        return nc.dram_tensor(name, list(shape), F32, kind="ExternalInput").ap()

    x = EI("x", [T, D])
    cvec = EI("c", [NSEG * NJ, 128])
    segf = EI("segf", [128, 16])
    consts = EI("consts", [128, C_TOTAL])
    lvlmask = EI("lvlmask", [128, 14 * 128])
    w_ada = EI("w_ada", [D, 6 * D])
    b_ada = EI("b_ada", [96, 128])
    norms = EI("norms", [64, 128])
    w_in = EI("w_in", [D, IN_COLS])
    conv_w = EI("conv_w", [120, 128])
    alog = EI("A_log", [1, 16])
    dtb = EI("dt_bias", [1, 16])
    dnw = EI("dn_norm_w", [1, 128])
    w_dn_out = EI("w_dn_out", [1024, D])
    w_at_out = EI("w_at_out", [512, D])
    w_out = EI("w_out", [D, D])
    w_ff1 = EI("w_ff1", [D, DFF])
    w_ff2 = EI("w_ff2", [DFF, D])
    y = nc.dram_tensor("y", [T, D], F32, kind="ExternalOutput").ap()
    dbg_names = []

    def dump(fw, name, ap, keys, dt=F32):
        if not KDEBUG:
            return
        dtn = nc.dram_tensor("dbg_" + name, list(ap.shape), dt, kind="ExternalOutput").ap()
        dbg_names.append("dbg_" + name)
        fw.dma("sp", lambda h: h.dma_start(out=dtn, in_=ap), reads=list(keys), writes=["dbg_" + name])

    def DR(name, shape, dt=F32):
        if KDEBUG and name in KDUMP:
            dbg_names.append(name)
            return nc.dram_tensor(name, list(shape), dt, kind="ExternalOutput").ap()
        return nc.dram_tensor(name, list(shape), dt, kind="Internal").ap()

    wb_mg = DR("wb_mg", [32, 128, 16, 128], BF16)
    wb_dn = DR("wb_dn", [16, 128, 8, 128], BF16)
    wb_at = DR("wb_at", [16, 128, 4, 128], BF16)
    wb_out = DR("wb_out", [16, 128, 16, 128], BF16)
    wb_f1 = DR("wb_f1", [64, 128, 16, 128], BF16)
    wb_f2 = DR("wb_f2", [32, 128, 32, 128], BF16)
    ydnT = DR("ydnT", [1024, T], BF16)
    yatT = DR("yatT", [512, T], BF16)
    MERGE0 = 3072 + 1024 + 32 + 4608

    es = ExitStack()
    with es:
        fw = FW(nc, es)

        uid = [0]

        def SB(st, name, shape, dt=F32):
            uid[0] += 1
            return st.enter_context(nc.sbuf_tensor(f"{name}_{uid[0]}", list(shape), dt))

        def PSM(st, name, shape, dt=F32):
            uid[0] += 1
            return st.enter_context(nc.psum_tensor(f"{name}_{uid[0]}", list(shape), dt))

        cst = SB(es, "cst", [128, C_TOTAL])
        sgf = SB(es, "sgf", [128, 16])
        nrm = SB(es, "nrm", [128, 64])
        cT = SB(es, "cT", [128, 64])
        badaT = SB(es, "badaT", [128, 96])
        modT = SB(es, "modT", [128, 96, 4])
        A1 = SB(es, "A1", [128, 16, 4]); G1 = SB(es, "G1", [128, 16, 4])
        A2 = SB(es, "A2", [128, 16, 4]); G2 = SB(es, "G2", [128, 16, 4])
        ident = cst[:, C_ID:C_ID + 128]
        meanm = cst[:, C_MEAN:C_MEAN + 128]
        fw.dma("sp", lambda h: h.dma_start(out=cst[:], in_=consts[:, :]), writes=["cst"])
        fw.dma("sp", lambda h: h.dma_start(out=sgf[:], in_=segf[:, :]), writes=["sgf"])

        def cast(dst, src, key):
            fw.dma("pool", lambda h: h.dma_start(out=dst, in_=src), writes=[key])

        for b in range(32):
            cast(wb_mg[b], w_in[:, MERGE0 + b * 128:MERGE0 + (b + 1) * 128].rearrange("(j p) c -> p j c", p=128), ("wb_mg", b))
        for b in range(16):
            cast(wb_dn[b], w_dn_out[:, b * 128:(b + 1) * 128].rearrange("(j p) c -> p j c", p=128), ("wb_dn", b))
            cast(wb_at[b], w_at_out[:, b * 128:(b + 1) * 128].rearrange("(j p) c -> p j c", p=128), ("wb_at", b))
            cast(wb_out[b], w_out[:, b * 128:(b + 1) * 128].rearrange("(j p) c -> p j c", p=128), ("wb_out", b))
        for b in range(64):
            cast(wb_f1[b], w_ff1[:, b * 128:(b + 1) * 128].rearrange("(j p) c -> p j c", p=128), ("wb_f1", b))
        for hf in range(2):
            for b in range(16):
                cast(wb_f2[hf * 16 + b],
                     w_ff2[hf * 4096:(hf + 1) * 4096, b * 128:(b + 1) * 128].rearrange("(j p) c -> p j c", p=128),
                     ("wb_f2", hf * 16 + b))

        fw.phase = "P0mod"
        with ExitStack() as st:
            stg = SB(st, "stg0", [128, 128])
            stg2 = SB(st, "stg1", [128, 128])
            pt = PSM(st, "p0t", [128, 512])
            pm = [PSM(st, f"p0m{i}", [128, 4]) for i in range(2)]
            fw.dma("sp", lambda h: h.dma_start(out=stg[0:64, :], in_=norms[:, :]), writes=["stg0"])
            fw.dma("sp", lambda h: h.dma_start(out=stg[64:128, :], in_=cvec[:, :]), writes=["stg0"])
            fw.op("pe", lambda h: h.transpose(out=pt[:, 0:128], in_=stg[:], identity=ident), reads=["stg0", "cst"], writes=["p0t"])
            fw.op("dve", lambda h: h.tensor_copy(out=nrm[:], in_=pt[:, 0:64]), reads=["p0t"], writes=["nrm"])
            fw.op("act", lambda h: h.activation(out=cT[:], in_=pt[:, 64:128], func=AF.Silu), reads=["p0t"], writes=["cT"])
            fw.dma("sp", lambda h: h.dma_start(out=stg2[0:96, :], in_=b_ada[:, :]), writes=["stg1"])
            fw.op("pe", lambda h: h.transpose(out=pt[:, 128:224], in_=stg2[0:96, :], identity=ident[0:96, 0:96]), reads=["stg1", "cst"], writes=["p0t"])
            fw.op("dve", lambda h: h.tensor_copy(out=badaT[:], in_=pt[:, 128:224]), reads=["p0t"], writes=["badaT"])
            cTv = cT[:].rearrange("p (s j) -> p j s", j=16)
            wbig = [SB(st, f"wbig{i}", [128, 16, 1024]) for i in range(2)]
            for ch in range(12):
                wt = wbig[ch % 2]
                for j in range(16):
                    fw.dma("sp", lambda h, wt=wt, ch=ch, j=j: h.dma_start(out=wt[:, j, :], in_=w_ada[j * 128:(j + 1) * 128, ch * 1024:(ch + 1) * 1024]),
                           writes=[("wbig", ch % 2, j)])
                for cbl in range(8):
                    cb = ch * 8 + cbl

                    def mm(h, wt=wt, cb=cb, cbl=cbl):
                        for j in range(16):
                            ins = h.matmul(pm[cb % 2][:, :], lhsT=wt[:, j, cbl * 128:(cbl + 1) * 128], rhs=cTv[:, j, :], start=(j == 0), stop=(j == 15))
                        return ins
                    fw.op("pe", mm, reads=[("wbig", ch % 2, j) for j in range(16)] + ["cT"], writes=[("p0m", cb % 2)])
                    fw.op("dve", lambda h, cb=cb: h.tensor_scalar(out=modT[:, cb, :], in0=pm[cb % 2][:, :], scalar1=badaT[:, cb:cb + 1],
                                                                  scalar2=None, op0=ALU.add),
                          reads=[("p0m", cb % 2), "badaT"], writes=["modT"])

            def nv(v):
                return nrm[:, v * 16:(v + 1) * 16].unsqueeze(2).to_broadcast([128, 16, 4])
            fw.op("dve", lambda h: h.scalar_tensor_tensor(out=A1[:], in0=modT[:, 16:32, :], scalar=1.0, in1=nv(0), op0=ALU.add, op1=ALU.mult),
                  reads=["modT", "nrm"], writes=["A1"])
            fw.op("dve", lambda h: h.tensor_tensor(out=G1[:], in0=modT[:, 32:48, :], in1=nv(1), op=ALU.mult), reads=["modT", "nrm"], writes=["G1"])
            fw.op("dve", lambda h: h.scalar_tensor_tensor(out=A2[:], in0=modT[:, 64:80, :], scalar=1.0, in1=nv(2), op0=ALU.add, op1=ALU.mult),
                  reads=["modT", "nrm"], writes=["A2"])
            fw.op("dve", lambda h: h.tensor_tensor(out=G2[:], in0=modT[:, 80:96, :], in1=nv(3), op=ALU.mult), reads=["modT", "nrm"], writes=["G2"])
            dump(fw, "modT", modT[:], ["modT"])
            dump(fw, "A1", A1[:], ["A1"])
            dump(fw, "nrm", nrm[:], ["nrm"])
            fw.barrier()

        def front_end(st_bufs, t0, seg):
            xin, xT, sq, rstd, pxt, pss = st_bufs
            for q in range(4):
                fw.dma("sp", lambda h, q=q: h.dma_start(out=xin[q][:], in_=x[t0 + q * 128:t0 + (q + 1) * 128, :]), writes=[("xin", q)])
            for j in range(16):
                pb = pxt[j % 2]

                def tr(h, j=j, pb=pb):
                    for q in range(4):
                        ins = h.transpose(out=pb[:, q * 128:(q + 1) * 128], in_=xin[q][:, j * 128:(j + 1) * 128], identity=ident)
                    return ins
                fw.op("pe", tr, reads=[("xin", q) for q in range(4)] + ["cst"], writes=[("pxt", j % 2)])
                fw.op("act", lambda h, j=j, pb=pb: h.activation(out=xT[:, j, :], in_=pb[:, :], func=AF.Copy), reads=[("pxt", j % 2)], writes=[("xT", j)])
                fw.op("dve", lambda h, j=j, pb=pb: h.tensor_tensor(out=sq[j % 2][:], in0=pb[:, :], in1=xT[:, j, :], op=ALU.mult),
                      reads=[("pxt", j % 2), ("xT", j)], writes=[("sq", j % 2)])
                fw.op("pe", lambda h, j=j: h.matmul(pss[:, :], lhsT=meanm, rhs=sq[j % 2][:], start=(j == 0), stop=(j == 15)),
                      reads=[("sq", j % 2), "cst"], writes=["pss"])
            fw.op("act", lambda h: h.activation(out=rstd[:], in_=pss[:, :], func=AF.Sqrt, bias=EPS, scale=1.0), reads=["pss"], writes=["rstd"])
            fw.op("dve", lambda h: h.reciprocal(out=rstd[:], in_=rstd[:]), reads=["rstd"], writes=["rstd"])

        def sumsq_rstd(src_fn, nblk, sq, pss, rstd, src_keys, scale_mean=True):
            for j in range(nblk):
                fw.op("pool", lambda h, j=j: h.tensor_tensor(out=sq[j % 2][:], in0=src_fn(j), in1=src_fn(j), op=ALU.mult),
                      reads=[src_keys(j)], writes=[("sq", j % 2)])
                fw.op("pe", lambda h, j=j: h.matmul(pss[:, :], lhsT=meanm, rhs=sq[j % 2][:], start=(j == 0), stop=(j == nblk - 1)),
                      reads=[("sq", j % 2), "cst"], writes=["pss"])
            fw.op("act", lambda h: h.activation(out=rstd[:], in_=pss[:, :], func=AF.Sqrt, bias=EPS, scale=1.0), reads=["pss"], writes=["rstd"])
            fw.op("dve", lambda h: h.reciprocal(out=rstd[:], in_=rstd[:]), reads=["rstd"], writes=["rstd"])

        DNQ0, Z0, AB0, ATQ0, ATK0, ATV0 = 0, 3072, 4096, 4128, 4128 + 1536, 4128 + 3072
        DIL = (1, 4, 16)
        MC = [T // d_ for d_ in DIL]
        MS = [SEG // d_ for d_ in DIL]
        CL = [m_ + 128 for m_ in MC]
        qkvpre = DR("qkvpre", [3072, T + 4])
        szT = DR("szT", [1024, T])
        ab_tm = DR("ab_tm", [T, 32])
        aqT = [DR(f"aqT{g}", [512, DIL[g] * CL[g]], BF16) for g in range(3)]
        akT = [DR(f"akT{g}", [512, DIL[g] * CL[g]], BF16) for g in range(3)]
        avs = [DR(f"av{g}", [DIL[g] * CL[g], 512], BF16) for g in range(3)]
        qn = DR("qn", [1024, T]); kn = DR("kn", [1024, T])
        k_tm = DR("k_tm", [T, 1024]); v_tm = DR("v_tm", [T, 1024])
        GB = DR("GB", [2, 2, 64, 8, 128])
        o_dir = DR("o_dir", [2, T, 1024])
        Oat = DR("Oat", [3, T, 4, 130])
        wb_fm = DR("wb_fm", [56, 128, 16, 128], BF16)
        wb_v = DR("wb_v", [3, 128, 16, 512], BF16)
        wb_ab = DR("wb_ab", [128, 16, 32], BF16)
        for b in range(56):
            c0 = (DNQ0 + 128 * b) if b < 24 else (Z0 + 128 * (b - 24)) if b < 32 else (ATQ0 + 128 * (b - 32)) if b < 44 else (ATK0 + 128 * (b - 44))
            cast(wb_fm[b], w_in[:, c0:c0 + 128].rearrange("(j p) c -> p j c", p=128), ("wb_fm", b))
        for g in range(3):
            cast(wb_v[g], w_in[:, ATV0 + 512 * g:ATV0 + 512 * (g + 1)].rearrange("(j p) c -> p j c", p=128), ("wb_v", g))
        cast(wb_ab[:, :, :], w_in[:, AB0:AB0 + 32].rearrange("(j p) c -> p j c", p=128), "wb_ab")

        convT = SB(es, "convT", [128, 120])
        dnwT = SB(es, "dnwT", [128, 1])
        dtb16 = SB(es, "dtb16", [128, 16])
        nA16 = SB(es, "nA16", [128, 16])
        Gtm = SB(es, "Gtm", [128, 64, 16])
        Btm = SB(es, "Btm", [128, 64, 16])
        ones = cst[:, C_ONE:C_ONE + 128]
        with ExitStack() as st:
            stg = SB(st, "stgc", [128, 128])
            pt = PSM(st, "p1t", [128, 512])
            fw.dma("sp", lambda h: h.dma_start(out=stg[0:120, :], in_=conv_w[:, :]), writes=["stgc"])
            fw.op("pe", lambda h: h.transpose(out=pt[:, 0:120], in_=stg[0:120, :], identity=ident[0:120, 0:120]), reads=["stgc", "cst"], writes=["p1t"])
            fw.op("dve", lambda h: h.tensor_copy(out=convT[:], in_=pt[:, 0:120]), reads=["p1t"], writes=["convT"])
            fw.dma("sp", lambda h: h.dma_start(out=stg[0:1, :], in_=dnw[:, :]), writes=["stgc"])
            fw.op("pe", lambda h: h.transpose(out=pt[:, 128:129], in_=stg[0:1, :], identity=ident[0:1, 0:1]), reads=["stgc", "cst"], writes=["p1t"])
            fw.op("dve", lambda h: h.tensor_copy(out=dnwT[:], in_=pt[:, 128:129]), reads=["p1t"], writes=["dnwT"])
            fw.dma("sp", lambda h: h.dma_start(out=dtb16[:], in_=dtb.partition_broadcast(128)), writes=["dtb16"])
            fw.dma("sp", lambda h: h.dma_start(out=nA16[:], in_=alog.partition_broadcast(128)), writes=["nA16"])
            fw.op("act", lambda h: h.activation(out=nA16[:], in_=nA16[:], func=AF.Exp), reads=["nA16"], writes=["nA16"])
            fw.op("dve", lambda h: h.tensor_scalar(out=nA16[:], in0=nA16[:], scalar1=-1.0, scalar2=None, op0=ALU.mult), reads=["nA16"], writes=["nA16"])
            zb = SB(st, "zb", [128, 16, 512], BF16)
            fw.op("pool", lambda h: h.memset(zb[:], 0.0), writes=["zb"])
            for g in range(3):
                dil = DIL[g]
                for hh in range(4):
                    kv = akT[g][hh * 128:(hh + 1) * 128, :].rearrange("p (r c) -> p r c", c=CL[g])
                    fw.dma("sp", lambda h, kv=kv, dil=dil: h.dma_start(out=kv[:, :, 0:64], in_=zb[:, 0:dil, 0:64]), reads=["zb"], writes=[("akT", g)])
                    fw.dma("sp", lambda h, kv=kv, dil=dil, g=g: h.dma_start(out=kv[:, :, 64 + MC[g]:128 + MC[g]], in_=zb[:, 0:dil, 0:64]), reads=["zb"], writes=[("akT", g)])
                vv = avs[g].rearrange("(r c) d -> c r d", c=CL[g])
                fw.dma("sp", lambda h, vv=vv, dil=dil: h.dma_start(out=vv[0:64, :, :], in_=zb[0:64, 0:dil, :]), reads=["zb"], writes=[("av", g)])
                fw.dma("sp", lambda h, vv=vv, dil=dil, g=g: h.dma_start(out=vv[64 + MC[g]:128 + MC[g], :, :], in_=zb[0:64, 0:dil, :]), reads=["zb"], writes=[("av", g)])
            fw.barrier()

        fw.phase = "P1"
        with ExitStack() as st:
            xin = [SB(st, f"xin{q}", [128, D]) for q in range(4)]
            hTs = SB(st, "hTs", [128, 16, SEG], BF16)
            ssq = SB(st, "ssq", [128, 4])
            stgf = [SB(st, f"stgf{i}", [128, SEG]) for i in range(2)]
            stgb = [SB(st, f"stgb{i}", [128, SEG], BF16) for i in range(2)]
            wk = [SB(st, f"wk{i}", [128, 16, 128], BF16) for i in range(3)]
            wv = SB(st, "wv", [128, 16, 512], BF16)
            wab = SB(st, "wab", [128, 16, 32], BF16)
            cosT = SB(st, "cosT", [128, 4, 512]); sinT = SB(st, "sinT", [128, 4, 512])
            tA = SB(st, "tA", [128, 512]); tB = SB(st, "tB", [128, 512]); tB2 = SB(st, "tB2", [128, 512]); tC = SB(st, "tC", [128, 512]); tD = SB(st, "tD", [128, 512])
            tI = SB(st, "tI", [128, 512], I32)
            hpi = SB(st, "hpi", [128, 1])
            vst = [SB(st, f"vst{i}", [128, 512], BF16) for i in range(2)]
            abst = SB(st, "abst", [128, 16, 32])
            pxt = [PSM(st, f"pxt{i}", [128, 512]) for i in range(2)]
            pmm = [PSM(st, f"pmm{i}", [128, 512]) for i in range(4)]
            prr = PSM(st, "prr", [128, 512])
            print("P1 sbuf remaining", nc.sbuf_bytes_remaining)
            fw.op("pool", lambda h: h.memset(hpi[:], float(np.pi / 2)), writes=["hpi"])
            fw.dma("sp", lambda h: h.dma_start(out=wab[:], in_=wb_ab[:, :, :]), reads=["wb_ab"], writes=["wab"])
            rperm = cst[:, C_RPERM:C_RPERM + 128]
            ifr = cst[:, C_IFREQ:C_IFREQ + 1]
            iota = cst[:, C_IOTA:C_IOTA + 512]
            wsl = [0]

            def trig(dst, yap):
                fw.op("dve", lambda h: h.tensor_copy(out=tI[:], in_=yap), reads=["tA"], writes=["tI"])
                fw.op("dve", lambda h: h.tensor_copy(out=tB[:], in_=tI[:]), reads=["tI"], writes=["tB"])
                fw.op("dve", lambda h: h.tensor_tensor(out=tB[:], in0=yap, in1=tB[:], op=ALU.subtract), reads=["tA", "tB"], writes=["tB"])
                fw.op("act", lambda h: h.activation(out=tC[:], in_=tB[:], func=AF.Abs), reads=["tB"], writes=["tC"])
                fw.op("act", lambda h: h.activation(out=tD[:], in_=tB[:], func=AF.Sin, scale=float(np.pi)), reads=["tB"], writes=["tD"])
                fw.op("act", lambda h: h.activation(out=tC[:], in_=tC[:], func=AF.Sin, bias=hpi[:], scale=-float(np.pi)), reads=["tC", "hpi"], writes=["tC"])
                fw.op("dve", lambda h: h.scalar_tensor_tensor(out=dst, in0=tD[:], scalar=2.0, in1=tC[:], op0=ALU.mult, op1=ALU.mult),
                      reads=["tC", "tD"], writes=["trig"])

            for s in range(NSEG):
                for tt in range(4):
                    t0 = s * SEG + tt * 512
                    for q in range(4):
                        fw.dma("sp", lambda h, q=q, t0=t0: h.dma_start(out=xin[q][:], in_=x[t0 + q * 128:t0 + (q + 1) * 128, :]), writes=[("xin", q)])
                        fw.op("act", lambda h, q=q: h.activation(out=stgb[0][:], in_=xin[q][:], func=AF.Square, accum_out=ssq[:, q:q + 1]),
                              reads=[("xin", q)], writes=[("stgb", 0), ("ssq", q)])
                        fw.op("act", lambda h, q=q: h.activation(out=ssq[:, q:q + 1], in_=ssq[:, q:q + 1], func=AF.Sqrt, bias=EPS, scale=1.0 / D),
                              reads=[("ssq", q)], writes=[("ssq", q)])
                        fw.op("dve", lambda h, q=q: h.reciprocal(out=ssq[:, q:q + 1], in_=ssq[:, q:q + 1]), reads=[("ssq", q)], writes=[("ssq", q)])
                        fw.op("dve", lambda h, q=q: h.tensor_scalar(out=xin[q][:], in0=xin[q][:], scalar1=ssq[:, q:q + 1], scalar2=None, op0=ALU.mult),
                              reads=[("xin", q), ("ssq", q)], writes=[("xin", q)])
                    for j in range(16):
                        pb = pxt[j % 2]

                        def tr(h, j=j, pb=pb):
                            for q in range(4):
                                ins = h.transpose(out=pb[:, q * 128:(q + 1) * 128], in_=xin[q][:, j * 128:(j + 1) * 128], identity=ident)
                            return ins
                        fw.op("pe", tr, reads=[("xin", q) for q in range(4)] + ["cst"], writes=[("pxt", j % 2)])
                        fw.op("dve" if j % 2 else "act",
                              (lambda h, j=j, pb=pb, tt=tt, s=s: h.tensor_scalar(out=hTs[:, j, tt * 512:(tt + 1) * 512], in0=pb[:, :], scalar1=A1[:, j, s:s + 1],
                                                                              scalar2=modT[:, j, s:s + 1], op0=ALU.mult, op1=ALU.add)) if j % 2 else
                              (lambda h, j=j, pb=pb, tt=tt, s=s: h.activation(out=hTs[:, j, tt * 512:(tt + 1) * 512], in_=pb[:, :], func=AF.Identity,
                                                                           bias=modT[:, j, s:s + 1], scale=A1[:, j, s:s + 1])),
                              reads=[("pxt", j % 2), "A1", "modT"], writes=[("hTs", j, tt)])
                hkeys = [("hTs", j, tt) for j in range(16) for tt in range(4)]

                def wload_(b):
                    i = wsl[0] % 3
                    wsl[0] += 1
                    fw.dma("sp", lambda h, i=i, b=b: h.dma_start(out=wk[i][:], in_=wb_fm[b]), reads=[("wb_fm", b)], writes=[("wk", i)])
                    return i
                plan = list(range(32)) + [32 + qk_ * 12 + g_ * 4 + hh_ for g_ in range(3) for qk_ in range(2) for hh_ in range(4)]
                wst = {"issued": 0, "used": 0, "slots": {}}

                def wload(b):
                    while wst["issued"] < len(plan) and wst["issued"] <= wst["used"] + 2:
                        k_ = wst["issued"]
                        wst["slots"][k_] = wload_(plan[k_])
                        wst["issued"] += 1
                    k_ = wst["used"]
                    assert plan[k_] == b, (plan[k_], b)
                    wst["used"] += 1
                    return wst["slots"][k_]

                def fm_mm(wi, rhs_fn, pb, out_ap=None):
                    def mm(h):
                        for j in range(16):
                            ins = h.matmul(out_ap if out_ap is not None else pmm[pb][:, :], lhsT=wk[wi][:, j, :], rhs=rhs_fn(j), start=(j == 0), stop=(j == 15))
                        return ins
                    fw.op("pe", mm, reads=[("wk", wi)] + hkeys, writes=[("pmm", pb)])

                for b in range(32):
                    wi = wload(b)
                    sf = stgf[b % 2]
                    for tt in range(4):
                        pb = (b * 4 + tt) % 4
                        fm_mm(wi, lambda j, tt=tt: hTs[:, j, tt * 512:(tt + 1) * 512], pb)
                        fw.op("act", lambda h, sf=sf, tt=tt, pb=pb, b=b: h.activation(out=sf[:, tt * 512:(tt + 1) * 512], in_=pmm[pb][:, :],
                                                                                     func=(AF.Copy if b < 24 else AF.Silu)),
                              reads=[("pmm", pb)], writes=[("stgf", b % 2)])
                    if b < 24:
                        fw.dma("sp", lambda h, sf=sf, b=b, s=s: h.dma_start(out=qkvpre[b * 128:(b + 1) * 128, 2 + s * SEG:2 + (s + 1) * SEG], in_=sf[:]),
                               reads=[("stgf", b % 2)], writes=["qkvpre"])
                    else:
                        fw.dma("sp", lambda h, sf=sf, b=b, s=s: h.dma_start(out=szT[(b - 24) * 128:(b - 23) * 128, s * SEG:(s + 1) * SEG], in_=sf[:]),
                               reads=[("stgf", b % 2)], writes=["szT"])
                def abmm(h):
                    for n in range(16):
                        for j in range(16):
                            ins = h.matmul(pmm[0][:, n * 32:(n + 1) * 32], lhsT=hTs[:, j, n * 128:(n + 1) * 128], rhs=wab[:, j, :], start=(j == 0), stop=(j == 15))
                    return ins
                fw.op("pe", abmm, reads=["wab"] + hkeys, writes=[("pmm", 0)])
                fw.op("dve", lambda h: h.tensor_copy(out=abst[:].rearrange("p n c -> p (n c)"), in_=pmm[0][:, :]), reads=[("pmm", 0)], writes=["abst"])
                fw.dma("sp", lambda h, s=s: h.dma_start(out=ab_tm[s * SEG:(s + 1) * SEG, :].rearrange("(n p) c -> p n c", p=128), in_=abst[:]),
                       reads=["abst"], writes=["ab_tm"])
                for g in range(3):
                    dil = DIL[g]
                    for tt in range(4):
                        if dil == 16:
                            for rl in range(4):
                                fw.op("dve", lambda h, rl=rl, tt=tt, s=s: h.tensor_scalar(out=tA[:, rl * 128:(rl + 1) * 128], in0=iota[:, 0:128], scalar1=16.0,
                                                                                         scalar2=sgf[:, 8 + s:9 + s], op0=ALU.mult, op1=ALU.add),
                                      reads=["cst", "sgf"], writes=["tA"])
                                fw.op("dve", lambda h, rl=rl, tt=tt: h.tensor_scalar(out=tA[:, rl * 128:(rl + 1) * 128], in0=tA[:, rl * 128:(rl + 1) * 128],
                                                                                    scalar1=float(4 * tt + rl), scalar2=ifr, op0=ALU.add, op1=ALU.mult),
                                      reads=["tA", "cst"], writes=["tA"])
                        else:
                            cadd = float(512 * tt) if dil == 1 else float(tt)
                            fw.op("dve", lambda h, s=s, dil=dil: h.tensor_scalar(out=tA[:], in0=iota, scalar1=float(dil), scalar2=sgf[:, 8 + s:9 + s],
                                                                                op0=ALU.mult, op1=ALU.add), reads=["cst", "sgf"], writes=["tA"])
                            fw.op("dve", lambda h, cadd=cadd: h.tensor_scalar(out=tA[:], in0=tA[:], scalar1=cadd, scalar2=ifr, op0=ALU.add, op1=ALU.mult),
                                  reads=["tA", "cst"], writes=["tA"])
                        trig(sinT[:, tt, :], tA[:])
                        fw.op("dve", lambda h: h.tensor_scalar(out=tA[:], in0=tA[:], scalar1=0.25, scalar2=None, op0=ALU.add), reads=["tA", "trig"], writes=["tA"])
                        trig(cosT[:, tt, :], tA[:])
                    for qk in range(2):
                        for hh in range(4):
                            b = 32 + qk * 12 + g * 4 + hh
                            wi = wload(b)
                            sb_ = stgb[(qk * 4 + hh) % 2]
                            hview = None
                            def rot_stage(tt, sb_=None):
                                tBx = tB if tt % 2 == 0 else tB2
                                kB = "tB" if tt % 2 == 0 else "tB2"
                                fw.op("pe", lambda h: h.matmul(prr[:, :], lhsT=rperm, rhs=tBx[:], start=True, stop=True), reads=[kB, "cst"], writes=["prr"])
                                fw.op("pool", lambda h: h.tensor_tensor(out=tC[:], in0=tBx[:], in1=cosT[:, tt, :], op=ALU.mult), reads=[kB, "trig"], writes=["tC"])
                                fw.op("dve", lambda h: h.tensor_tensor(out=tD[:], in0=prr[:, :], in1=sinT[:, tt, :], op=ALU.mult), reads=["prr", "trig"], writes=["tD"])
                                fw.op("pool", lambda h: h.tensor_tensor(out=sb_[:, tt * 512:(tt + 1) * 512], in0=tC[:], in1=tD[:], op=ALU.add),
                                      reads=["tC", "tD"], writes=[("stgb", (qk * 4 + hh) % 2)])

                            for tt in range(4):
                                pb = tt % 4
                                if dil == 1:
                                    rf = lambda j, tt=tt: hTs[:, j, tt * 512:(tt + 1) * 512]
                                    oap = None
                                elif dil == 4:
                                    rf = lambda j, tt=tt: hTs[:, j, :].rearrange("p (m r) -> p r m", r=4)[:, tt, :]
                                    oap = None
                                else:
                                    rf = lambda j, tt=tt: hTs[:, j, :].rearrange("p (m r) -> p r m", r=16)[:, 4 * tt:4 * tt + 4, :]
                                    oap = pmm[pb][:, :].rearrange("p (a b) -> p a b", a=4)
                                fm_mm(wi, rf, pb, oap)
                                tBx = tB if tt % 2 == 0 else tB2
                                kB = "tB" if tt % 2 == 0 else "tB2"
                                fw.op("act", lambda h: h.activation(out=tBx[:], in_=pmm[pb][:, :], func=AF.Copy, scale=(128.0 ** -0.5 if qk == 0 else 1.0)),
                                      reads=[("pmm", pb)], writes=[kB])
                                if tt >= 1:
                                    rot_stage(tt - 1, sb_)
                            rot_stage(3, sb_)
                            dst = (aqT if qk == 0 else akT)[g][hh * 128:(hh + 1) * 128, :].rearrange("p (r c) -> p r c", c=CL[g])
                            fw.dma("sp", lambda h, dst=dst, sb_=sb_, g=g, s=s, dil=dil: h.dma_start(
                                out=dst[:, :, 64 + s * MS[g]:64 + (s + 1) * MS[g]], in_=sb_[:].rearrange("p (r m) -> p r m", r=dil)),
                                reads=[("stgb", (qk * 4 + hh) % 2)], writes=[("aqk", g)])
                    fw.dma("sp", lambda h, g=g: h.dma_start(out=wv[:], in_=wb_v[g]), reads=[("wb_v", g)], writes=["wv"])
                    for ct in range(16):
                        if dil == 1:
                            lf = lambda j, ct=ct: hTs[:, j, ct * 128:(ct + 1) * 128]
                            r_, m0 = 0, ct * 128
                        elif dil == 4:
                            lf = lambda j, ct=ct: hTs[:, j, :].rearrange("p (m r) -> p r m", r=4)[:, ct // 4, (ct % 4) * 128:(ct % 4 + 1) * 128]
                            r_, m0 = ct // 4, (ct % 4) * 128
                        else:
                            lf = lambda j, ct=ct: hTs[:, j, :].rearrange("p (m r) -> p r m", r=16)[:, ct, :]
                            r_, m0 = ct, 0
                        pb = ct % 4

                        def vmm(h, lf=lf, pb=pb):
                            for j in range(16):
                                ins = h.matmul(pmm[pb][:, :], lhsT=lf(j), rhs=wv[:, j, :], start=(j == 0), stop=(j == 15))
                            return ins
                        fw.op("pe", vmm, reads=["wv"] + hkeys, writes=[("pmm", pb)])
                        vs = vst[ct % 2]
                        if ct % 2:
                            fw.op("dve", lambda h, vs=vs, pb=pb: h.tensor_copy(out=vs[:], in_=pmm[pb][:, :]), reads=[("pmm", pb)], writes=[("vst", ct % 2)])
                        else:
                            fw.op("act", lambda h, vs=vs, pb=pb: h.activation(out=vs[:], in_=pmm[pb][:, :], func=AF.Copy), reads=[("pmm", pb)], writes=[("vst", ct % 2)])
                        row0 = r_ * CL[g] + 64 + s * MS[g] + m0
                        fw.dma("sp", lambda h, vs=vs, row0=row0, g=g: h.dma_start(out=avs[g][row0:row0 + 128, :], in_=vs[:]),
                               reads=[("vst", ct % 2)], writes=[("av", g)])
            fw.barrier()

        fw.phase = "P2"
        with ExitStack() as st:
            xc = [SB(st, f"xc{i}", [128, 516]) for i in range(2)]
            ca_l = [SB(st, f"ca{i}", [128, 512]) for i in range(2)]; cs_l = [SB(st, f"cs{i}", [128, 512]) for i in range(2)]
            cq_l = [SB(st, f"cq{i}", [128, 512]) for i in range(2)]; cr_l = [SB(st, f"cr{i}", [128, 512]) for i in range(2)]
            cn = [SB(st, f"cn{i}", [128, 512]) for i in range(2)]
            ctm = [SB(st, f"ctm{i}", [128, 4, 128]) for i in range(2)]
            pss2_l = [PSM(st, f"pss2{i}", [128, 512]) for i in range(2)]
            ptr = [PSM(st, f"ptr{i}", [128, 512]) for i in range(2)]
            def p2_load(it):
                cb, tile = it // 16, it % 16
                t0 = tile * 512
                xw = xc[it % 2]
                fw.dma("sp", lambda h: h.dma_start(out=xw[:], in_=qkvpre[cb * 128:(cb + 1) * 128, t0:t0 + 516]), reads=["qkvpre"], writes=[("xc", it % 2)])

            NT2 = 24 * 16

            def p2_A(it):
                cb, tile = it // 16, it % 16
                s = tile // 4
                xw = xc[it % 2]
                kx = ("xc", it % 2)
                pss2 = pss2_l[it % 2]
                ca, cs_, cq, cr = ca_l[it % 2], cs_l[it % 2], cq_l[it % 2], cr_l[it % 2]
                kca, kcs, kcq, kcr = ("ca", it % 2), ("cs_", it % 2), ("cq", it % 2), ("cr", it % 2)
                if tile == 0:
                    fw.op("pool", lambda h: h.memset(xw[:, 0:2], 0.0), writes=[kx])
                elif tile % 4 == 0:
                    fw.op("pool", lambda h: h.tensor_scalar(out=xw[:, 0:2], in0=xw[:, 0:2], scalar1=sgf[:, s:s + 1], scalar2=None, op0=ALU.mult), reads=[kx, "sgf"], writes=[kx])
                if tile == 15:
                    fw.op("pool", lambda h: h.memset(xw[:, 514:516], 0.0), writes=[kx])
                elif tile % 4 == 3:
                    fw.op("pool", lambda h: h.tensor_scalar(out=xw[:, 514:516], in0=xw[:, 514:516], scalar1=sgf[:, 4 + s:5 + s], scalar2=None, op0=ALU.mult), reads=[kx, "sgf"], writes=[kx])
                fw.op("dve", lambda h: h.tensor_scalar(out=ca[:], in0=xw[:, 0:512], scalar1=convT[:, cb:cb + 1], scalar2=None, op0=ALU.mult), reads=[kx, "convT"], writes=[kca])
                for k in range(1, 5):
                    fw.op("dve", lambda h: h.scalar_tensor_tensor(out=ca[:], in0=xw[:, k:k + 512], scalar=convT[:, k * 24 + cb:k * 24 + cb + 1], in1=ca[:], op0=ALU.mult, op1=ALU.add),
                          reads=[kx, "convT", kca], writes=[kca])
                fw.op("act", lambda h: h.activation(out=cs_[:], in_=ca[:], func=AF.Silu), reads=[kca], writes=[kcs])
                if cb < 16:
                    fw.op("pool", lambda h: h.tensor_tensor(out=cq[:], in0=cs_[:], in1=cs_[:], op=ALU.mult), reads=[kcs], writes=[kcq])
                    fw.op("pe", lambda h: h.matmul(pss2[:, :], lhsT=ones, rhs=cq[:], start=True, stop=True), reads=[kcq, "cst"], writes=[("pss2", it % 2)])
                    if cb < 8:
                        fw.op("act", lambda h: h.activation(out=cr[:], in_=pss2[:, :], func=AF.Sqrt, bias=128.0 * EPS, scale=128.0), reads=[("pss2", it % 2)], writes=[kcr])
                    else:
                        fw.op("act", lambda h: h.activation(out=cr[:], in_=pss2[:, :], func=AF.Sqrt, bias=EPS, scale=1.0), reads=[("pss2", it % 2)], writes=[kcr])

            def p2_B(it):
                cb, tile = it // 16, it % 16
                t0 = tile * 512
                cs_, cr = cs_l[it % 2], cr_l[it % 2]
                kcs, kcr = ("cs_", it % 2), ("cr", it % 2)
                res_t, res_k = cs_, kcs
                if cb < 16:
                    fw.op("dve", lambda h: h.reciprocal(out=cr[:], in_=cr[:]), reads=[kcr], writes=[kcr])
                    cnb = cn[it % 2]
                    fw.op("pool", lambda h: h.tensor_tensor(out=cnb[:], in0=cs_[:], in1=cr[:], op=ALU.mult), reads=[kcs, kcr], writes=[("cn", it % 2)])
                    res_t, res_k = cnb, ("cn", it % 2)
                    dstn = qn if cb < 8 else kn
                    fw.dma("sp", lambda h: h.dma_start(out=dstn[(cb % 8) * 128:(cb % 8 + 1) * 128, t0:t0 + 512], in_=cnb[:]), reads=[res_k], writes=["qkn"])
                if cb >= 8:
                    pb = ptr[it % 2]

                    def tr(h):
                        for q in range(4):
                            ins = h.transpose(out=pb[:, q * 128:(q + 1) * 128], in_=res_t[:, q * 128:(q + 1) * 128], identity=ident)
                        return ins
                    fw.op("pe", tr, reads=[res_k, "cst"], writes=[("ptr", it % 2)])
                    cm = ctm[it % 2]
                    fw.op("act", lambda h: h.activation(out=cm[:].rearrange("p q c -> p (q c)"), in_=pb[:, :], func=AF.Copy), reads=[("ptr", it % 2)], writes=[("ctm", it % 2)])
                    dstt = k_tm if cb < 16 else v_tm
                    fw.dma("sp", lambda h: h.dma_start(out=dstt[t0:t0 + 512, (cb % 8) * 128:(cb % 8 + 1) * 128].rearrange("(q p) c -> p q c", p=128), in_=cm[:]),
                           reads=[("ctm", it % 2)], writes=["kv_tm"])

            p2_load(0)
            p2_load(1)
            for it in range(NT2 + 1):
                if it < NT2:
                    p2_A(it)
                if it >= 1:
                    p2_B(it - 1)
                    if it + 1 < NT2:
                        p2_load(it + 1)
            fw.barrier()

        fw.phase = "P3"
        with ExitStack() as st:
            ab = SB(st, "ab", [128, 64, 32])
            g_ = SB(st, "g_", [128, 64, 16]); t1_ = SB(st, "t1_", [128, 64, 16]); t2_ = SB(st, "t2_", [128, 64, 16])
            rows = [SB(st, f"rows{i}", [8, 8, 4, 128]) for i in range(2)]
            pg = PSM(st, "pg", [128, 1024])
            pr = [PSM(st, f"pr{i}", [8, 512]) for i in range(2)]
            fw.dma("sp", lambda h: h.dma_start(out=ab[:], in_=ab_tm.rearrange("(n p) c -> p n c", p=128)), reads=["ab_tm"], writes=["ab"])
            bc = lambda t: t[:].unsqueeze(1).to_broadcast([128, 64, 16])
            fw.op("dve", lambda h: h.tensor_tensor(out=t1_[:], in0=ab[:, :, 0:16], in1=bc(dtb16), op=ALU.add), reads=["ab", "dtb16"], writes=["t1_"])
            fw.op("act", lambda h: h.activation(out=t2_[:], in_=t1_[:], func=AF.Abs), reads=["t1_"], writes=["t2_"])
            fw.op("act", lambda h: h.activation(out=t2_[:], in_=t2_[:], func=AF.Exp, scale=-1.0), reads=["t2_"], writes=["t2_"])
            fw.op("act", lambda h: h.activation(out=t2_[:], in_=t2_[:], func=AF.Ln, bias=1.0, scale=1.0), reads=["t2_"], writes=["t2_"])
            fw.op("dve", lambda h: h.scalar_tensor_tensor(out=t1_[:], in0=t1_[:], scalar=0.0, in1=t2_[:], op0=ALU.max, op1=ALU.add), reads=["t1_", "t2_"], writes=["t1_"])
            fw.op("dve", lambda h: h.tensor_tensor(out=g_[:], in0=t1_[:], in1=bc(nA16), op=ALU.mult), reads=["t1_", "nA16"], writes=["g_"])
            fw.op("act", lambda h: h.activation(out=Btm[:], in_=ab[:, :, 16:32], func=AF.Sigmoid), reads=["ab"], writes=["Btm"])
            mlow = cst[:, C_MLOW:C_MLOW + 128]; mup = cst[:, C_MUP:C_MUP + 128]

            def gmm(h):
                for c in range(64):
                    h.matmul(pg[:, c * 16:c * 16 + 8], lhsT=mlow, rhs=g_[:, c, 0:8], start=True, stop=True)
                    ins = h.matmul(pg[:, c * 16 + 8:c * 16 + 16], lhsT=mup, rhs=g_[:, c, 8:16], start=True, stop=True)
                return ins
            fw.op("pe", gmm, reads=["g_", "cst"], writes=["pg"])
            fw.op("dve", lambda h: h.tensor_copy(out=Gtm[:].rearrange("p c k -> p (c k)"), in_=pg[:, :]), reads=["pg"], writes=["Gtm"])
            for c in range(64):
                pp = pr[c % 2]
                rw = rows[(c // 8) % 2]

                def rmm(h, c=c, pp=pp):
                    h.matmul(pp[:, 0:128], lhsT=g_[:, c, 0:8], rhs=mlow, start=True, stop=True)
                    h.matmul(pp[:, 128:256], lhsT=g_[:, c, 8:16], rhs=mup, start=True, stop=True)
                    h.matmul(pp[:, 256:384], lhsT=Btm[:, c, 0:8], rhs=ident, start=True, stop=True)
                    return h.matmul(pp[:, 384:512], lhsT=Btm[:, c, 8:16], rhs=ident, start=True, stop=True)
                fw.op("pe", rmm, reads=["g_", "Btm", "cst"], writes=[("pr", c % 2)])
                fw.op("dve", lambda h, c=c, pp=pp, rw=rw: h.tensor_copy(out=rw[:, c % 8, :, :].rearrange("p k t -> p (k t)"), in_=pp[:, :]),
                      reads=[("pr", c % 2)], writes=[("rows", (c // 8) % 2)])
                if c % 8 == 7:
                    c0 = c - 7
                    for k4 in range(4):
                        dr_, kd_ = k4 % 2, k4 // 2
                        fw.dma("sp", lambda h, rw=rw, c0=c0, k4=k4, dr_=dr_, kd_=kd_: h.dma_start(
                            out=GB[dr_, kd_, c0:c0 + 8, :, :].rearrange("c h t -> h c t"), in_=rw[:, :, k4, :]),
                            reads=[("rows", (c // 8) % 2)], writes=["GB"])
            fw.barrier()

        fw.phase = "P4"
        with ExitStack() as st:
            ld_q = [SB(st, f"ldq{i}", [128, 8, 128]) for i in range(2)]
            ld_k = [SB(st, f"ldk{i}", [128, 8, 128]) for i in range(2)]
            ld_kt = [SB(st, f"ldkt{i}", [128, 8, 128]) for i in range(2)]
            ld_vt = [SB(st, f"ldvt{i}", [128, 8, 128]) for i in range(2)]
            ld_g = [SB(st, f"ldg{i}", [128, 8, 128]) for i in range(2)]
            ld_b = [SB(st, f"ldb{i}", [128, 8, 128]) for i in range(2)]
            f1 = SB(st, "f1", [128, 8, 128]); f2 = SB(st, "f2", [128, 8, 128]); f3 = SB(st, "f3", [128, 8, 128]); f4 = SB(st, "f4", [128, 8, 128])
            f5 = SB(st, "f5", [128, 8, 128]); f6 = SB(st, "f6", [128, 8, 128])
            kbf = SB(st, "kbf", [128, 8, 128], BF16); qbf = SB(st, "qbf", [128, 8, 128], BF16)
            P1b = SB(st, "P1b", [128, 8, 128], BF16); Q1b = SB(st, "Q1b", [128, 8, 128], BF16)
            Dm = SB(st, "Dm", [128, 8, 128], BF16); Em = SB(st, "Em", [128, 8, 128], BF16)
            Xb = SB(st, "Xb", [128, 8, 128], BF16); X2b = SB(st, "X2b", [128, 8, 128], BF16)
            lvl = SB(st, "lvl", [128, 14 * 128])
            fw.dma("sp", lambda h: h.dma_start(out=lvl[:], in_=lvlmask[:, :]), writes=["lvl"])
            lvu = SB(st, "lvu", [128, 14 * 128], mybir.dt.uint8)
            fw.op("dve", lambda h: h.tensor_copy(out=lvu[:], in_=lvl[:]), reads=["lvl"], writes=["lvu"])
            qkm = SB(st, "qkm", [128, 8, 128], BF16); qgT = SB(st, "qgT", [128, 8, 128], BF16)
            kbg = SB(st, "kbg", [128, 8, 128], BF16); vb = SB(st, "vb", [128, 8, 128], BF16); kd = SB(st, "kd", [128, 8, 128], BF16)
            wT = SB(st, "wT", [128, 8, 128], BF16); vnew = SB(st, "vnew", [128, 8, 128], BF16)
            u_ = SB(st, "u_", [128, 8, 128]); S_ = SB(st, "S_", [128, 8, 128]); Sbf = SB(st, "Sbf", [128, 8, 128], BF16)
            ost = [SB(st, f"ost{i}", [128, 8, 128]) for i in range(2)]
            sm = SB(st, "sm", [128, 8, 4])
            PA = PSM(st, "PA", [128, 1024]); PB = PSM(st, "PB", [128, 1024]); PC = PSM(st, "PC", [128, 1024]); PD = PSM(st, "PD", [128, 1024])
            print("P4 sbuf remaining", nc.sbuf_bytes_remaining)
            v3 = lambda t: t[:, :].rearrange("p (h c) -> p h c", h=8)
            maskc = lambda off: cst[:, off:off + 128].unsqueeze(1).to_broadcast([128, 8, 128])
            identb = cst[:, C_ID:C_ID + 128].unsqueeze(1).to_broadcast([128, 8, 128])
            def p4_load(dr_, c, b2):
                t0 = c * 128
                lq, lk, lkt, lvt, lg, lb = ld_q[b2], ld_k[b2], ld_kt[b2], ld_vt[b2], ld_g[b2], ld_b[b2]
                K = lambda n: (n, b2)
                fw.dma("sp", lambda h, lq=lq, t0=t0: h.dma_start(out=lq[:], in_=qn[:, t0:t0 + 128].rearrange("(h p) t -> p h t", p=128)), reads=["qkn"], writes=[K("ldq")])
                fw.dma("sp", lambda h, lk=lk, t0=t0: h.dma_start(out=lk[:], in_=kn[:, t0:t0 + 128].rearrange("(h p) t -> p h t", p=128)), reads=["qkn"], writes=[K("ldk")])
                fw.dma("sp", lambda h, lkt=lkt, t0=t0: h.dma_start(out=lkt[:], in_=k_tm[t0:t0 + 128, :].rearrange("p (h c) -> p h c", h=8)), reads=["kv_tm"], writes=[K("ldkt")])
                fw.dma("sp", lambda h, lvt=lvt, t0=t0: h.dma_start(out=lvt[:], in_=v_tm[t0:t0 + 128, :].rearrange("p (h c) -> p h c", h=8)), reads=["kv_tm"], writes=[K("ldvt")])
                fw.dma("sp", lambda h, lg=lg, c=c, dr_=dr_: h.dma_start(out=lg[:].rearrange("p h t -> p (h t)"),
                                                                      in_=GB[dr_, 0, c:c + 1, :, :].rearrange("c h t -> c (h t)").partition_broadcast(128)),
                       reads=["GB"], writes=[K("ldg")])
                fw.dma("sp", lambda h, lb=lb, c=c, dr_=dr_: h.dma_start(out=lb[:].rearrange("p h t -> p (h t)"),
                                                                      in_=GB[dr_, 1, c:c + 1, :, :].rearrange("c h t -> c (h t)").partition_broadcast(128)),
                       reads=["GB"], writes=[K("ldb")])

            seq_all = [(0, c) for c in range(64)] + [(1, c) for c in range(63, -1, -1)]
            p4_load(0, 0, 0)
            it = 0
            for dr_ in range(2):
                fw.op("pool", lambda h: h.memset(S_[:], 0.0), writes=[("S_", 0), ("S_", 1)])
                fw.op("pool", lambda h: h.memset(Sbf[:], 0.0), writes=[("Sbf", 0), ("Sbf", 1)])
                order = list(range(64)) if dr_ == 0 else list(range(63, -1, -1))
                m_p1 = C_GT if dr_ == 0 else C_LT
                m_q1 = C_LT if dr_ == 0 else C_GT
                m_qk = C_LE if dr_ == 0 else C_GE
                last = 127 if dr_ == 0 else 0
                for c in order:
                    b2 = it % 2
                    it += 1
                    if it < 128:
                        p4_load(seq_all[it][0], seq_all[it][1], it % 2)
                    t0 = c * 128
                    lq, lk, lkt, lvt, lg, lb = ld_q[b2], ld_k[b2], ld_kt[b2], ld_vt[b2], ld_g[b2], ld_b[b2]
                    K = lambda n: (n, b2)
                    Gp = Gtm[:, c, dr_ * 8:dr_ * 8 + 8]
                    Bp = Btm[:, c, dr_ * 8:dr_ * 8 + 8]
                    Gpb = Gp.unsqueeze(2).to_broadcast([128, 8, 128])
                    Bpb = Bp.unsqueeze(2).to_broadcast([128, 8, 128])
                    HV = (slice(0, 4), slice(4, 8))
                    lm = lambda k, up: lvl[:, (2 * k + up) * 128:(2 * k + up + 1) * 128].unsqueeze(1).to_broadcast([128, 4, 128])
                    lmu = lambda k, up: lvu[:, (2 * k + up) * 128:(2 * k + up + 1) * 128].unsqueeze(1).to_broadcast([128, 4, 128])
                    mk4 = lambda off: cst[:, off:off + 128].unsqueeze(1).to_broadcast([128, 4, 128])
                    id4 = cst[:, C_ID:C_ID + 128].unsqueeze(1).to_broadcast([128, 4, 128])
                    dsel = 0 if dr_ == 0 else 1
                    seg = c // 16

                    def H(n, hf):
                        return (n, hf)

                    def pv(PX, hf):
                        return v3(PX)[:, HV[hf], :]

                    def both(fn):
                        for hf in range(2):
                            fn(hf, HV[hf])

                    def s_cast(hf, hs):
                        fw.op("act", lambda h: h.activation(out=kbf[:, hs, :], in_=lk[:, hs, :], func=AF.Copy), reads=[K("ldk")], writes=[H("kbf", hf)])
                        fw.op("act", lambda h: h.activation(out=qbf[:, hs, :], in_=lq[:, hs, :], func=AF.Copy), reads=[K("ldq")], writes=[H("qbf", hf)])

                        def kk(h):
                            for hd in range(hs.start, hs.stop):
                                h.matmul(PA[:, hd * 128:(hd + 1) * 128], lhsT=kbf[:, hd, :], rhs=kbf[:, hd, :], start=True, stop=True)
                                ins = h.matmul(PB[:, hd * 128:(hd + 1) * 128], lhsT=kbf[:, hd, :], rhs=qbf[:, hd, :], start=True, stop=True)
                            return ins
                        fw.op("pe", kk, reads=[H("kbf", hf), H("qbf", hf)], writes=[H("PA", hf), H("PB", hf)])
                    both(s_cast)

                    def s_dec1(hf, hs):
                        Gpb4 = Gp[:, hs].unsqueeze(2).to_broadcast([128, 4, 128])
                        fw.op("dve", lambda h: h.tensor_tensor(out=f1[:, hs, :], in0=lg[:, hs, :], in1=Gpb4, op=ALU.subtract), reads=[K("ldg"), "Gtm"], writes=[H("f1", hf)])
                        fw.op("pool", lambda h: h.tensor_scalar(out=f2[:, hs, :], in0=f1[:, hs, :], scalar1=0.0, scalar2=None, op0=ALU.max), reads=[H("f1", hf)], writes=[H("f2", hf)])
                        fw.op("act", lambda h: h.activation(out=f2[:, hs, :], in_=f2[:, hs, :], func=AF.Exp, scale=-1.0), reads=[H("f2", hf)], writes=[H("f2", hf)])
                        fw.op("pool", lambda h: h.tensor_scalar(out=f3[:, hs, :], in0=f1[:, hs, :], scalar1=0.0, scalar2=None, op0=ALU.min), reads=[H("f1", hf)], writes=[H("f3", hf)])
                        fw.op("act", lambda h: h.activation(out=f3[:, hs, :], in_=f3[:, hs, :], func=AF.Exp), reads=[H("f3", hf)], writes=[H("f3", hf)])
                    both(s_dec1)

                    def s_dec2(hf, hs):
                        Bpb4 = Bp[:, hs].unsqueeze(2).to_broadcast([128, 4, 128])
                        fw.op("pool", lambda h: h.tensor_tensor(out=f2[:, hs, :], in0=f2[:, hs, :], in1=mk4(m_p1), op=ALU.mult), reads=[H("f2", hf), "cst"], writes=[H("f2", hf)])
                        fw.op("dve", lambda h: h.scalar_tensor_tensor(out=f2[:, hs, :], in0=f2[:, hs, :], scalar=-1.0, in1=Bpb4, op0=ALU.mult, op1=ALU.mult),
                              reads=[H("f2", hf), "Btm"], writes=[H("f2", hf)])
                        fw.op("dve", lambda h: h.tensor_tensor(out=P1b[:, hs, :], in0=pv(PA, hf), in1=f2[:, hs, :], op=ALU.mult), reads=[H("PA", hf), H("f2", hf)], writes=[H("P1b", hf)])
                        fw.op("dve", lambda h: h.scalar_tensor_tensor(out=f4[:, hs, :], in0=f3[:, hs, :], scalar=-1.0, in1=lb[:, hs, :], op0=ALU.mult, op1=ALU.mult),
                              reads=[H("f3", hf), K("ldb")], writes=[H("f4", hf)])
                        fw.op("pool", lambda h: h.tensor_tensor(out=f4[:, hs, :], in0=f4[:, hs, :], in1=mk4(m_q1), op=ALU.mult), reads=[H("f4", hf), "cst"], writes=[H("f4", hf)])
                        fw.op("dve", lambda h: h.tensor_tensor(out=Q1b[:, hs, :], in0=pv(PA, hf), in1=f4[:, hs, :], op=ALU.mult), reads=[H("PA", hf), H("f4", hf)], writes=[H("Q1b", hf)])
                        fw.op("pool", lambda h: h.tensor_tensor(out=f3[:, hs, :], in0=f3[:, hs, :], in1=mk4(m_qk), op=ALU.mult), reads=[H("f3", hf), "cst", H("f4", hf)], writes=[H("f3", hf)])
                        fw.op("dve", lambda h: h.tensor_tensor(out=qkm[:, hs, :], in0=pv(PB, hf), in1=f3[:, hs, :], op=ALU.mult), reads=[H("PB", hf), H("f3", hf)], writes=[H("qkm", hf)])
                    both(s_dec2)

                    def s_lvl0(hf, hs):
                        fw.op("pool", lambda h: h.tensor_tensor(out=Dm[:, hs, :], in0=P1b[:, hs, :], in1=lm(0, dsel), op=ALU.mult), reads=[H("P1b", hf), "lvl"], writes=[H("Dm", hf)])
                        fw.op("pool", lambda h: h.tensor_tensor(out=Dm[:, hs, :], in0=Dm[:, hs, :], in1=id4, op=ALU.add), reads=[H("Dm", hf), "cst"], writes=[H("Dm", hf)])
                        fw.op("pool", lambda h: h.tensor_tensor(out=Em[:, hs, :], in0=Q1b[:, hs, :], in1=lm(0, 1 - dsel), op=ALU.mult), reads=[H("Q1b", hf), "lvl"], writes=[H("Em", hf)])
                        fw.op("pool", lambda h: h.tensor_tensor(out=Em[:, hs, :], in0=Em[:, hs, :], in1=id4, op=ALU.add), reads=[H("Em", hf), "cst"], writes=[H("Em", hf)])
                    both(s_lvl0)

                    fw.op("act", lambda h: h.activation(out=f5[:], in_=lg[:], func=AF.Exp), reads=[K("ldg")], writes=["f5"])
                    fw.op("pool", lambda h: h.tensor_tensor(out=qgT[:], in0=lq[:], in1=f5[:], op=ALU.mult), reads=[K("ldq"), "f5"], writes=[H("qgT", 0), H("qgT", 1)])
                    fw.op("act", lambda h: h.activation(out=sm[:, :, 0], in_=Gp, func=AF.Exp), reads=["Gtm"], writes=["sm0"])
                    fw.op("dve", lambda h: h.tensor_tensor(out=sm[:, :, 0], in0=sm[:, :, 0], in1=Bp, op=ALU.mult), reads=["sm0", "Btm"], writes=["sm0"])
                    fw.op("dve", lambda h: h.tensor_tensor(out=sm[:, :, 1], in0=lg[:, :, last], in1=Gp, op=ALU.subtract), reads=[K("ldg"), "Gtm"], writes=["sm1"])
                    fw.op("act", lambda h: h.activation(out=sm[:, :, 1], in_=sm[:, :, 1], func=AF.Exp), reads=["sm1"], writes=["sm1"])
                    fw.op("act", lambda h: h.activation(out=sm[:, :, 2], in_=lg[:, :, last], func=AF.Exp), reads=[K("ldg")], writes=["sm2"])
                    smb = lambda i: sm[:, :, i:i + 1].to_broadcast([128, 8, 128])
                    fw.op("pool", lambda h: h.tensor_tensor(out=kbg[:], in0=lkt[:], in1=smb(0), op=ALU.mult), reads=[K("ldkt"), "sm0"], writes=[H("kbg", 0), H("kbg", 1)])
                    fw.op("pool", lambda h: h.tensor_tensor(out=vb[:], in0=lvt[:], in1=Bpb, op=ALU.mult), reads=[K("ldvt"), "Btm"], writes=[H("vb", 0), H("vb", 1)])
                    fw.op("pool", lambda h: h.tensor_tensor(out=kd[:], in0=lkt[:], in1=smb(1), op=ALU.mult), reads=[K("ldkt"), "sm1"], writes=[H("kd", 0), H("kd", 1)])

                    for k in range(1, 7):
                        def s_x(hf, hs):
                            def xmm(h):
                                for hd in range(hs.start, hs.stop):
                                    h.matmul(PC[:, hd * 128:(hd + 1) * 128], lhsT=Q1b[:, hd, :], rhs=Dm[:, hd, :], start=True, stop=True)
                                    ins = h.matmul(PD[:, hd * 128:(hd + 1) * 128], lhsT=P1b[:, hd, :], rhs=Em[:, hd, :], start=True, stop=True)
                                return ins
                            fw.op("pe", xmm, reads=[H("Q1b", hf), H("P1b", hf), H("Dm", hf), H("Em", hf)], writes=[H("PC", hf), H("PD", hf)])
                            fw.op("act", lambda h: h.activation(out=Xb[:, hs, :], in_=pv(PC, hf), func=AF.Copy), reads=[H("PC", hf)], writes=[H("Xb", hf)])
                            fw.op("act", lambda h: h.activation(out=X2b[:, hs, :], in_=pv(PD, hf), func=AF.Copy), reads=[H("PD", hf)], writes=[H("X2b", hf)])
                        both(s_x)

                        def s_z(hf, hs):
                            def zmm(h):
                                for hd in range(hs.start, hs.stop):
                                    h.matmul(PA[:, hd * 128:(hd + 1) * 128], lhsT=Em[:, hd, :], rhs=Xb[:, hd, :], start=True, stop=True)
                                    ins = h.matmul(PB[:, hd * 128:(hd + 1) * 128], lhsT=Dm[:, hd, :], rhs=X2b[:, hd, :], start=True, stop=True)
                                return ins
                            fw.op("pe", zmm, reads=[H("Em", hf), H("Dm", hf), H("Xb", hf), H("X2b", hf)], writes=[H("PA", hf), H("PB", hf)])
                            fw.op("dve", lambda h: h.copy_predicated(out=Dm[:, hs, :], mask=lmu(k, dsel), data=pv(PA, hf)), reads=[H("PA", hf), "lvu", H("Dm", hf)], writes=[H("Dm", hf)])
                            fw.op("dve", lambda h: h.copy_predicated(out=Em[:, hs, :], mask=lmu(k, 1 - dsel), data=pv(PB, hf)), reads=[H("PB", hf), "lvu", H("Em", hf)], writes=[H("Em", hf)])
                        both(s_z)

                    def s_u(hf, hs):
                        def umm(h):
                            for hd in range(hs.start, hs.stop):
                                h.matmul(PB[:, hd * 128:(hd + 1) * 128], lhsT=Em[:, hd, :], rhs=vb[:, hd, :], start=True, stop=True)
                                ins = h.matmul(PA[:, hd * 128:(hd + 1) * 128], lhsT=kbg[:, hd, :], rhs=Em[:, hd, :], start=True, stop=True)
                            return ins
                        fw.op("pe", umm, reads=[H("Em", hf), H("vb", hf), H("kbg", hf)], writes=[H("PA", hf), H("PB", hf)])
                        fw.op("act", lambda h: h.activation(out=u_[:, hs, :], in_=pv(PB, hf), func=AF.Copy), reads=[H("PB", hf)], writes=[H("u_", hf)])
                        fw.op("dve", lambda h: h.tensor_copy(out=wT[:, hs, :], in_=pv(PA, hf)), reads=[H("PA", hf)], writes=[H("wT", hf)])
                    both(s_u)

                    def s_seq(hf, hs):
                        if dr_ == 0 and c % 16 == 0 and c > 0:
                            fw.op("dve", lambda h: h.tensor_scalar(out=S_[:, hs, :], in0=S_[:, hs, :], scalar1=sgf[:, seg:seg + 1], scalar2=None, op0=ALU.mult), reads=[H("S_", hf), "sgf"], writes=[H("S_", hf)])
                            fw.op("act", lambda h: h.activation(out=Sbf[:, hs, :], in_=S_[:, hs, :], func=AF.Copy), reads=[H("S_", hf)], writes=[H("Sbf", hf)])
                        if dr_ == 1 and c % 16 == 15 and c < 63:
                            fw.op("dve", lambda h: h.tensor_scalar(out=S_[:, hs, :], in0=S_[:, hs, :], scalar1=sgf[:, 4 + seg:5 + seg], scalar2=None, op0=ALU.mult), reads=[H("S_", hf), "sgf"], writes=[H("S_", hf)])
                            fw.op("act", lambda h: h.activation(out=Sbf[:, hs, :], in_=S_[:, hs, :], func=AF.Copy), reads=[H("S_", hf)], writes=[H("Sbf", hf)])

                        def wsmm(h):
                            for hd in range(hs.start, hs.stop):
                                ins = h.matmul(PC[:, hd * 128:(hd + 1) * 128], lhsT=wT[:, hd, :], rhs=Sbf[:, hd, :], start=True, stop=True)
                            return ins
                        fw.op("pe", wsmm, reads=[H("wT", hf), H("Sbf", hf)], writes=[H("PC", hf)])
                        fw.op("dve", lambda h: h.tensor_tensor(out=vnew[:, hs, :], in0=u_[:, hs, :], in1=pv(PC, hf), op=ALU.subtract), reads=[H("u_", hf), H("PC", hf)], writes=[H("vnew", hf)])

                        def omm(h):
                            for hd in range(hs.start, hs.stop):
                                h.matmul(PD[:, hd * 128:(hd + 1) * 128], lhsT=qgT[:, hd, :], rhs=Sbf[:, hd, :], start=True, stop=False)
                                h.matmul(PD[:, hd * 128:(hd + 1) * 128], lhsT=qkm[:, hd, :], rhs=vnew[:, hd, :], start=False, stop=True)
                                ins = h.matmul(PB[:, hd * 128:(hd + 1) * 128], lhsT=kd[:, hd, :], rhs=vnew[:, hd, :], start=True, stop=True)
                            return ins
                        fw.op("pe", omm, reads=[H("qgT", hf), H("Sbf", hf), H("qkm", hf), H("vnew", hf), H("kd", hf)], writes=[H("PD", hf), H("PB", hf)])
                        fw.op("act", lambda h: h.activation(out=ost[b2][:, hs, :], in_=pv(PD, hf), func=AF.Copy), reads=[H("PD", hf)], writes=[K("ost")])
                        fw.op("pool", lambda h: h.tensor_tensor(out=S_[:, hs, :], in0=S_[:, hs, :], in1=sm[:, hs, 2:3].to_broadcast([128, 4, 128]), op=ALU.mult), reads=[H("S_", hf), "sm2"], writes=[H("S_", hf)])
                        fw.op("dve", lambda h: h.tensor_tensor(out=S_[:, hs, :], in0=S_[:, hs, :], in1=pv(PB, hf), op=ALU.add), reads=[H("S_", hf), H("PB", hf)], writes=[H("S_", hf)])
                        fw.op("act", lambda h: h.activation(out=Sbf[:, hs, :], in_=S_[:, hs, :], func=AF.Copy), reads=[H("S_", hf)], writes=[H("Sbf", hf)])
                    both(s_seq)
                    fw.dma("sp", lambda h: h.dma_start(out=o_dir[dr_, t0:t0 + 128, :].rearrange("p (h c) -> p h c", h=8), in_=ost[b2][:]),
                           reads=[K("ost")], writes=["o_dir"])
            fw.barrier()

        fw.phase = "P5"
        with ExitStack() as st:
            kw = [SB(st, f"kw{i}", [128, 256], BF16) for i in range(4)]
            qw = [SB(st, f"qw{i}", [128, 128], BF16) for i in range(4)]
            vw = [SB(st, f"vw{i}", [128, 2, 128], BF16) for i in range(4)]
            sc_ = [SB(st, f"sc{i}", [128, 256]) for i in range(4)]
            pe_ = [SB(st, f"pe{i}", [128, 256], BF16) for i in range(4)]
            pT = [SB(st, f"pT{i}", [128, 2, 128], BF16) for i in range(4)]
            oo = [SB(st, f"oo{i}", [128, 130]) for i in range(4)]
            nmx = [SB(st, f"nmx{i}", [128, 1]) for i in range(4)]
            idb = SB(st, "idb", [128, 128], BF16)
            msk = SB(st, "msk", [128, 4, 3, 256])
            mk1 = SB(st, "mk1", [128, 1])
            psc_t = [PSM(st, f"psc{i}", [128, 512]) for i in range(2)]
            psc = [psc_t[i // 2][:, (i % 2) * 256:(i % 2 + 1) * 256] for i in range(4)]
            ppt_t = PSM(st, "ppt", [128, 4, 2, 128], BF16)
            ppt = [ppt_t[:, i, :, :] for i in range(4)]
            pov_t = PSM(st, "pov", [128, 4, 128])
            pov = [pov_t[:, i, :] for i in range(4)]
            fw.op("dve", lambda h: h.tensor_copy(out=idb[:], in_=ident), reads=["cst"], writes=["idb"])
            band = cst[:, C_BAND:C_BAND + 256]; negl = cst[:, C_NEGL:C_NEGL + 256]; negr = cst[:, C_NEGR:C_NEGR + 256]
            for s in range(NSEG):
                fw.op("dve", lambda h, s=s: h.tensor_scalar(out=mk1[:], in0=sgf[:, s:s + 1], scalar1=-1.0, scalar2=1.0, op0=ALU.mult, op1=ALU.add), reads=["sgf"], writes=["mk1"])
                fw.op("dve", lambda h, s=s: h.scalar_tensor_tensor(out=msk[:, s, 0, :], in0=negl, scalar=mk1[:, 0:1], in1=band, op0=ALU.mult, op1=ALU.add),
                      reads=["mk1", "cst"], writes=["msk"])
                fw.op("dve", lambda h, s=s: h.tensor_scalar(out=mk1[:], in0=sgf[:, 4 + s:5 + s], scalar1=-1.0, scalar2=1.0, op0=ALU.mult, op1=ALU.add), reads=["sgf", "msk"], writes=["mk1"])
                fw.op("dve", lambda h, s=s: h.scalar_tensor_tensor(out=msk[:, s, 1, :], in0=negr, scalar=mk1[:, 0:1], in1=band, op0=ALU.mult, op1=ALU.add),
                      reads=["mk1", "cst"], writes=["msk"])
                fw.op("dve", lambda h, s=s: h.scalar_tensor_tensor(out=msk[:, s, 2, :], in0=negr, scalar=mk1[:, 0:1], in1=msk[:, s, 0, :], op0=ALU.mult, op1=ALU.add),
                      reads=["mk1", "cst", "msk"], writes=["msk"])
            units = []
            for g in range(3):
                for hh in range(4):
                    for r in range(DIL[g]):
                        for b in range(MC[g] // 128):
                            units.append((g, hh, r, b))

            def u_load(it):
                g, hh, r, b = units[it]
                i2 = it % 4
                K = lambda n: (n, i2)
                kfull = akT[g][hh * 128:(hh + 1) * 128, :].rearrange("p (r c) -> p r c", c=CL[g])
                qfull = aqT[g][hh * 128:(hh + 1) * 128, :].rearrange("p (r c) -> p r c", c=CL[g])
                vfull = avs[g][:, hh * 128:(hh + 1) * 128].rearrange("(r c) d -> r c d", c=CL[g])
                fw.dma("sp", lambda h: h.dma_start(out=kw[i2][:], in_=kfull[:, r, 128 * b:128 * b + 256]), reads=[("aqk", g), ("akT", g)], writes=[K("kw")])
                fw.dma("sp", lambda h: h.dma_start(out=qw[i2][:], in_=qfull[:, r, 64 + 128 * b:64 + 128 * b + 128]), reads=[("aqk", g)], writes=[K("qw")])
                fw.dma("sp", lambda h: h.dma_start(out=vw[i2][:], in_=vfull[r, 128 * b:128 * b + 256, :].rearrange("(k p) d -> p k d", p=128)),
                       reads=[("av", g)], writes=[K("vw")])

            def u_comp(it):
                g, hh, r, b = units[it]
                dil = DIL[g]
                tps = MS[g] // 128
                i2 = it % 4
                K = lambda n: (n, i2)
                s = b // tps
                first = (b % tps == 0)
                lastt = (b % tps == tps - 1)
                fw.op("pe", lambda h: h.matmul(psc[i2], lhsT=qw[i2][:], rhs=kw[i2][:], start=True, stop=True), reads=[K("qw"), K("kw")], writes=[K("psc")])
                if first and lastt:
                    mask = msk[:, s, 2, :]
                elif first:
                    mask = msk[:, s, 0, :]
                elif lastt:
                    mask = msk[:, s, 1, :]
                else:
                    mask = band
                fw.op("dve", lambda h: h.tensor_tensor(out=sc_[i2][:], in0=psc[i2], in1=mask, op=ALU.add), reads=[K("psc"), "msk", "cst"], writes=[K("sc")])
                fw.op("dve", lambda h: h.reduce_max(out=oo[i2][:, 128:129], in_=sc_[i2][:], axis=AX.X), reads=[K("sc")], writes=[K("oo")])
                fw.op("dve", lambda h: h.tensor_scalar(out=nmx[i2][:], in0=oo[i2][:, 128:129], scalar1=-1.0, scalar2=None, op0=ALU.mult), reads=[K("oo")], writes=[K("nmx")])
                fw.op("act", lambda h: h.activation(out=pe_[i2][:], in_=sc_[i2][:], func=AF.Exp, bias=nmx[i2][:], scale=1.0, accum_out=oo[i2][:, 129:130]),
                      reads=[K("sc"), K("nmx")], writes=[K("pe"), K("oo")])

                def ptr_(h):
                    h.transpose(out=ppt[i2][:, 0, :], in_=pe_[i2][:, 0:128], identity=idb[:])
                    return h.transpose(out=ppt[i2][:, 1, :], in_=pe_[i2][:, 128:256], identity=idb[:])
                fw.op("pe", ptr_, reads=[K("pe"), "idb"], writes=[K("ppt")])
                fw.op("act", lambda h: h.activation(out=pT[i2][:], in_=ppt[i2], func=AF.Copy), reads=[K("ppt")], writes=[K("pT")])

                def pv(h):
                    h.matmul(pov[i2], lhsT=pT[i2][:, 0, :], rhs=vw[i2][:, 0, :], start=True, stop=False)
                    return h.matmul(pov[i2], lhsT=pT[i2][:, 1, :], rhs=vw[i2][:, 1, :], start=False, stop=True)
                fw.op("pe", pv, reads=[K("pT"), K("vw")], writes=[K("pov")])
                fw.op("dve", lambda h: h.tensor_copy(out=oo[i2][:, 0:128], in_=pov[i2]), reads=[K("pov")], writes=[K("oo")])
                dst = Oat[g, :, hh, :].rearrange("(m r) c -> r m c", r=dil)[r, 128 * b:128 * b + 128, :]
                fw.dma("pool", lambda h: h.dma_start(out=dst, in_=oo[i2][:]), reads=[K("oo")], writes=["Oat"])

            NU = len(units)
            for it in range(NU + 2):
                if it < NU:
                    u_load(it)
                if it >= 2:
                    u_comp(it - 2)
            fw.barrier()

        fw.phase = "P5b"
        with ExitStack() as st:
            of_ = [SB(st, f"of{i}", [128, 8, 128]) for i in range(2)]
            ob_ = [SB(st, f"ob{i}", [128, 8, 128]) for i in range(2)]
            szt = [SB(st, f"szt{i}", [128, 8, 128]) for i in range(2)]
            osq = SB(st, "osq", [128, 8, 128])
            ss8 = SB(st, "ss8", [128, 8])
            yd = [SB(st, f"yd{i}", [128, 8, 128], BF16) for i in range(2)]
            og = [[SB(st, f"og{g}_{i}", [128, 4, 130]) for i in range(2)] for g in range(3)]
            m4 = SB(st, "m4", [128, 4]); w4 = SB(st, "w4", [128, 3, 4]); d4 = SB(st, "d4", [128, 4]); t4 = SB(st, "t4", [128, 4])
            am = SB(st, "am", [128, 4, 128]); am2 = SB(st, "am2", [128, 4, 128])
            ya = [SB(st, f"ya{i}", [128, 4, 128], BF16) for i in range(2)]
            pdn = PSM(st, "pdn", [128, 1024])
            pat = PSM(st, "pat", [128, 512])
            for tl in range(64):
                t0 = tl * 128
                i2 = tl % 2
                K = lambda n: (n, i2)
                fw.dma("sp", lambda h, i2=i2, t0=t0: h.dma_start(out=of_[i2][:], in_=o_dir[0, t0:t0 + 128, :].rearrange("p (h c) -> p h c", h=8)), reads=["o_dir"], writes=[K("of")])
                fw.dma("sp", lambda h, i2=i2, t0=t0: h.dma_start(out=ob_[i2][:], in_=o_dir[1, t0:t0 + 128, :].rearrange("p (h c) -> p h c", h=8)), reads=["o_dir"], writes=[K("ob")])
                fw.dma("sp", lambda h, i2=i2, t0=t0: h.dma_start(out=szt[i2][:], in_=szT[:, t0:t0 + 128].rearrange("(h p) t -> p h t", p=128)), reads=["szT"], writes=[K("szt")])
                fw.op("pool", lambda h, i2=i2: h.tensor_tensor(out=of_[i2][:], in0=of_[i2][:], in1=ob_[i2][:], op=ALU.add), reads=[K("of"), K("ob")], writes=[K("of")])
                fw.op("pool", lambda h, i2=i2: h.tensor_tensor(out=osq[:], in0=of_[i2][:], in1=of_[i2][:], op=ALU.mult), reads=[K("of")], writes=["osq"])
                fw.op("dve", lambda h: h.tensor_reduce(out=ss8[:], in_=osq[:], axis=AX.X, op=ALU.add), reads=["osq"], writes=["ss8"])
                fw.op("act", lambda h: h.activation(out=ss8[:], in_=ss8[:], func=AF.Sqrt, bias=EPS, scale=1.0 / 128.0), reads=["ss8"], writes=["ss8"])
                fw.op("dve", lambda h: h.reciprocal(out=ss8[:], in_=ss8[:]), reads=["ss8"], writes=["ss8"])
                fw.op("dve", lambda h, i2=i2: h.tensor_tensor(out=of_[i2][:], in0=of_[i2][:], in1=ss8[:].unsqueeze(2).to_broadcast([128, 8, 128]), op=ALU.mult),
                      reads=[K("of"), "ss8"], writes=[K("of")])

                def trd(h, i2=i2):
                    for hd in range(8):
                        ins = h.transpose(out=pdn[:, hd * 128:(hd + 1) * 128], in_=of_[i2][:, hd, :], identity=ident)
                    return ins
                fw.op("pe", trd, reads=[K("of"), "cst"], writes=["pdn"])
                fw.op("dve", lambda h, i2=i2: h.scalar_tensor_tensor(out=yd[i2][:], in0=pdn[:, :].rearrange("p (h c) -> p h c", h=8), scalar=dnwT[:, 0:1], in1=szt[i2][:],
                                                                   op0=ALU.mult, op1=ALU.mult), reads=["pdn", "dnwT", K("szt")], writes=[K("yd")])
                fw.dma("sp", lambda h, i2=i2, t0=t0: h.dma_start(out=ydnT[:, t0:t0 + 128].rearrange("(h p) t -> p h t", p=128), in_=yd[i2][:]), reads=[K("yd")], writes=["ydnT"])
                for g in range(3):
                    fw.dma("sp", lambda h, g=g, i2=i2, t0=t0: h.dma_start(out=og[g][i2][:], in_=Oat[g, t0:t0 + 128, :, :]), reads=["Oat"], writes=[K(f"og{g}")])
                mxs = [og[g][i2][:, :, 128] for g in range(3)]
                dns = [og[g][i2][:, :, 129] for g in range(3)]
                fw.op("dve", lambda h: h.tensor_tensor(out=m4[:], in0=mxs[0], in1=mxs[1], op=ALU.max), reads=[K("og0"), K("og1")], writes=["m4"])
                fw.op("dve", lambda h: h.tensor_tensor(out=m4[:], in0=m4[:], in1=mxs[2], op=ALU.max), reads=["m4", K("og2")], writes=["m4"])
                for g in range(3):
                    fw.op("dve", lambda h, g=g: h.tensor_tensor(out=w4[:, g, :], in0=mxs[g], in1=m4[:], op=ALU.subtract), reads=[K(f"og{g}"), "m4"], writes=["w4"])
                fw.op("act", lambda h: h.activation(out=w4[:], in_=w4[:], func=AF.Exp), reads=["w4"], writes=["w4"])
                fw.op("dve", lambda h: h.tensor_tensor(out=d4[:], in0=w4[:, 0, :], in1=dns[0], op=ALU.mult), reads=["w4", K("og0")], writes=["d4"])
                for g in (1, 2):
                    fw.op("dve", lambda h, g=g: h.tensor_tensor(out=t4[:], in0=w4[:, g, :], in1=dns[g], op=ALU.mult), reads=["w4", K(f"og{g}")], writes=["t4"])
                    fw.op("dve", lambda h: h.tensor_tensor(out=d4[:], in0=d4[:], in1=t4[:], op=ALU.add), reads=["d4", "t4"], writes=["d4"])
                fw.op("dve", lambda h: h.reciprocal(out=d4[:], in_=d4[:]), reads=["d4"], writes=["d4"])
                for g in range(3):
                    fw.op("dve", lambda h, g=g: h.tensor_tensor(out=w4[:, g, :], in0=w4[:, g, :], in1=d4[:], op=ALU.mult), reads=["w4", "d4"], writes=["w4"])
                fw.op("pool", lambda h, i2=i2: h.tensor_tensor(out=am[:], in0=og[0][i2][:, :, 0:128], in1=w4[:, 0, :].unsqueeze(2).to_broadcast([128, 4, 128]), op=ALU.mult),
                      reads=[K("og0"), "w4"], writes=["am"])
                for g in (1, 2):
                    fw.op("pool", lambda h, g=g, i2=i2: h.tensor_tensor(out=am2[:], in0=og[g][i2][:, :, 0:128], in1=w4[:, g, :].unsqueeze(2).to_broadcast([128, 4, 128]), op=ALU.mult),
                          reads=[K(f"og{g}"), "w4"], writes=["am2"])
                    fw.op("pool", lambda h: h.tensor_tensor(out=am[:], in0=am[:], in1=am2[:], op=ALU.add), reads=["am", "am2"], writes=["am"])

                def tra(h):
                    for hd in range(4):
                        ins = h.transpose(out=pat[:, hd * 128:(hd + 1) * 128], in_=am[:, hd, :], identity=ident)
                    return ins
                fw.op("pe", tra, reads=["am", "cst"], writes=["pat"])
                fw.op("act", lambda h, i2=i2: h.activation(out=ya[i2][:], in_=pat[:, :].rearrange("p (h c) -> p h c", h=4), func=AF.Copy), reads=["pat"], writes=[K("ya")])
                fw.dma("sp", lambda h, i2=i2, t0=t0: h.dma_start(out=yatT[:, t0:t0 + 128].rearrange("(h p) t -> p h t", p=128), in_=ya[i2][:]), reads=[K("ya")], writes=["yatT"])
            fw.barrier()

        fw.phase = "P6"
        with ExitStack() as st:
            xin = [SB(st, f"xin{q}", [128, D]) for q in range(4)]
            xT = SB(st, "xT", [128, 16, 512])
            acc = SB(st, "acc", [128, 16, 512])
            sq = [SB(st, f"sq{i}", [128, 512]) for i in range(2)]
            rstd = SB(st, "rstd", [128, 512])
            big = SB(st, "big", [128, 32, 512], BF16)
            h2T = SB(st, "h2T", [128, 16, 512], BF16)
            wk = [SB(st, f"wk{i}", [128, 32, 128], BF16) for i in range(3)]
            print("sbuf remaining", nc.sbuf_bytes_remaining)
            tmp = [SB(st, f"tmp{i}", [128, 512]) for i in range(4)]
            pxt = [PSM(st, f"pxt{i}", [128, 512]) for i in range(2)]
            pss = PSM(st, "pss", [128, 512])
            pmm = [PSM(st, f"pmm{i}", [128, 512]) for i in range(4)]
            hT = big[:, 0:16, :]
            ydn_s = big[:, 16:24, :]
            yat_s = big[:, 24:28, :]
            mixedT = h2T

            wslot = [0]

            def load_w(src_ap, nj, key):
                i = wslot[0] % 3
                wslot[0] += 1
                fw.dma("sp", lambda h, i=i: h.dma_start(out=wk[i][:, 0:nj, :], in_=src_ap), reads=[key], writes=[("wk", i)])
                return i

            def proj(widx, nj, rhs_fn, rhs_keys, pbank, first=True, last=True, j0=0):
                def mm(h):
                    for j in range(nj):
                        ins = h.matmul(pmm[pbank][:, :], lhsT=wk[widx][:, j, :], rhs=rhs_fn(j),
                                       start=(first and j == 0), stop=(last and j == nj - 1))
                    return ins
                fw.op("pe", mm, reads=[("wk", widx)] + rhs_keys, writes=[("pmm", pbank)])

            for tile in range(min(T // 512, KTILES)):
                t0 = tile * 512
                s = tile // 4
                front_end((xin, xT, sq, rstd, pxt, pss), t0, s)
                for j in range(16):
                    fw.op("pool", lambda h, j=j: h.tensor_tensor(out=tmp[j % 2][:], in0=xT[:, j, :], in1=rstd[:], op=ALU.mult),
                          reads=[("xT", j), "rstd"], writes=[("tmp", j % 2)])
                    fw.op("dve", lambda h, j=j: h.tensor_scalar(out=hT[:, j, :], in0=tmp[j % 2][:], scalar1=A1[:, j, s:s + 1],
                                                                scalar2=modT[:, j, s:s + 1], op0=ALU.mult, op1=ALU.add),
                          reads=[("tmp", j % 2), "A1", "modT"], writes=[("big", j)])
                fw.dma("sp", lambda h: h.dma_start(out=ydn_s, in_=ydnT[:, t0:t0 + 512].rearrange("(j p) t -> p j t", p=128)),
                       reads=["ydnT"], writes=[("big", 16 + j) for j in range(8)])
                fw.dma("sp", lambda h: h.dma_start(out=yat_s, in_=yatT[:, t0:t0 + 512].rearrange("(j p) t -> p j t", p=128)),
                       reads=["yatT"], writes=[("big", 24 + j) for j in range(4)])
                for cb in range(16):
                    w1 = load_w(wb_mg[cb], 16, ("wb_mg", cb))
                    proj(w1, 16, lambda j: hT[:, j, :], [("big", j) for j in range(16)], 0)
                    w2 = load_w(wb_mg[16 + cb], 16, ("wb_mg", 16 + cb))
                    proj(w2, 16, lambda j: hT[:, j, :], [("big", j) for j in range(16)], 1)
                    w3 = load_w(wb_dn[cb], 8, ("wb_dn", cb))
                    proj(w3, 8, lambda j: ydn_s[:, j, :], [("big", 16 + j) for j in range(8)], 2)
                    w4 = load_w(wb_at[cb], 4, ("wb_at", cb))
                    proj(w4, 4, lambda j: yat_s[:, j, :], [("big", 24 + j) for j in range(4)], 3)
                    fw.op("act", lambda h: h.activation(out=tmp[0][:], in_=pmm[0][:, :], func=AF.Sigmoid), reads=[("pmm", 0)], writes=[("tmp", 0)])
                    fw.op("act", lambda h: h.activation(out=tmp[1][:], in_=pmm[1][:, :], func=AF.Sigmoid), reads=[("pmm", 1)], writes=[("tmp", 1)])
                    fw.op("dve", lambda h: h.tensor_tensor(out=tmp[2][:], in0=pmm[2][:, :], in1=tmp[0][:], op=ALU.mult),
                          reads=[("pmm", 2), ("tmp", 0)], writes=[("tmp", 2)])
                    fw.op("dve", lambda h: h.tensor_tensor(out=tmp[3][:], in0=pmm[3][:, :], in1=tmp[1][:], op=ALU.mult),
                          reads=[("pmm", 3), ("tmp", 1)], writes=[("tmp", 3)])
                    fw.op("pool", lambda h, cb=cb: h.tensor_tensor(out=mixedT[:, cb, :], in0=tmp[2][:], in1=tmp[3][:], op=ALU.add),
                          reads=[("tmp", 2), ("tmp", 3)], writes=[("h2T", cb)])
                for cb in range(16):
                    w1 = load_w(wb_out[cb], 16, ("wb_out", cb))
                    proj(w1, 16, lambda j: mixedT[:, j, :], [("h2T", j) for j in range(16)], cb % 4)
                    fw.op("act", lambda h, cb=cb: h.activation(out=acc[:, cb, :], in_=pmm[cb % 4][:, :], func=AF.Copy),
                          reads=[("pmm", cb % 4)], writes=[("acc", cb)])
                sumsq_rstd(lambda j: acc[:, j, :], 16, sq, pss, rstd, lambda j: ("acc", j))
                for j in range(16):
                    fw.op("pool", lambda h, j=j: h.tensor_tensor(out=tmp[j % 2][:], in0=acc[:, j, :], in1=rstd[:], op=ALU.mult),
                          reads=[("acc", j), "rstd"], writes=[("tmp", j % 2)])
                    fw.op("dve", lambda h, j=j: h.scalar_tensor_tensor(out=xT[:, j, :], in0=tmp[j % 2][:], scalar=G1[:, j, s:s + 1], in1=xT[:, j, :],
                                                                       op0=ALU.mult, op1=ALU.add),
                          reads=[("tmp", j % 2), "G1", ("xT", j)], writes=[("xT", j)])
                sumsq_rstd(lambda j: xT[:, j, :], 16, sq, pss, rstd, lambda j: ("xT", j))
                for j in range(16):
                    fw.op("pool", lambda h, j=j: h.tensor_tensor(out=tmp[j % 2][:], in0=xT[:, j, :], in1=rstd[:], op=ALU.mult),
                          reads=[("xT", j), "rstd"], writes=[("tmp", j % 2)])
                    fw.op("dve", lambda h, j=j: h.tensor_scalar(out=h2T[:, j, :], in0=tmp[j % 2][:], scalar1=A2[:, j, s:s + 1],
                                                                scalar2=modT[:, 48 + j, s:s + 1], op0=ALU.mult, op1=ALU.add),
                          reads=[("tmp", j % 2), "A2", "modT"], writes=[("h2T", j)])
                for hf in range(2):
                    for fb in range(32):
                        w1 = load_w(wb_f1[hf * 32 + fb], 16, ("wb_f1", hf * 32 + fb))
                        pb = fb % 4
                        proj(w1, 16, lambda j: h2T[:, j, :], [("h2T", j) for j in range(16)], pb)
                        if fb % 2 == 0:
                            fw.op("act", lambda h, pb=pb: h.activation(out=tmp[pb][:], in_=pmm[pb][:, :], func=AF.Relu), reads=[("pmm", pb)], writes=[("tmp", pb)])
                            fw.op("pool", lambda h, pb=pb, fb=fb: h.tensor_tensor(out=big[:, fb, :], in0=tmp[pb][:], in1=tmp[pb][:], op=ALU.mult),
                                  reads=[("tmp", pb)], writes=[("big", fb)])
                        else:
                            fw.op("dve", lambda h, pb=pb: h.tensor_scalar(out=tmp[pb][:], in0=pmm[pb][:, :], scalar1=0.0, scalar2=None, op0=ALU.max),
                                  reads=[("pmm", pb)], writes=[("tmp", pb)])
                            fw.op("pool", lambda h, pb=pb, fb=fb: h.tensor_tensor(out=big[:, fb, :], in0=tmp[pb][:], in1=tmp[pb][:], op=ALU.mult),
                                  reads=[("tmp", pb)], writes=[("big", fb)])
                    for cb in range(16):
                        w1 = load_w(wb_f2[hf * 16 + cb], 32, ("wb_f2", hf * 16 + cb))
                        pb = cb % 4
                        proj(w1, 32, lambda j: big[:, j, :], [("big", j) for j in range(32)], pb)
                        if hf == 0:
                            fw.op("act", lambda h, cb=cb, pb=pb: h.activation(out=acc[:, cb, :], in_=pmm[pb][:, :], func=AF.Copy),
                                  reads=[("pmm", pb)], writes=[("acc", cb)])
                        else:
                            fw.op("dve", lambda h, cb=cb, pb=pb: h.tensor_tensor(out=acc[:, cb, :], in0=pmm[pb][:, :], in1=acc[:, cb, :], op=ALU.add),
                                  reads=[("pmm", pb), ("acc", cb)], writes=[("acc", cb)])
                sumsq_rstd(lambda j: acc[:, j, :], 16, sq, pss, rstd, lambda j: ("acc", j))
                for j in range(16):
                    fw.op("pool", lambda h, j=j: h.tensor_tensor(out=tmp[j % 2][:], in0=acc[:, j, :], in1=rstd[:], op=ALU.mult),
                          reads=[("acc", j), "rstd"], writes=[("tmp", j % 2)])
                    fw.op("dve", lambda h, j=j: h.scalar_tensor_tensor(out=acc[:, j, :], in0=tmp[j % 2][:], scalar=G2[:, j, s:s + 1], in1=xT[:, j, :],
                                                                       op0=ALU.mult, op1=ALU.add),
                          reads=[("tmp", j % 2), "G2", ("xT", j)], writes=[("acc", j)])
                for q in range(4):
                    for jg in range(4):
                        pb = pmm[(q * 4 + jg) % 4]

                        def tr(h, q=q, jg=jg, pb=pb):
                            for jj in range(4):
                                j = jg * 4 + jj
                                ins = h.transpose(out=pb[:, jj * 128:(jj + 1) * 128], in_=acc[:, j, q * 128:(q + 1) * 128], identity=ident)
                            return ins
                        fw.op("pe", tr, reads=[("acc", jg * 4 + jj) for jj in range(4)] + ["cst"], writes=[("pmm", (q * 4 + jg) % 4)])
                        eng = "act" if jg % 2 == 0 else "dve"
                        if eng == "act":
                            fw.op("act", lambda h, q=q, jg=jg, pb=pb: h.activation(out=xin[q][:, jg * 512:(jg + 1) * 512], in_=pb[:, :], func=AF.Copy),
                                  reads=[("pmm", (q * 4 + jg) % 4)], writes=[("xin", q)])
                        else:
                            fw.op("dve", lambda h, q=q, jg=jg, pb=pb: h.tensor_copy(out=xin[q][:, jg * 512:(jg + 1) * 512], in_=pb[:, :]),
                                  reads=[("pmm", (q * 4 + jg) % 4)], writes=[("xin", q)])
                    fw.dma("sp", lambda h, q=q: h.dma_start(out=y[t0 + q * 128:t0 + (q + 1) * 128, :], in_=xin[q][:]),
                           reads=[("xin", q)], writes=["y"])
            fw.barrier()
        fw.emit_all()
    return nc


_NC_CACHE = {}

SAMPLE_MAP = {2: [0, 1, 2], 3: [3, 4, 5], 4: [6, 7, 8], 5: [9, 10, 11], 6: [12, 13], 7: [14, 15]}


def kernel(x_prompt, x_sample, c_prompt, c_sample, w_ada, b_ada, norm_pre_mix, norm_post_mix,
           norm_pre_ffn, norm_post_ffn, w_in, conv_w, A_log, dt_bias, dn_norm_w, w_dn_out,
           w_at_out, w_out, w_ff1, w_ff2):
    f = lambda a: np.ascontiguousarray(np.asarray(a, dtype=np.float32))
    x_prompt, x_sample, c_prompt, c_sample = f(x_prompt), f(x_sample), f(c_prompt), f(c_sample)
    if "nc" not in _NC_CACHE:
        _NC_CACHE["nc"] = build_program()
    nc = _NC_CACHE["nc"]
    shared = {
        "consts": make_consts(), "lvlmask": make_lvlmask(),
        "w_ada": f(w_ada)[0], "b_ada": f(b_ada)[0].reshape(96, 128),
        "norms": np.concatenate([f(norm_pre_mix)[0], f(norm_post_mix)[0], f(norm_pre_ffn)[0], f(norm_post_ffn)[0]]).reshape(64, 128),
        "w_in": f(w_in)[0], "conv_w": f(conv_w)[0].reshape(120, 128), "A_log": f(A_log)[0].reshape(1, 16),
        "dt_bias": f(dt_bias)[0].reshape(1, 16), "dn_norm_w": f(dn_norm_w)[0].reshape(1, 128),
        "w_dn_out": f(w_dn_out)[0], "w_at_out": f(w_at_out)[0], "w_out": f(w_out)[0],
        "w_ff1": f(w_ff1)[0], "w_ff2": f(w_ff2)[0],
    }
    in_maps = []
    for core in range(8):
        xs = np.zeros((T, D), np.float32)
        cs = np.zeros((NSEG, D), np.float32)
        sg = np.zeros((128, 16), np.float32)
        if core < 2:
            xs[:] = x_prompt[core]
            cs[:] = c_prompt[core][None, :]
            for s in range(NSEG):
                sg[:, s] = 1.0 if s > 0 else 0.0
                sg[:, 4 + s] = 1.0 if s < NSEG - 1 else 0.0
                sg[:, 8 + s] = s * SEG
        else:
            seqs = SAMPLE_MAP[core]
            for s in range(NSEG):
                b = seqs[s] if s < len(seqs) else seqs[0]
                xs[s * SEG:(s + 1) * SEG] = x_sample[b]
                cs[s] = c_sample[b]
        m = dict(shared)
        m["x"] = xs
        m["c"] = cs.reshape(NSEG * NJ, 128)
        m["segf"] = sg
        in_maps.append(m)
    if KSCOPES:
        _NC_CACHE["in_maps"] = in_maps
        res = run_bass_kernel_spmd(nc, in_maps, core_ids=list(range(8)), trace=True)
        _NC_CACHE["res"] = res
    else:
        res = run_bass_kernel_spmd(nc, in_maps, core_ids=list(range(8)))
    if KDEBUG:
        _NC_CACHE["res"] = res
    y_prompt = np.stack([np.asarray(res.results[c]["y"], dtype=np.float32) for c in range(2)])
    y_sample = np.zeros_like(x_sample)
    for core, seqs in SAMPLE_MAP.items():
        yc = np.asarray(res.results[core]["y"], dtype=np.float32)
        for s, b in enumerate(seqs):
            y_sample[b] = yc[s * SEG:(s + 1) * SEG]
    return (y_prompt, y_sample)
```

```python
from contextlib import ExitStack
import numpy as np
import concourse.bass as bass
import concourse.mybir as mybir
from concourse.bass_utils import run_bass_kernel_spmd

F32 = mybir.dt.float32
BF16 = mybir.dt.bfloat16
I32 = mybir.dt.int32
AF = mybir.ActivationFunctionType
ALU = mybir.AluOpType
AX = mybir.AxisListType

D = 2048
NJ = 16
T = 8192
NSEG = 4
SEG = 2048
DFF = 8192
EPS = 1e-6
IN_COLS = 12832
NEG = -1.0e30

import os
KDEBUG = int(os.environ.get("KDEBUG", "0"))
KTILES = int(os.environ.get("KTILES", "16"))
KDUMP = os.environ.get("KDUMP", "").split(",")
KSCOPES = int(os.environ.get("KSCOPES", "0"))
COMPUTE = ("pe", "act", "dve", "pool")
N_DMA_SEMS = 12


class Ticket:
    __slots__ = ("kind", "eng", "val", "sem")

    def __init__(self, kind, eng, val, sem=None):
        self.kind, self.eng, self.val, self.sem = kind, eng, val, sem


class _Rec:
    def __init__(self):
        self.calls = []

    def __getattr__(self, name):
        def f(*a, **k):
            self.calls.append((name, a, k))
            return self
        return f


def _replay(h, calls):
    ins = None
    for name, a, k in calls:
        ins = getattr(h, name)(*a, **k)
    return ins


class FW:
    def __init__(self, nc, es):
        self.nc = nc
        self.streams = {k: [] for k in ("pe", "act", "dve", "pool", "sp")}
        self.sem = {}
        self.count = {}
        for k in COMPUTE:
            self.sem[k] = es.enter_context(nc.semaphore("s_" + k))
            self.count[k] = 0
        self.dsem, self.dcount, self.dnext = {}, {}, {}
        for q in ("sp", "act", "pool"):
            self.dsem[q] = [es.enter_context(nc.semaphore(f"d_{q}{i}")) for i in range(N_DMA_SEMS)]
            self.dcount[q] = [0] * N_DMA_SEMS
            self.dnext[q] = 0
        self.known = {k: {} for k in self.streams}
        self.lastw = {}
        self.readers = {}
        self.n_ops = 0
        self.phase = "P0"

    def _need(self, stream, t, waits):
        if t is None:
            return
        if t.kind == "c":
            if t.eng == stream and stream == "pe":
                return
            key = ("c", t.eng)
            sem = self.sem[t.eng]
        else:
            key = ("d", id(t.sem))
            sem = t.sem
        if self.known[stream].get(key, 0) >= t.val:
            return
        cur = waits.get(key)
        if cur is None or cur[1] < t.val:
            waits[key] = (sem, t.val)

    def _deps(self, stream, reads, writes):
        waits = {}
        for r in reads:
            self._need(stream, self.lastw.get(r), waits)
        for w in writes:
            self._need(stream, self.lastw.get(w), waits)
            for t in self.readers.get(w, {}).values():
                self._need(stream, t, waits)
        out = []
        for key, (sem, val) in waits.items():
            self.known[stream][key] = val
            out.append((sem, val))
        return out

    def _commit(self, t, reads, writes):
        for w in writes:
            self.lastw[w] = t
            self.readers[w] = {}
        k = ("c", t.eng) if t.kind == "c" else ("d", id(t.sem))
        for r in reads:
            self.readers.setdefault(r, {})[k] = t

    def op(self, eng, fn, reads=(), writes=()):
        waits = self._deps(eng, reads, writes)
        self.count[eng] += 1
        val = self.count[eng]
        sem = self.sem[eng]

        rec = _Rec()
        fn(rec)
        calls = rec.calls

        def emit(h, waits=waits, calls=calls, sem=sem):
            for s, v in waits:
                h.wait_ge(s, v)
            _replay(h, calls).then_inc(sem, 1)

        emit.phase = self.phase
        self.streams[eng].append(emit)
        self._commit(Ticket("c", eng, val), reads, writes)
        self.n_ops += 1

    def dma(self, q, fn, reads=(), writes=()):
        i = self.dnext[q]
        self.dnext[q] = (i + 1) % N_DMA_SEMS
        sem = self.dsem[q][i]
        prev = self.dcount[q][i]
        waits = self._deps(q, reads, writes)
        key = ("d", id(sem))
        if prev > 0 and self.known[q].get(key, 0) < prev:
            waits.append((sem, prev))
            self.known[q][key] = prev
        val = prev + 16
        self.dcount[q][i] = val

        rec = _Rec()
        fn(rec)
        calls = rec.calls

        def emit(h, waits=waits, calls=calls, sem=sem):
            for s, v in waits:
                h.wait_ge(s, v)
            _replay(h, calls).then_inc(sem, 16)

        emit.phase = self.phase
        self.streams[q].append(emit)
        self._commit(Ticket("d", q, val, sem), reads, writes)
        self.n_ops += 1

    def barrier(self):
        targets = [(self.sem[k], self.count[k], ("c", k)) for k in COMPUTE if self.count[k] > 0]
        for q in self.dsem:
            for i, s in enumerate(self.dsem[q]):
                if self.dcount[q][i] > 0:
                    targets.append((s, self.dcount[q][i], ("d", id(s))))
        for stream in self.streams:
            ws = []
            for s, v, key in targets:
                if self.known[stream].get(key, 0) < v:
                    ws.append((s, v))
                    self.known[stream][key] = v

            def emit(h, ws=ws):
                for s, v in ws:
                    h.wait_ge(s, v)

            self.streams[stream].append(emit)
        self.lastw = {}
        self.readers = {}

    def emit_all(self):
        nc = self.nc

        def run(h, fs):
            if not KSCOPES:
                for f in fs:
                    f(h)
                return
            cur = None
            sid = None
            for f in fs:
                ph = getattr(f, "phase", cur)
                if ph != cur:
                    if cur is not None:
                        nc.leave_named_scope(cur, sid, False)
                    sid, _ = nc.enter_named_scope(ph, False)
                    cur = ph
                f(h)
            if cur is not None:
                nc.leave_named_scope(cur, sid, False)

        with nc.Block() as block:
            @block.tensor
            def _(h):
                run(h, self.streams["pe"])

            @block.scalar
            def _(h):
                run(h, self.streams["act"])

            @block.vector
            def _(h):
                run(h, self.streams["dve"])

            @block.gpsimd
            def _(h):
                run(h, self.streams["pool"])

            @block.sync
            def _(h):
                run(h, self.streams["sp"])


C_ID, C_MEAN, C_ONE, C_MLOW, C_MUP, C_GT, C_LT, C_LE, C_GE = 0, 128, 256, 384, 512, 640, 768, 896, 1024
C_RPERM, C_IOTA, C_IFREQ, C_BAND, C_NEGL, C_NEGR, C_TOTAL = 1152, 1280, 1792, 1793, 2049, 2305, 2561


def make_lvlmask():
    p = np.arange(128)[:, None]
    f = np.arange(128)[None, :]
    m = np.zeros((128, 14 * 128), np.float32)
    for k in range(7):
        same_hi = (p >> (k + 1)) == (f >> (k + 1))
        diff_lo = (p >> k) != (f >> k)
        m[:, (2 * k) * 128:(2 * k + 1) * 128] = same_hi & diff_lo & (p > f)
        m[:, (2 * k + 1) * 128:(2 * k + 2) * 128] = same_hi & diff_lo & (p < f)
    return m


def make_consts():
    c = np.zeros((128, C_TOTAL), np.float32)
    p = np.arange(128)[:, None]
    f = np.arange(128)[None, :]
    c[:, C_ID:C_ID + 128] = (p == f)
    c[:, C_MEAN:C_MEAN + 128] = 1.0 / D
    c[:, C_ONE:C_ONE + 128] = 1.0
    c[:, C_MLOW:C_MLOW + 128] = (p <= f)
    c[:, C_MUP:C_MUP + 128] = (p >= f)
    c[:, C_GT:C_GT + 128] = (p > f)
    c[:, C_LT:C_LT + 128] = (p < f)
    c[:, C_LE:C_LE + 128] = (p <= f)
    c[:, C_GE:C_GE + 128] = (p >= f)
    r = np.zeros((128, 128), np.float32)
    for m in range(64):
        r[m + 64, m] = -1.0
        r[m, m + 64] = 1.0
    c[:, C_RPERM:C_RPERM + 128] = r
    c[:, C_IOTA:C_IOTA + 512] = np.arange(512)[None, :]
    c[:, C_IFREQ] = (10000.0 ** (-(np.arange(128) % 64) / 64.0)) / (2 * np.pi)
    a = np.arange(128)[:, None]
    b = np.arange(256)[None, :]
    c[:, C_BAND:C_BAND + 256] = np.where((b - a >= 0) & (b - a <= 128), 0.0, NEG)
    c[:, C_NEGL:C_NEGL + 256] = np.where(b < 64, NEG, 0.0) * np.ones((128, 1))
    c[:, C_NEGR:C_NEGR + 256] = np.where(b >= 192, NEG, 0.0) * np.ones((128, 1))
    return c


def build_program():
    nc = bass.Bass("TRN2", target_bir_lowering=False)

    def EI(name, shape):
        return nc.dram_tensor(name, list(shape), F32, kind="ExternalInput").ap()

    x = EI("x", [T, D])
    cvec = EI("c", [NSEG * NJ, 128])
    segf = EI("segf", [128, 16])
    consts = EI("consts", [128, C_TOTAL])
    lvlmask = EI("lvlmask", [128, 14 * 128])
    w_ada = EI("w_ada", [D, 6 * D])
    b_ada = EI("b_ada", [96, 128])
    norms = EI("norms", [64, 128])
    w_in = EI("w_in", [D, IN_COLS])
    conv_w = EI("conv_w", [120, 128])
    alog = EI("A_log", [1, 16])
    dtb = EI("dt_bias", [1, 16])
    dnw = EI("dn_norm_w", [1, 128])
    w_dn_out = EI("w_dn_out", [1024, D])
    w_at_out = EI("w_at_out", [512, D])
    w_out = EI("w_out", [D, D])
    w_ff1 = EI("w_ff1", [D, DFF])
    w_ff2 = EI("w_ff2", [DFF, D])
    y = nc.dram_tensor("y", [T, D], F32, kind="ExternalOutput").ap()
    dbg_names = []

    def dump(fw, name, ap, keys, dt=F32):
        if not KDEBUG:
            return
        dtn = nc.dram_tensor("dbg_" + name, list(ap.shape), dt, kind="ExternalOutput").ap()
        dbg_names.append("dbg_" + name)
        fw.dma("sp", lambda h: h.dma_start(out=dtn, in_=ap), reads=list(keys), writes=["dbg_" + name])

    def DR(name, shape, dt=F32):
        if KDEBUG and name in KDUMP:
            dbg_names.append(name)
            return nc.dram_tensor(name, list(shape), dt, kind="ExternalOutput").ap()
        return nc.dram_tensor(name, list(shape), dt, kind="Internal").ap()

    wb_mg = DR("wb_mg", [32, 128, 16, 128], BF16)
    wb_dn = DR("wb_dn", [16, 128, 8, 128], BF16)
    wb_at = DR("wb_at", [16, 128, 4, 128], BF16)
    wb_out = DR("wb_out", [16, 128, 16, 128], BF16)
    wb_f1 = DR("wb_f1", [64, 128, 16, 128], BF16)
    wb_f2 = DR("wb_f2", [32, 128, 32, 128], BF16)
    ydnT = DR("ydnT", [1024, T], BF16)
    yatT = DR("yatT", [512, T], BF16)
    MERGE0 = 3072 + 1024 + 32 + 4608

    es = ExitStack()
    with es:
        fw = FW(nc, es)

        uid = [0]

        def SB(st, name, shape, dt=F32):
            uid[0] += 1
            return st.enter_context(nc.sbuf_tensor(f"{name}_{uid[0]}", list(shape), dt))

        def PSM(st, name, shape, dt=F32):
            uid[0] += 1
            return st.enter_context(nc.psum_tensor(f"{name}_{uid[0]}", list(shape), dt))

        cst = SB(es, "cst", [128, C_TOTAL])
        sgf = SB(es, "sgf", [128, 16])
        nrm = SB(es, "nrm", [128, 64])
        cT = SB(es, "cT", [128, 64])
        badaT = SB(es, "badaT", [128, 96])
        modT = SB(es, "modT", [128, 96, 4])
        A1 = SB(es, "A1", [128, 16, 4]); G1 = SB(es, "G1", [128, 16, 4])
        A2 = SB(es, "A2", [128, 16, 4]); G2 = SB(es, "G2", [128, 16, 4])
        ident = cst[:, C_ID:C_ID + 128]
        meanm = cst[:, C_MEAN:C_MEAN + 128]
        fw.dma("sp", lambda h: h.dma_start(out=cst[:], in_=consts[:, :]), writes=["cst"])
        fw.dma("sp", lambda h: h.dma_start(out=sgf[:], in_=segf[:, :]), writes=["sgf"])

        def cast(dst, src, key):
            fw.dma("pool", lambda h: h.dma_start(out=dst, in_=src), writes=[key])

        for b in range(32):
            cast(wb_mg[b], w_in[:, MERGE0 + b * 128:MERGE0 + (b + 1) * 128].rearrange("(j p) c -> p j c", p=128), ("wb_mg", b))
        for b in range(16):
            cast(wb_dn[b], w_dn_out[:, b * 128:(b + 1) * 128].rearrange("(j p) c -> p j c", p=128), ("wb_dn", b))
            cast(wb_at[b], w_at_out[:, b * 128:(b + 1) * 128].rearrange("(j p) c -> p j c", p=128), ("wb_at", b))
            cast(wb_out[b], w_out[:, b * 128:(b + 1) * 128].rearrange("(j p) c -> p j c", p=128), ("wb_out", b))
        for b in range(64):
            cast(wb_f1[b], w_ff1[:, b * 128:(b + 1) * 128].rearrange("(j p) c -> p j c", p=128), ("wb_f1", b))
        for hf in range(2):
            for b in range(16):
                cast(wb_f2[hf * 16 + b],
                     w_ff2[hf * 4096:(hf + 1) * 4096, b * 128:(b + 1) * 128].rearrange("(j p) c -> p j c", p=128),
                     ("wb_f2", hf * 16 + b))

        fw.phase = "P0mod"
        with ExitStack() as st:
            stg = SB(st, "stg0", [128, 128])
            stg2 = SB(st, "stg1", [128, 128])
            pt = PSM(st, "p0t", [128, 512])
            pm = [PSM(st, f"p0m{i}", [128, 4]) for i in range(2)]
            fw.dma("sp", lambda h: h.dma_start(out=stg[0:64, :], in_=norms[:, :]), writes=["stg0"])
            fw.dma("sp", lambda h: h.dma_start(out=stg[64:128, :], in_=cvec[:, :]), writes=["stg0"])
            fw.op("pe", lambda h: h.transpose(out=pt[:, 0:128], in_=stg[:], identity=ident), reads=["stg0", "cst"], writes=["p0t"])
            fw.op("dve", lambda h: h.tensor_copy(out=nrm[:], in_=pt[:, 0:64]), reads=["p0t"], writes=["nrm"])
            fw.op("act", lambda h: h.activation(out=cT[:], in_=pt[:, 64:128], func=AF.Silu), reads=["p0t"], writes=["cT"])
            fw.dma("sp", lambda h: h.dma_start(out=stg2[0:96, :], in_=b_ada[:, :]), writes=["stg1"])
            fw.op("pe", lambda h: h.transpose(out=pt[:, 128:224], in_=stg2[0:96, :], identity=ident[0:96, 0:96]), reads=["stg1", "cst"], writes=["p0t"])
            fw.op("dve", lambda h: h.tensor_copy(out=badaT[:], in_=pt[:, 128:224]), reads=["p0t"], writes=["badaT"])
            cTv = cT[:].rearrange("p (s j) -> p j s", j=16)
            wbig = [SB(st, f"wbig{i}", [128, 16, 1024]) for i in range(2)]
            for ch in range(12):
                wt = wbig[ch % 2]
                for j in range(16):
                    fw.dma("sp", lambda h, wt=wt, ch=ch, j=j: h.dma_start(out=wt[:, j, :], in_=w_ada[j * 128:(j + 1) * 128, ch * 1024:(ch + 1) * 1024]),
                           writes=[("wbig", ch % 2, j)])
                for cbl in range(8):
                    cb = ch * 8 + cbl

                    def mm(h, wt=wt, cb=cb, cbl=cbl):
                        for j in range(16):
                            ins = h.matmul(pm[cb % 2][:, :], lhsT=wt[:, j, cbl * 128:(cbl + 1) * 128], rhs=cTv[:, j, :], start=(j == 0), stop=(j == 15))
                        return ins
                    fw.op("pe", mm, reads=[("wbig", ch % 2, j) for j in range(16)] + ["cT"], writes=[("p0m", cb % 2)])
                    fw.op("dve", lambda h, cb=cb: h.tensor_scalar(out=modT[:, cb, :], in0=pm[cb % 2][:, :], scalar1=badaT[:, cb:cb + 1],
                                                                  scalar2=None, op0=ALU.add),
                          reads=[("p0m", cb % 2), "badaT"], writes=["modT"])

            def nv(v):
                return nrm[:, v * 16:(v + 1) * 16].unsqueeze(2).to_broadcast([128, 16, 4])
            fw.op("dve", lambda h: h.scalar_tensor_tensor(out=A1[:], in0=modT[:, 16:32, :], scalar=1.0, in1=nv(0), op0=ALU.add, op1=ALU.mult),
                  reads=["modT", "nrm"], writes=["A1"])
            fw.op("dve", lambda h: h.tensor_tensor(out=G1[:], in0=modT[:, 32:48, :], in1=nv(1), op=ALU.mult), reads=["modT", "nrm"], writes=["G1"])
            fw.op("dve", lambda h: h.scalar_tensor_tensor(out=A2[:], in0=modT[:, 64:80, :], scalar=1.0, in1=nv(2), op0=ALU.add, op1=ALU.mult),
                  reads=["modT", "nrm"], writes=["A2"])
            fw.op("dve", lambda h: h.tensor_tensor(out=G2[:], in0=modT[:, 80:96, :], in1=nv(3), op=ALU.mult), reads=["modT", "nrm"], writes=["G2"])
            dump(fw, "modT", modT[:], ["modT"])
            dump(fw, "A1", A1[:], ["A1"])
            dump(fw, "nrm", nrm[:], ["nrm"])
            fw.barrier()

        def front_end(st_bufs, t0, seg):
            xin, xT, sq, rstd, pxt, pss = st_bufs
            for q in range(4):
                fw.dma("sp", lambda h, q=q: h.dma_start(out=xin[q][:], in_=x[t0 + q * 128:t0 + (q + 1) * 128, :]), writes=[("xin", q)])
            for j in range(16):
                pb = pxt[j % 2]

                def tr(h, j=j, pb=pb):
                    for q in range(4):
                        ins = h.transpose(out=pb[:, q * 128:(q + 1) * 128], in_=xin[q][:, j * 128:(j + 1) * 128], identity=ident)
                    return ins
                fw.op("pe", tr, reads=[("xin", q) for q in range(4)] + ["cst"], writes=[("pxt", j % 2)])
                fw.op("act", lambda h, j=j, pb=pb: h.activation(out=xT[:, j, :], in_=pb[:, :], func=AF.Copy), reads=[("pxt", j % 2)], writes=[("xT", j)])
                fw.op("dve", lambda h, j=j, pb=pb: h.tensor_tensor(out=sq[j % 2][:], in0=pb[:, :], in1=xT[:, j, :], op=ALU.mult),
                      reads=[("pxt", j % 2), ("xT", j)], writes=[("sq", j % 2)])
                fw.op("pe", lambda h, j=j: h.matmul(pss[:, :], lhsT=meanm, rhs=sq[j % 2][:], start=(j == 0), stop=(j == 15)),
                      reads=[("sq", j % 2), "cst"], writes=["pss"])
            fw.op("act", lambda h: h.activation(out=rstd[:], in_=pss[:, :], func=AF.Sqrt, bias=EPS, scale=1.0), reads=["pss"], writes=["rstd"])
            fw.op("dve", lambda h: h.reciprocal(out=rstd[:], in_=rstd[:]), reads=["rstd"], writes=["rstd"])

        def sumsq_rstd(src_fn, nblk, sq, pss, rstd, src_keys, scale_mean=True):
            for j in range(nblk):
                fw.op("pool", lambda h, j=j: h.tensor_tensor(out=sq[j % 2][:], in0=src_fn(j), in1=src_fn(j), op=ALU.mult),
                      reads=[src_keys(j)], writes=[("sq", j % 2)])
                fw.op("pe", lambda h, j=j: h.matmul(pss[:, :], lhsT=meanm, rhs=sq[j % 2][:], start=(j == 0), stop=(j == nblk - 1)),
                      reads=[("sq", j % 2), "cst"], writes=["pss"])
            fw.op("act", lambda h: h.activation(out=rstd[:], in_=pss[:, :], func=AF.Sqrt, bias=EPS, scale=1.0), reads=["pss"], writes=["rstd"])
            fw.op("dve", lambda h: h.reciprocal(out=rstd[:], in_=rstd[:]), reads=["rstd"], writes=["rstd"])

        DNQ0, Z0, AB0, ATQ0, ATK0, ATV0 = 0, 3072, 4096, 4128, 4128 + 1536, 4128 + 3072
        DIL = (1, 4, 16)
        MC = [T // d_ for d_ in DIL]
        MS = [SEG // d_ for d_ in DIL]
        CL = [m_ + 128 for m_ in MC]
        qkvpre = DR("qkvpre", [3072, T + 4])
        szT = DR("szT", [1024, T])
        ab_tm = DR("ab_tm", [T, 32])
        aqT = [DR(f"aqT{g}", [512, DIL[g] * CL[g]], BF16) for g in range(3)]
        akT = [DR(f"akT{g}", [512, DIL[g] * CL[g]], BF16) for g in range(3)]
        avs = [DR(f"av{g}", [DIL[g] * CL[g], 512], BF16) for g in range(3)]
        qn = DR("qn", [1024, T]); kn = DR("kn", [1024, T])
        k_tm = DR("k_tm", [T, 1024]); v_tm = DR("v_tm", [T, 1024])
        GB = DR("GB", [2, 2, 64, 8, 128])
        o_dir = DR("o_dir", [2, T, 1024])
        Oat = DR("Oat", [3, T, 4, 130])
        wb_fm = DR("wb_fm", [56, 128, 16, 128], BF16)
        wb_v = DR("wb_v", [3, 128, 16, 512], BF16)
        wb_ab = DR("wb_ab", [128, 16, 32], BF16)
        for b in range(56):
            c0 = (DNQ0 + 128 * b) if b < 24 else (Z0 + 128 * (b - 24)) if b < 32 else (ATQ0 + 128 * (b - 32)) if b < 44 else (ATK0 + 128 * (b - 44))
            cast(wb_fm[b], w_in[:, c0:c0 + 128].rearrange("(j p) c -> p j c", p=128), ("wb_fm", b))
        for g in range(3):
            cast(wb_v[g], w_in[:, ATV0 + 512 * g:ATV0 + 512 * (g + 1)].rearrange("(j p) c -> p j c", p=128), ("wb_v", g))
        cast(wb_ab[:, :, :], w_in[:, AB0:AB0 + 32].rearrange("(j p) c -> p j c", p=128), "wb_ab")

        convT = SB(es, "convT", [128, 120])
        dnwT = SB(es, "dnwT", [128, 1])
        dtb16 = SB(es, "dtb16", [128, 16])
        nA16 = SB(es, "nA16", [128, 16])
        Gtm = SB(es, "Gtm", [128, 64, 16])
        Btm = SB(es, "Btm", [128, 64, 16])
        ones = cst[:, C_ONE:C_ONE + 128]
        with ExitStack() as st:
            stg = SB(st, "stgc", [128, 128])
            pt = PSM(st, "p1t", [128, 512])
            fw.dma("sp", lambda h: h.dma_start(out=stg[0:120, :], in_=conv_w[:, :]), writes=["stgc"])
            fw.op("pe", lambda h: h.transpose(out=pt[:, 0:120], in_=stg[0:120, :], identity=ident[0:120, 0:120]), reads=["stgc", "cst"], writes=["p1t"])
            fw.op("dve", lambda h: h.tensor_copy(out=convT[:], in_=pt[:, 0:120]), reads=["p1t"], writes=["convT"])
            fw.dma("sp", lambda h: h.dma_start(out=stg[0:1, :], in_=dnw[:, :]), writes=["stgc"])
            fw.op("pe", lambda h: h.transpose(out=pt[:, 128:129], in_=stg[0:1, :], identity=ident[0:1, 0:1]), reads=["stgc", "cst"], writes=["p1t"])
            fw.op("dve", lambda h: h.tensor_copy(out=dnwT[:], in_=pt[:, 128:129]), reads=["p1t"], writes=["dnwT"])
            fw.dma("sp", lambda h: h.dma_start(out=dtb16[:], in_=dtb.partition_broadcast(128)), writes=["dtb16"])
            fw.dma("sp", lambda h: h.dma_start(out=nA16[:], in_=alog.partition_broadcast(128)), writes=["nA16"])
            fw.op("act", lambda h: h.activation(out=nA16[:], in_=nA16[:], func=AF.Exp), reads=["nA16"], writes=["nA16"])
            fw.op("dve", lambda h: h.tensor_scalar(out=nA16[:], in0=nA16[:], scalar1=-1.0, scalar2=None, op0=ALU.mult), reads=["nA16"], writes=["nA16"])
            zb = SB(st, "zb", [128, 16, 512], BF16)
            fw.op("pool", lambda h: h.memset(zb[:], 0.0), writes=["zb"])
            for g in range(3):
                dil = DIL[g]
                for hh in range(4):
                    kv = akT[g][hh * 128:(hh + 1) * 128, :].rearrange("p (r c) -> p r c", c=CL[g])
                    fw.dma("sp", lambda h, kv=kv, dil=dil: h.dma_start(out=kv[:, :, 0:64], in_=zb[:, 0:dil, 0:64]), reads=["zb"], writes=[("akT", g)])
                    fw.dma("sp", lambda h, kv=kv, dil=dil, g=g: h.dma_start(out=kv[:, :, 64 + MC[g]:128 + MC[g]], in_=zb[:, 0:dil, 0:64]), reads=["zb"], writes=[("akT", g)])
                vv = avs[g].rearrange("(r c) d -> c r d", c=CL[g])
                fw.dma("sp", lambda h, vv=vv, dil=dil: h.dma_start(out=vv[0:64, :, :], in_=zb[0:64, 0:dil, :]), reads=["zb"], writes=[("av", g)])
                fw.dma("sp", lambda h, vv=vv, dil=dil, g=g: h.dma_start(out=vv[64 + MC[g]:128 + MC[g], :, :], in_=zb[0:64, 0:dil, :]), reads=["zb"], writes=[("av", g)])
            fw.barrier()

        fw.phase = "P1"
        with ExitStack() as st:
            xin = [SB(st, f"xin{q}", [128, D]) for q in range(4)]
            hTs = SB(st, "hTs", [128, 16, SEG], BF16)
            ssq = SB(st, "ssq", [128, 4])
            stgf = [SB(st, f"stgf{i}", [128, SEG]) for i in range(2)]
            stgb = [SB(st, f"stgb{i}", [128, SEG], BF16) for i in range(2)]
            wk = [SB(st, f"wk{i}", [128, 16, 128], BF16) for i in range(3)]
            wv = SB(st, "wv", [128, 16, 512], BF16)
            wab = SB(st, "wab", [128, 16, 32], BF16)
            cosT = SB(st, "cosT", [128, 4, 512]); sinT = SB(st, "sinT", [128, 4, 512])
            tA = SB(st, "tA", [128, 512]); tB = SB(st, "tB", [128, 512]); tB2 = SB(st, "tB2", [128, 512]); tC = SB(st, "tC", [128, 512]); tD = SB(st, "tD", [128, 512])
            tI = SB(st, "tI", [128, 512], I32)
            hpi = SB(st, "hpi", [128, 1])
            vst = [SB(st, f"vst{i}", [128, 512], BF16) for i in range(2)]
            abst = SB(st, "abst", [128, 16, 32])
            pxt = [PSM(st, f"pxt{i}", [128, 512]) for i in range(2)]
            pmm = [PSM(st, f"pmm{i}", [128, 512]) for i in range(4)]
            prr = PSM(st, "prr", [128, 512])
            print("P1 sbuf remaining", nc.sbuf_bytes_remaining)
            fw.op("pool", lambda h: h.memset(hpi[:], float(np.pi / 2)), writes=["hpi"])
            fw.dma("sp", lambda h: h.dma_start(out=wab[:], in_=wb_ab[:, :, :]), reads=["wb_ab"], writes=["wab"])
            rperm = cst[:, C_RPERM:C_RPERM + 128]
            ifr = cst[:, C_IFREQ:C_IFREQ + 1]
            iota = cst[:, C_IOTA:C_IOTA + 512]
            wsl = [0]

            def trig(dst, yap):
                fw.op("dve", lambda h: h.tensor_copy(out=tI[:], in_=yap), reads=["tA"], writes=["tI"])
                fw.op("dve", lambda h: h.tensor_copy(out=tB[:], in_=tI[:]), reads=["tI"], writes=["tB"])
                fw.op("dve", lambda h: h.tensor_tensor(out=tB[:], in0=yap, in1=tB[:], op=ALU.subtract), reads=["tA", "tB"], writes=["tB"])
                fw.op("act", lambda h: h.activation(out=tC[:], in_=tB[:], func=AF.Abs), reads=["tB"], writes=["tC"])
                fw.op("act", lambda h: h.activation(out=tD[:], in_=tB[:], func=AF.Sin, scale=float(np.pi)), reads=["tB"], writes=["tD"])
                fw.op("act", lambda h: h.activation(out=tC[:], in_=tC[:], func=AF.Sin, bias=hpi[:], scale=-float(np.pi)), reads=["tC", "hpi"], writes=["tC"])
                fw.op("dve", lambda h: h.scalar_tensor_tensor(out=dst, in0=tD[:], scalar=2.0, in1=tC[:], op0=ALU.mult, op1=ALU.mult),
                      reads=["tC", "tD"], writes=["trig"])

            for s in range(NSEG):
                for tt in range(4):
                    t0 = s * SEG + tt * 512
                    for q in range(4):
                        fw.dma("sp", lambda h, q=q, t0=t0: h.dma_start(out=xin[q][:], in_=x[t0 + q * 128:t0 + (q + 1) * 128, :]), writes=[("xin", q)])
                        fw.op("act", lambda h, q=q: h.activation(out=stgb[0][:], in_=xin[q][:], func=AF.Square, accum_out=ssq[:, q:q + 1]),
                              reads=[("xin", q)], writes=[("stgb", 0), ("ssq", q)])
                        fw.op("act", lambda h, q=q: h.activation(out=ssq[:, q:q + 1], in_=ssq[:, q:q + 1], func=AF.Sqrt, bias=EPS, scale=1.0 / D),
                              reads=[("ssq", q)], writes=[("ssq", q)])
                        fw.op("dve", lambda h, q=q: h.reciprocal(out=ssq[:, q:q + 1], in_=ssq[:, q:q + 1]), reads=[("ssq", q)], writes=[("ssq", q)])
                        fw.op("dve", lambda h, q=q: h.tensor_scalar(out=xin[q][:], in0=xin[q][:], scalar1=ssq[:, q:q + 1], scalar2=None, op0=ALU.mult),
                              reads=[("xin", q), ("ssq", q)], writes=[("xin", q)])
                    for j in range(16):
                        pb = pxt[j % 2]

                        def tr(h, j=j, pb=pb):
                            for q in range(4):
                                ins = h.transpose(out=pb[:, q * 128:(q + 1) * 128], in_=xin[q][:, j * 128:(j + 1) * 128], identity=ident)
                            return ins
                        fw.op("pe", tr, reads=[("xin", q) for q in range(4)] + ["cst"], writes=[("pxt", j % 2)])
                        fw.op("dve" if j % 2 else "act",
                              (lambda h, j=j, pb=pb, tt=tt, s=s: h.tensor_scalar(out=hTs[:, j, tt * 512:(tt + 1) * 512], in0=pb[:, :], scalar1=A1[:, j, s:s + 1],
                                                                              scalar2=modT[:, j, s:s + 1], op0=ALU.mult, op1=ALU.add)) if j % 2 else
                              (lambda h, j=j, pb=pb, tt=tt, s=s: h.activation(out=hTs[:, j, tt * 512:(tt + 1) * 512], in_=pb[:, :], func=AF.Identity,
                                                                           bias=modT[:, j, s:s + 1], scale=A1[:, j, s:s + 1])),
                              reads=[("pxt", j % 2), "A1", "modT"], writes=[("hTs", j, tt)])
                hkeys = [("hTs", j, tt) for j in range(16) for tt in range(4)]

                def wload_(b):
                    i = wsl[0] % 3
                    wsl[0] += 1
                    fw.dma("sp", lambda h, i=i, b=b: h.dma_start(out=wk[i][:], in_=wb_fm[b]), reads=[("wb_fm", b)], writes=[("wk", i)])
                    return i
                plan = list(range(32)) + [32 + qk_ * 12 + g_ * 4 + hh_ for g_ in range(3) for qk_ in range(2) for hh_ in range(4)]
                wst = {"issued": 0, "used": 0, "slots": {}}

                def wload(b):
                    while wst["issued"] < len(plan) and wst["issued"] <= wst["used"] + 2:
                        k_ = wst["issued"]
                        wst["slots"][k_] = wload_(plan[k_])
                        wst["issued"] += 1
                    k_ = wst["used"]
                    assert plan[k_] == b, (plan[k_], b)
                    wst["used"] += 1
                    return wst["slots"][k_]

                def fm_mm(wi, rhs_fn, pb, out_ap=None):
                    def mm(h):
                        for j in range(16):
                            ins = h.matmul(out_ap if out_ap is not None else pmm[pb][:, :], lhsT=wk[wi][:, j, :], rhs=rhs_fn(j), start=(j == 0), stop=(j == 15))
                        return ins
                    fw.op("pe", mm, reads=[("wk", wi)] + hkeys, writes=[("pmm", pb)])

                for b in range(32):
                    wi = wload(b)
                    sf = stgf[b % 2]
                    for tt in range(4):
                        pb = (b * 4 + tt) % 4
                        fm_mm(wi, lambda j, tt=tt: hTs[:, j, tt * 512:(tt + 1) * 512], pb)
                        fw.op("act", lambda h, sf=sf, tt=tt, pb=pb, b=b: h.activation(out=sf[:, tt * 512:(tt + 1) * 512], in_=pmm[pb][:, :],
                                                                                     func=(AF.Copy if b < 24 else AF.Silu)),
                              reads=[("pmm", pb)], writes=[("stgf", b % 2)])
                    if b < 24:
                        fw.dma("sp", lambda h, sf=sf, b=b, s=s: h.dma_start(out=qkvpre[b * 128:(b + 1) * 128, 2 + s * SEG:2 + (s + 1) * SEG], in_=sf[:]),
                               reads=[("stgf", b % 2)], writes=["qkvpre"])
                    else:
                        fw.dma("sp", lambda h, sf=sf, b=b, s=s: h.dma_start(out=szT[(b - 24) * 128:(b - 23) * 128, s * SEG:(s + 1) * SEG], in_=sf[:]),
                               reads=[("stgf", b % 2)], writes=["szT"])
                def abmm(h):
                    for n in range(16):
                        for j in range(16):
                            ins = h.matmul(pmm[0][:, n * 32:(n + 1) * 32], lhsT=hTs[:, j, n * 128:(n + 1) * 128], rhs=wab[:, j, :], start=(j == 0), stop=(j == 15))
                    return ins
                fw.op("pe", abmm, reads=["wab"] + hkeys, writes=[("pmm", 0)])
                fw.op("dve", lambda h: h.tensor_copy(out=abst[:].rearrange("p n c -> p (n c)"), in_=pmm[0][:, :]), reads=[("pmm", 0)], writes=["abst"])
                fw.dma("sp", lambda h, s=s: h.dma_start(out=ab_tm[s * SEG:(s + 1) * SEG, :].rearrange("(n p) c -> p n c", p=128), in_=abst[:]),
                       reads=["abst"], writes=["ab_tm"])
                for g in range(3):
                    dil = DIL[g]
                    for tt in range(4):
                        if dil == 16:
                            for rl in range(4):
                                fw.op("dve", lambda h, rl=rl, tt=tt, s=s: h.tensor_scalar(out=tA[:, rl * 128:(rl + 1) * 128], in0=iota[:, 0:128], scalar1=16.0,
                                                                                         scalar2=sgf[:, 8 + s:9 + s], op0=ALU.mult, op1=ALU.add),
                                      reads=["cst", "sgf"], writes=["tA"])
                                fw.op("dve", lambda h, rl=rl, tt=tt: h.tensor_scalar(out=tA[:, rl * 128:(rl + 1) * 128], in0=tA[:, rl * 128:(rl + 1) * 128],
                                                                                    scalar1=float(4 * tt + rl), scalar2=ifr, op0=ALU.add, op1=ALU.mult),
                                      reads=["tA", "cst"], writes=["tA"])
                        else:
                            cadd = float(512 * tt) if dil == 1 else float(tt)
                            fw.op("dve", lambda h, s=s, dil=dil: h.tensor_scalar(out=tA[:], in0=iota, scalar1=float(dil), scalar2=sgf[:, 8 + s:9 + s],
                                                                                op0=ALU.mult, op1=ALU.add), reads=["cst", "sgf"], writes=["tA"])
                            fw.op("dve", lambda h, cadd=cadd: h.tensor_scalar(out=tA[:], in0=tA[:], scalar1=cadd, scalar2=ifr, op0=ALU.add, op1=ALU.mult),
                                  reads=["tA", "cst"], writes=["tA"])
                        trig(sinT[:, tt, :], tA[:])
                        fw.op("dve", lambda h: h.tensor_scalar(out=tA[:], in0=tA[:], scalar1=0.25, scalar2=None, op0=ALU.add), reads=["tA", "trig"], writes=["tA"])
                        trig(cosT[:, tt, :], tA[:])
                    for qk in range(2):
                        for hh in range(4):
                            b = 32 + qk * 12 + g * 4 + hh
                            wi = wload(b)
                            sb_ = stgb[(qk * 4 + hh) % 2]
                            hview = None
                            def rot_stage(tt, sb_=None):
                                tBx = tB if tt % 2 == 0 else tB2
                                kB = "tB" if tt % 2 == 0 else "tB2"
                                fw.op("pe", lambda h: h.matmul(prr[:, :], lhsT=rperm, rhs=tBx[:], start=True, stop=True), reads=[kB, "cst"], writes=["prr"])
                                fw.op("pool", lambda h: h.tensor_tensor(out=tC[:], in0=tBx[:], in1=cosT[:, tt, :], op=ALU.mult), reads=[kB, "trig"], writes=["tC"])
                                fw.op("dve", lambda h: h.tensor_tensor(out=tD[:], in0=prr[:, :], in1=sinT[:, tt, :], op=ALU.mult), reads=["prr", "trig"], writes=["tD"])
                                fw.op("pool", lambda h: h.tensor_tensor(out=sb_[:, tt * 512:(tt + 1) * 512], in0=tC[:], in1=tD[:], op=ALU.add),
                                      reads=["tC", "tD"], writes=[("stgb", (qk * 4 + hh) % 2)])

                            for tt in range(4):
                                pb = tt % 4
                                if dil == 1:
                                    rf = lambda j, tt=tt: hTs[:, j, tt * 512:(tt + 1) * 512]
                                    oap = None
                                elif dil == 4:
                                    rf = lambda j, tt=tt: hTs[:, j, :].rearrange("p (m r) -> p r m", r=4)[:, tt, :]
                                    oap = None
                                else:
                                    rf = lambda j, tt=tt: hTs[:, j, :].rearrange("p (m r) -> p r m", r=16)[:, 4 * tt:4 * tt + 4, :]
                                    oap = pmm[pb][:, :].rearrange("p (a b) -> p a b", a=4)
                                fm_mm(wi, rf, pb, oap)
                                tBx = tB if tt % 2 == 0 else tB2
                                kB = "tB" if tt % 2 == 0 else "tB2"
                                fw.op("act", lambda h: h.activation(out=tBx[:], in_=pmm[pb][:, :], func=AF.Copy, scale=(128.0 ** -0.5 if qk == 0 else 1.0)),
                                      reads=[("pmm", pb)], writes=[kB])
                                if tt >= 1:
                                    rot_stage(tt - 1, sb_)
                            rot_stage(3, sb_)
                            dst = (aqT if qk == 0 else akT)[g][hh * 128:(hh + 1) * 128, :].rearrange("p (r c) -> p r c", c=CL[g])
                            fw.dma("sp", lambda h, dst=dst, sb_=sb_, g=g, s=s, dil=dil: h.dma_start(
                                out=dst[:, :, 64 + s * MS[g]:64 + (s + 1) * MS[g]], in_=sb_[:].rearrange("p (r m) -> p r m", r=dil)),
                                reads=[("stgb", (qk * 4 + hh) % 2)], writes=[("aqk", g)])
                    fw.dma("sp", lambda h, g=g: h.dma_start(out=wv[:], in_=wb_v[g]), reads=[("wb_v", g)], writes=["wv"])
                    for ct in range(16):
                        if dil == 1:
                            lf = lambda j, ct=ct: hTs[:, j, ct * 128:(ct + 1) * 128]
                            r_, m0 = 0, ct * 128
                        elif dil == 4:
                            lf = lambda j, ct=ct: hTs[:, j, :].rearrange("p (m r) -> p r m", r=4)[:, ct // 4, (ct % 4) * 128:(ct % 4 + 1) * 128]
                            r_, m0 = ct // 4, (ct % 4) * 128
                        else:
                            lf = lambda j, ct=ct: hTs[:, j, :].rearrange("p (m r) -> p r m", r=16)[:, ct, :]
                            r_, m0 = ct, 0
                        pb = ct % 4

                        def vmm(h, lf=lf, pb=pb):
                            for j in range(16):
                                ins = h.matmul(pmm[pb][:, :], lhsT=lf(j), rhs=wv[:, j, :], start=(j == 0), stop=(j == 15))
                            return ins
                        fw.op("pe", vmm, reads=["wv"] + hkeys, writes=[("pmm", pb)])
                        vs = vst[ct % 2]
                        if ct % 2:
                            fw.op("dve", lambda h, vs=vs, pb=pb: h.tensor_copy(out=vs[:], in_=pmm[pb][:, :]), reads=[("pmm", pb)], writes=[("vst", ct % 2)])
                        else:
                            fw.op("act", lambda h, vs=vs, pb=pb: h.activation(out=vs[:], in_=pmm[pb][:, :], func=AF.Copy), reads=[("pmm", pb)], writes=[("vst", ct % 2)])
                        row0 = r_ * CL[g] + 64 + s * MS[g] + m0
                        fw.dma("sp", lambda h, vs=vs, row0=row0, g=g: h.dma_start(out=avs[g][row0:row0 + 128, :], in_=vs[:]),
                               reads=[("vst", ct % 2)], writes=[("av", g)])
            fw.barrier()

        fw.phase = "P2"
        with ExitStack() as st:
            xc = [SB(st, f"xc{i}", [128, 516]) for i in range(4)]
            ca_l = [SB(st, f"ca{i}", [128, 512]) for i in range(2)]; cs_l = [SB(st, f"cs{i}", [128, 512]) for i in range(2)]
            cq_l = [SB(st, f"cq{i}", [128, 512]) for i in range(2)]; cr_l = [SB(st, f"cr{i}", [128, 512]) for i in range(2)]
            cn = [SB(st, f"cn{i}", [128, 512]) for i in range(2)]
            ctm = [SB(st, f"ctm{i}", [128, 4, 128]) for i in range(2)]
            pss2_l = [PSM(st, f"pss2{i}", [128, 512]) for i in range(2)]
            ptr = [PSM(st, f"ptr{i}", [128, 512]) for i in range(2)]
            def p2_load(it):
                cb, tile = it // 16, it % 16
                t0 = tile * 512
                xw = xc[it % 4]
                fw.dma("sp", lambda h: h.dma_start(out=xw[:], in_=qkvpre[cb * 128:(cb + 1) * 128, t0:t0 + 516]), reads=["qkvpre"], writes=[("xc", it % 4)])

            NT2 = 24 * 16

            def p2_A(it):
                cb, tile = it // 16, it % 16
                s = tile // 4
                xw = xc[it % 4]
                kx = ("xc", it % 4)
                pss2 = pss2_l[it % 2]
                ca, cs_, cq, cr = ca_l[it % 2], cs_l[it % 2], cq_l[it % 2], cr_l[it % 2]
                kca, kcs, kcq, kcr = ("ca", it % 2), ("cs_", it % 2), ("cq", it % 2), ("cr", it % 2)
                if tile == 0:
                    fw.op("pool", lambda h: h.memset(xw[:, 0:2], 0.0), writes=[kx])
                elif tile % 4 == 0:
                    fw.op("pool", lambda h: h.tensor_scalar(out=xw[:, 0:2], in0=xw[:, 0:2], scalar1=sgf[:, s:s + 1], scalar2=None, op0=ALU.mult), reads=[kx, "sgf"], writes=[kx])
                if tile == 15:
                    fw.op("pool", lambda h: h.memset(xw[:, 514:516], 0.0), writes=[kx])
                elif tile % 4 == 3:
                    fw.op("pool", lambda h: h.tensor_scalar(out=xw[:, 514:516], in0=xw[:, 514:516], scalar1=sgf[:, 4 + s:5 + s], scalar2=None, op0=ALU.mult), reads=[kx, "sgf"], writes=[kx])
                fw.op("dve", lambda h: h.tensor_scalar(out=ca[:], in0=xw[:, 0:512], scalar1=convT[:, cb:cb + 1], scalar2=None, op0=ALU.mult), reads=[kx, "convT"], writes=[kca])
                for k in range(1, 5):
                    fw.op("dve", lambda h: h.scalar_tensor_tensor(out=ca[:], in0=xw[:, k:k + 512], scalar=convT[:, k * 24 + cb:k * 24 + cb + 1], in1=ca[:], op0=ALU.mult, op1=ALU.add),
                          reads=[kx, "convT", kca], writes=[kca])
                fw.op("act", lambda h: h.activation(out=cs_[:], in_=ca[:], func=AF.Silu), reads=[kca], writes=[kcs])
                if cb < 16:
                    fw.op("pool", lambda h: h.tensor_tensor(out=cq[:], in0=cs_[:], in1=cs_[:], op=ALU.mult), reads=[kcs], writes=[kcq])
                    fw.op("pe", lambda h: h.matmul(pss2[:, :], lhsT=ones, rhs=cq[:], start=True, stop=True), reads=[kcq, "cst"], writes=[("pss2", it % 2)])
                    if cb < 8:
                        fw.op("act", lambda h: h.activation(out=cr[:], in_=pss2[:, :], func=AF.Sqrt, bias=128.0 * EPS, scale=128.0), reads=[("pss2", it % 2)], writes=[kcr])
                    else:
                        fw.op("act", lambda h: h.activation(out=cr[:], in_=pss2[:, :], func=AF.Sqrt, bias=EPS, scale=1.0), reads=[("pss2", it % 2)], writes=[kcr])

            def p2_B(it):
                cb, tile = it // 16, it % 16
                t0 = tile * 512
                cs_, cr = cs_l[it % 2], cr_l[it % 2]
                kcs, kcr = ("cs_", it % 2), ("cr", it % 2)
                res_t, res_k = cs_, kcs
                if cb < 16:
                    fw.op("dve", lambda h: h.reciprocal(out=cr[:], in_=cr[:]), reads=[kcr], writes=[kcr])
                    cnb = cn[it % 2]
                    fw.op("pool", lambda h: h.tensor_tensor(out=cnb[:], in0=cs_[:], in1=cr[:], op=ALU.mult), reads=[kcs, kcr], writes=[("cn", it % 2)])
                    res_t, res_k = cnb, ("cn", it % 2)
                    dstn = qn if cb < 8 else kn
                    fw.dma("sp", lambda h: h.dma_start(out=dstn[(cb % 8) * 128:(cb % 8 + 1) * 128, t0:t0 + 512], in_=cnb[:]), reads=[res_k], writes=["qkn"])
                if cb >= 8:
                    pb = ptr[it % 2]

                    def tr(h):
                        for q in range(4):
                            ins = h.transpose(out=pb[:, q * 128:(q + 1) * 128], in_=res_t[:, q * 128:(q + 1) * 128], identity=ident)
                        return ins
                    fw.op("pe", tr, reads=[res_k, "cst"], writes=[("ptr", it % 2)])
                    cm = ctm[it % 2]
                    fw.op("act", lambda h: h.activation(out=cm[:].rearrange("p q c -> p (q c)"), in_=pb[:, :], func=AF.Copy), reads=[("ptr", it % 2)], writes=[("ctm", it % 2)])
                    dstt = k_tm if cb < 16 else v_tm
                    fw.dma("sp", lambda h: h.dma_start(out=dstt[t0:t0 + 512, (cb % 8) * 128:(cb % 8 + 1) * 128].rearrange("(q p) c -> p q c", p=128), in_=cm[:]),
                           reads=[("ctm", it % 2)], writes=["kv_tm"])

            p2_load(0)
            p2_load(1)
            p2_load(2)
            for it in range(NT2 + 1):
                if it + 3 < NT2:
                    p2_load(it + 3)
                if it < NT2:
                    p2_A(it)
                if it >= 1:
                    p2_B(it - 1)
            fw.barrier()

        fw.phase = "P3"
        with ExitStack() as st:
            ab = SB(st, "ab", [128, 64, 32])
            g_ = SB(st, "g_", [128, 64, 16]); t1_ = SB(st, "t1_", [128, 64, 16]); t2_ = SB(st, "t2_", [128, 64, 16])
            rows = [SB(st, f"rows{i}", [8, 8, 4, 128]) for i in range(2)]
            pg = PSM(st, "pg", [128, 1024])
            pr = [PSM(st, f"pr{i}", [8, 512]) for i in range(2)]
            fw.dma("sp", lambda h: h.dma_start(out=ab[:], in_=ab_tm.rearrange("(n p) c -> p n c", p=128)), reads=["ab_tm"], writes=["ab"])
            bc = lambda t: t[:].unsqueeze(1).to_broadcast([128, 64, 16])
            fw.op("dve", lambda h: h.tensor_tensor(out=t1_[:], in0=ab[:, :, 0:16], in1=bc(dtb16), op=ALU.add), reads=["ab", "dtb16"], writes=["t1_"])
            fw.op("act", lambda h: h.activation(out=t2_[:], in_=t1_[:], func=AF.Abs), reads=["t1_"], writes=["t2_"])
            fw.op("act", lambda h: h.activation(out=t2_[:], in_=t2_[:], func=AF.Exp, scale=-1.0), reads=["t2_"], writes=["t2_"])
            fw.op("act", lambda h: h.activation(out=t2_[:], in_=t2_[:], func=AF.Ln, bias=1.0, scale=1.0), reads=["t2_"], writes=["t2_"])
            fw.op("dve", lambda h: h.scalar_tensor_tensor(out=t1_[:], in0=t1_[:], scalar=0.0, in1=t2_[:], op0=ALU.max, op1=ALU.add), reads=["t1_", "t2_"], writes=["t1_"])
            fw.op("dve", lambda h: h.tensor_tensor(out=g_[:], in0=t1_[:], in1=bc(nA16), op=ALU.mult), reads=["t1_", "nA16"], writes=["g_"])
            fw.op("act", lambda h: h.activation(out=Btm[:], in_=ab[:, :, 16:32], func=AF.Sigmoid), reads=["ab"], writes=["Btm"])
            mlow = cst[:, C_MLOW:C_MLOW + 128]; mup = cst[:, C_MUP:C_MUP + 128]

            def gmm(h):
                for c in range(64):
                    h.matmul(pg[:, c * 16:c * 16 + 8], lhsT=mlow, rhs=g_[:, c, 0:8], start=True, stop=True)
                    ins = h.matmul(pg[:, c * 16 + 8:c * 16 + 16], lhsT=mup, rhs=g_[:, c, 8:16], start=True, stop=True)
                return ins
            fw.op("pe", gmm, reads=["g_", "cst"], writes=["pg"])
            fw.op("dve", lambda h: h.tensor_copy(out=Gtm[:].rearrange("p c k -> p (c k)"), in_=pg[:, :]), reads=["pg"], writes=["Gtm"])
            for c in range(64):
                pp = pr[c % 2]
                rw = rows[(c // 8) % 2]

                def rmm(h, c=c, pp=pp):
                    h.matmul(pp[:, 0:128], lhsT=g_[:, c, 0:8], rhs=mlow, start=True, stop=True)
                    h.matmul(pp[:, 128:256], lhsT=g_[:, c, 8:16], rhs=mup, start=True, stop=True)
                    h.matmul(pp[:, 256:384], lhsT=Btm[:, c, 0:8], rhs=ident, start=True, stop=True)
                    return h.matmul(pp[:, 384:512], lhsT=Btm[:, c, 8:16], rhs=ident, start=True, stop=True)
                fw.op("pe", rmm, reads=["g_", "Btm", "cst"], writes=[("pr", c % 2)])
                fw.op("dve", lambda h, c=c, pp=pp, rw=rw: h.tensor_copy(out=rw[:, c % 8, :, :].rearrange("p k t -> p (k t)"), in_=pp[:, :]),
                      reads=[("pr", c % 2)], writes=[("rows", (c // 8) % 2)])
                if c % 8 == 7:
                    c0 = c - 7
                    for k4 in range(4):
                        dr_, kd_ = k4 % 2, k4 // 2
                        fw.dma("sp", lambda h, rw=rw, c0=c0, k4=k4, dr_=dr_, kd_=kd_: h.dma_start(
                            out=GB[dr_, kd_, c0:c0 + 8, :, :].rearrange("c h t -> h c t"), in_=rw[:, :, k4, :]),
                            reads=[("rows", (c // 8) % 2)], writes=["GB"])
            fw.barrier()

        fw.phase = "P4"
        with ExitStack() as st:
            ld_q = [SB(st, f"ldq{i}", [128, 8, 128]) for i in range(2)]
            ld_k = [SB(st, f"ldk{i}", [128, 8, 128]) for i in range(2)]
            ld_kt = [SB(st, f"ldkt{i}", [128, 8, 128]) for i in range(2)]
            ld_vt = [SB(st, f"ldvt{i}", [128, 8, 128]) for i in range(2)]
            ld_g = [SB(st, f"ldg{i}", [128, 8, 128]) for i in range(2)]
            ld_b = [SB(st, f"ldb{i}", [128, 8, 128]) for i in range(2)]
            f1 = SB(st, "f1", [128, 8, 128]); f2 = SB(st, "f2", [128, 8, 128]); f3 = SB(st, "f3", [128, 8, 128]); f4 = SB(st, "f4", [128, 8, 128])
            f5 = SB(st, "f5", [128, 8, 128]); f6 = SB(st, "f6", [128, 8, 128])
            kbf = SB(st, "kbf", [128, 8, 128], BF16); qbf = SB(st, "qbf", [128, 8, 128], BF16)
            P1b = SB(st, "P1b", [128, 8, 128], BF16); Q1b = SB(st, "Q1b", [128, 8, 128], BF16)
            Dm = SB(st, "Dm", [128, 8, 128], BF16); Em = SB(st, "Em", [128, 8, 128], BF16)
            Xb = SB(st, "Xb", [128, 8, 128], BF16); X2b = SB(st, "X2b", [128, 8, 128], BF16)
            lvl = SB(st, "lvl", [128, 14 * 128])
            fw.dma("sp", lambda h: h.dma_start(out=lvl[:], in_=lvlmask[:, :]), writes=["lvl"])
            lvu = SB(st, "lvu", [128, 14 * 128], mybir.dt.uint8)
            fw.op("dve", lambda h: h.tensor_copy(out=lvu[:], in_=lvl[:]), reads=["lvl"], writes=["lvu"])
            qkm = SB(st, "qkm", [128, 8, 128], BF16); qgT = SB(st, "qgT", [128, 8, 128], BF16)
            kbg = SB(st, "kbg", [128, 8, 128], BF16); vb = SB(st, "vb", [128, 8, 128], BF16); kd = SB(st, "kd", [128, 8, 128], BF16)
            wT = SB(st, "wT", [128, 8, 128], BF16); vnew = SB(st, "vnew", [128, 8, 128], BF16)
            u_ = SB(st, "u_", [128, 8, 128]); S_ = SB(st, "S_", [128, 8, 128]); Sbf = SB(st, "Sbf", [128, 8, 128], BF16)
            ost = [SB(st, f"ost{i}", [128, 8, 128]) for i in range(2)]
            sm = SB(st, "sm", [128, 8, 4])
            PA = PSM(st, "PA", [128, 1024]); PB = PSM(st, "PB", [128, 1024]); PC = PSM(st, "PC", [128, 1024]); PD = PSM(st, "PD", [128, 1024])
            print("P4 sbuf remaining", nc.sbuf_bytes_remaining)
            v3 = lambda t: t[:, :].rearrange("p (h c) -> p h c", h=8)
            maskc = lambda off: cst[:, off:off + 128].unsqueeze(1).to_broadcast([128, 8, 128])
            identb = cst[:, C_ID:C_ID + 128].unsqueeze(1).to_broadcast([128, 8, 128])
            def p4_load(dr_, c, b2):
                t0 = c * 128
                lq, lk, lkt, lvt, lg, lb = ld_q[b2], ld_k[b2], ld_kt[b2], ld_vt[b2], ld_g[b2], ld_b[b2]
                K = lambda n: (n, b2)
                fw.dma("sp", lambda h, lq=lq, t0=t0: h.dma_start(out=lq[:], in_=qn[:, t0:t0 + 128].rearrange("(h p) t -> p h t", p=128)), reads=["qkn"], writes=[K("ldq")])
                fw.dma("sp", lambda h, lk=lk, t0=t0: h.dma_start(out=lk[:], in_=kn[:, t0:t0 + 128].rearrange("(h p) t -> p h t", p=128)), reads=["qkn"], writes=[K("ldk")])
                fw.dma("sp", lambda h, lkt=lkt, t0=t0: h.dma_start(out=lkt[:], in_=k_tm[t0:t0 + 128, :].rearrange("p (h c) -> p h c", h=8)), reads=["kv_tm"], writes=[K("ldkt")])
                fw.dma("sp", lambda h, lvt=lvt, t0=t0: h.dma_start(out=lvt[:], in_=v_tm[t0:t0 + 128, :].rearrange("p (h c) -> p h c", h=8)), reads=["kv_tm"], writes=[K("ldvt")])
                fw.dma("sp", lambda h, lg=lg, c=c, dr_=dr_: h.dma_start(out=lg[:].rearrange("p h t -> p (h t)"),
                                                                      in_=GB[dr_, 0, c:c + 1, :, :].rearrange("c h t -> c (h t)").partition_broadcast(128)),
                       reads=["GB"], writes=[K("ldg")])
                fw.dma("sp", lambda h, lb=lb, c=c, dr_=dr_: h.dma_start(out=lb[:].rearrange("p h t -> p (h t)"),
                                                                      in_=GB[dr_, 1, c:c + 1, :, :].rearrange("c h t -> c (h t)").partition_broadcast(128)),
                       reads=["GB"], writes=[K("ldb")])

            seq_all = [(0, c) for c in range(64)] + [(1, c) for c in range(63, -1, -1)]
            p4_load(0, 0, 0)
            it = 0
            for dr_ in range(2):
                fw.op("pool", lambda h: h.memset(S_[:], 0.0), writes=[("S_", 0), ("S_", 1)])
                fw.op("pool", lambda h: h.memset(Sbf[:], 0.0), writes=[("Sbf", 0), ("Sbf", 1)])
                order = list(range(64)) if dr_ == 0 else list(range(63, -1, -1))
                m_p1 = C_GT if dr_ == 0 else C_LT
                m_q1 = C_LT if dr_ == 0 else C_GT
                m_qk = C_LE if dr_ == 0 else C_GE
                last = 127 if dr_ == 0 else 0
                for c in order:
                    b2 = it % 2
                    it += 1
                    if it < 128:
                        p4_load(seq_all[it][0], seq_all[it][1], it % 2)
                    t0 = c * 128
                    lq, lk, lkt, lvt, lg, lb = ld_q[b2], ld_k[b2], ld_kt[b2], ld_vt[b2], ld_g[b2], ld_b[b2]
                    K = lambda n: (n, b2)
                    Gp = Gtm[:, c, dr_ * 8:dr_ * 8 + 8]
                    Bp = Btm[:, c, dr_ * 8:dr_ * 8 + 8]
                    Gpb = Gp.unsqueeze(2).to_broadcast([128, 8, 128])
                    Bpb = Bp.unsqueeze(2).to_broadcast([128, 8, 128])
                    HV = (slice(0, 4), slice(4, 8))
                    lm = lambda k, up: lvl[:, (2 * k + up) * 128:(2 * k + up + 1) * 128].unsqueeze(1).to_broadcast([128, 4, 128])
                    lmu = lambda k, up: lvu[:, (2 * k + up) * 128:(2 * k + up + 1) * 128].unsqueeze(1).to_broadcast([128, 4, 128])
                    mk4 = lambda off: cst[:, off:off + 128].unsqueeze(1).to_broadcast([128, 4, 128])
                    id4 = cst[:, C_ID:C_ID + 128].unsqueeze(1).to_broadcast([128, 4, 128])
                    dsel = 0 if dr_ == 0 else 1
                    seg = c // 16

                    def H(n, hf):
                        return (n, hf)

                    def pv(PX, hf):
                        return v3(PX)[:, HV[hf], :]

                    def both(fn):
                        for hf in range(2):
                            fn(hf, HV[hf])

                    def s_cast(hf, hs):
                        fw.op("act", lambda h: h.activation(out=kbf[:, hs, :], in_=lk[:, hs, :], func=AF.Copy), reads=[K("ldk")], writes=[H("kbf", hf)])
                        fw.op("act", lambda h: h.activation(out=qbf[:, hs, :], in_=lq[:, hs, :], func=AF.Copy), reads=[K("ldq")], writes=[H("qbf", hf)])

                        def kk(h):
                            for hd in range(hs.start, hs.stop):
                                h.matmul(PA[:, hd * 128:(hd + 1) * 128], lhsT=kbf[:, hd, :], rhs=kbf[:, hd, :], start=True, stop=True)
                                ins = h.matmul(PB[:, hd * 128:(hd + 1) * 128], lhsT=kbf[:, hd, :], rhs=qbf[:, hd, :], start=True, stop=True)
                            return ins
                        fw.op("pe", kk, reads=[H("kbf", hf), H("qbf", hf)], writes=[H("PA", hf), H("PB", hf)])
                    both(s_cast)

                    def s_dec1(hf, hs):
                        Gpb4 = Gp[:, hs].unsqueeze(2).to_broadcast([128, 4, 128])
                        fw.op("dve", lambda h: h.tensor_tensor(out=f1[:, hs, :], in0=lg[:, hs, :], in1=Gpb4, op=ALU.subtract), reads=[K("ldg"), "Gtm"], writes=[H("f1", hf)])
                        fw.op("pool", lambda h: h.tensor_scalar(out=f2[:, hs, :], in0=f1[:, hs, :], scalar1=0.0, scalar2=None, op0=ALU.max), reads=[H("f1", hf)], writes=[H("f2", hf)])
                        fw.op("act", lambda h: h.activation(out=f2[:, hs, :], in_=f2[:, hs, :], func=AF.Exp, scale=-1.0), reads=[H("f2", hf)], writes=[H("f2", hf)])
                        fw.op("pool", lambda h: h.tensor_scalar(out=f3[:, hs, :], in0=f1[:, hs, :], scalar1=0.0, scalar2=None, op0=ALU.min), reads=[H("f1", hf)], writes=[H("f3", hf)])
                        fw.op("act", lambda h: h.activation(out=f3[:, hs, :], in_=f3[:, hs, :], func=AF.Exp), reads=[H("f3", hf)], writes=[H("f3", hf)])
                    both(s_dec1)

                    def s_dec2(hf, hs):
                        Bpb4 = Bp[:, hs].unsqueeze(2).to_broadcast([128, 4, 128])
                        fw.op("pool", lambda h: h.tensor_tensor(out=f2[:, hs, :], in0=f2[:, hs, :], in1=mk4(m_p1), op=ALU.mult), reads=[H("f2", hf), "cst"], writes=[H("f2", hf)])
                        fw.op("dve", lambda h: h.scalar_tensor_tensor(out=f2[:, hs, :], in0=f2[:, hs, :], scalar=-1.0, in1=Bpb4, op0=ALU.mult, op1=ALU.mult),
                              reads=[H("f2", hf), "Btm"], writes=[H("f2", hf)])
                        fw.op("dve", lambda h: h.tensor_tensor(out=P1b[:, hs, :], in0=pv(PA, hf), in1=f2[:, hs, :], op=ALU.mult), reads=[H("PA", hf), H("f2", hf)], writes=[H("P1b", hf)])
                        fw.op("dve", lambda h: h.scalar_tensor_tensor(out=f4[:, hs, :], in0=f3[:, hs, :], scalar=-1.0, in1=lb[:, hs, :], op0=ALU.mult, op1=ALU.mult),
                              reads=[H("f3", hf), K("ldb")], writes=[H("f4", hf)])
                        fw.op("pool", lambda h: h.tensor_tensor(out=f4[:, hs, :], in0=f4[:, hs, :], in1=mk4(m_q1), op=ALU.mult), reads=[H("f4", hf), "cst"], writes=[H("f4", hf)])
                        fw.op("dve", lambda h: h.tensor_tensor(out=Q1b[:, hs, :], in0=pv(PA, hf), in1=f4[:, hs, :], op=ALU.mult), reads=[H("PA", hf), H("f4", hf)], writes=[H("Q1b", hf)])
                        fw.op("pool", lambda h: h.tensor_tensor(out=f3[:, hs, :], in0=f3[:, hs, :], in1=mk4(m_qk), op=ALU.mult), reads=[H("f3", hf), "cst", H("f4", hf)], writes=[H("f3", hf)])
                        fw.op("dve", lambda h: h.tensor_tensor(out=qkm[:, hs, :], in0=pv(PB, hf), in1=f3[:, hs, :], op=ALU.mult), reads=[H("PB", hf), H("f3", hf)], writes=[H("qkm", hf)])
                    both(s_dec2)

                    def s_lvl0(hf, hs):
                        fw.op("pool", lambda h: h.tensor_tensor(out=Dm[:, hs, :], in0=P1b[:, hs, :], in1=lm(0, dsel), op=ALU.mult), reads=[H("P1b", hf), "lvl"], writes=[H("Dm", hf)])
                        fw.op("pool", lambda h: h.tensor_tensor(out=Dm[:, hs, :], in0=Dm[:, hs, :], in1=id4, op=ALU.add), reads=[H("Dm", hf), "cst"], writes=[H("Dm", hf)])
                        fw.op("pool", lambda h: h.tensor_tensor(out=Em[:, hs, :], in0=Q1b[:, hs, :], in1=lm(0, 1 - dsel), op=ALU.mult), reads=[H("Q1b", hf), "lvl"], writes=[H("Em", hf)])
                        fw.op("pool", lambda h: h.tensor_tensor(out=Em[:, hs, :], in0=Em[:, hs, :], in1=id4, op=ALU.add), reads=[H("Em", hf), "cst"], writes=[H("Em", hf)])
                    both(s_lvl0)

                    fw.op("act", lambda h: h.activation(out=f5[:], in_=lg[:], func=AF.Exp), reads=[K("ldg")], writes=["f5"])
                    fw.op("pool", lambda h: h.tensor_tensor(out=qgT[:], in0=lq[:], in1=f5[:], op=ALU.mult), reads=[K("ldq"), "f5"], writes=[H("qgT", 0), H("qgT", 1)])
                    fw.op("act", lambda h: h.activation(out=sm[:, :, 0], in_=Gp, func=AF.Exp), reads=["Gtm"], writes=["sm0"])
                    fw.op("dve", lambda h: h.tensor_tensor(out=sm[:, :, 0], in0=sm[:, :, 0], in1=Bp, op=ALU.mult), reads=["sm0", "Btm"], writes=["sm0"])
                    fw.op("dve", lambda h: h.tensor_tensor(out=sm[:, :, 1], in0=lg[:, :, last], in1=Gp, op=ALU.subtract), reads=[K("ldg"), "Gtm"], writes=["sm1"])
                    fw.op("act", lambda h: h.activation(out=sm[:, :, 1], in_=sm[:, :, 1], func=AF.Exp), reads=["sm1"], writes=["sm1"])
                    fw.op("act", lambda h: h.activation(out=sm[:, :, 2], in_=lg[:, :, last], func=AF.Exp), reads=[K("ldg")], writes=["sm2"])
                    smb = lambda i: sm[:, :, i:i + 1].to_broadcast([128, 8, 128])
                    fw.op("pool", lambda h: h.tensor_tensor(out=kbg[:], in0=lkt[:], in1=smb(0), op=ALU.mult), reads=[K("ldkt"), "sm0"], writes=[H("kbg", 0), H("kbg", 1)])
                    fw.op("pool", lambda h: h.tensor_tensor(out=vb[:], in0=lvt[:], in1=Bpb, op=ALU.mult), reads=[K("ldvt"), "Btm"], writes=[H("vb", 0), H("vb", 1)])
                    fw.op("pool", lambda h: h.tensor_tensor(out=kd[:], in0=lkt[:], in1=smb(1), op=ALU.mult), reads=[K("ldkt"), "sm1"], writes=[H("kd", 0), H("kd", 1)])

                    for k in range(1, 7):
                        def s_x(hf, hs):
                            def xmm(h):
                                for hd in range(hs.start, hs.stop):
                                    h.matmul(PC[:, hd * 128:(hd + 1) * 128], lhsT=Q1b[:, hd, :], rhs=Dm[:, hd, :], start=True, stop=True)
                                    ins = h.matmul(PD[:, hd * 128:(hd + 1) * 128], lhsT=P1b[:, hd, :], rhs=Em[:, hd, :], start=True, stop=True)
                                return ins
                            fw.op("pe", xmm, reads=[H("Q1b", hf), H("P1b", hf), H("Dm", hf), H("Em", hf)], writes=[H("PC", hf), H("PD", hf)])
                            fw.op("act", lambda h: h.activation(out=Xb[:, hs, :], in_=pv(PC, hf), func=AF.Copy), reads=[H("PC", hf)], writes=[H("Xb", hf)])
                            fw.op("act", lambda h: h.activation(out=X2b[:, hs, :], in_=pv(PD, hf), func=AF.Copy), reads=[H("PD", hf)], writes=[H("X2b", hf)])
                        both(s_x)

                        def s_z(hf, hs):
                            def zmm(h):
                                for hd in range(hs.start, hs.stop):
                                    h.matmul(PA[:, hd * 128:(hd + 1) * 128], lhsT=Em[:, hd, :], rhs=Xb[:, hd, :], start=True, stop=True)
                                    ins = h.matmul(PB[:, hd * 128:(hd + 1) * 128], lhsT=Dm[:, hd, :], rhs=X2b[:, hd, :], start=True, stop=True)
                                return ins
                            fw.op("pe", zmm, reads=[H("Em", hf), H("Dm", hf), H("Xb", hf), H("X2b", hf)], writes=[H("PA", hf), H("PB", hf)])
                            fw.op("dve", lambda h: h.copy_predicated(out=Dm[:, hs, :], mask=lmu(k, dsel), data=pv(PA, hf)), reads=[H("PA", hf), "lvu", H("Dm", hf)], writes=[H("Dm", hf)])
                            fw.op("dve", lambda h: h.copy_predicated(out=Em[:, hs, :], mask=lmu(k, 1 - dsel), data=pv(PB, hf)), reads=[H("PB", hf), "lvu", H("Em", hf)], writes=[H("Em", hf)])
                        both(s_z)

                    def s_u(hf, hs):
                        def umm(h):
                            for hd in range(hs.start, hs.stop):
                                h.matmul(PB[:, hd * 128:(hd + 1) * 128], lhsT=Em[:, hd, :], rhs=vb[:, hd, :], start=True, stop=True)
                                ins = h.matmul(PA[:, hd * 128:(hd + 1) * 128], lhsT=kbg[:, hd, :], rhs=Em[:, hd, :], start=True, stop=True)
                            return ins
                        fw.op("pe", umm, reads=[H("Em", hf), H("vb", hf), H("kbg", hf)], writes=[H("PA", hf), H("PB", hf)])
                        fw.op("act", lambda h: h.activation(out=u_[:, hs, :], in_=pv(PB, hf), func=AF.Copy), reads=[H("PB", hf)], writes=[H("u_", hf)])
                        fw.op("dve", lambda h: h.tensor_copy(out=wT[:, hs, :], in_=pv(PA, hf)), reads=[H("PA", hf)], writes=[H("wT", hf)])
                    both(s_u)

                    def s_seq(hf, hs):
                        if dr_ == 0 and c % 16 == 0 and c > 0:
                            fw.op("dve", lambda h: h.tensor_scalar(out=S_[:, hs, :], in0=S_[:, hs, :], scalar1=sgf[:, seg:seg + 1], scalar2=None, op0=ALU.mult), reads=[H("S_", hf), "sgf"], writes=[H("S_", hf)])
                            fw.op("act", lambda h: h.activation(out=Sbf[:, hs, :], in_=S_[:, hs, :], func=AF.Copy), reads=[H("S_", hf)], writes=[H("Sbf", hf)])
                        if dr_ == 1 and c % 16 == 15 and c < 63:
                            fw.op("dve", lambda h: h.tensor_scalar(out=S_[:, hs, :], in0=S_[:, hs, :], scalar1=sgf[:, 4 + seg:5 + seg], scalar2=None, op0=ALU.mult), reads=[H("S_", hf), "sgf"], writes=[H("S_", hf)])
                            fw.op("act", lambda h: h.activation(out=Sbf[:, hs, :], in_=S_[:, hs, :], func=AF.Copy), reads=[H("S_", hf)], writes=[H("Sbf", hf)])

                        def wsmm(h):
                            for hd in range(hs.start, hs.stop):
                                ins = h.matmul(PC[:, hd * 128:(hd + 1) * 128], lhsT=wT[:, hd, :], rhs=Sbf[:, hd, :], start=True, stop=True)
                            return ins
                        fw.op("pe", wsmm, reads=[H("wT", hf), H("Sbf", hf)], writes=[H("PC", hf)])
                        fw.op("dve", lambda h: h.tensor_tensor(out=vnew[:, hs, :], in0=u_[:, hs, :], in1=pv(PC, hf), op=ALU.subtract), reads=[H("u_", hf), H("PC", hf)], writes=[H("vnew", hf)])

                        def omm(h):
                            for hd in range(hs.start, hs.stop):
                                h.matmul(PD[:, hd * 128:(hd + 1) * 128], lhsT=qgT[:, hd, :], rhs=Sbf[:, hd, :], start=True, stop=False)
                                h.matmul(PD[:, hd * 128:(hd + 1) * 128], lhsT=qkm[:, hd, :], rhs=vnew[:, hd, :], start=False, stop=True)
                                ins = h.matmul(PB[:, hd * 128:(hd + 1) * 128], lhsT=kd[:, hd, :], rhs=vnew[:, hd, :], start=True, stop=True)
                            return ins
                        fw.op("pe", omm, reads=[H("qgT", hf), H("Sbf", hf), H("qkm", hf), H("vnew", hf), H("kd", hf)], writes=[H("PD", hf), H("PB", hf)])
                        fw.op("act", lambda h: h.activation(out=ost[b2][:, hs, :], in_=pv(PD, hf), func=AF.Copy), reads=[H("PD", hf)], writes=[K("ost")])
                        fw.op("pool", lambda h: h.tensor_tensor(out=S_[:, hs, :], in0=S_[:, hs, :], in1=sm[:, hs, 2:3].to_broadcast([128, 4, 128]), op=ALU.mult), reads=[H("S_", hf), "sm2"], writes=[H("S_", hf)])
                        fw.op("dve", lambda h: h.tensor_tensor(out=S_[:, hs, :], in0=S_[:, hs, :], in1=pv(PB, hf), op=ALU.add), reads=[H("S_", hf), H("PB", hf)], writes=[H("S_", hf)])
                        fw.op("act", lambda h: h.activation(out=Sbf[:, hs, :], in_=S_[:, hs, :], func=AF.Copy), reads=[H("S_", hf)], writes=[H("Sbf", hf)])
                    both(s_seq)
                    fw.dma("sp", lambda h: h.dma_start(out=o_dir[dr_, t0:t0 + 128, :].rearrange("p (h c) -> p h c", h=8), in_=ost[b2][:]),
                           reads=[K("ost")], writes=["o_dir"])
            fw.barrier()

        fw.phase = "P5"
        with ExitStack() as st:
            kw = [SB(st, f"kw{i}", [128, 256], BF16) for i in range(4)]
            qw = [SB(st, f"qw{i}", [128, 128], BF16) for i in range(4)]
            vw = [SB(st, f"vw{i}", [128, 2, 128], BF16) for i in range(4)]
            sc_ = [SB(st, f"sc{i}", [128, 256]) for i in range(4)]
            pe_ = [SB(st, f"pe{i}", [128, 256], BF16) for i in range(4)]
            pT = [SB(st, f"pT{i}", [128, 2, 128], BF16) for i in range(4)]
            oo = [SB(st, f"oo{i}", [128, 130]) for i in range(4)]
            nmx = [SB(st, f"nmx{i}", [128, 1]) for i in range(4)]
            idb = SB(st, "idb", [128, 128], BF16)
            msk = SB(st, "msk", [128, 4, 3, 256])
            mk1 = SB(st, "mk1", [128, 1])
            psc_t = [PSM(st, f"psc{i}", [128, 512]) for i in range(2)]
            psc = [psc_t[i // 2][:, (i % 2) * 256:(i % 2 + 1) * 256] for i in range(4)]
            ppt_t = PSM(st, "ppt", [128, 4, 2, 128], BF16)
            ppt = [ppt_t[:, i, :, :] for i in range(4)]
            pov_t = PSM(st, "pov", [128, 4, 128])
            pov = [pov_t[:, i, :] for i in range(4)]
            fw.op("dve", lambda h: h.tensor_copy(out=idb[:], in_=ident), reads=["cst"], writes=["idb"])
            band = cst[:, C_BAND:C_BAND + 256]; negl = cst[:, C_NEGL:C_NEGL + 256]; negr = cst[:, C_NEGR:C_NEGR + 256]
            for s in range(NSEG):
                fw.op("dve", lambda h, s=s: h.tensor_scalar(out=mk1[:], in0=sgf[:, s:s + 1], scalar1=-1.0, scalar2=1.0, op0=ALU.mult, op1=ALU.add), reads=["sgf"], writes=["mk1"])
                fw.op("dve", lambda h, s=s: h.scalar_tensor_tensor(out=msk[:, s, 0, :], in0=negl, scalar=mk1[:, 0:1], in1=band, op0=ALU.mult, op1=ALU.add),
                      reads=["mk1", "cst"], writes=["msk"])
                fw.op("dve", lambda h, s=s: h.tensor_scalar(out=mk1[:], in0=sgf[:, 4 + s:5 + s], scalar1=-1.0, scalar2=1.0, op0=ALU.mult, op1=ALU.add), reads=["sgf", "msk"], writes=["mk1"])
                fw.op("dve", lambda h, s=s: h.scalar_tensor_tensor(out=msk[:, s, 1, :], in0=negr, scalar=mk1[:, 0:1], in1=band, op0=ALU.mult, op1=ALU.add),
                      reads=["mk1", "cst"], writes=["msk"])
                fw.op("dve", lambda h, s=s: h.scalar_tensor_tensor(out=msk[:, s, 2, :], in0=negr, scalar=mk1[:, 0:1], in1=msk[:, s, 0, :], op0=ALU.mult, op1=ALU.add),
                      reads=["mk1", "cst", "msk"], writes=["msk"])
            units = []
            for g in range(3):
                for hh in range(4):
                    for r in range(DIL[g]):
                        for b in range(MC[g] // 128):
                            units.append((g, hh, r, b))

            def u_load(it):
                g, hh, r, b = units[it]
                i2 = it % 4
                K = lambda n: (n, i2)
                kfull = akT[g][hh * 128:(hh + 1) * 128, :].rearrange("p (r c) -> p r c", c=CL[g])
                qfull = aqT[g][hh * 128:(hh + 1) * 128, :].rearrange("p (r c) -> p r c", c=CL[g])
                vfull = avs[g][:, hh * 128:(hh + 1) * 128].rearrange("(r c) d -> r c d", c=CL[g])
                fw.dma("sp", lambda h: h.dma_start(out=kw[i2][:], in_=kfull[:, r, 128 * b:128 * b + 256]), reads=[("aqk", g), ("akT", g)], writes=[K("kw")])
                fw.dma("sp", lambda h: h.dma_start(out=qw[i2][:], in_=qfull[:, r, 64 + 128 * b:64 + 128 * b + 128]), reads=[("aqk", g)], writes=[K("qw")])
                fw.dma("sp", lambda h: h.dma_start(out=vw[i2][:], in_=vfull[r, 128 * b:128 * b + 256, :].rearrange("(k p) d -> p k d", p=128)),
                       reads=[("av", g)], writes=[K("vw")])

            def u_comp(it):
                g, hh, r, b = units[it]
                dil = DIL[g]
                tps = MS[g] // 128
                i2 = it % 4
                K = lambda n: (n, i2)
                s = b // tps
                first = (b % tps == 0)
                lastt = (b % tps == tps - 1)
                fw.op("pe", lambda h: h.matmul(psc[i2], lhsT=qw[i2][:], rhs=kw[i2][:], start=True, stop=True), reads=[K("qw"), K("kw")], writes=[K("psc")])
                if first and lastt:
                    mask = msk[:, s, 2, :]
                elif first:
                    mask = msk[:, s, 0, :]
                elif lastt:
                    mask = msk[:, s, 1, :]
                else:
                    mask = band
                fw.op("dve", lambda h: h.tensor_tensor(out=sc_[i2][:], in0=psc[i2], in1=mask, op=ALU.add), reads=[K("psc"), "msk", "cst"], writes=[K("sc")])
                fw.op("dve", lambda h: h.reduce_max(out=oo[i2][:, 128:129], in_=sc_[i2][:], axis=AX.X), reads=[K("sc")], writes=[K("oo")])
                fw.op("dve", lambda h: h.tensor_scalar(out=nmx[i2][:], in0=oo[i2][:, 128:129], scalar1=-1.0, scalar2=None, op0=ALU.mult), reads=[K("oo")], writes=[K("nmx")])
                fw.op("act", lambda h: h.activation(out=pe_[i2][:], in_=sc_[i2][:], func=AF.Exp, bias=nmx[i2][:], scale=1.0, accum_out=oo[i2][:, 129:130]),
                      reads=[K("sc"), K("nmx")], writes=[K("pe"), K("oo")])

                def ptr_(h):
                    h.transpose(out=ppt[i2][:, 0, :], in_=pe_[i2][:, 0:128], identity=idb[:])
                    return h.transpose(out=ppt[i2][:, 1, :], in_=pe_[i2][:, 128:256], identity=idb[:])
                fw.op("pe", ptr_, reads=[K("pe"), "idb"], writes=[K("ppt")])
                fw.op("act", lambda h: h.activation(out=pT[i2][:], in_=ppt[i2], func=AF.Copy), reads=[K("ppt")], writes=[K("pT")])

                def pv(h):
                    h.matmul(pov[i2], lhsT=pT[i2][:, 0, :], rhs=vw[i2][:, 0, :], start=True, stop=False)
                    return h.matmul(pov[i2], lhsT=pT[i2][:, 1, :], rhs=vw[i2][:, 1, :], start=False, stop=True)
                fw.op("pe", pv, reads=[K("pT"), K("vw")], writes=[K("pov")])
                fw.op("dve", lambda h: h.tensor_copy(out=oo[i2][:, 0:128], in_=pov[i2]), reads=[K("pov")], writes=[K("oo")])
                dst = Oat[g, :, hh, :].rearrange("(m r) c -> r m c", r=dil)[r, 128 * b:128 * b + 128, :]
                fw.dma("pool", lambda h: h.dma_start(out=dst, in_=oo[i2][:]), reads=[K("oo")], writes=["Oat"])

            NU = len(units)
            for it in range(NU + 2):
                if it < NU:
                    u_load(it)
                if it >= 2:
                    u_comp(it - 2)
            fw.barrier()

        fw.phase = "P5b"
        with ExitStack() as st:
            of_ = [SB(st, f"of{i}", [128, 8, 128]) for i in range(2)]
            ob_ = [SB(st, f"ob{i}", [128, 8, 128]) for i in range(2)]
            szt = [SB(st, f"szt{i}", [128, 8, 128]) for i in range(2)]
            osq = SB(st, "osq", [128, 8, 128])
            ss8 = SB(st, "ss8", [128, 8])
            yd = [SB(st, f"yd{i}", [128, 8, 128], BF16) for i in range(2)]
            og = [[SB(st, f"og{g}_{i}", [128, 4, 130]) for i in range(2)] for g in range(3)]
            m4 = SB(st, "m4", [128, 4]); w4 = SB(st, "w4", [128, 3, 4]); d4 = SB(st, "d4", [128, 4]); t4 = SB(st, "t4", [128, 4])
            am = SB(st, "am", [128, 4, 128]); am2 = SB(st, "am2", [128, 4, 128])
            ya = [SB(st, f"ya{i}", [128, 4, 128], BF16) for i in range(2)]
            pdn = PSM(st, "pdn", [128, 1024])
            pat = PSM(st, "pat", [128, 512])
            for tl in range(64):
                t0 = tl * 128
                i2 = tl % 2
                K = lambda n: (n, i2)
                fw.dma("sp", lambda h, i2=i2, t0=t0: h.dma_start(out=of_[i2][:], in_=o_dir[0, t0:t0 + 128, :].rearrange("p (h c) -> p h c", h=8)), reads=["o_dir"], writes=[K("of")])
                fw.dma("sp", lambda h, i2=i2, t0=t0: h.dma_start(out=ob_[i2][:], in_=o_dir[1, t0:t0 + 128, :].rearrange("p (h c) -> p h c", h=8)), reads=["o_dir"], writes=[K("ob")])
                fw.dma("sp", lambda h, i2=i2, t0=t0: h.dma_start(out=szt[i2][:], in_=szT[:, t0:t0 + 128].rearrange("(h p) t -> p h t", p=128)), reads=["szT"], writes=[K("szt")])
                fw.op("pool", lambda h, i2=i2: h.tensor_tensor(out=of_[i2][:], in0=of_[i2][:], in1=ob_[i2][:], op=ALU.add), reads=[K("of"), K("ob")], writes=[K("of")])
                fw.op("pool", lambda h, i2=i2: h.tensor_tensor(out=osq[:], in0=of_[i2][:], in1=of_[i2][:], op=ALU.mult), reads=[K("of")], writes=["osq"])
                fw.op("dve", lambda h: h.tensor_reduce(out=ss8[:], in_=osq[:], axis=AX.X, op=ALU.add), reads=["osq"], writes=["ss8"])
                fw.op("act", lambda h: h.activation(out=ss8[:], in_=ss8[:], func=AF.Sqrt, bias=EPS, scale=1.0 / 128.0), reads=["ss8"], writes=["ss8"])
                fw.op("dve", lambda h: h.reciprocal(out=ss8[:], in_=ss8[:]), reads=["ss8"], writes=["ss8"])
                fw.op("dve", lambda h, i2=i2: h.tensor_tensor(out=of_[i2][:], in0=of_[i2][:], in1=ss8[:].unsqueeze(2).to_broadcast([128, 8, 128]), op=ALU.mult),
                      reads=[K("of"), "ss8"], writes=[K("of")])

                def trd(h, i2=i2):
                    for hd in range(8):
                        ins = h.transpose(out=pdn[:, hd * 128:(hd + 1) * 128], in_=of_[i2][:, hd, :], identity=ident)
                    return ins
                fw.op("pe", trd, reads=[K("of"), "cst"], writes=["pdn"])
                fw.op("dve", lambda h, i2=i2: h.scalar_tensor_tensor(out=yd[i2][:], in0=pdn[:, :].rearrange("p (h c) -> p h c", h=8), scalar=dnwT[:, 0:1], in1=szt[i2][:],
                                                                   op0=ALU.mult, op1=ALU.mult), reads=["pdn", "dnwT", K("szt")], writes=[K("yd")])
                fw.dma("sp", lambda h, i2=i2, t0=t0: h.dma_start(out=ydnT[:, t0:t0 + 128].rearrange("(h p) t -> p h t", p=128), in_=yd[i2][:]), reads=[K("yd")], writes=["ydnT"])
                for g in range(3):
                    fw.dma("sp", lambda h, g=g, i2=i2, t0=t0: h.dma_start(out=og[g][i2][:], in_=Oat[g, t0:t0 + 128, :, :]), reads=["Oat"], writes=[K(f"og{g}")])
                mxs = [og[g][i2][:, :, 128] for g in range(3)]
                dns = [og[g][i2][:, :, 129] for g in range(3)]
                fw.op("dve", lambda h: h.tensor_tensor(out=m4[:], in0=mxs[0], in1=mxs[1], op=ALU.max), reads=[K("og0"), K("og1")], writes=["m4"])
                fw.op("dve", lambda h: h.tensor_tensor(out=m4[:], in0=m4[:], in1=mxs[2], op=ALU.max), reads=["m4", K("og2")], writes=["m4"])
                for g in range(3):
                    fw.op("dve", lambda h, g=g: h.tensor_tensor(out=w4[:, g, :], in0=mxs[g], in1=m4[:], op=ALU.subtract), reads=[K(f"og{g}"), "m4"], writes=["w4"])
                fw.op("act", lambda h: h.activation(out=w4[:], in_=w4[:], func=AF.Exp), reads=["w4"], writes=["w4"])
                fw.op("dve", lambda h: h.tensor_tensor(out=d4[:], in0=w4[:, 0, :], in1=dns[0], op=ALU.mult), reads=["w4", K("og0")], writes=["d4"])
                for g in (1, 2):
                    fw.op("dve", lambda h, g=g: h.tensor_tensor(out=t4[:], in0=w4[:, g, :], in1=dns[g], op=ALU.mult), reads=["w4", K(f"og{g}")], writes=["t4"])
                    fw.op("dve", lambda h: h.tensor_tensor(out=d4[:], in0=d4[:], in1=t4[:], op=ALU.add), reads=["d4", "t4"], writes=["d4"])
                fw.op("dve", lambda h: h.reciprocal(out=d4[:], in_=d4[:]), reads=["d4"], writes=["d4"])
                for g in range(3):
                    fw.op("dve", lambda h, g=g: h.tensor_tensor(out=w4[:, g, :], in0=w4[:, g, :], in1=d4[:], op=ALU.mult), reads=["w4", "d4"], writes=["w4"])
                fw.op("pool", lambda h, i2=i2: h.tensor_tensor(out=am[:], in0=og[0][i2][:, :, 0:128], in1=w4[:, 0, :].unsqueeze(2).to_broadcast([128, 4, 128]), op=ALU.mult),
                      reads=[K("og0"), "w4"], writes=["am"])
                for g in (1, 2):
                    fw.op("pool", lambda h, g=g, i2=i2: h.tensor_tensor(out=am2[:], in0=og[g][i2][:, :, 0:128], in1=w4[:, g, :].unsqueeze(2).to_broadcast([128, 4, 128]), op=ALU.mult),
                          reads=[K(f"og{g}"), "w4"], writes=["am2"])
                    fw.op("pool", lambda h: h.tensor_tensor(out=am[:], in0=am[:], in1=am2[:], op=ALU.add), reads=["am", "am2"], writes=["am"])

                def tra(h):
                    for hd in range(4):
                        ins = h.transpose(out=pat[:, hd * 128:(hd + 1) * 128], in_=am[:, hd, :], identity=ident)
                    return ins
                fw.op("pe", tra, reads=["am", "cst"], writes=["pat"])
                fw.op("act", lambda h, i2=i2: h.activation(out=ya[i2][:], in_=pat[:, :].rearrange("p (h c) -> p h c", h=4), func=AF.Copy), reads=["pat"], writes=[K("ya")])
                fw.dma("sp", lambda h, i2=i2, t0=t0: h.dma_start(out=yatT[:, t0:t0 + 128].rearrange("(h p) t -> p h t", p=128), in_=ya[i2][:]), reads=[K("ya")], writes=["yatT"])
            fw.barrier()

        fw.phase = "P6"
        with ExitStack() as st:
            xin = [SB(st, f"xin{q}", [128, D]) for q in range(4)]
            xT = SB(st, "xT", [128, 16, 512])
            acc = SB(st, "acc", [128, 16, 512])
            sq = [SB(st, f"sq{i}", [128, 512]) for i in range(2)]
            rstd = SB(st, "rstd", [128, 512])
            big = SB(st, "big", [128, 32, 512], BF16)
            h2T = SB(st, "h2T", [128, 16, 512], BF16)
            wk = [SB(st, f"wk{i}", [128, 32, 128], BF16) for i in range(3)]
            print("sbuf remaining", nc.sbuf_bytes_remaining)
            tmp = [SB(st, f"tmp{i}", [128, 512]) for i in range(4)]
            pxt = [PSM(st, f"pxt{i}", [128, 512]) for i in range(2)]
            pss = PSM(st, "pss", [128, 512])
            pmm = [PSM(st, f"pmm{i}", [128, 512]) for i in range(4)]
            hT = big[:, 0:16, :]
            ydn_s = big[:, 16:24, :]
            yat_s = big[:, 24:28, :]
            mixedT = h2T

            wslot = [0]

            def load_w(src_ap, nj, key):
                i = wslot[0] % 3
                wslot[0] += 1
                fw.dma("sp", lambda h, i=i: h.dma_start(out=wk[i][:, 0:nj, :], in_=src_ap), reads=[key], writes=[("wk", i)])
                return i

            def proj(widx, nj, rhs_fn, rhs_keys, pbank, first=True, last=True, j0=0):
                def mm(h):
                    for j in range(nj):
                        ins = h.matmul(pmm[pbank][:, :], lhsT=wk[widx][:, j, :], rhs=rhs_fn(j),
                                       start=(first and j == 0), stop=(last and j == nj - 1))
                    return ins
                fw.op("pe", mm, reads=[("wk", widx)] + rhs_keys, writes=[("pmm", pbank)])

            for tile in range(min(T // 512, KTILES)):
                t0 = tile * 512
                s = tile // 4
                front_end((xin, xT, sq, rstd, pxt, pss), t0, s)
                for j in range(16):
                    fw.op("pool", lambda h, j=j: h.tensor_tensor(out=tmp[j % 2][:], in0=xT[:, j, :], in1=rstd[:], op=ALU.mult),
                          reads=[("xT", j), "rstd"], writes=[("tmp", j % 2)])
                    fw.op("dve", lambda h, j=j: h.tensor_scalar(out=hT[:, j, :], in0=tmp[j % 2][:], scalar1=A1[:, j, s:s + 1],
                                                                scalar2=modT[:, j, s:s + 1], op0=ALU.mult, op1=ALU.add),
                          reads=[("tmp", j % 2), "A1", "modT"], writes=[("big", j)])
                fw.dma("sp", lambda h: h.dma_start(out=ydn_s, in_=ydnT[:, t0:t0 + 512].rearrange("(j p) t -> p j t", p=128)),
                       reads=["ydnT"], writes=[("big", 16 + j) for j in range(8)])
                fw.dma("sp", lambda h: h.dma_start(out=yat_s, in_=yatT[:, t0:t0 + 512].rearrange("(j p) t -> p j t", p=128)),
                       reads=["yatT"], writes=[("big", 24 + j) for j in range(4)])
                for cb in range(16):
                    w1 = load_w(wb_mg[cb], 16, ("wb_mg", cb))
                    proj(w1, 16, lambda j: hT[:, j, :], [("big", j) for j in range(16)], 0)
                    w2 = load_w(wb_mg[16 + cb], 16, ("wb_mg", 16 + cb))
                    proj(w2, 16, lambda j: hT[:, j, :], [("big", j) for j in range(16)], 1)
                    w3 = load_w(wb_dn[cb], 8, ("wb_dn", cb))
                    proj(w3, 8, lambda j: ydn_s[:, j, :], [("big", 16 + j) for j in range(8)], 2)
                    w4 = load_w(wb_at[cb], 4, ("wb_at", cb))
                    proj(w4, 4, lambda j: yat_s[:, j, :], [("big", 24 + j) for j in range(4)], 3)
                    fw.op("act", lambda h: h.activation(out=tmp[0][:], in_=pmm[0][:, :], func=AF.Sigmoid), reads=[("pmm", 0)], writes=[("tmp", 0)])
                    fw.op("act", lambda h: h.activation(out=tmp[1][:], in_=pmm[1][:, :], func=AF.Sigmoid), reads=[("pmm", 1)], writes=[("tmp", 1)])
                    fw.op("dve", lambda h: h.tensor_tensor(out=tmp[2][:], in0=pmm[2][:, :], in1=tmp[0][:], op=ALU.mult),
                          reads=[("pmm", 2), ("tmp", 0)], writes=[("tmp", 2)])
                    fw.op("dve", lambda h: h.tensor_tensor(out=tmp[3][:], in0=pmm[3][:, :], in1=tmp[1][:], op=ALU.mult),
                          reads=[("pmm", 3), ("tmp", 1)], writes=[("tmp", 3)])
                    fw.op("pool", lambda h, cb=cb: h.tensor_tensor(out=mixedT[:, cb, :], in0=tmp[2][:], in1=tmp[3][:], op=ALU.add),
                          reads=[("tmp", 2), ("tmp", 3)], writes=[("h2T", cb)])
                for cb in range(16):
                    w1 = load_w(wb_out[cb], 16, ("wb_out", cb))
                    proj(w1, 16, lambda j: mixedT[:, j, :], [("h2T", j) for j in range(16)], cb % 4)
                    fw.op("act", lambda h, cb=cb: h.activation(out=acc[:, cb, :], in_=pmm[cb % 4][:, :], func=AF.Copy),
                          reads=[("pmm", cb % 4)], writes=[("acc", cb)])
                sumsq_rstd(lambda j: acc[:, j, :], 16, sq, pss, rstd, lambda j: ("acc", j))
                for j in range(16):
                    fw.op("pool", lambda h, j=j: h.tensor_tensor(out=tmp[j % 2][:], in0=acc[:, j, :], in1=rstd[:], op=ALU.mult),
                          reads=[("acc", j), "rstd"], writes=[("tmp", j % 2)])
                    fw.op("dve", lambda h, j=j: h.scalar_tensor_tensor(out=xT[:, j, :], in0=tmp[j % 2][:], scalar=G1[:, j, s:s + 1], in1=xT[:, j, :],
                                                                       op0=ALU.mult, op1=ALU.add),
                          reads=[("tmp", j % 2), "G1", ("xT", j)], writes=[("xT", j)])
                sumsq_rstd(lambda j: xT[:, j, :], 16, sq, pss, rstd, lambda j: ("xT", j))
                for j in range(16):
                    fw.op("pool", lambda h, j=j: h.tensor_tensor(out=tmp[j % 2][:], in0=xT[:, j, :], in1=rstd[:], op=ALU.mult),
                          reads=[("xT", j), "rstd"], writes=[("tmp", j % 2)])
                    fw.op("dve", lambda h, j=j: h.tensor_scalar(out=h2T[:, j, :], in0=tmp[j % 2][:], scalar1=A2[:, j, s:s + 1],
                                                                scalar2=modT[:, 48 + j, s:s + 1], op0=ALU.mult, op1=ALU.add),
                          reads=[("tmp", j % 2), "A2", "modT"], writes=[("h2T", j)])
                for hf in range(2):
                    for fb in range(32):
                        w1 = load_w(wb_f1[hf * 32 + fb], 16, ("wb_f1", hf * 32 + fb))
                        pb = fb % 4
                        proj(w1, 16, lambda j: h2T[:, j, :], [("h2T", j) for j in range(16)], pb)
                        if fb % 2 == 0:
                            fw.op("act", lambda h, pb=pb: h.activation(out=tmp[pb][:], in_=pmm[pb][:, :], func=AF.Relu), reads=[("pmm", pb)], writes=[("tmp", pb)])
                            fw.op("pool", lambda h, pb=pb, fb=fb: h.tensor_tensor(out=big[:, fb, :], in0=tmp[pb][:], in1=tmp[pb][:], op=ALU.mult),
                                  reads=[("tmp", pb)], writes=[("big", fb)])
                        else:
                            fw.op("dve", lambda h, pb=pb: h.tensor_scalar(out=tmp[pb][:], in0=pmm[pb][:, :], scalar1=0.0, scalar2=None, op0=ALU.max),
                                  reads=[("pmm", pb)], writes=[("tmp", pb)])
                            fw.op("pool", lambda h, pb=pb, fb=fb: h.tensor_tensor(out=big[:, fb, :], in0=tmp[pb][:], in1=tmp[pb][:], op=ALU.mult),
                                  reads=[("tmp", pb)], writes=[("big", fb)])
                    for cb in range(16):
                        w1 = load_w(wb_f2[hf * 16 + cb], 32, ("wb_f2", hf * 16 + cb))
                        pb = cb % 4
                        proj(w1, 32, lambda j: big[:, j, :], [("big", j) for j in range(32)], pb)
                        if hf == 0:
                            fw.op("act", lambda h, cb=cb, pb=pb: h.activation(out=acc[:, cb, :], in_=pmm[pb][:, :], func=AF.Copy),
                                  reads=[("pmm", pb)], writes=[("acc", cb)])
                        else:
                            fw.op("dve", lambda h, cb=cb, pb=pb: h.tensor_tensor(out=acc[:, cb, :], in0=pmm[pb][:, :], in1=acc[:, cb, :], op=ALU.add),
                                  reads=[("pmm", pb), ("acc", cb)], writes=[("acc", cb)])
                sumsq_rstd(lambda j: acc[:, j, :], 16, sq, pss, rstd, lambda j: ("acc", j))
                for j in range(16):
                    fw.op("pool", lambda h, j=j: h.tensor_tensor(out=tmp[j % 2][:], in0=acc[:, j, :], in1=rstd[:], op=ALU.mult),
                          reads=[("acc", j), "rstd"], writes=[("tmp", j % 2)])
                    fw.op("dve", lambda h, j=j: h.scalar_tensor_tensor(out=acc[:, j, :], in0=tmp[j % 2][:], scalar=G2[:, j, s:s + 1], in1=xT[:, j, :],
                                                                       op0=ALU.mult, op1=ALU.add),
                          reads=[("tmp", j % 2), "G2", ("xT", j)], writes=[("acc", j)])
                for q in range(4):
                    for jg in range(4):
                        pb = pmm[(q * 4 + jg) % 4]

                        def tr(h, q=q, jg=jg, pb=pb):
                            for jj in range(4):
                                j = jg * 4 + jj
                                ins = h.transpose(out=pb[:, jj * 128:(jj + 1) * 128], in_=acc[:, j, q * 128:(q + 1) * 128], identity=ident)
                            return ins
                        fw.op("pe", tr, reads=[("acc", jg * 4 + jj) for jj in range(4)] + ["cst"], writes=[("pmm", (q * 4 + jg) % 4)])
                        eng = "act" if jg % 2 == 0 else "dve"
                        if eng == "act":
                            fw.op("act", lambda h, q=q, jg=jg, pb=pb: h.activation(out=xin[q][:, jg * 512:(jg + 1) * 512], in_=pb[:, :], func=AF.Copy),
                                  reads=[("pmm", (q * 4 + jg) % 4)], writes=[("xin", q)])
                        else:
                            fw.op("dve", lambda h, q=q, jg=jg, pb=pb: h.tensor_copy(out=xin[q][:, jg * 512:(jg + 1) * 512], in_=pb[:, :]),
                                  reads=[("pmm", (q * 4 + jg) % 4)], writes=[("xin", q)])
                    fw.dma("sp", lambda h, q=q: h.dma_start(out=y[t0 + q * 128:t0 + (q + 1) * 128, :], in_=xin[q][:]),
                           reads=[("xin", q)], writes=["y"])
            fw.barrier()
        fw.emit_all()
    return nc


_NC_CACHE = {}

SAMPLE_MAP = {2: [0, 1, 2], 3: [3, 4, 5], 4: [6, 7, 8], 5: [9, 10, 11], 6: [12, 13], 7: [14, 15]}


def kernel(x_prompt, x_sample, c_prompt, c_sample, w_ada, b_ada, norm_pre_mix, norm_post_mix,
           norm_pre_ffn, norm_post_ffn, w_in, conv_w, A_log, dt_bias, dn_norm_w, w_dn_out,
           w_at_out, w_out, w_ff1, w_ff2):
    f = lambda a: np.ascontiguousarray(np.asarray(a, dtype=np.float32))
    x_prompt, x_sample, c_prompt, c_sample = f(x_prompt), f(x_sample), f(c_prompt), f(c_sample)
    if "nc" not in _NC_CACHE:
        _NC_CACHE["nc"] = build_program()
    nc = _NC_CACHE["nc"]
    shared = {
        "consts": make_consts(), "lvlmask": make_lvlmask(),
        "w_ada": f(w_ada)[0], "b_ada": f(b_ada)[0].reshape(96, 128),
        "norms": np.concatenate([f(norm_pre_mix)[0], f(norm_post_mix)[0], f(norm_pre_ffn)[0], f(norm_post_ffn)[0]]).reshape(64, 128),
        "w_in": f(w_in)[0], "conv_w": f(conv_w)[0].reshape(120, 128), "A_log": f(A_log)[0].reshape(1, 16),
        "dt_bias": f(dt_bias)[0].reshape(1, 16), "dn_norm_w": f(dn_norm_w)[0].reshape(1, 128),
        "w_dn_out": f(w_dn_out)[0], "w_at_out": f(w_at_out)[0], "w_out": f(w_out)[0],
        "w_ff1": f(w_ff1)[0], "w_ff2": f(w_ff2)[0],
    }
    in_maps = []
    for core in range(8):
        xs = np.zeros((T, D), np.float32)
        cs = np.zeros((NSEG, D), np.float32)
        sg = np.zeros((128, 16), np.float32)
        if core < 2:
            xs[:] = x_prompt[core]
            cs[:] = c_prompt[core][None, :]
            for s in range(NSEG):
                sg[:, s] = 1.0 if s > 0 else 0.0
                sg[:, 4 + s] = 1.0 if s < NSEG - 1 else 0.0
                sg[:, 8 + s] = s * SEG
        else:
            seqs = SAMPLE_MAP[core]
            for s in range(NSEG):
                b = seqs[s] if s < len(seqs) else seqs[0]
                xs[s * SEG:(s + 1) * SEG] = x_sample[b]
                cs[s] = c_sample[b]
        m = dict(shared)
        m["x"] = xs
        m["c"] = cs.reshape(NSEG * NJ, 128)
        m["segf"] = sg
        in_maps.append(m)
    if KSCOPES:
        _NC_CACHE["in_maps"] = in_maps
        res = run_bass_kernel_spmd(nc, in_maps, core_ids=list(range(8)), trace=True)
        _NC_CACHE["res"] = res
    else:
        res = run_bass_kernel_spmd(nc, in_maps, core_ids=list(range(8)))
    if KDEBUG:
        _NC_CACHE["res"] = res
    y_prompt = np.stack([np.asarray(res.results[c]["y"], dtype=np.float32) for c in range(2)])
    y_sample = np.zeros_like(x_sample)
    for core, seqs in SAMPLE_MAP.items():
        yc = np.asarray(res.results[core]["y"], dtype=np.float32)
        for s, b in enumerate(seqs):
            y_sample[b] = yc[s * SEG:(s + 1) * SEG]
    return (y_prompt, y_sample)
```

```python
from contextlib import ExitStack
import numpy as np
import concourse.bass as bass
import concourse.mybir as mybir
from concourse.bass_utils import run_bass_kernel_spmd

F32 = mybir.dt.float32
BF16 = mybir.dt.bfloat16
I32 = mybir.dt.int32
AF = mybir.ActivationFunctionType
ALU = mybir.AluOpType
AX = mybir.AxisListType

D = 2048
NJ = 16
T = 8192
NSEG = 4
SEG = 2048
DFF = 8192
EPS = 1e-6
IN_COLS = 12832
NEG = -1.0e30

import os
KDEBUG = int(os.environ.get("KDEBUG", "0"))
KTILES = int(os.environ.get("KTILES", "16"))
KDUMP = os.environ.get("KDUMP", "").split(",")
KSCOPES = int(os.environ.get("KSCOPES", "0"))
COMPUTE = ("pe", "act", "dve", "pool")
N_DMA_SEMS = 12


class Ticket:
    __slots__ = ("kind", "eng", "val", "sem")

    def __init__(self, kind, eng, val, sem=None):
        self.kind, self.eng, self.val, self.sem = kind, eng, val, sem


class _Rec:
    def __init__(self):
        self.calls = []

    def __getattr__(self, name):
        def f(*a, **k):
            self.calls.append((name, a, k))
            return self
        return f


def _replay(h, calls):
    ins = None
    for name, a, k in calls:
        ins = getattr(h, name)(*a, **k)
    return ins


class FW:
    def __init__(self, nc, es):
        self.nc = nc
        self.streams = {k: [] for k in ("pe", "act", "dve", "pool", "sp")}
        self.sem = {}
        self.count = {}
        for k in COMPUTE:
            self.sem[k] = es.enter_context(nc.semaphore("s_" + k))
            self.count[k] = 0
        self.dsem, self.dcount, self.dnext = {}, {}, {}
        for q in ("sp", "act", "pool"):
            self.dsem[q] = [es.enter_context(nc.semaphore(f"d_{q}{i}")) for i in range(N_DMA_SEMS)]
            self.dcount[q] = [0] * N_DMA_SEMS
            self.dnext[q] = 0
        self.known = {k: {} for k in self.streams}
        self.lastw = {}
        self.readers = {}
        self.n_ops = 0
        self.phase = "P0"

    def _need(self, stream, t, waits):
        if t is None:
            return
        if t.kind == "c":
            if t.eng == stream and stream == "pe":
                return
            key = ("c", t.eng)
            sem = self.sem[t.eng]
        else:
            key = ("d", id(t.sem))
            sem = t.sem
        if self.known[stream].get(key, 0) >= t.val:
            return
        cur = waits.get(key)
        if cur is None or cur[1] < t.val:
            waits[key] = (sem, t.val)

    def _deps(self, stream, reads, writes):
        waits = {}
        for r in reads:
            self._need(stream, self.lastw.get(r), waits)
        for w in writes:
            self._need(stream, self.lastw.get(w), waits)
            for t in self.readers.get(w, {}).values():
                self._need(stream, t, waits)
        out = []
        for key, (sem, val) in waits.items():
            self.known[stream][key] = val
            out.append((sem, val))
        return out

    def _commit(self, t, reads, writes):
        for w in writes:
            self.lastw[w] = t
            self.readers[w] = {}
        k = ("c", t.eng) if t.kind == "c" else ("d", id(t.sem))
        for r in reads:
            self.readers.setdefault(r, {})[k] = t

    def op(self, eng, fn, reads=(), writes=()):
        waits = self._deps(eng, reads, writes)
        self.count[eng] += 1
        val = self.count[eng]
        sem = self.sem[eng]

        rec = _Rec()
        fn(rec)
        calls = rec.calls

        def emit(h, waits=waits, calls=calls, sem=sem):
            for s, v in waits:
                h.wait_ge(s, v)
            _replay(h, calls).then_inc(sem, 1)

        emit.phase = self.phase
        self.streams[eng].append(emit)
        self._commit(Ticket("c", eng, val), reads, writes)
        self.n_ops += 1

    def dma(self, q, fn, reads=(), writes=()):
        i = self.dnext[q]
        self.dnext[q] = (i + 1) % N_DMA_SEMS
        sem = self.dsem[q][i]
        prev = self.dcount[q][i]
        waits = self._deps(q, reads, writes)
        key = ("d", id(sem))
        if prev > 0 and self.known[q].get(key, 0) < prev:
            waits.append((sem, prev))
            self.known[q][key] = prev
        val = prev + 16
        self.dcount[q][i] = val

        rec = _Rec()
        fn(rec)
        calls = rec.calls

        def emit(h, waits=waits, calls=calls, sem=sem):
            for s, v in waits:
                h.wait_ge(s, v)
            _replay(h, calls).then_inc(sem, 16)

        emit.phase = self.phase
        self.streams[q].append(emit)
        self._commit(Ticket("d", q, val, sem), reads, writes)
        self.n_ops += 1

    def barrier(self):
        targets = [(self.sem[k], self.count[k], ("c", k)) for k in COMPUTE if self.count[k] > 0]
        for q in self.dsem:
            for i, s in enumerate(self.dsem[q]):
                if self.dcount[q][i] > 0:
                    targets.append((s, self.dcount[q][i], ("d", id(s))))
        for stream in self.streams:
            ws = []
            for s, v, key in targets:
                if self.known[stream].get(key, 0) < v:
                    ws.append((s, v))
                    self.known[stream][key] = v

            def emit(h, ws=ws):
                for s, v in ws:
                    h.wait_ge(s, v)

            self.streams[stream].append(emit)
        self.lastw = {}
        self.readers = {}

    def emit_all(self):
        nc = self.nc

        def run(h, fs):
            if not KSCOPES:
                for f in fs:
                    f(h)
                return
            cur = None
            sid = None
            for f in fs:
                ph = getattr(f, "phase", cur)
                if ph != cur:
                    if cur is not None:
                        nc.leave_named_scope(cur, sid, False)
                    sid, _ = nc.enter_named_scope(ph, False)
                    cur = ph
                f(h)
            if cur is not None:
                nc.leave_named_scope(cur, sid, False)

        with nc.Block() as block:
            @block.tensor
            def _(h):
                run(h, self.streams["pe"])

            @block.scalar
            def _(h):
                run(h, self.streams["act"])

            @block.vector
            def _(h):
                run(h, self.streams["dve"])

            @block.gpsimd
            def _(h):
                run(h, self.streams["pool"])

            @block.sync
            def _(h):
                run(h, self.streams["sp"])


C_ID, C_MEAN, C_ONE, C_MLOW, C_MUP, C_GT, C_LT, C_LE, C_GE = 0, 128, 256, 384, 512, 640, 768, 896, 1024
C_RPERM, C_IOTA, C_IFREQ, C_BAND, C_NEGL, C_NEGR, C_TOTAL = 1152, 1280, 1792, 1793, 2049, 2305, 2561


def make_lvlmask():
    p = np.arange(128)[:, None]
    f = np.arange(128)[None, :]
    m = np.zeros((128, 14 * 128), np.float32)
    for k in range(7):
        same_hi = (p >> (k + 1)) == (f >> (k + 1))
        diff_lo = (p >> k) != (f >> k)
        m[:, (2 * k) * 128:(2 * k + 1) * 128] = same_hi & diff_lo & (p > f)
        m[:, (2 * k + 1) * 128:(2 * k + 2) * 128] = same_hi & diff_lo & (p < f)
    return m


def make_consts():
    c = np.zeros((128, C_TOTAL), np.float32)
    p = np.arange(128)[:, None]
    f = np.arange(128)[None, :]
    c[:, C_ID:C_ID + 128] = (p == f)
    c[:, C_MEAN:C_MEAN + 128] = 1.0 / D
    c[:, C_ONE:C_ONE + 128] = 1.0
    c[:, C_MLOW:C_MLOW + 128] = (p <= f)
    c[:, C_MUP:C_MUP + 128] = (p >= f)
    c[:, C_GT:C_GT + 128] = (p > f)
    c[:, C_LT:C_LT + 128] = (p < f)
    c[:, C_LE:C_LE + 128] = (p <= f)
    c[:, C_GE:C_GE + 128] = (p >= f)
    r = np.zeros((128, 128), np.float32)
    for m in range(64):
        r[m + 64, m] = -1.0
        r[m, m + 64] = 1.0
    c[:, C_RPERM:C_RPERM + 128] = r
    c[:, C_IOTA:C_IOTA + 512] = np.arange(512)[None, :]
    c[:, C_IFREQ] = (10000.0 ** (-(np.arange(128) % 64) / 64.0)) / (2 * np.pi)
    a = np.arange(128)[:, None]
    b = np.arange(256)[None, :]
    c[:, C_BAND:C_BAND + 256] = np.where((b - a >= 0) & (b - a <= 128), 0.0, NEG)
    c[:, C_NEGL:C_NEGL + 256] = np.where(b < 64, NEG, 0.0) * np.ones((128, 1))
    c[:, C_NEGR:C_NEGR + 256] = np.where(b >= 192, NEG, 0.0) * np.ones((128, 1))
    return c


def build_program():
    nc = bass.Bass("TRN2", target_bir_lowering=False)

    def EI(name, shape):
        return nc.dram_tensor(name, list(shape), F32, kind="ExternalInput").ap()

    x = EI("x", [T, D])
    cvec = EI("c", [NSEG * NJ, 128])
    segf = EI("segf", [128, 16])
    consts = EI("consts", [128, C_TOTAL])
    lvlmask = EI("lvlmask", [128, 14 * 128])
    w_ada = EI("w_ada", [D, 6 * D])
    b_ada = EI("b_ada", [96, 128])
    norms = EI("norms", [64, 128])
    w_in = EI("w_in", [D, IN_COLS])
    conv_w = EI("conv_w", [120, 128])
    alog = EI("A_log", [1, 16])
    dtb = EI("dt_bias", [1, 16])
    dnw = EI("dn_norm_w", [1, 128])
    w_dn_out = EI("w_dn_out", [1024, D])
    w_at_out = EI("w_at_out", [512, D])
    w_out = EI("w_out", [D, D])
    w_ff1 = EI("w_ff1", [D, DFF])
    w_ff2 = EI("w_ff2", [DFF, D])
    y = nc.dram_tensor("y", [T, D], F32, kind="ExternalOutput").ap()
    dbg_names = []

    def dump(fw, name, ap, keys, dt=F32):
        if not KDEBUG:
            return
        dtn = nc.dram_tensor("dbg_" + name, list(ap.shape), dt, kind="ExternalOutput").ap()
        dbg_names.append("dbg_" + name)
        fw.dma("sp", lambda h: h.dma_start(out=dtn, in_=ap), reads=list(keys), writes=["dbg_" + name])

    def DR(name, shape, dt=F32):
        if KDEBUG and name in KDUMP:
            dbg_names.append(name)
            return nc.dram_tensor(name, list(shape), dt, kind="ExternalOutput").ap()
        return nc.dram_tensor(name, list(shape), dt, kind="Internal").ap()

    wb_mg = DR("wb_mg", [32, 128, 16, 128], BF16)
    wb_dn = DR("wb_dn", [16, 128, 8, 128], BF16)
    wb_at = DR("wb_at", [16, 128, 4, 128], BF16)
    wb_out = DR("wb_out", [16, 128, 16, 128], BF16)
    wb_f1 = DR("wb_f1", [64, 128, 16, 128], BF16)
    wb_f2 = DR("wb_f2", [32, 128, 32, 128], BF16)
    ydnT = DR("ydnT", [1024, T], BF16)
    yatT = DR("yatT", [512, T], BF16)
    MERGE0 = 3072 + 1024 + 32 + 4608

    es = ExitStack()
    with es:
        fw = FW(nc, es)

        uid = [0]

        def SB(st, name, shape, dt=F32):
            uid[0] += 1
            return st.enter_context(nc.sbuf_tensor(f"{name}_{uid[0]}", list(shape), dt))

        def PSM(st, name, shape, dt=F32):
            uid[0] += 1
            return st.enter_context(nc.psum_tensor(f"{name}_{uid[0]}", list(shape), dt))

        cst = SB(es, "cst", [128, C_TOTAL])
        sgf = SB(es, "sgf", [128, 16])
        nrm = SB(es, "nrm", [128, 64])
        cT = SB(es, "cT", [128, 64])
        badaT = SB(es, "badaT", [128, 96])
        modT = SB(es, "modT", [128, 96, 4])
        A1 = SB(es, "A1", [128, 16, 4]); G1 = SB(es, "G1", [128, 16, 4])
        A2 = SB(es, "A2", [128, 16, 4]); G2 = SB(es, "G2", [128, 16, 4])
        ident = cst[:, C_ID:C_ID + 128]
        meanm = cst[:, C_MEAN:C_MEAN + 128]
        fw.dma("sp", lambda h: h.dma_start(out=cst[:], in_=consts[:, :]), writes=["cst"])
        fw.dma("sp", lambda h: h.dma_start(out=sgf[:], in_=segf[:, :]), writes=["sgf"])

        def cast(dst, src, key):
            fw.dma("pool", lambda h: h.dma_start(out=dst, in_=src), writes=[key])

        DNQ0, Z0, AB0, ATQ0, ATK0, ATV0 = 0, 3072, 4096, 4128, 4128 + 1536, 4128 + 3072
        wb_fm = DR("wb_fm", [56, 128, 16, 128], BF16)
        wb_v = DR("wb_v", [3, 128, 16, 512], BF16)
        wb_ab = DR("wb_ab", [128, 16, 32], BF16)
        for b in range(56):
            c0 = (DNQ0 + 128 * b) if b < 24 else (Z0 + 128 * (b - 24)) if b < 32 else (ATQ0 + 128 * (b - 32)) if b < 44 else (ATK0 + 128 * (b - 44))
            cast(wb_fm[b], w_in[:, c0:c0 + 128].rearrange("(j p) c -> p j c", p=128), ("wb_fm", b))
        for g in range(3):
            cast(wb_v[g], w_in[:, ATV0 + 512 * g:ATV0 + 512 * (g + 1)].rearrange("(j p) c -> p j c", p=128), ("wb_v", g))
        cast(wb_ab[:, :, :], w_in[:, AB0:AB0 + 32].rearrange("(j p) c -> p j c", p=128), "wb_ab")

        fw.phase = "P0mod"
        with ExitStack() as st:
            stg = SB(st, "stg0", [128, 128])
            stg2 = SB(st, "stg1", [128, 128])
            pt = PSM(st, "p0t", [128, 512])
            pm = [PSM(st, f"p0m{i}", [128, 4]) for i in range(2)]
            fw.dma("sp", lambda h: h.dma_start(out=stg[0:64, :], in_=norms[:, :]), writes=["stg0"])
            fw.dma("sp", lambda h: h.dma_start(out=stg[64:128, :], in_=cvec[:, :]), writes=["stg0"])
            fw.op("pe", lambda h: h.transpose(out=pt[:, 0:128], in_=stg[:], identity=ident), reads=["stg0", "cst"], writes=["p0t"])
            fw.op("dve", lambda h: h.tensor_copy(out=nrm[:], in_=pt[:, 0:64]), reads=["p0t"], writes=["nrm"])
            fw.op("act", lambda h: h.activation(out=cT[:], in_=pt[:, 64:128], func=AF.Silu), reads=["p0t"], writes=["cT"])
            fw.dma("sp", lambda h: h.dma_start(out=stg2[0:96, :], in_=b_ada[:, :]), writes=["stg1"])
            fw.op("pe", lambda h: h.transpose(out=pt[:, 128:224], in_=stg2[0:96, :], identity=ident[0:96, 0:96]), reads=["stg1", "cst"], writes=["p0t"])
            fw.op("dve", lambda h: h.tensor_copy(out=badaT[:], in_=pt[:, 128:224]), reads=["p0t"], writes=["badaT"])
            cTv = cT[:].rearrange("p (s j) -> p j s", j=16)
            wbig = [SB(st, f"wbig{i}", [128, 16, 1024]) for i in range(2)]
            for ch in range(12):
                wt = wbig[ch % 2]
                for j in range(16):
                    fw.dma("sp", lambda h, wt=wt, ch=ch, j=j: h.dma_start(out=wt[:, j, :], in_=w_ada[j * 128:(j + 1) * 128, ch * 1024:(ch + 1) * 1024]),
                           writes=[("wbig", ch % 2, j)])
                for cbl in range(8):
                    cb = ch * 8 + cbl

                    def mm(h, wt=wt, cb=cb, cbl=cbl):
                        for j in range(16):
                            ins = h.matmul(pm[cb % 2][:, :], lhsT=wt[:, j, cbl * 128:(cbl + 1) * 128], rhs=cTv[:, j, :], start=(j == 0), stop=(j == 15))
                        return ins
                    fw.op("pe", mm, reads=[("wbig", ch % 2, j) for j in range(16)] + ["cT"], writes=[("p0m", cb % 2)])
                    fw.op("dve", lambda h, cb=cb: h.tensor_scalar(out=modT[:, cb, :], in0=pm[cb % 2][:, :], scalar1=badaT[:, cb:cb + 1],
                                                                  scalar2=None, op0=ALU.add),
                          reads=[("p0m", cb % 2), "badaT"], writes=["modT"])

            def nv(v):
                return nrm[:, v * 16:(v + 1) * 16].unsqueeze(2).to_broadcast([128, 16, 4])
            fw.op("dve", lambda h: h.scalar_tensor_tensor(out=A1[:], in0=modT[:, 16:32, :], scalar=1.0, in1=nv(0), op0=ALU.add, op1=ALU.mult),
                  reads=["modT", "nrm"], writes=["A1"])
            fw.op("dve", lambda h: h.tensor_tensor(out=G1[:], in0=modT[:, 32:48, :], in1=nv(1), op=ALU.mult), reads=["modT", "nrm"], writes=["G1"])
            fw.op("dve", lambda h: h.scalar_tensor_tensor(out=A2[:], in0=modT[:, 64:80, :], scalar=1.0, in1=nv(2), op0=ALU.add, op1=ALU.mult),
                  reads=["modT", "nrm"], writes=["A2"])
            fw.op("dve", lambda h: h.tensor_tensor(out=G2[:], in0=modT[:, 80:96, :], in1=nv(3), op=ALU.mult), reads=["modT", "nrm"], writes=["G2"])
            dump(fw, "modT", modT[:], ["modT"])
            dump(fw, "A1", A1[:], ["A1"])
            dump(fw, "nrm", nrm[:], ["nrm"])
            fw.barrier()

        def front_end(st_bufs, t0, seg):
            xin, xT, sq, rstd, pxt, pss = st_bufs
            for q in range(4):
                fw.dma("sp", lambda h, q=q: h.dma_start(out=xin[q][:], in_=x[t0 + q * 128:t0 + (q + 1) * 128, :]), writes=[("xin", q)])
            for j in range(16):
                pb = pxt[j % 2]

                def tr(h, j=j, pb=pb):
                    for q in range(4):
                        ins = h.transpose(out=pb[:, q * 128:(q + 1) * 128], in_=xin[q][:, j * 128:(j + 1) * 128], identity=ident)
                    return ins
                fw.op("pe", tr, reads=[("xin", q) for q in range(4)] + ["cst"], writes=[("pxt", j % 2)])
                fw.op("act", lambda h, j=j, pb=pb: h.activation(out=xT[:, j, :], in_=pb[:, :], func=AF.Copy), reads=[("pxt", j % 2)], writes=[("xT", j)])
                fw.op("dve", lambda h, j=j, pb=pb: h.tensor_tensor(out=sq[j % 2][:], in0=pb[:, :], in1=xT[:, j, :], op=ALU.mult),
                      reads=[("pxt", j % 2), ("xT", j)], writes=[("sq", j % 2)])
                fw.op("pe", lambda h, j=j: h.matmul(pss[:, :], lhsT=meanm, rhs=sq[j % 2][:], start=(j == 0), stop=(j == 15)),
                      reads=[("sq", j % 2), "cst"], writes=["pss"])
            fw.op("act", lambda h: h.activation(out=rstd[:], in_=pss[:, :], func=AF.Sqrt, bias=EPS, scale=1.0), reads=["pss"], writes=["rstd"])
            fw.op("dve", lambda h: h.reciprocal(out=rstd[:], in_=rstd[:]), reads=["rstd"], writes=["rstd"])

        def sumsq_rstd(src_fn, nblk, sq, pss, rstd, src_keys, scale_mean=True):
            for j in range(nblk):
                fw.op("pool", lambda h, j=j: h.tensor_tensor(out=sq[j % 2][:], in0=src_fn(j), in1=src_fn(j), op=ALU.mult),
                      reads=[src_keys(j)], writes=[("sq", j % 2)])
                fw.op("pe", lambda h, j=j: h.matmul(pss[:, :], lhsT=meanm, rhs=sq[j % 2][:], start=(j == 0), stop=(j == nblk - 1)),
                      reads=[("sq", j % 2), "cst"], writes=["pss"])
            fw.op("act", lambda h: h.activation(out=rstd[:], in_=pss[:, :], func=AF.Sqrt, bias=EPS, scale=1.0), reads=["pss"], writes=["rstd"])
            fw.op("dve", lambda h: h.reciprocal(out=rstd[:], in_=rstd[:]), reads=["rstd"], writes=["rstd"])

        DNQ0, Z0, AB0, ATQ0, ATK0, ATV0 = 0, 3072, 4096, 4128, 4128 + 1536, 4128 + 3072
        DIL = (1, 4, 16)
        MC = [T // d_ for d_ in DIL]
        MS = [SEG // d_ for d_ in DIL]
        CL = [m_ + 128 for m_ in MC]
        qkvpre = DR("qkvpre", [3072, T + 4])
        szT = DR("szT", [1024, T])
        ab_tm = DR("ab_tm", [T, 32])
        aqT = [DR(f"aqT{g}", [512, DIL[g] * CL[g]], BF16) for g in range(3)]
        akT = [DR(f"akT{g}", [512, DIL[g] * CL[g]], BF16) for g in range(3)]
        avs = [DR(f"av{g}", [DIL[g] * CL[g], 512], BF16) for g in range(3)]
        qn = DR("qn", [1024, T]); kn = DR("kn", [1024, T])
        k_tm = DR("k_tm", [T, 1024]); v_tm = DR("v_tm", [T, 1024])
        GB = DR("GB", [2, 2, 64, 8, 128])
        o_dir = DR("o_dir", [2, T, 1024])
        Oat = DR("Oat", [3, T, 4, 130])
        convT = SB(es, "convT", [128, 120])
        dnwT = SB(es, "dnwT", [128, 1])
        dtb16 = SB(es, "dtb16", [128, 16])
        nA16 = SB(es, "nA16", [128, 16])
        Gtm = SB(es, "Gtm", [128, 64, 16])
        Btm = SB(es, "Btm", [128, 64, 16])
        ones = cst[:, C_ONE:C_ONE + 128]
        with ExitStack() as st:
            stg = SB(st, "stgc", [128, 128])
            pt = PSM(st, "p1t", [128, 512])
            fw.dma("sp", lambda h: h.dma_start(out=stg[0:120, :], in_=conv_w[:, :]), writes=["stgc"])
            fw.op("pe", lambda h: h.transpose(out=pt[:, 0:120], in_=stg[0:120, :], identity=ident[0:120, 0:120]), reads=["stgc", "cst"], writes=["p1t"])
            fw.op("dve", lambda h: h.tensor_copy(out=convT[:], in_=pt[:, 0:120]), reads=["p1t"], writes=["convT"])
            fw.dma("sp", lambda h: h.dma_start(out=stg[0:1, :], in_=dnw[:, :]), writes=["stgc"])
            fw.op("pe", lambda h: h.transpose(out=pt[:, 128:129], in_=stg[0:1, :], identity=ident[0:1, 0:1]), reads=["stgc", "cst"], writes=["p1t"])
            fw.op("dve", lambda h: h.tensor_copy(out=dnwT[:], in_=pt[:, 128:129]), reads=["p1t"], writes=["dnwT"])
            fw.dma("sp", lambda h: h.dma_start(out=dtb16[:], in_=dtb.partition_broadcast(128)), writes=["dtb16"])
            fw.dma("sp", lambda h: h.dma_start(out=nA16[:], in_=alog.partition_broadcast(128)), writes=["nA16"])
            fw.op("act", lambda h: h.activation(out=nA16[:], in_=nA16[:], func=AF.Exp), reads=["nA16"], writes=["nA16"])
            fw.op("dve", lambda h: h.tensor_scalar(out=nA16[:], in0=nA16[:], scalar1=-1.0, scalar2=None, op0=ALU.mult), reads=["nA16"], writes=["nA16"])
            zb = SB(st, "zb", [128, 16, 512], BF16)
            fw.op("pool", lambda h: h.memset(zb[:], 0.0), writes=["zb"])
            for g in range(3):
                dil = DIL[g]
                for hh in range(4):
                    kv = akT[g][hh * 128:(hh + 1) * 128, :].rearrange("p (r c) -> p r c", c=CL[g])
                    fw.dma("sp", lambda h, kv=kv, dil=dil: h.dma_start(out=kv[:, :, 0:64], in_=zb[:, 0:dil, 0:64]), reads=["zb"], writes=[("akT", g)])
                    fw.dma("sp", lambda h, kv=kv, dil=dil, g=g: h.dma_start(out=kv[:, :, 64 + MC[g]:128 + MC[g]], in_=zb[:, 0:dil, 0:64]), reads=["zb"], writes=[("akT", g)])
                vv = avs[g].rearrange("(r c) d -> c r d", c=CL[g])
                fw.dma("sp", lambda h, vv=vv, dil=dil: h.dma_start(out=vv[0:64, :, :], in_=zb[0:64, 0:dil, :]), reads=["zb"], writes=[("av", g)])
                fw.dma("sp", lambda h, vv=vv, dil=dil, g=g: h.dma_start(out=vv[64 + MC[g]:128 + MC[g], :, :], in_=zb[0:64, 0:dil, :]), reads=["zb"], writes=[("av", g)])
            fw.barrier()

        fw.phase = "P1"
        for b in range(32):
            cast(wb_mg[b], w_in[:, MERGE0 + b * 128:MERGE0 + (b + 1) * 128].rearrange("(j p) c -> p j c", p=128), ("wb_mg", b))
        for b in range(16):
            cast(wb_dn[b], w_dn_out[:, b * 128:(b + 1) * 128].rearrange("(j p) c -> p j c", p=128), ("wb_dn", b))
            cast(wb_at[b], w_at_out[:, b * 128:(b + 1) * 128].rearrange("(j p) c -> p j c", p=128), ("wb_at", b))
            cast(wb_out[b], w_out[:, b * 128:(b + 1) * 128].rearrange("(j p) c -> p j c", p=128), ("wb_out", b))
        for b in range(64):
            cast(wb_f1[b], w_ff1[:, b * 128:(b + 1) * 128].rearrange("(j p) c -> p j c", p=128), ("wb_f1", b))
        for hf in range(2):
            for b in range(16):
                cast(wb_f2[hf * 16 + b],
                     w_ff2[hf * 4096:(hf + 1) * 4096, b * 128:(b + 1) * 128].rearrange("(j p) c -> p j c", p=128),
                     ("wb_f2", hf * 16 + b))

        with ExitStack() as st:
            xin = [SB(st, f"xin{q}", [128, D]) for q in range(4)]
            hTs = SB(st, "hTs", [128, 16, SEG], BF16)
            ssq = SB(st, "ssq", [128, 4])
            stgf = [SB(st, f"stgf{i}", [128, SEG]) for i in range(2)]
            stgb = [SB(st, f"stgb{i}", [128, SEG], BF16) for i in range(2)]
            wk = [SB(st, f"wk{i}", [128, 16, 128], BF16) for i in range(3)]
            wv = SB(st, "wv", [128, 16, 512], BF16)
            wab = SB(st, "wab", [128, 16, 32], BF16)
            cosT = SB(st, "cosT", [128, 4, 512]); sinT = SB(st, "sinT", [128, 4, 512])
            tA = SB(st, "tA", [128, 512]); tB = SB(st, "tB", [128, 512]); tB2 = SB(st, "tB2", [128, 512]); tC = SB(st, "tC", [128, 512]); tD = SB(st, "tD", [128, 512])
            tI = SB(st, "tI", [128, 512], I32)
            hpi = SB(st, "hpi", [128, 1])
            vst = [SB(st, f"vst{i}", [128, 512], BF16) for i in range(2)]
            abst = SB(st, "abst", [128, 16, 32])
            pxt = [PSM(st, f"pxt{i}", [128, 512]) for i in range(2)]
            pmm = [PSM(st, f"pmm{i}", [128, 512]) for i in range(4)]
            prr = PSM(st, "prr", [128, 512])
            print("P1 sbuf remaining", nc.sbuf_bytes_remaining)
            fw.op("pool", lambda h: h.memset(hpi[:], float(np.pi / 2)), writes=["hpi"])
            fw.dma("sp", lambda h: h.dma_start(out=wab[:], in_=wb_ab[:, :, :]), reads=["wb_ab"], writes=["wab"])
            rperm = cst[:, C_RPERM:C_RPERM + 128]
            ifr = cst[:, C_IFREQ:C_IFREQ + 1]
            iota = cst[:, C_IOTA:C_IOTA + 512]
            wsl = [0]

            def trig(dst, yap):
                fw.op("dve", lambda h: h.tensor_copy(out=tI[:], in_=yap), reads=["tA"], writes=["tI"])
                fw.op("dve", lambda h: h.tensor_copy(out=tB[:], in_=tI[:]), reads=["tI"], writes=["tB"])
                fw.op("dve", lambda h: h.tensor_tensor(out=tB[:], in0=yap, in1=tB[:], op=ALU.subtract), reads=["tA", "tB"], writes=["tB"])
                fw.op("act", lambda h: h.activation(out=tC[:], in_=tB[:], func=AF.Abs), reads=["tB"], writes=["tC"])
                fw.op("act", lambda h: h.activation(out=tD[:], in_=tB[:], func=AF.Sin, scale=float(np.pi)), reads=["tB"], writes=["tD"])
                fw.op("act", lambda h: h.activation(out=tC[:], in_=tC[:], func=AF.Sin, bias=hpi[:], scale=-float(np.pi)), reads=["tC", "hpi"], writes=["tC"])
                fw.op("dve", lambda h: h.scalar_tensor_tensor(out=dst, in0=tD[:], scalar=2.0, in1=tC[:], op0=ALU.mult, op1=ALU.mult),
                      reads=["tC", "tD"], writes=["trig"])

            for s in range(NSEG):
                for tt in range(4):
                    t0 = s * SEG + tt * 512
                    for q in range(4):
                        fw.dma("sp", lambda h, q=q, t0=t0: h.dma_start(out=xin[q][:], in_=x[t0 + q * 128:t0 + (q + 1) * 128, :]), writes=[("xin", q)])
                        fw.op("act", lambda h, q=q: h.activation(out=stgb[0][:], in_=xin[q][:], func=AF.Square, accum_out=ssq[:, q:q + 1]),
                              reads=[("xin", q)], writes=[("stgb", 0), ("ssq", q)])
                        fw.op("act", lambda h, q=q: h.activation(out=ssq[:, q:q + 1], in_=ssq[:, q:q + 1], func=AF.Sqrt, bias=EPS, scale=1.0 / D),
                              reads=[("ssq", q)], writes=[("ssq", q)])
                        fw.op("dve", lambda h, q=q: h.reciprocal(out=ssq[:, q:q + 1], in_=ssq[:, q:q + 1]), reads=[("ssq", q)], writes=[("ssq", q)])
                        fw.op("dve", lambda h, q=q: h.tensor_scalar(out=xin[q][:], in0=xin[q][:], scalar1=ssq[:, q:q + 1], scalar2=None, op0=ALU.mult),
                              reads=[("xin", q), ("ssq", q)], writes=[("xin", q)])
                    for j in range(16):
                        pb = pxt[j % 2]

                        def tr(h, j=j, pb=pb):
                            for q in range(4):
                                ins = h.transpose(out=pb[:, q * 128:(q + 1) * 128], in_=xin[q][:, j * 128:(j + 1) * 128], identity=ident)
                            return ins
                        fw.op("pe", tr, reads=[("xin", q) for q in range(4)] + ["cst"], writes=[("pxt", j % 2)])
                        fw.op("dve" if j % 2 else "act",
                              (lambda h, j=j, pb=pb, tt=tt, s=s: h.tensor_scalar(out=hTs[:, j, tt * 512:(tt + 1) * 512], in0=pb[:, :], scalar1=A1[:, j, s:s + 1],
                                                                              scalar2=modT[:, j, s:s + 1], op0=ALU.mult, op1=ALU.add)) if j % 2 else
                              (lambda h, j=j, pb=pb, tt=tt, s=s: h.activation(out=hTs[:, j, tt * 512:(tt + 1) * 512], in_=pb[:, :], func=AF.Identity,
                                                                           bias=modT[:, j, s:s + 1], scale=A1[:, j, s:s + 1])),
                              reads=[("pxt", j % 2), "A1", "modT"], writes=[("hTs", j, tt)])
                hkeys = [("hTs", j, tt) for j in range(16) for tt in range(4)]

                def wload_(b):
                    i = wsl[0] % 3
                    wsl[0] += 1
                    fw.dma("sp", lambda h, i=i, b=b: h.dma_start(out=wk[i][:], in_=wb_fm[b]), reads=[("wb_fm", b)], writes=[("wk", i)])
                    return i
                plan = list(range(32)) + [32 + qk_ * 12 + g_ * 4 + hh_ for g_ in range(3) for qk_ in range(2) for hh_ in range(4)]
                wst = {"issued": 0, "used": 0, "slots": {}}

                def wload(b):
                    while wst["issued"] < len(plan) and wst["issued"] <= wst["used"] + 2:
                        k_ = wst["issued"]
                        wst["slots"][k_] = wload_(plan[k_])
                        wst["issued"] += 1
                    k_ = wst["used"]
                    assert plan[k_] == b, (plan[k_], b)
                    wst["used"] += 1
                    return wst["slots"][k_]

                def fm_mm(wi, rhs_fn, pb, out_ap=None):
                    def mm(h):
                        for j in range(16):
                            ins = h.matmul(out_ap if out_ap is not None else pmm[pb][:, :], lhsT=wk[wi][:, j, :], rhs=rhs_fn(j), start=(j == 0), stop=(j == 15))
                        return ins
                    fw.op("pe", mm, reads=[("wk", wi)] + hkeys, writes=[("pmm", pb)])

                for b in range(32):
                    wi = wload(b)
                    sf = stgf[b % 2]
                    for tt in range(4):
                        pb = (b * 4 + tt) % 4
                        fm_mm(wi, lambda j, tt=tt: hTs[:, j, tt * 512:(tt + 1) * 512], pb)
                        fw.op("act", lambda h, sf=sf, tt=tt, pb=pb, b=b: h.activation(out=sf[:, tt * 512:(tt + 1) * 512], in_=pmm[pb][:, :],
                                                                                     func=(AF.Copy if b < 24 else AF.Silu)),
                              reads=[("pmm", pb)], writes=[("stgf", b % 2)])
                    if b < 24:
                        fw.dma("sp", lambda h, sf=sf, b=b, s=s: h.dma_start(out=qkvpre[b * 128:(b + 1) * 128, 2 + s * SEG:2 + (s + 1) * SEG], in_=sf[:]),
                               reads=[("stgf", b % 2)], writes=["qkvpre"])
                    else:
                        fw.dma("sp", lambda h, sf=sf, b=b, s=s: h.dma_start(out=szT[(b - 24) * 128:(b - 23) * 128, s * SEG:(s + 1) * SEG], in_=sf[:]),
                               reads=[("stgf", b % 2)], writes=["szT"])
                def abmm(h):
                    for n in range(16):
                        for j in range(16):
                            ins = h.matmul(pmm[0][:, n * 32:(n + 1) * 32], lhsT=hTs[:, j, n * 128:(n + 1) * 128], rhs=wab[:, j, :], start=(j == 0), stop=(j == 15))
                    return ins
                fw.op("pe", abmm, reads=["wab"] + hkeys, writes=[("pmm", 0)])
                fw.op("dve", lambda h: h.tensor_copy(out=abst[:].rearrange("p n c -> p (n c)"), in_=pmm[0][:, :]), reads=[("pmm", 0)], writes=["abst"])
                fw.dma("sp", lambda h, s=s: h.dma_start(out=ab_tm[s * SEG:(s + 1) * SEG, :].rearrange("(n p) c -> p n c", p=128), in_=abst[:]),
                       reads=["abst"], writes=["ab_tm"])
                for g in range(3):
                    dil = DIL[g]
                    for tt in range(4):
                        if dil == 16:
                            for rl in range(4):
                                fw.op("dve", lambda h, rl=rl, tt=tt, s=s: h.tensor_scalar(out=tA[:, rl * 128:(rl + 1) * 128], in0=iota[:, 0:128], scalar1=16.0,
                                                                                         scalar2=sgf[:, 8 + s:9 + s], op0=ALU.mult, op1=ALU.add),
                                      reads=["cst", "sgf"], writes=["tA"])
                                fw.op("dve", lambda h, rl=rl, tt=tt: h.tensor_scalar(out=tA[:, rl * 128:(rl + 1) * 128], in0=tA[:, rl * 128:(rl + 1) * 128],
                                                                                    scalar1=float(4 * tt + rl), scalar2=ifr, op0=ALU.add, op1=ALU.mult),
                                      reads=["tA", "cst"], writes=["tA"])
                        else:
                            cadd = float(512 * tt) if dil == 1 else float(tt)
                            fw.op("dve", lambda h, s=s, dil=dil: h.tensor_scalar(out=tA[:], in0=iota, scalar1=float(dil), scalar2=sgf[:, 8 + s:9 + s],
                                                                                op0=ALU.mult, op1=ALU.add), reads=["cst", "sgf"], writes=["tA"])
                            fw.op("dve", lambda h, cadd=cadd: h.tensor_scalar(out=tA[:], in0=tA[:], scalar1=cadd, scalar2=ifr, op0=ALU.add, op1=ALU.mult),
                                  reads=["tA", "cst"], writes=["tA"])
                        trig(sinT[:, tt, :], tA[:])
                        fw.op("dve", lambda h: h.tensor_scalar(out=tA[:], in0=tA[:], scalar1=0.25, scalar2=None, op0=ALU.add), reads=["tA", "trig"], writes=["tA"])
                        trig(cosT[:, tt, :], tA[:])
                    for qk in range(2):
                        for hh in range(4):
                            b = 32 + qk * 12 + g * 4 + hh
                            wi = wload(b)
                            sb_ = stgb[(qk * 4 + hh) % 2]
                            hview = None
                            def rot_stage(tt, sb_=None):
                                tBx = tB if tt % 2 == 0 else tB2
                                kB = "tB" if tt % 2 == 0 else "tB2"
                                fw.op("pe", lambda h: h.matmul(prr[:, :], lhsT=rperm, rhs=tBx[:], start=True, stop=True), reads=[kB, "cst"], writes=["prr"])
                                fw.op("pool", lambda h: h.tensor_tensor(out=tC[:], in0=tBx[:], in1=cosT[:, tt, :], op=ALU.mult), reads=[kB, "trig"], writes=["tC"])
                                fw.op("dve", lambda h: h.tensor_tensor(out=tD[:], in0=prr[:, :], in1=sinT[:, tt, :], op=ALU.mult), reads=["prr", "trig"], writes=["tD"])
                                fw.op("pool", lambda h: h.tensor_tensor(out=sb_[:, tt * 512:(tt + 1) * 512], in0=tC[:], in1=tD[:], op=ALU.add),
                                      reads=["tC", "tD"], writes=[("stgb", (qk * 4 + hh) % 2)])

                            for tt in range(4):
                                pb = tt % 4
                                if dil == 1:
                                    rf = lambda j, tt=tt: hTs[:, j, tt * 512:(tt + 1) * 512]
                                    oap = None
                                elif dil == 4:
                                    rf = lambda j, tt=tt: hTs[:, j, :].rearrange("p (m r) -> p r m", r=4)[:, tt, :]
                                    oap = None
                                else:
                                    rf = lambda j, tt=tt: hTs[:, j, :].rearrange("p (m r) -> p r m", r=16)[:, 4 * tt:4 * tt + 4, :]
                                    oap = pmm[pb][:, :].rearrange("p (a b) -> p a b", a=4)
                                fm_mm(wi, rf, pb, oap)
                                tBx = tB if tt % 2 == 0 else tB2
                                kB = "tB" if tt % 2 == 0 else "tB2"
                                fw.op("act", lambda h: h.activation(out=tBx[:], in_=pmm[pb][:, :], func=AF.Copy, scale=(128.0 ** -0.5 if qk == 0 else 1.0)),
                                      reads=[("pmm", pb)], writes=[kB])
                                if tt >= 1:
                                    rot_stage(tt - 1, sb_)
                            rot_stage(3, sb_)
                            dst = (aqT if qk == 0 else akT)[g][hh * 128:(hh + 1) * 128, :].rearrange("p (r c) -> p r c", c=CL[g])
                            fw.dma("sp", lambda h, dst=dst, sb_=sb_, g=g, s=s, dil=dil: h.dma_start(
                                out=dst[:, :, 64 + s * MS[g]:64 + (s + 1) * MS[g]], in_=sb_[:].rearrange("p (r m) -> p r m", r=dil)),
                                reads=[("stgb", (qk * 4 + hh) % 2)], writes=[("aqk", g)])
                    fw.dma("sp", lambda h, g=g: h.dma_start(out=wv[:], in_=wb_v[g]), reads=[("wb_v", g)], writes=["wv"])
                    for ct in range(16):
                        if dil == 1:
                            lf = lambda j, ct=ct: hTs[:, j, ct * 128:(ct + 1) * 128]
                            r_, m0 = 0, ct * 128
                        elif dil == 4:
                            lf = lambda j, ct=ct: hTs[:, j, :].rearrange("p (m r) -> p r m", r=4)[:, ct // 4, (ct % 4) * 128:(ct % 4 + 1) * 128]
                            r_, m0 = ct // 4, (ct % 4) * 128
                        else:
                            lf = lambda j, ct=ct: hTs[:, j, :].rearrange("p (m r) -> p r m", r=16)[:, ct, :]
                            r_, m0 = ct, 0
                        pb = ct % 4

                        def vmm(h, lf=lf, pb=pb):
                            for j in range(16):
                                ins = h.matmul(pmm[pb][:, :], lhsT=lf(j), rhs=wv[:, j, :], start=(j == 0), stop=(j == 15))
                            return ins
                        fw.op("pe", vmm, reads=["wv"] + hkeys, writes=[("pmm", pb)])
                        vs = vst[ct % 2]
                        if ct % 2:
                            fw.op("dve", lambda h, vs=vs, pb=pb: h.tensor_copy(out=vs[:], in_=pmm[pb][:, :]), reads=[("pmm", pb)], writes=[("vst", ct % 2)])
                        else:
                            fw.op("act", lambda h, vs=vs, pb=pb: h.activation(out=vs[:], in_=pmm[pb][:, :], func=AF.Copy), reads=[("pmm", pb)], writes=[("vst", ct % 2)])
                        row0 = r_ * CL[g] + 64 + s * MS[g] + m0
                        fw.dma("sp", lambda h, vs=vs, row0=row0, g=g: h.dma_start(out=avs[g][row0:row0 + 128, :], in_=vs[:]),
                               reads=[("vst", ct % 2)], writes=[("av", g)])
            fw.barrier()

        fw.phase = "P2"
        with ExitStack() as st:
            xc = [SB(st, f"xc{i}", [128, 516]) for i in range(4)]
            ca_l = [SB(st, f"ca{i}", [128, 512]) for i in range(2)]; cs_l = [SB(st, f"cs{i}", [128, 512]) for i in range(2)]
            cq_l = [SB(st, f"cq{i}", [128, 512]) for i in range(2)]; cr_l = [SB(st, f"cr{i}", [128, 512]) for i in range(2)]
            cn = [SB(st, f"cn{i}", [128, 512]) for i in range(2)]
            ctm = [SB(st, f"ctm{i}", [128, 4, 128]) for i in range(2)]
            pss2_l = [PSM(st, f"pss2{i}", [128, 512]) for i in range(2)]
            ptr = [PSM(st, f"ptr{i}", [128, 512]) for i in range(2)]
            def p2_load(it):
                cb, tile = it // 16, it % 16
                t0 = tile * 512
                xw = xc[it % 4]
                fw.dma("sp", lambda h: h.dma_start(out=xw[:], in_=qkvpre[cb * 128:(cb + 1) * 128, t0:t0 + 516]), reads=["qkvpre"], writes=[("xc", it % 4)])

            NT2 = 24 * 16

            def p2_A(it):
                cb, tile = it // 16, it % 16
                s = tile // 4
                xw = xc[it % 4]
                kx = ("xc", it % 4)
                pss2 = pss2_l[it % 2]
                ca, cs_, cq, cr = ca_l[it % 2], cs_l[it % 2], cq_l[it % 2], cr_l[it % 2]
                kca, kcs, kcq, kcr = ("ca", it % 2), ("cs_", it % 2), ("cq", it % 2), ("cr", it % 2)
                if tile == 0:
                    fw.op("pool", lambda h: h.memset(xw[:, 0:2], 0.0), writes=[kx])
                elif tile % 4 == 0:
                    fw.op("pool", lambda h: h.tensor_scalar(out=xw[:, 0:2], in0=xw[:, 0:2], scalar1=sgf[:, s:s + 1], scalar2=None, op0=ALU.mult), reads=[kx, "sgf"], writes=[kx])
                if tile == 15:
                    fw.op("pool", lambda h: h.memset(xw[:, 514:516], 0.0), writes=[kx])
                elif tile % 4 == 3:
                    fw.op("pool", lambda h: h.tensor_scalar(out=xw[:, 514:516], in0=xw[:, 514:516], scalar1=sgf[:, 4 + s:5 + s], scalar2=None, op0=ALU.mult), reads=[kx, "sgf"], writes=[kx])
                fw.op("dve", lambda h: h.tensor_scalar(out=ca[:], in0=xw[:, 0:512], scalar1=convT[:, cb:cb + 1], scalar2=None, op0=ALU.mult), reads=[kx, "convT"], writes=[kca])
                for k in range(1, 5):
                    fw.op("dve", lambda h: h.scalar_tensor_tensor(out=ca[:], in0=xw[:, k:k + 512], scalar=convT[:, k * 24 + cb:k * 24 + cb + 1], in1=ca[:], op0=ALU.mult, op1=ALU.add),
                          reads=[kx, "convT", kca], writes=[kca])
                fw.op("act", lambda h: h.activation(out=cs_[:], in_=ca[:], func=AF.Silu), reads=[kca], writes=[kcs])
                if cb < 16:
                    fw.op("pool", lambda h: h.tensor_tensor(out=cq[:], in0=cs_[:], in1=cs_[:], op=ALU.mult), reads=[kcs], writes=[kcq])
                    fw.op("pe", lambda h: h.matmul(pss2[:, :], lhsT=ones, rhs=cq[:], start=True, stop=True), reads=[kcq, "cst"], writes=[("pss2", it % 2)])
                    if cb < 8:
                        fw.op("act", lambda h: h.activation(out=cr[:], in_=pss2[:, :], func=AF.Sqrt, bias=128.0 * EPS, scale=128.0), reads=[("pss2", it % 2)], writes=[kcr])
                    else:
                        fw.op("act", lambda h: h.activation(out=cr[:], in_=pss2[:, :], func=AF.Sqrt, bias=EPS, scale=1.0), reads=[("pss2", it % 2)], writes=[kcr])

            def p2_B(it):
                cb, tile = it // 16, it % 16
                t0 = tile * 512
                cs_, cr = cs_l[it % 2], cr_l[it % 2]
                kcs, kcr = ("cs_", it % 2), ("cr", it % 2)
                res_t, res_k = cs_, kcs
                if cb < 16:
                    fw.op("dve", lambda h: h.reciprocal(out=cr[:], in_=cr[:]), reads=[kcr], writes=[kcr])
                    cnb = cn[it % 2]
                    fw.op("pool", lambda h: h.tensor_tensor(out=cnb[:], in0=cs_[:], in1=cr[:], op=ALU.mult), reads=[kcs, kcr], writes=[("cn", it % 2)])
                    res_t, res_k = cnb, ("cn", it % 2)
                    dstn = qn if cb < 8 else kn
                    fw.dma("sp", lambda h: h.dma_start(out=dstn[(cb % 8) * 128:(cb % 8 + 1) * 128, t0:t0 + 512], in_=cnb[:]), reads=[res_k], writes=["qkn"])
                if cb >= 8:
                    pb = ptr[it % 2]

                    def tr(h):
                        for q in range(4):
                            ins = h.transpose(out=pb[:, q * 128:(q + 1) * 128], in_=res_t[:, q * 128:(q + 1) * 128], identity=ident)
                        return ins
                    fw.op("pe", tr, reads=[res_k, "cst"], writes=[("ptr", it % 2)])
                    cm = ctm[it % 2]
                    fw.op("act", lambda h: h.activation(out=cm[:].rearrange("p q c -> p (q c)"), in_=pb[:, :], func=AF.Copy), reads=[("ptr", it % 2)], writes=[("ctm", it % 2)])
                    dstt = k_tm if cb < 16 else v_tm
                    fw.dma("sp", lambda h: h.dma_start(out=dstt[t0:t0 + 512, (cb % 8) * 128:(cb % 8 + 1) * 128].rearrange("(q p) c -> p q c", p=128), in_=cm[:]),
                           reads=[("ctm", it % 2)], writes=["kv_tm"])

            p2_load(0)
            p2_load(1)
            p2_load(2)
            for it in range(NT2 + 1):
                if it + 3 < NT2:
                    p2_load(it + 3)
                if it < NT2:
                    p2_A(it)
                if it >= 1:
                    p2_B(it - 1)
            fw.barrier()

        fw.phase = "P3"
        with ExitStack() as st:
            ab = SB(st, "ab", [128, 64, 32])
            g_ = SB(st, "g_", [128, 64, 16]); t1_ = SB(st, "t1_", [128, 64, 16]); t2_ = SB(st, "t2_", [128, 64, 16])
            rows = [SB(st, f"rows{i}", [8, 8, 4, 128]) for i in range(2)]
            pg = PSM(st, "pg", [128, 1024])
            pr = [PSM(st, f"pr{i}", [8, 512]) for i in range(2)]
            fw.dma("sp", lambda h: h.dma_start(out=ab[:], in_=ab_tm.rearrange("(n p) c -> p n c", p=128)), reads=["ab_tm"], writes=["ab"])
            bc = lambda t: t[:].unsqueeze(1).to_broadcast([128, 64, 16])
            fw.op("dve", lambda h: h.tensor_tensor(out=t1_[:], in0=ab[:, :, 0:16], in1=bc(dtb16), op=ALU.add), reads=["ab", "dtb16"], writes=["t1_"])
            fw.op("act", lambda h: h.activation(out=t2_[:], in_=t1_[:], func=AF.Abs), reads=["t1_"], writes=["t2_"])
            fw.op("act", lambda h: h.activation(out=t2_[:], in_=t2_[:], func=AF.Exp, scale=-1.0), reads=["t2_"], writes=["t2_"])
            fw.op("act", lambda h: h.activation(out=t2_[:], in_=t2_[:], func=AF.Ln, bias=1.0, scale=1.0), reads=["t2_"], writes=["t2_"])
            fw.op("dve", lambda h: h.scalar_tensor_tensor(out=t1_[:], in0=t1_[:], scalar=0.0, in1=t2_[:], op0=ALU.max, op1=ALU.add), reads=["t1_", "t2_"], writes=["t1_"])
            fw.op("dve", lambda h: h.tensor_tensor(out=g_[:], in0=t1_[:], in1=bc(nA16), op=ALU.mult), reads=["t1_", "nA16"], writes=["g_"])
            fw.op("act", lambda h: h.activation(out=Btm[:], in_=ab[:, :, 16:32], func=AF.Sigmoid), reads=["ab"], writes=["Btm"])
            mlow = cst[:, C_MLOW:C_MLOW + 128]; mup = cst[:, C_MUP:C_MUP + 128]

            def gmm(h):
                for c in range(64):
                    h.matmul(pg[:, c * 16:c * 16 + 8], lhsT=mlow, rhs=g_[:, c, 0:8], start=True, stop=True)
                    ins = h.matmul(pg[:, c * 16 + 8:c * 16 + 16], lhsT=mup, rhs=g_[:, c, 8:16], start=True, stop=True)
                return ins
            fw.op("pe", gmm, reads=["g_", "cst"], writes=["pg"])
            fw.op("dve", lambda h: h.tensor_copy(out=Gtm[:].rearrange("p c k -> p (c k)"), in_=pg[:, :]), reads=["pg"], writes=["Gtm"])
            for c in range(64):
                pp = pr[c % 2]
                rw = rows[(c // 8) % 2]

                def rmm(h, c=c, pp=pp):
                    h.matmul(pp[:, 0:128], lhsT=g_[:, c, 0:8], rhs=mlow, start=True, stop=True)
                    h.matmul(pp[:, 128:256], lhsT=g_[:, c, 8:16], rhs=mup, start=True, stop=True)
                    h.matmul(pp[:, 256:384], lhsT=Btm[:, c, 0:8], rhs=ident, start=True, stop=True)
                    return h.matmul(pp[:, 384:512], lhsT=Btm[:, c, 8:16], rhs=ident, start=True, stop=True)
                fw.op("pe", rmm, reads=["g_", "Btm", "cst"], writes=[("pr", c % 2)])
                fw.op("dve", lambda h, c=c, pp=pp, rw=rw: h.tensor_copy(out=rw[:, c % 8, :, :].rearrange("p k t -> p (k t)"), in_=pp[:, :]),
                      reads=[("pr", c % 2)], writes=[("rows", (c // 8) % 2)])
                if c % 8 == 7:
                    c0 = c - 7
                    for k4 in range(4):
                        dr_, kd_ = k4 % 2, k4 // 2
                        fw.dma("sp", lambda h, rw=rw, c0=c0, k4=k4, dr_=dr_, kd_=kd_: h.dma_start(
                            out=GB[dr_, kd_, c0:c0 + 8, :, :].rearrange("c h t -> h c t"), in_=rw[:, :, k4, :]),
                            reads=[("rows", (c // 8) % 2)], writes=["GB"])
            fw.barrier()

        fw.phase = "P4"
        with ExitStack() as st:
            ld_q = [SB(st, f"ldq{i}", [128, 8, 128]) for i in range(2)]
            ld_k = [SB(st, f"ldk{i}", [128, 8, 128]) for i in range(2)]
            ld_kt = [SB(st, f"ldkt{i}", [128, 8, 128]) for i in range(2)]
            ld_vt = [SB(st, f"ldvt{i}", [128, 8, 128]) for i in range(2)]
            ld_g = [SB(st, f"ldg{i}", [128, 8, 128]) for i in range(2)]
            ld_b = [SB(st, f"ldb{i}", [128, 8, 128]) for i in range(2)]
            f1 = SB(st, "f1", [128, 8, 128]); f2 = SB(st, "f2", [128, 8, 128]); f3 = SB(st, "f3", [128, 8, 128]); f4 = SB(st, "f4", [128, 8, 128])
            f5 = SB(st, "f5", [128, 8, 128]); f6 = SB(st, "f6", [128, 8, 128])
            kbf = SB(st, "kbf", [128, 8, 128], BF16); qbf = SB(st, "qbf", [128, 8, 128], BF16)
            P1b = SB(st, "P1b", [128, 8, 128], BF16); Q1b = SB(st, "Q1b", [128, 8, 128], BF16)
            Dm = SB(st, "Dm", [128, 8, 128], BF16); Em = SB(st, "Em", [128, 8, 128], BF16)
            Xb = SB(st, "Xb", [128, 8, 128], BF16); X2b = SB(st, "X2b", [128, 8, 128], BF16)
            lvl = SB(st, "lvl", [128, 14 * 128])
            fw.dma("sp", lambda h: h.dma_start(out=lvl[:], in_=lvlmask[:, :]), writes=["lvl"])
            lvu = SB(st, "lvu", [128, 14 * 128], mybir.dt.uint8)
            fw.op("dve", lambda h: h.tensor_copy(out=lvu[:], in_=lvl[:]), reads=["lvl"], writes=["lvu"])
            qkm = SB(st, "qkm", [128, 8, 128], BF16); qgT = SB(st, "qgT", [128, 8, 128], BF16)
            kbg = SB(st, "kbg", [128, 8, 128], BF16); vb = SB(st, "vb", [128, 8, 128], BF16); kd = SB(st, "kd", [128, 8, 128], BF16)
            wT = SB(st, "wT", [128, 8, 128], BF16); vnew = SB(st, "vnew", [128, 8, 128], BF16)
            u_ = SB(st, "u_", [128, 8, 128]); S_ = SB(st, "S_", [128, 8, 128]); Sbf = SB(st, "Sbf", [128, 8, 128], BF16)
            ost = [SB(st, f"ost{i}", [128, 8, 128]) for i in range(2)]
            sm = SB(st, "sm", [128, 8, 4])
            PA = PSM(st, "PA", [128, 1024]); PB = PSM(st, "PB", [128, 1024]); PC = PSM(st, "PC", [128, 1024]); PD = PSM(st, "PD", [128, 1024])
            print("P4 sbuf remaining", nc.sbuf_bytes_remaining)
            v3 = lambda t: t[:, :].rearrange("p (h c) -> p h c", h=8)
            maskc = lambda off: cst[:, off:off + 128].unsqueeze(1).to_broadcast([128, 8, 128])
            identb = cst[:, C_ID:C_ID + 128].unsqueeze(1).to_broadcast([128, 8, 128])
            def p4_load(dr_, c, b2):
                t0 = c * 128
                lq, lk, lkt, lvt, lg, lb = ld_q[b2], ld_k[b2], ld_kt[b2], ld_vt[b2], ld_g[b2], ld_b[b2]
                K = lambda n: (n, b2)
                fw.dma("sp", lambda h, lq=lq, t0=t0: h.dma_start(out=lq[:], in_=qn[:, t0:t0 + 128].rearrange("(h p) t -> p h t", p=128)), reads=["qkn"], writes=[K("ldq")])
                fw.dma("sp", lambda h, lk=lk, t0=t0: h.dma_start(out=lk[:], in_=kn[:, t0:t0 + 128].rearrange("(h p) t -> p h t", p=128)), reads=["qkn"], writes=[K("ldk")])
                fw.dma("sp", lambda h, lkt=lkt, t0=t0: h.dma_start(out=lkt[:], in_=k_tm[t0:t0 + 128, :].rearrange("p (h c) -> p h c", h=8)), reads=["kv_tm"], writes=[K("ldkt")])
                fw.dma("sp", lambda h, lvt=lvt, t0=t0: h.dma_start(out=lvt[:], in_=v_tm[t0:t0 + 128, :].rearrange("p (h c) -> p h c", h=8)), reads=["kv_tm"], writes=[K("ldvt")])
                fw.dma("sp", lambda h, lg=lg, c=c, dr_=dr_: h.dma_start(out=lg[:].rearrange("p h t -> p (h t)"),
                                                                      in_=GB[dr_, 0, c:c + 1, :, :].rearrange("c h t -> c (h t)").partition_broadcast(128)),
                       reads=["GB"], writes=[K("ldg")])
                fw.dma("sp", lambda h, lb=lb, c=c, dr_=dr_: h.dma_start(out=lb[:].rearrange("p h t -> p (h t)"),
                                                                      in_=GB[dr_, 1, c:c + 1, :, :].rearrange("c h t -> c (h t)").partition_broadcast(128)),
                       reads=["GB"], writes=[K("ldb")])

            seq_all = [(0, c) for c in range(64)] + [(1, c) for c in range(63, -1, -1)]
            p4_load(0, 0, 0)
            it = 0
            for dr_ in range(2):
                fw.op("pool", lambda h: h.memset(S_[:], 0.0), writes=[("S_", 0), ("S_", 1)])
                fw.op("pool", lambda h: h.memset(Sbf[:], 0.0), writes=[("Sbf", 0), ("Sbf", 1)])
                order = list(range(64)) if dr_ == 0 else list(range(63, -1, -1))
                m_p1 = C_GT if dr_ == 0 else C_LT
                m_q1 = C_LT if dr_ == 0 else C_GT
                m_qk = C_LE if dr_ == 0 else C_GE
                last = 127 if dr_ == 0 else 0
                for c in order:
                    b2 = it % 2
                    it += 1
                    if it < 128:
                        p4_load(seq_all[it][0], seq_all[it][1], it % 2)
                    t0 = c * 128
                    lq, lk, lkt, lvt, lg, lb = ld_q[b2], ld_k[b2], ld_kt[b2], ld_vt[b2], ld_g[b2], ld_b[b2]
                    K = lambda n: (n, b2)
                    Gp = Gtm[:, c, dr_ * 8:dr_ * 8 + 8]
                    Bp = Btm[:, c, dr_ * 8:dr_ * 8 + 8]
                    Gpb = Gp.unsqueeze(2).to_broadcast([128, 8, 128])
                    Bpb = Bp.unsqueeze(2).to_broadcast([128, 8, 128])
                    HV = (slice(0, 4), slice(4, 8))
                    lm = lambda k, up: lvl[:, (2 * k + up) * 128:(2 * k + up + 1) * 128].unsqueeze(1).to_broadcast([128, 4, 128])
                    lmu = lambda k, up: lvu[:, (2 * k + up) * 128:(2 * k + up + 1) * 128].unsqueeze(1).to_broadcast([128, 4, 128])
                    mk4 = lambda off: cst[:, off:off + 128].unsqueeze(1).to_broadcast([128, 4, 128])
                    id4 = cst[:, C_ID:C_ID + 128].unsqueeze(1).to_broadcast([128, 4, 128])
                    dsel = 0 if dr_ == 0 else 1
                    seg = c // 16

                    def H(n, hf):
                        return (n, hf)

                    def pv(PX, hf):
                        return v3(PX)[:, HV[hf], :]

                    def both(fn):
                        for hf in range(2):
                            fn(hf, HV[hf])

                    def s_cast(hf, hs):
                        fw.op("act", lambda h: h.activation(out=kbf[:, hs, :], in_=lk[:, hs, :], func=AF.Copy), reads=[K("ldk")], writes=[H("kbf", hf)])
                        fw.op("act", lambda h: h.activation(out=qbf[:, hs, :], in_=lq[:, hs, :], func=AF.Copy), reads=[K("ldq")], writes=[H("qbf", hf)])

                        def kk(h):
                            for hd in range(hs.start, hs.stop):
                                h.matmul(PA[:, hd * 128:(hd + 1) * 128], lhsT=kbf[:, hd, :], rhs=kbf[:, hd, :], start=True, stop=True)
                                ins = h.matmul(PB[:, hd * 128:(hd + 1) * 128], lhsT=kbf[:, hd, :], rhs=qbf[:, hd, :], start=True, stop=True)
                            return ins
                        fw.op("pe", kk, reads=[H("kbf", hf), H("qbf", hf)], writes=[H("PA", hf), H("PB", hf)])
                    both(s_cast)

                    def s_dec1(hf, hs):
                        Gpb4 = Gp[:, hs].unsqueeze(2).to_broadcast([128, 4, 128])
                        fw.op("dve", lambda h: h.tensor_tensor(out=f1[:, hs, :], in0=lg[:, hs, :], in1=Gpb4, op=ALU.subtract), reads=[K("ldg"), "Gtm"], writes=[H("f1", hf)])
                        fw.op("pool", lambda h: h.tensor_scalar(out=f2[:, hs, :], in0=f1[:, hs, :], scalar1=0.0, scalar2=None, op0=ALU.max), reads=[H("f1", hf)], writes=[H("f2", hf)])
                        fw.op("act", lambda h: h.activation(out=f2[:, hs, :], in_=f2[:, hs, :], func=AF.Exp, scale=-1.0), reads=[H("f2", hf)], writes=[H("f2", hf)])
                        fw.op("pool", lambda h: h.tensor_scalar(out=f3[:, hs, :], in0=f1[:, hs, :], scalar1=0.0, scalar2=None, op0=ALU.min), reads=[H("f1", hf)], writes=[H("f3", hf)])
                        fw.op("act", lambda h: h.activation(out=f3[:, hs, :], in_=f3[:, hs, :], func=AF.Exp), reads=[H("f3", hf)], writes=[H("f3", hf)])
                    both(s_dec1)

                    def s_dec2(hf, hs):
                        Bpb4 = Bp[:, hs].unsqueeze(2).to_broadcast([128, 4, 128])
                        fw.op("pool", lambda h: h.tensor_tensor(out=f2[:, hs, :], in0=f2[:, hs, :], in1=mk4(m_p1), op=ALU.mult), reads=[H("f2", hf), "cst"], writes=[H("f2", hf)])
                        fw.op("dve", lambda h: h.scalar_tensor_tensor(out=f2[:, hs, :], in0=f2[:, hs, :], scalar=-1.0, in1=Bpb4, op0=ALU.mult, op1=ALU.mult),
                              reads=[H("f2", hf), "Btm"], writes=[H("f2", hf)])
                        fw.op("dve", lambda h: h.tensor_tensor(out=P1b[:, hs, :], in0=pv(PA, hf), in1=f2[:, hs, :], op=ALU.mult), reads=[H("PA", hf), H("f2", hf)], writes=[H("P1b", hf)])
                        fw.op("dve", lambda h: h.scalar_tensor_tensor(out=f4[:, hs, :], in0=f3[:, hs, :], scalar=-1.0, in1=lb[:, hs, :], op0=ALU.mult, op1=ALU.mult),
                              reads=[H("f3", hf), K("ldb")], writes=[H("f4", hf)])
                        fw.op("pool", lambda h: h.tensor_tensor(out=f4[:, hs, :], in0=f4[:, hs, :], in1=mk4(m_q1), op=ALU.mult), reads=[H("f4", hf), "cst"], writes=[H("f4", hf)])
                        fw.op("dve", lambda h: h.tensor_tensor(out=Q1b[:, hs, :], in0=pv(PA, hf), in1=f4[:, hs, :], op=ALU.mult), reads=[H("PA", hf), H("f4", hf)], writes=[H("Q1b", hf)])
                        fw.op("pool", lambda h: h.tensor_tensor(out=f3[:, hs, :], in0=f3[:, hs, :], in1=mk4(m_qk), op=ALU.mult), reads=[H("f3", hf), "cst", H("f4", hf)], writes=[H("f3", hf)])
                        fw.op("dve", lambda h: h.tensor_tensor(out=qkm[:, hs, :], in0=pv(PB, hf), in1=f3[:, hs, :], op=ALU.mult), reads=[H("PB", hf), H("f3", hf)], writes=[H("qkm", hf)])
                    both(s_dec2)

                    def s_lvl0(hf, hs):
                        fw.op("pool", lambda h: h.tensor_tensor(out=Dm[:, hs, :], in0=P1b[:, hs, :], in1=lm(0, dsel), op=ALU.mult), reads=[H("P1b", hf), "lvl"], writes=[H("Dm", hf)])
                        fw.op("pool", lambda h: h.tensor_tensor(out=Dm[:, hs, :], in0=Dm[:, hs, :], in1=id4, op=ALU.add), reads=[H("Dm", hf), "cst"], writes=[H("Dm", hf)])
                        fw.op("pool", lambda h: h.tensor_tensor(out=Em[:, hs, :], in0=Q1b[:, hs, :], in1=lm(0, 1 - dsel), op=ALU.mult), reads=[H("Q1b", hf), "lvl"], writes=[H("Em", hf)])
                        fw.op("pool", lambda h: h.tensor_tensor(out=Em[:, hs, :], in0=Em[:, hs, :], in1=id4, op=ALU.add), reads=[H("Em", hf), "cst"], writes=[H("Em", hf)])
                    both(s_lvl0)

                    fw.op("act", lambda h: h.activation(out=f5[:], in_=lg[:], func=AF.Exp), reads=[K("ldg")], writes=["f5"])
                    fw.op("pool", lambda h: h.tensor_tensor(out=qgT[:], in0=lq[:], in1=f5[:], op=ALU.mult), reads=[K("ldq"), "f5"], writes=[H("qgT", 0), H("qgT", 1)])
                    fw.op("act", lambda h: h.activation(out=sm[:, :, 0], in_=Gp, func=AF.Exp), reads=["Gtm"], writes=["sm0"])
                    fw.op("dve", lambda h: h.tensor_tensor(out=sm[:, :, 0], in0=sm[:, :, 0], in1=Bp, op=ALU.mult), reads=["sm0", "Btm"], writes=["sm0"])
                    fw.op("dve", lambda h: h.tensor_tensor(out=sm[:, :, 1], in0=lg[:, :, last], in1=Gp, op=ALU.subtract), reads=[K("ldg"), "Gtm"], writes=["sm1"])
                    fw.op("act", lambda h: h.activation(out=sm[:, :, 1], in_=sm[:, :, 1], func=AF.Exp), reads=["sm1"], writes=["sm1"])
                    fw.op("act", lambda h: h.activation(out=sm[:, :, 2], in_=lg[:, :, last], func=AF.Exp), reads=[K("ldg")], writes=["sm2"])
                    smb = lambda i: sm[:, :, i:i + 1].to_broadcast([128, 8, 128])
                    fw.op("pool", lambda h: h.tensor_tensor(out=kbg[:], in0=lkt[:], in1=smb(0), op=ALU.mult), reads=[K("ldkt"), "sm0"], writes=[H("kbg", 0), H("kbg", 1)])
                    fw.op("pool", lambda h: h.tensor_tensor(out=vb[:], in0=lvt[:], in1=Bpb, op=ALU.mult), reads=[K("ldvt"), "Btm"], writes=[H("vb", 0), H("vb", 1)])
                    fw.op("pool", lambda h: h.tensor_tensor(out=kd[:], in0=lkt[:], in1=smb(1), op=ALU.mult), reads=[K("ldkt"), "sm1"], writes=[H("kd", 0), H("kd", 1)])

                    for k in range(1, 7):
                        def s_x(hf, hs):
                            def xmm(h):
                                for hd in range(hs.start, hs.stop):
                                    h.matmul(PC[:, hd * 128:(hd + 1) * 128], lhsT=Q1b[:, hd, :], rhs=Dm[:, hd, :], start=True, stop=True)
                                    ins = h.matmul(PD[:, hd * 128:(hd + 1) * 128], lhsT=P1b[:, hd, :], rhs=Em[:, hd, :], start=True, stop=True)
                                return ins
                            fw.op("pe", xmm, reads=[H("Q1b", hf), H("P1b", hf), H("Dm", hf), H("Em", hf)], writes=[H("PC", hf), H("PD", hf)])
                            fw.op("act", lambda h: h.activation(out=Xb[:, hs, :], in_=pv(PC, hf), func=AF.Copy), reads=[H("PC", hf)], writes=[H("Xb", hf)])
                            fw.op("act", lambda h: h.activation(out=X2b[:, hs, :], in_=pv(PD, hf), func=AF.Copy), reads=[H("PD", hf)], writes=[H("X2b", hf)])
                        both(s_x)

                        def s_z(hf, hs):
                            def zmm(h):
                                for hd in range(hs.start, hs.stop):
                                    h.matmul(PA[:, hd * 128:(hd + 1) * 128], lhsT=Em[:, hd, :], rhs=Xb[:, hd, :], start=True, stop=True)
                                    ins = h.matmul(PB[:, hd * 128:(hd + 1) * 128], lhsT=Dm[:, hd, :], rhs=X2b[:, hd, :], start=True, stop=True)
                                return ins
                            fw.op("pe", zmm, reads=[H("Em", hf), H("Dm", hf), H("Xb", hf), H("X2b", hf)], writes=[H("PA", hf), H("PB", hf)])
                            fw.op("dve", lambda h: h.copy_predicated(out=Dm[:, hs, :], mask=lmu(k, dsel), data=pv(PA, hf)), reads=[H("PA", hf), "lvu", H("Dm", hf)], writes=[H("Dm", hf)])
                            fw.op("dve", lambda h: h.copy_predicated(out=Em[:, hs, :], mask=lmu(k, 1 - dsel), data=pv(PB, hf)), reads=[H("PB", hf), "lvu", H("Em", hf)], writes=[H("Em", hf)])
                        both(s_z)

                    def s_u(hf, hs):
                        def umm(h):
                            for hd in range(hs.start, hs.stop):
                                h.matmul(PB[:, hd * 128:(hd + 1) * 128], lhsT=Em[:, hd, :], rhs=vb[:, hd, :], start=True, stop=True)
                                ins = h.matmul(PA[:, hd * 128:(hd + 1) * 128], lhsT=kbg[:, hd, :], rhs=Em[:, hd, :], start=True, stop=True)
                            return ins
                        fw.op("pe", umm, reads=[H("Em", hf), H("vb", hf), H("kbg", hf)], writes=[H("PA", hf), H("PB", hf)])
                        fw.op("act", lambda h: h.activation(out=u_[:, hs, :], in_=pv(PB, hf), func=AF.Copy), reads=[H("PB", hf)], writes=[H("u_", hf)])
                        fw.op("dve", lambda h: h.tensor_copy(out=wT[:, hs, :], in_=pv(PA, hf)), reads=[H("PA", hf)], writes=[H("wT", hf)])
                    both(s_u)

                    def s_seq(hf, hs):
                        if dr_ == 0 and c % 16 == 0 and c > 0:
                            fw.op("dve", lambda h: h.tensor_scalar(out=S_[:, hs, :], in0=S_[:, hs, :], scalar1=sgf[:, seg:seg + 1], scalar2=None, op0=ALU.mult), reads=[H("S_", hf), "sgf"], writes=[H("S_", hf)])
                            fw.op("act", lambda h: h.activation(out=Sbf[:, hs, :], in_=S_[:, hs, :], func=AF.Copy), reads=[H("S_", hf)], writes=[H("Sbf", hf)])
                        if dr_ == 1 and c % 16 == 15 and c < 63:
                            fw.op("dve", lambda h: h.tensor_scalar(out=S_[:, hs, :], in0=S_[:, hs, :], scalar1=sgf[:, 4 + seg:5 + seg], scalar2=None, op0=ALU.mult), reads=[H("S_", hf), "sgf"], writes=[H("S_", hf)])
                            fw.op("act", lambda h: h.activation(out=Sbf[:, hs, :], in_=S_[:, hs, :], func=AF.Copy), reads=[H("S_", hf)], writes=[H("Sbf", hf)])

                        def wsmm(h):
                            for hd in range(hs.start, hs.stop):
                                ins = h.matmul(PC[:, hd * 128:(hd + 1) * 128], lhsT=wT[:, hd, :], rhs=Sbf[:, hd, :], start=True, stop=True)
                            return ins
                        fw.op("pe", wsmm, reads=[H("wT", hf), H("Sbf", hf)], writes=[H("PC", hf)])
                        fw.op("dve", lambda h: h.tensor_tensor(out=vnew[:, hs, :], in0=u_[:, hs, :], in1=pv(PC, hf), op=ALU.subtract), reads=[H("u_", hf), H("PC", hf)], writes=[H("vnew", hf)])

                        def omm(h):
                            for hd in range(hs.start, hs.stop):
                                h.matmul(PD[:, hd * 128:(hd + 1) * 128], lhsT=qgT[:, hd, :], rhs=Sbf[:, hd, :], start=True, stop=False)
                                h.matmul(PD[:, hd * 128:(hd + 1) * 128], lhsT=qkm[:, hd, :], rhs=vnew[:, hd, :], start=False, stop=True)
                                ins = h.matmul(PB[:, hd * 128:(hd + 1) * 128], lhsT=kd[:, hd, :], rhs=vnew[:, hd, :], start=True, stop=True)
                            return ins
                        fw.op("pe", omm, reads=[H("qgT", hf), H("Sbf", hf), H("qkm", hf), H("vnew", hf), H("kd", hf)], writes=[H("PD", hf), H("PB", hf)])
                        fw.op("act", lambda h: h.activation(out=ost[b2][:, hs, :], in_=pv(PD, hf), func=AF.Copy), reads=[H("PD", hf)], writes=[K("ost")])
                        fw.op("pool", lambda h: h.tensor_tensor(out=S_[:, hs, :], in0=S_[:, hs, :], in1=sm[:, hs, 2:3].to_broadcast([128, 4, 128]), op=ALU.mult), reads=[H("S_", hf), "sm2"], writes=[H("S_", hf)])
                        fw.op("dve", lambda h: h.tensor_tensor(out=S_[:, hs, :], in0=S_[:, hs, :], in1=pv(PB, hf), op=ALU.add), reads=[H("S_", hf), H("PB", hf)], writes=[H("S_", hf)])
                        fw.op("act", lambda h: h.activation(out=Sbf[:, hs, :], in_=S_[:, hs, :], func=AF.Copy), reads=[H("S_", hf)], writes=[H("Sbf", hf)])
                    both(s_seq)
                    fw.dma("sp", lambda h: h.dma_start(out=o_dir[dr_, t0:t0 + 128, :].rearrange("p (h c) -> p h c", h=8), in_=ost[b2][:]),
                           reads=[K("ost")], writes=["o_dir"])
            fw.barrier()

        fw.phase = "P5"
        with ExitStack() as st:
            kw = [SB(st, f"kw{i}", [128, 256], BF16) for i in range(4)]
            qw = [SB(st, f"qw{i}", [128, 128], BF16) for i in range(4)]
            vw = [SB(st, f"vw{i}", [128, 2, 128], BF16) for i in range(4)]
            sc_ = [SB(st, f"sc{i}", [128, 256]) for i in range(4)]
            pe_ = [SB(st, f"pe{i}", [128, 256], BF16) for i in range(4)]
            pT = [SB(st, f"pT{i}", [128, 2, 128], BF16) for i in range(4)]
            oo = [SB(st, f"oo{i}", [128, 130]) for i in range(4)]
            nmx = [SB(st, f"nmx{i}", [128, 1]) for i in range(4)]
            idb = SB(st, "idb", [128, 128], BF16)
            msk = SB(st, "msk", [128, 4, 3, 256])
            mk1 = SB(st, "mk1", [128, 1])
            psc_t = [PSM(st, f"psc{i}", [128, 512]) for i in range(2)]
            psc = [psc_t[i // 2][:, (i % 2) * 256:(i % 2 + 1) * 256] for i in range(4)]
            ppt_t = PSM(st, "ppt", [128, 4, 2, 128], BF16)
            ppt = [ppt_t[:, i, :, :] for i in range(4)]
            pov_t = PSM(st, "pov", [128, 4, 128])
            pov = [pov_t[:, i, :] for i in range(4)]
            fw.op("dve", lambda h: h.tensor_copy(out=idb[:], in_=ident), reads=["cst"], writes=["idb"])
            band = cst[:, C_BAND:C_BAND + 256]; negl = cst[:, C_NEGL:C_NEGL + 256]; negr = cst[:, C_NEGR:C_NEGR + 256]
            for s in range(NSEG):
                fw.op("dve", lambda h, s=s: h.tensor_scalar(out=mk1[:], in0=sgf[:, s:s + 1], scalar1=-1.0, scalar2=1.0, op0=ALU.mult, op1=ALU.add), reads=["sgf"], writes=["mk1"])
                fw.op("dve", lambda h, s=s: h.scalar_tensor_tensor(out=msk[:, s, 0, :], in0=negl, scalar=mk1[:, 0:1], in1=band, op0=ALU.mult, op1=ALU.add),
                      reads=["mk1", "cst"], writes=["msk"])
                fw.op("dve", lambda h, s=s: h.tensor_scalar(out=mk1[:], in0=sgf[:, 4 + s:5 + s], scalar1=-1.0, scalar2=1.0, op0=ALU.mult, op1=ALU.add), reads=["sgf", "msk"], writes=["mk1"])
                fw.op("dve", lambda h, s=s: h.scalar_tensor_tensor(out=msk[:, s, 1, :], in0=negr, scalar=mk1[:, 0:1], in1=band, op0=ALU.mult, op1=ALU.add),
                      reads=["mk1", "cst"], writes=["msk"])
                fw.op("dve", lambda h, s=s: h.scalar_tensor_tensor(out=msk[:, s, 2, :], in0=negr, scalar=mk1[:, 0:1], in1=msk[:, s, 0, :], op0=ALU.mult, op1=ALU.add),
                      reads=["mk1", "cst", "msk"], writes=["msk"])
            units = []
            for g in range(3):
                for hh in range(4):
                    for r in range(DIL[g]):
                        for b in range(MC[g] // 128):
                            units.append((g, hh, r, b))

            def u_load(it):
                g, hh, r, b = units[it]
                i2 = it % 4
                K = lambda n: (n, i2)
                kfull = akT[g][hh * 128:(hh + 1) * 128, :].rearrange("p (r c) -> p r c", c=CL[g])
                qfull = aqT[g][hh * 128:(hh + 1) * 128, :].rearrange("p (r c) -> p r c", c=CL[g])
                vfull = avs[g][:, hh * 128:(hh + 1) * 128].rearrange("(r c) d -> r c d", c=CL[g])
                fw.dma("sp", lambda h: h.dma_start(out=kw[i2][:], in_=kfull[:, r, 128 * b:128 * b + 256]), reads=[("aqk", g), ("akT", g)], writes=[K("kw")])
                fw.dma("sp", lambda h: h.dma_start(out=qw[i2][:], in_=qfull[:, r, 64 + 128 * b:64 + 128 * b + 128]), reads=[("aqk", g)], writes=[K("qw")])
                fw.dma("sp", lambda h: h.dma_start(out=vw[i2][:], in_=vfull[r, 128 * b:128 * b + 256, :].rearrange("(k p) d -> p k d", p=128)),
                       reads=[("av", g)], writes=[K("vw")])

            def u_comp(it):
                g, hh, r, b = units[it]
                dil = DIL[g]
                tps = MS[g] // 128
                i2 = it % 4
                K = lambda n: (n, i2)
                s = b // tps
                first = (b % tps == 0)
                lastt = (b % tps == tps - 1)
                fw.op("pe", lambda h: h.matmul(psc[i2], lhsT=qw[i2][:], rhs=kw[i2][:], start=True, stop=True), reads=[K("qw"), K("kw")], writes=[K("psc")])
                if first and lastt:
                    mask = msk[:, s, 2, :]
                elif first:
                    mask = msk[:, s, 0, :]
                elif lastt:
                    mask = msk[:, s, 1, :]
                else:
                    mask = band
                fw.op("dve", lambda h: h.tensor_tensor(out=sc_[i2][:], in0=psc[i2], in1=mask, op=ALU.add), reads=[K("psc"), "msk", "cst"], writes=[K("sc")])
                fw.op("dve", lambda h: h.reduce_max(out=oo[i2][:, 128:129], in_=sc_[i2][:], axis=AX.X), reads=[K("sc")], writes=[K("oo")])
                fw.op("dve", lambda h: h.tensor_scalar(out=nmx[i2][:], in0=oo[i2][:, 128:129], scalar1=-1.0, scalar2=None, op0=ALU.mult), reads=[K("oo")], writes=[K("nmx")])
                fw.op("act", lambda h: h.activation(out=pe_[i2][:], in_=sc_[i2][:], func=AF.Exp, bias=nmx[i2][:], scale=1.0, accum_out=oo[i2][:, 129:130]),
                      reads=[K("sc"), K("nmx")], writes=[K("pe"), K("oo")])

                def ptr_(h):
                    h.transpose(out=ppt[i2][:, 0, :], in_=pe_[i2][:, 0:128], identity=idb[:])
                    return h.transpose(out=ppt[i2][:, 1, :], in_=pe_[i2][:, 128:256], identity=idb[:])
                fw.op("pe", ptr_, reads=[K("pe"), "idb"], writes=[K("ppt")])
                fw.op("act", lambda h: h.activation(out=pT[i2][:], in_=ppt[i2], func=AF.Copy), reads=[K("ppt")], writes=[K("pT")])

                def pv(h):
                    h.matmul(pov[i2], lhsT=pT[i2][:, 0, :], rhs=vw[i2][:, 0, :], start=True, stop=False)
                    return h.matmul(pov[i2], lhsT=pT[i2][:, 1, :], rhs=vw[i2][:, 1, :], start=False, stop=True)
                fw.op("pe", pv, reads=[K("pT"), K("vw")], writes=[K("pov")])
                fw.op("dve", lambda h: h.tensor_copy(out=oo[i2][:, 0:128], in_=pov[i2]), reads=[K("pov")], writes=[K("oo")])
                dst = Oat[g, :, hh, :].rearrange("(m r) c -> r m c", r=dil)[r, 128 * b:128 * b + 128, :]
                fw.dma("pool", lambda h: h.dma_start(out=dst, in_=oo[i2][:]), reads=[K("oo")], writes=["Oat"])

            NU = len(units)
            for it in range(NU + 2):
                if it < NU:
                    u_load(it)
                if it >= 2:
                    u_comp(it - 2)
            fw.barrier()

        fw.phase = "P5b"
        with ExitStack() as st:
            of_ = [SB(st, f"of{i}", [128, 8, 128]) for i in range(2)]
            ob_ = [SB(st, f"ob{i}", [128, 8, 128]) for i in range(2)]
            szt = [SB(st, f"szt{i}", [128, 8, 128]) for i in range(2)]
            osq = SB(st, "osq", [128, 8, 128])
            ss8 = SB(st, "ss8", [128, 8])
            yd = [SB(st, f"yd{i}", [128, 8, 128], BF16) for i in range(2)]
            og = [[SB(st, f"og{g}_{i}", [128, 4, 130]) for i in range(2)] for g in range(3)]
            m4 = SB(st, "m4", [128, 4]); w4 = SB(st, "w4", [128, 3, 4]); d4 = SB(st, "d4", [128, 4]); t4 = SB(st, "t4", [128, 4])
            am = SB(st, "am", [128, 4, 128]); am2 = SB(st, "am2", [128, 4, 128])
            ya = [SB(st, f"ya{i}", [128, 4, 128], BF16) for i in range(2)]
            pdn = PSM(st, "pdn", [128, 1024])
            pat = PSM(st, "pat", [128, 512])
            def p5b_load(tl):
                t0 = tl * 128
                i2 = tl % 2
                K = lambda n: (n, i2)
                fw.dma("sp", lambda h: h.dma_start(out=of_[i2][:], in_=o_dir[0, t0:t0 + 128, :].rearrange("p (h c) -> p h c", h=8)), reads=["o_dir"], writes=[K("of")])
                fw.dma("sp", lambda h: h.dma_start(out=ob_[i2][:], in_=o_dir[1, t0:t0 + 128, :].rearrange("p (h c) -> p h c", h=8)), reads=["o_dir"], writes=[K("ob")])
                fw.dma("sp", lambda h: h.dma_start(out=szt[i2][:], in_=szT[:, t0:t0 + 128].rearrange("(h p) t -> p h t", p=128)), reads=["szT"], writes=[K("szt")])
                for g in range(3):
                    fw.dma("sp", lambda h, g=g: h.dma_start(out=og[g][i2][:], in_=Oat[g, t0:t0 + 128, :, :]), reads=["Oat"], writes=[K(f"og{g}")])

            p5b_load(0)
            for tl in range(64):
                t0 = tl * 128
                i2 = tl % 2
                K = lambda n: (n, i2)
                if tl + 1 < 64:
                    p5b_load(tl + 1)
                fw.op("pool", lambda h, i2=i2: h.tensor_tensor(out=of_[i2][:], in0=of_[i2][:], in1=ob_[i2][:], op=ALU.add), reads=[K("of"), K("ob")], writes=[K("of")])
                fw.op("pool", lambda h, i2=i2: h.tensor_tensor(out=osq[:], in0=of_[i2][:], in1=of_[i2][:], op=ALU.mult), reads=[K("of")], writes=["osq"])
                fw.op("dve", lambda h: h.tensor_reduce(out=ss8[:], in_=osq[:], axis=AX.X, op=ALU.add), reads=["osq"], writes=["ss8"])
                fw.op("act", lambda h: h.activation(out=ss8[:], in_=ss8[:], func=AF.Sqrt, bias=EPS, scale=1.0 / 128.0), reads=["ss8"], writes=["ss8"])
                fw.op("dve", lambda h: h.reciprocal(out=ss8[:], in_=ss8[:]), reads=["ss8"], writes=["ss8"])
                fw.op("dve", lambda h, i2=i2: h.tensor_tensor(out=of_[i2][:], in0=of_[i2][:], in1=ss8[:].unsqueeze(2).to_broadcast([128, 8, 128]), op=ALU.mult),
                      reads=[K("of"), "ss8"], writes=[K("of")])

                def trd(h, i2=i2):
                    for hd in range(8):
                        ins = h.transpose(out=pdn[:, hd * 128:(hd + 1) * 128], in_=of_[i2][:, hd, :], identity=ident)
                    return ins
                fw.op("pe", trd, reads=[K("of"), "cst"], writes=["pdn"])
                fw.op("dve", lambda h, i2=i2: h.scalar_tensor_tensor(out=yd[i2][:], in0=pdn[:, :].rearrange("p (h c) -> p h c", h=8), scalar=dnwT[:, 0:1], in1=szt[i2][:],
                                                                   op0=ALU.mult, op1=ALU.mult), reads=["pdn", "dnwT", K("szt")], writes=[K("yd")])
                fw.dma("sp", lambda h, i2=i2, t0=t0: h.dma_start(out=ydnT[:, t0:t0 + 128].rearrange("(h p) t -> p h t", p=128), in_=yd[i2][:]), reads=[K("yd")], writes=["ydnT"])
                mxs = [og[g][i2][:, :, 128] for g in range(3)]
                dns = [og[g][i2][:, :, 129] for g in range(3)]
                fw.op("dve", lambda h: h.tensor_tensor(out=m4[:], in0=mxs[0], in1=mxs[1], op=ALU.max), reads=[K("og0"), K("og1")], writes=["m4"])
                fw.op("dve", lambda h: h.tensor_tensor(out=m4[:], in0=m4[:], in1=mxs[2], op=ALU.max), reads=["m4", K("og2")], writes=["m4"])
                for g in range(3):
                    fw.op("dve", lambda h, g=g: h.tensor_tensor(out=w4[:, g, :], in0=mxs[g], in1=m4[:], op=ALU.subtract), reads=[K(f"og{g}"), "m4"], writes=["w4"])
                fw.op("act", lambda h: h.activation(out=w4[:], in_=w4[:], func=AF.Exp), reads=["w4"], writes=["w4"])
                fw.op("dve", lambda h: h.tensor_tensor(out=d4[:], in0=w4[:, 0, :], in1=dns[0], op=ALU.mult), reads=["w4", K("og0")], writes=["d4"])
                for g in (1, 2):
                    fw.op("dve", lambda h, g=g: h.tensor_tensor(out=t4[:], in0=w4[:, g, :], in1=dns[g], op=ALU.mult), reads=["w4", K(f"og{g}")], writes=["t4"])
                    fw.op("dve", lambda h: h.tensor_tensor(out=d4[:], in0=d4[:], in1=t4[:], op=ALU.add), reads=["d4", "t4"], writes=["d4"])
                fw.op("dve", lambda h: h.reciprocal(out=d4[:], in_=d4[:]), reads=["d4"], writes=["d4"])
                for g in range(3):
                    fw.op("dve", lambda h, g=g: h.tensor_tensor(out=w4[:, g, :], in0=w4[:, g, :], in1=d4[:], op=ALU.mult), reads=["w4", "d4"], writes=["w4"])
                fw.op("pool", lambda h, i2=i2: h.tensor_tensor(out=am[:], in0=og[0][i2][:, :, 0:128], in1=w4[:, 0, :].unsqueeze(2).to_broadcast([128, 4, 128]), op=ALU.mult),
                      reads=[K("og0"), "w4"], writes=["am"])
                for g in (1, 2):
                    fw.op("pool", lambda h, g=g, i2=i2: h.tensor_tensor(out=am2[:], in0=og[g][i2][:, :, 0:128], in1=w4[:, g, :].unsqueeze(2).to_broadcast([128, 4, 128]), op=ALU.mult),
                          reads=[K(f"og{g}"), "w4"], writes=["am2"])
                    fw.op("pool", lambda h: h.tensor_tensor(out=am[:], in0=am[:], in1=am2[:], op=ALU.add), reads=["am", "am2"], writes=["am"])

                def tra(h):
                    for hd in range(4):
                        ins = h.transpose(out=pat[:, hd * 128:(hd + 1) * 128], in_=am[:, hd, :], identity=ident)
                    return ins
                fw.op("pe", tra, reads=["am", "cst"], writes=["pat"])
                fw.op("act", lambda h, i2=i2: h.activation(out=ya[i2][:], in_=pat[:, :].rearrange("p (h c) -> p h c", h=4), func=AF.Copy), reads=["pat"], writes=[K("ya")])
                fw.dma("sp", lambda h, i2=i2, t0=t0: h.dma_start(out=yatT[:, t0:t0 + 128].rearrange("(h p) t -> p h t", p=128), in_=ya[i2][:]), reads=[K("ya")], writes=["yatT"])
            fw.barrier()

        fw.phase = "P6"
        with ExitStack() as st:
            xin = [SB(st, f"xin{q}", [128, D]) for q in range(4)]
            xT = SB(st, "xT", [128, 16, 512])
            acc = SB(st, "acc", [128, 16, 512])
            sq = [SB(st, f"sq{i}", [128, 512]) for i in range(2)]
            rstd = SB(st, "rstd", [128, 512])
            big = SB(st, "big", [128, 32, 512], BF16)
            h2T = SB(st, "h2T", [128, 16, 512], BF16)
            wk = [SB(st, f"wk{i}", [128, 32, 128], BF16) for i in range(3)]
            print("sbuf remaining", nc.sbuf_bytes_remaining)
            tmp = [SB(st, f"tmp{i}", [128, 512]) for i in range(4)]
            pxt = [PSM(st, f"pxt{i}", [128, 512]) for i in range(2)]
            pss = PSM(st, "pss", [128, 512])
            pmm = [PSM(st, f"pmm{i}", [128, 512]) for i in range(4)]
            hT = big[:, 0:16, :]
            ydn_s = big[:, 16:24, :]
            yat_s = big[:, 24:28, :]
            mixedT = h2T

            wslot = [0]

            def load_w(src_ap, nj, key):
                i = wslot[0] % 3
                wslot[0] += 1
                fw.dma("sp", lambda h, i=i: h.dma_start(out=wk[i][:, 0:nj, :], in_=src_ap), reads=[key], writes=[("wk", i)])
                return i

            def proj(widx, nj, rhs_fn, rhs_keys, pbank, first=True, last=True, j0=0):
                def mm(h):
                    for j in range(nj):
                        ins = h.matmul(pmm[pbank][:, :], lhsT=wk[widx][:, j, :], rhs=rhs_fn(j),
                                       start=(first and j == 0), stop=(last and j == nj - 1))
                    return ins
                fw.op("pe", mm, reads=[("wk", widx)] + rhs_keys, writes=[("pmm", pbank)])

            for tile in range(min(T // 512, KTILES)):
                t0 = tile * 512
                s = tile // 4
                front_end((xin, xT, sq, rstd, pxt, pss), t0, s)
                for j in range(16):
                    fw.op("pool", lambda h, j=j: h.tensor_tensor(out=tmp[j % 2][:], in0=xT[:, j, :], in1=rstd[:], op=ALU.mult),
                          reads=[("xT", j), "rstd"], writes=[("tmp", j % 2)])
                    fw.op("dve", lambda h, j=j: h.tensor_scalar(out=hT[:, j, :], in0=tmp[j % 2][:], scalar1=A1[:, j, s:s + 1],
                                                                scalar2=modT[:, j, s:s + 1], op0=ALU.mult, op1=ALU.add),
                          reads=[("tmp", j % 2), "A1", "modT"], writes=[("big", j)])
                fw.dma("sp", lambda h: h.dma_start(out=ydn_s, in_=ydnT[:, t0:t0 + 512].rearrange("(j p) t -> p j t", p=128)),
                       reads=["ydnT"], writes=[("big", 16 + j) for j in range(8)])
                fw.dma("sp", lambda h: h.dma_start(out=yat_s, in_=yatT[:, t0:t0 + 512].rearrange("(j p) t -> p j t", p=128)),
                       reads=["yatT"], writes=[("big", 24 + j) for j in range(4)])
                for cb in range(16):
                    w1 = load_w(wb_mg[cb], 16, ("wb_mg", cb))
                    proj(w1, 16, lambda j: hT[:, j, :], [("big", j) for j in range(16)], 0)
                    w2 = load_w(wb_mg[16 + cb], 16, ("wb_mg", 16 + cb))
                    proj(w2, 16, lambda j: hT[:, j, :], [("big", j) for j in range(16)], 1)
                    w3 = load_w(wb_dn[cb], 8, ("wb_dn", cb))
                    proj(w3, 8, lambda j: ydn_s[:, j, :], [("big", 16 + j) for j in range(8)], 2)
                    w4 = load_w(wb_at[cb], 4, ("wb_at", cb))
                    proj(w4, 4, lambda j: yat_s[:, j, :], [("big", 24 + j) for j in range(4)], 3)
                    fw.op("act", lambda h: h.activation(out=tmp[0][:], in_=pmm[0][:, :], func=AF.Sigmoid), reads=[("pmm", 0)], writes=[("tmp", 0)])
                    fw.op("act", lambda h: h.activation(out=tmp[1][:], in_=pmm[1][:, :], func=AF.Sigmoid), reads=[("pmm", 1)], writes=[("tmp", 1)])
                    fw.op("dve", lambda h: h.tensor_tensor(out=tmp[2][:], in0=pmm[2][:, :], in1=tmp[0][:], op=ALU.mult),
                          reads=[("pmm", 2), ("tmp", 0)], writes=[("tmp", 2)])
                    fw.op("dve", lambda h: h.tensor_tensor(out=tmp[3][:], in0=pmm[3][:, :], in1=tmp[1][:], op=ALU.mult),
                          reads=[("pmm", 3), ("tmp", 1)], writes=[("tmp", 3)])
                    fw.op("pool", lambda h, cb=cb: h.tensor_tensor(out=mixedT[:, cb, :], in0=tmp[2][:], in1=tmp[3][:], op=ALU.add),
                          reads=[("tmp", 2), ("tmp", 3)], writes=[("h2T", cb)])
                for cb in range(16):
                    w1 = load_w(wb_out[cb], 16, ("wb_out", cb))
                    proj(w1, 16, lambda j: mixedT[:, j, :], [("h2T", j) for j in range(16)], cb % 4)
                    fw.op("act", lambda h, cb=cb: h.activation(out=acc[:, cb, :], in_=pmm[cb % 4][:, :], func=AF.Copy),
                          reads=[("pmm", cb % 4)], writes=[("acc", cb)])
                sumsq_rstd(lambda j: acc[:, j, :], 16, sq, pss, rstd, lambda j: ("acc", j))
                for j in range(16):
                    fw.op("pool", lambda h, j=j: h.tensor_tensor(out=tmp[j % 2][:], in0=acc[:, j, :], in1=rstd[:], op=ALU.mult),
                          reads=[("acc", j), "rstd"], writes=[("tmp", j % 2)])
                    fw.op("dve", lambda h, j=j: h.scalar_tensor_tensor(out=xT[:, j, :], in0=tmp[j % 2][:], scalar=G1[:, j, s:s + 1], in1=xT[:, j, :],
                                                                       op0=ALU.mult, op1=ALU.add),
                          reads=[("tmp", j % 2), "G1", ("xT", j)], writes=[("xT", j)])
                sumsq_rstd(lambda j: xT[:, j, :], 16, sq, pss, rstd, lambda j: ("xT", j))
                for j in range(16):
                    fw.op("pool", lambda h, j=j: h.tensor_tensor(out=tmp[j % 2][:], in0=xT[:, j, :], in1=rstd[:], op=ALU.mult),
                          reads=[("xT", j), "rstd"], writes=[("tmp", j % 2)])
                    fw.op("dve", lambda h, j=j: h.tensor_scalar(out=h2T[:, j, :], in0=tmp[j % 2][:], scalar1=A2[:, j, s:s + 1],
                                                                scalar2=modT[:, 48 + j, s:s + 1], op0=ALU.mult, op1=ALU.add),
                          reads=[("tmp", j % 2), "A2", "modT"], writes=[("h2T", j)])
                for hf in range(2):
                    for fb in range(32):
                        w1 = load_w(wb_f1[hf * 32 + fb], 16, ("wb_f1", hf * 32 + fb))
                        pb = fb % 4
                        proj(w1, 16, lambda j: h2T[:, j, :], [("h2T", j) for j in range(16)], pb)
                        if fb % 2 == 0:
                            fw.op("act", lambda h, pb=pb: h.activation(out=tmp[pb][:], in_=pmm[pb][:, :], func=AF.Relu), reads=[("pmm", pb)], writes=[("tmp", pb)])
                            fw.op("pool", lambda h, pb=pb, fb=fb: h.tensor_tensor(out=big[:, fb, :], in0=tmp[pb][:], in1=tmp[pb][:], op=ALU.mult),
                                  reads=[("tmp", pb)], writes=[("big", fb)])
                        else:
                            fw.op("dve", lambda h, pb=pb: h.tensor_scalar(out=tmp[pb][:], in0=pmm[pb][:, :], scalar1=0.0, scalar2=None, op0=ALU.max),
                                  reads=[("pmm", pb)], writes=[("tmp", pb)])
                            fw.op("pool", lambda h, pb=pb, fb=fb: h.tensor_tensor(out=big[:, fb, :], in0=tmp[pb][:], in1=tmp[pb][:], op=ALU.mult),
                                  reads=[("tmp", pb)], writes=[("big", fb)])
                    for cb in range(16):
                        w1 = load_w(wb_f2[hf * 16 + cb], 32, ("wb_f2", hf * 16 + cb))
                        pb = cb % 4
                        proj(w1, 32, lambda j: big[:, j, :], [("big", j) for j in range(32)], pb)
                        if hf == 0:
                            fw.op("act", lambda h, cb=cb, pb=pb: h.activation(out=acc[:, cb, :], in_=pmm[pb][:, :], func=AF.Copy),
                                  reads=[("pmm", pb)], writes=[("acc", cb)])
                        else:
                            fw.op("dve", lambda h, cb=cb, pb=pb: h.tensor_tensor(out=acc[:, cb, :], in0=pmm[pb][:, :], in1=acc[:, cb, :], op=ALU.add),
                                  reads=[("pmm", pb), ("acc", cb)], writes=[("acc", cb)])
                sumsq_rstd(lambda j: acc[:, j, :], 16, sq, pss, rstd, lambda j: ("acc", j))
                for j in range(16):
                    fw.op("pool", lambda h, j=j: h.tensor_tensor(out=tmp[j % 2][:], in0=acc[:, j, :], in1=rstd[:], op=ALU.mult),
                          reads=[("acc", j), "rstd"], writes=[("tmp", j % 2)])
                    fw.op("dve", lambda h, j=j: h.scalar_tensor_tensor(out=acc[:, j, :], in0=tmp[j % 2][:], scalar=G2[:, j, s:s + 1], in1=xT[:, j, :],
                                                                       op0=ALU.mult, op1=ALU.add),
                          reads=[("tmp", j % 2), "G2", ("xT", j)], writes=[("acc", j)])
                for q in range(4):
                    for jg in range(4):
                        pb = pmm[(q * 4 + jg) % 4]

                        def tr(h, q=q, jg=jg, pb=pb):
                            for jj in range(4):
                                j = jg * 4 + jj
                                ins = h.transpose(out=pb[:, jj * 128:(jj + 1) * 128], in_=acc[:, j, q * 128:(q + 1) * 128], identity=ident)
                            return ins
                        fw.op("pe", tr, reads=[("acc", jg * 4 + jj) for jj in range(4)] + ["cst"], writes=[("pmm", (q * 4 + jg) % 4)])
                        eng = "act" if jg % 2 == 0 else "dve"
                        if eng == "act":
                            fw.op("act", lambda h, q=q, jg=jg, pb=pb: h.activation(out=xin[q][:, jg * 512:(jg + 1) * 512], in_=pb[:, :], func=AF.Copy),
                                  reads=[("pmm", (q * 4 + jg) % 4)], writes=[("xin", q)])
                        else:
                            fw.op("dve", lambda h, q=q, jg=jg, pb=pb: h.tensor_copy(out=xin[q][:, jg * 512:(jg + 1) * 512], in_=pb[:, :]),
                                  reads=[("pmm", (q * 4 + jg) % 4)], writes=[("xin", q)])
                    fw.dma("sp", lambda h, q=q: h.dma_start(out=y[t0 + q * 128:t0 + (q + 1) * 128, :], in_=xin[q][:]),
                           reads=[("xin", q)], writes=["y"])
            fw.barrier()
        fw.emit_all()
    return nc


_NC_CACHE = {}

SAMPLE_MAP = {2: [0, 1, 2], 3: [3, 4, 5], 4: [6, 7, 8], 5: [9, 10, 11], 6: [12, 13], 7: [14, 15]}


def kernel(x_prompt, x_sample, c_prompt, c_sample, w_ada, b_ada, norm_pre_mix, norm_post_mix,
           norm_pre_ffn, norm_post_ffn, w_in, conv_w, A_log, dt_bias, dn_norm_w, w_dn_out,
           w_at_out, w_out, w_ff1, w_ff2):
    f = lambda a: np.ascontiguousarray(np.asarray(a, dtype=np.float32))
    x_prompt, x_sample, c_prompt, c_sample = f(x_prompt), f(x_sample), f(c_prompt), f(c_sample)
    if "nc" not in _NC_CACHE:
        _NC_CACHE["nc"] = build_program()
    nc = _NC_CACHE["nc"]
    shared = {
        "consts": make_consts(), "lvlmask": make_lvlmask(),
        "w_ada": f(w_ada)[0], "b_ada": f(b_ada)[0].reshape(96, 128),
        "norms": np.concatenate([f(norm_pre_mix)[0], f(norm_post_mix)[0], f(norm_pre_ffn)[0], f(norm_post_ffn)[0]]).reshape(64, 128),
        "w_in": f(w_in)[0], "conv_w": f(conv_w)[0].reshape(120, 128), "A_log": f(A_log)[0].reshape(1, 16),
        "dt_bias": f(dt_bias)[0].reshape(1, 16), "dn_norm_w": f(dn_norm_w)[0].reshape(1, 128),
        "w_dn_out": f(w_dn_out)[0], "w_at_out": f(w_at_out)[0], "w_out": f(w_out)[0],
        "w_ff1": f(w_ff1)[0], "w_ff2": f(w_ff2)[0],
    }
    in_maps = []
    for core in range(8):
        xs = np.zeros((T, D), np.float32)
        cs = np.zeros((NSEG, D), np.float32)
        sg = np.zeros((128, 16), np.float32)
        if core < 2:
            xs[:] = x_prompt[core]
            cs[:] = c_prompt[core][None, :]
            for s in range(NSEG):
                sg[:, s] = 1.0 if s > 0 else 0.0
                sg[:, 4 + s] = 1.0 if s < NSEG - 1 else 0.0
                sg[:, 8 + s] = s * SEG
        else:
            seqs = SAMPLE_MAP[core]
            for s in range(NSEG):
                b = seqs[s] if s < len(seqs) else seqs[0]
                xs[s * SEG:(s + 1) * SEG] = x_sample[b]
                cs[s] = c_sample[b]
        m = dict(shared)
        m["x"] = xs
        m["c"] = cs.reshape(NSEG * NJ, 128)
        m["segf"] = sg
        in_maps.append(m)
    if KSCOPES:
        _NC_CACHE["in_maps"] = in_maps
        res = run_bass_kernel_spmd(nc, in_maps, core_ids=list(range(8)), trace=True)
        _NC_CACHE["res"] = res
    else:
        res = run_bass_kernel_spmd(nc, in_maps, core_ids=list(range(8)))
    if KDEBUG:
        _NC_CACHE["res"] = res
    y_prompt = np.stack([np.asarray(res.results[c]["y"], dtype=np.float32) for c in range(2)])
    y_sample = np.zeros_like(x_sample)
    for core, seqs in SAMPLE_MAP.items():
        yc = np.asarray(res.results[core]["y"], dtype=np.float32)
        for s, b in enumerate(seqs):
            y_sample[b] = yc[s * SEG:(s + 1) * SEG]
    return (y_prompt, y_sample)
```
